# Optimizing a Trainium2 kernel written in Bass

```python
import math
import jax, jax.numpy as jnp
from jax import lax
import numpy as np

D_MODEL = 2048
BATCH = 4
SEQ = 4096
DEPTH = 2

GRID_W = 64
CTX_LEN = 256

ATT_HEADS = 8
ATT_KV_HEADS = 2
HEAD_DIM = 128
WINDOW = 128
ATT_BLOCK = 128
ROPE_BASE = 10000.0
SSM_WIDTH = 512
SSM_GROUP = 16
SSM_GROUPS = SSM_WIDTH // SSM_GROUP
SSM_STATE = 64
GLA_HEADS = 4
GLA_DK = 64
GLA_DV = 128
GLA_RANK = 16
GLA_TAU = 16.0
GLA_CHUNK = 64
N_EXPERTS = 16
EXPERT_FF = 2048
CAPACITY_FACTOR = 2
N_MOD = 6
ALPHA = (2 * DEPTH) ** 0.25
BETA = (8 * DEPTH) ** -0.25
LN_EPS = 1e-6
NEG_INF = -1e30

ATT_Q_W = ATT_HEADS * HEAD_DIM
ATT_KV_W = ATT_KV_HEADS * HEAD_DIM
GLA_QK_W = GLA_HEADS * GLA_DK
GLA_V_W = GLA_HEADS * GLA_DV
IN_SIZES = (ATT_Q_W, ATT_KV_W, ATT_KV_W, SSM_WIDTH, GLA_QK_W, GLA_QK_W, GLA_V_W, GLA_V_W, 2 * GLA_RANK)
IN_SPLITS = tuple(int(s) for s in np.cumsum(IN_SIZES)[:-1])
N_IN = sum(IN_SIZES)
MIX_WIDTH = ATT_Q_W + SSM_WIDTH + GLA_V_W

kernel_name = 'hybrid_s5_swa_gla_ec_moe_diffusion_trunk'


def layer_norm(x, g=None, b=None):
    xf = x.astype(jnp.float32)
    mu = jnp.mean(xf, axis=-1, keepdims=True)
    var = jnp.mean(jnp.square(xf - mu), axis=-1, keepdims=True)
    y = (xf - mu) * lax.rsqrt(var + LN_EPS)
    if g is not None:
        y = y * g + b
    return y.astype(x.dtype)


def rope_rotate(x, ang):
    m = x.shape[-1] // 2
    x1, x2 = x[..., :m], x[..., m:]
    cos = jnp.cos(ang)[:, None, :]
    sin = jnp.sin(ang)[:, None, :]
    return jnp.concatenate([x1 * cos - x2 * sin, x2 * cos + x1 * sin], axis=-1)


def axial_rope(x, rows, cols):
    half = x.shape[-1] // 2
    nf = half // 2
    inv = ROPE_BASE ** (-jnp.arange(nf, dtype=jnp.float32) / nf)
    ang_r = rows.astype(jnp.float32)[:, None] * inv
    ang_c = cols.astype(jnp.float32)[:, None] * inv
    xf = x.astype(jnp.float32)
    out = jnp.concatenate([rope_rotate(xf[..., :half], ang_r), rope_rotate(xf[..., half:], ang_c)], axis=-1)
    return out.astype(x.dtype)


def window_attention(q, k, v, kc, vc, sink):
    Bt, L, Hq, Dh = q.shape
    Hkv = k.shape[2]
    G = Hq // Hkv
    W = ATT_BLOCK
    nb = L // W
    Lc = kc.shape[1]
    qb = (q * (Dh ** -0.5)).reshape(Bt, nb, W, Hkv, G, Dh)

    def band(t):
        tp = jnp.pad(t, ((0, 0), (W, W), (0, 0), (0, 0))).reshape(Bt, nb + 2, W, Hkv, Dh)
        return jnp.concatenate([tp[:, :-2], tp[:, 1:-1], tp[:, 2:]], axis=2)

    kw, vw = band(k), band(v)
    s_loc = jnp.einsum('bnqhgd,bnkhd->bnhgqk', qb, kw).astype(jnp.float32)
    qi = jnp.arange(W)
    kj = jnp.arange(3 * W)
    rel = qi[:, None] + W - kj[None, :]
    key_pos = jnp.arange(nb)[:, None] * W - W + kj[None, :]
    valid = (jnp.abs(rel) <= WINDOW)[None] & ((key_pos >= 0) & (key_pos < L))[:, None, :]
    s_loc = jnp.where(valid[None, :, None, None], s_loc, NEG_INF)
    s_ctx = jnp.einsum('bnqhgd,bchd->bnhgqc', qb, kc).astype(jnp.float32)
    s_sink = jnp.broadcast_to(sink.astype(jnp.float32).reshape(Hkv, G)[None, None, :, :, None, None],
                              s_ctx.shape[:-1] + (1,))
    p = jax.nn.softmax(jnp.concatenate([s_loc, s_ctx, s_sink], axis=-1), axis=-1)
    o = (jnp.einsum('bnhgqk,bnkhd->bnqhgd', p[..., :3 * W].astype(vw.dtype), vw)
         + jnp.einsum('bnhgqc,bchd->bnqhgd', p[..., 3 * W:3 * W + Lc].astype(vc.dtype), vc))
    return o.reshape(Bt, L, Hq * Dh)


def ctx_attention(qc, kc, vc, sink):
    Bt, Lc, Hq, Dh = qc.shape
    Hkv = kc.shape[2]
    G = Hq // Hkv
    qg = (qc * (Dh ** -0.5)).reshape(Bt, Lc, Hkv, G, Dh)
    s = jnp.einsum('bqhgd,bkhd->bhgqk', qg, kc).astype(jnp.float32)
    s_sink = jnp.broadcast_to(sink.astype(jnp.float32).reshape(Hkv, G)[None, :, :, None, None], s.shape[:-1] + (1,))
    p = jax.nn.softmax(jnp.concatenate([s, s_sink], axis=-1), axis=-1)[..., :Lc]
    o = jnp.einsum('bhgqk,bkhd->bqhgd', p.astype(vc.dtype), vc)
    return o.reshape(Bt, Lc, Hq * Dh)


def _linrec_combine(left, right):
    a1, b1 = left
    a2, b2 = right
    return a1 * a2, a2 * b1 + b2


def s5_scan(u, lam_re, lam_im, log_dt, b_re, b_im, c_re, c_im, h0, reverse, readout):
    L = u.shape[1]
    lam = lax.complex(lam_re.astype(jnp.float32), lam_im.astype(jnp.float32))
    lam_dt = lam * jnp.exp(log_dt.astype(jnp.float32))[:, None]
    lam_bar = jnp.exp(lam_dt)
    b_bar = ((lam_bar - 1.0) / lam)[..., None] * lax.complex(b_re.astype(jnp.float32), b_im.astype(jnp.float32))
    bu = jnp.einsum('blgh,gph->blgp', u.astype(jnp.float32).astype(jnp.complex64), b_bar)
    a = jnp.broadcast_to(lam_bar, bu.shape)
    _, h = lax.associative_scan(_linrec_combine, (a, bu), axis=1, reverse=reverse)
    if h0 is not None:
        steps = jnp.arange(1, L + 1, dtype=jnp.float32)
        if reverse:
            steps = steps[::-1]
        h = h + jnp.exp(lam_dt[None] * steps[:, None, None])[None] * h0[:, None]
    h_last = h[:, 0] if reverse else h[:, -1]
    if not readout:
        return None, h_last
    c_mat = lax.complex(c_re.astype(jnp.float32), c_im.astype(jnp.float32))
    y = jnp.real(jnp.einsum('blgp,ghp->blgh', h, c_mat))
    return y, h_last


def s5_readout(y, u, p):
    Bt, L = y.shape[:2]
    z = jax.nn.gelu(y.reshape(Bt, L, SSM_WIDTH) + p['ssm_d'] * u.astype(jnp.float32))
    return z * jax.nn.sigmoid(z @ p['ssm_w_glu'] + p['ssm_b_glu'])


def s5_mixer(u, uc, p, need_ctx):
    Bt, L, _ = u.shape
    Lc = uc.shape[1]
    ug = u.astype(jnp.float32).reshape(Bt, L, SSM_GROUPS, SSM_GROUP)
    ucg = uc.astype(jnp.float32).reshape(Bt, Lc, SSM_GROUPS, SSM_GROUP)

    def dir_args(d):
        return (p['ssm_lam_re'][d], p['ssm_lam_im'][d], p['ssm_log_dt'][d], p['ssm_b_re'][d],
                p['ssm_b_im'][d], p['ssm_c_re'][d], p['ssm_c_im'][d])

    yc_f, hc_f = s5_scan(ucg, *dir_args(0), None, False, need_ctx)
    yc_b, hc_b = s5_scan(ucg, *dir_args(1), None, True, need_ctx)
    y_f, _ = s5_scan(ug, *dir_args(0), hc_f, False, True)
    y_b, _ = s5_scan(ug, *dir_args(1), hc_b, True, True)
    out = s5_readout(y_f + y_b, u, p)
    out_c = s5_readout(yc_f + yc_b, uc, p) if need_ctx else None
    return out, out_c


def gla_chunked(q, k, v, g, s0):
    Bt, L, H, K = q.shape
    V = v.shape[-1]
    T = GLA_CHUNK
    n = L // T
    qc = q.reshape(Bt, n, T, H, K)
    kc = k.reshape(Bt, n, T, H, K)
    vc = v.reshape(Bt, n, T, H, V)
    b = jnp.cumsum(g.reshape(Bt, n, T, H, K), axis=2)
    b_last = b[:, :, -1]
    q_in = qc * jnp.exp(b)
    attn = jnp.einsum('bnthk,bnshk->bnhts', q_in, kc * jnp.exp(-b))
    attn = jnp.where(jnp.tril(jnp.ones((T, T), dtype=bool)), attn, 0.0)
    o = jnp.einsum('bnhts,bnshv->bnthv', attn, vc)
    upd = jnp.einsum('bnshk,bnshv->bnhkv', kc * jnp.exp(b_last[:, :, None] - b), vc)
    decay = jnp.exp(b_last)
    if s0 is None:
        s0 = jnp.zeros((Bt, H, K, V), jnp.float32)

    def step(s, inp):
        dec, du = inp
        return dec[..., None] * s + du, s

    s_fin, s_prev = lax.scan(step, s0, (jnp.moveaxis(decay, 1, 0), jnp.moveaxis(upd, 1, 0)))
    o = o + jnp.einsum('bnthk,nbhkv->bnthv', q_in, s_prev)
    return o.reshape(Bt, L, H, V), s_fin


def gla_scan(q, k, v, g, s0, reverse):
    if not reverse:
        return gla_chunked(q, k, v, g, s0)
    fl = lambda t: jnp.flip(t, axis=1)
    o, s = gla_chunked(fl(q), fl(k), fl(v), fl(g), s0)
    return fl(o), s


def gla_prepare(q, k, v, z, p):
    Bt, L = q.shape[:2]
    qh = q.astype(jnp.float32).reshape(Bt, L, GLA_HEADS, GLA_DK) * (GLA_DK ** -0.5)
    kh = k.astype(jnp.float32).reshape(Bt, L, GLA_HEADS, GLA_DK)
    vh = v.astype(jnp.float32).reshape(Bt, L, GLA_HEADS, GLA_DV)
    zf = z.astype(jnp.float32)
    gs = [(jax.nn.log_sigmoid(zf[..., d * GLA_RANK:(d + 1) * GLA_RANK] @ p['gla_w_gate'][d] + p['gla_b_gate'][d])
           / GLA_TAU).reshape(Bt, L, GLA_HEADS, GLA_DK) for d in range(2)]
    return qh, kh, vh, gs


def gla_readout(o, r, p):
    Bt, L = o.shape[:2]
    o = o * lax.rsqrt(jnp.mean(jnp.square(o), axis=-1, keepdims=True) + LN_EPS) * p['gla_norm_g']
    return o.reshape(Bt, L, GLA_V_W) * jax.nn.silu(r.astype(jnp.float32))


def gla_mixer(q, k, v, r, z, qc_, kc_, vc_, rc, zc, p, need_ctx):
    qh, kh, vh, gs = gla_prepare(q, k, v, z, p)
    qch, kch, vch, gcs = gla_prepare(qc_, kc_, vc_, zc, p)
    oc_f, sc_f = gla_scan(qch, kch, vch, gcs[0], None, False)
    oc_b, sc_b = gla_scan(qch, kch, vch, gcs[1], None, True)
    o_f, _ = gla_scan(qh, kh, vh, gs[0], sc_f, False)
    o_b, _ = gla_scan(qh, kh, vh, gs[1], sc_b, True)
    out = gla_readout(o_f + o_b, r, p)
    out_c = gla_readout(oc_f + oc_b, rc, p) if need_ctx else None
    return out, out_c


def mixer_block(h, hc, p, rows, cols, need_ctx):
    Bt, L, _ = h.shape
    Lc = hc.shape[1]
    aq, ak, av, su, gq, gk, gv, gr, gz = jnp.split(h @ p['w_in'], IN_SPLITS, axis=-1)
    caq, cak, cav, csu, cgq, cgk, cgv, cgr, cgz = jnp.split(hc @ p['w_in'], IN_SPLITS, axis=-1)
    q = axial_rope(aq.reshape(Bt, L, ATT_HEADS, HEAD_DIM), rows, cols)
    k = axial_rope(ak.reshape(Bt, L, ATT_KV_HEADS, HEAD_DIM), rows, cols)
    v = av.reshape(Bt, L, ATT_KV_HEADS, HEAD_DIM)
    kc = cak.reshape(Bt, Lc, ATT_KV_HEADS, HEAD_DIM)
    vc = cav.reshape(Bt, Lc, ATT_KV_HEADS, HEAD_DIM)
    att = window_attention(q, k, v, kc, vc, p['attn_sink'])
    ssm, ssm_c = s5_mixer(su, csu, p, need_ctx)
    gla, gla_c = gla_mixer(gq, gk, gv, gr, gz, cgq, cgk, cgv, cgr, cgz, p, need_ctx)
    mix = jnp.concatenate([att.astype(h.dtype), ssm.astype(h.dtype), gla.astype(h.dtype)], axis=-1)
    if not need_ctx:
        return mix, None
    att_c = ctx_attention(caq.reshape(Bt, Lc, ATT_HEADS, HEAD_DIM), kc, vc, p['attn_sink'])
    mix_c = jnp.concatenate([att_c.astype(hc.dtype), ssm_c.astype(hc.dtype), gla_c.astype(hc.dtype)], axis=-1)
    return mix, mix_c


def expert_choice_ffn(h, router, w_gate, w_up, w_down):
    Bt, N, _ = h.shape
    cap = CAPACITY_FACTOR * N // N_EXPERTS
    aff = jax.nn.softmax(jnp.einsum('bnd,de->bne', h, router).astype(jnp.float32), axis=-1)
    g, idx = lax.top_k(jnp.swapaxes(aff, 1, 2), cap)
    bidx = jnp.arange(Bt)[:, None, None]
    xe = h[bidx, idx]
    hid = jax.nn.silu(jnp.einsum('becd,edf->becf', xe, w_gate)) * jnp.einsum('becd,edf->becf', xe, w_up)
    ye = jnp.einsum('becf,efd->becd', hid, w_down) * g[..., None].astype(h.dtype)
    return jnp.zeros_like(h).at[bidx, idx].add(ye)


def trunk_layer(x, xc, mod, mod_c, p, rows, cols, need_ctx):
    sh1, sc1, gt1, sh2, sc2, gt2 = jnp.split(mod, N_MOD, axis=-1)
    csh1, csc1, cgt1, csh2, csc2, cgt2 = jnp.split(mod_c, N_MOD, axis=-1)
    h = layer_norm(x) * (1.0 + sc1) + sh1
    hc = layer_norm(xc) * (1.0 + csc1) + csh1
    mix, mix_c = mixer_block(h, hc, p, rows, cols, need_ctx)
    x = layer_norm(ALPHA * x + gt1 * (mix @ p['w_out']), p['ln1_g'], p['ln1_b'])
    h2 = layer_norm(x) * (1.0 + sc2) + sh2
    ffn = expert_choice_ffn(h2, p['router'], p['exp_w_gate'], p['exp_w_up'], p['exp_w_down'])
    x = layer_norm(ALPHA * x + gt2 * ffn, p['ln2_g'], p['ln2_b'])
    if not need_ctx:
        return x, None
    xc = layer_norm(ALPHA * xc + cgt1 * (mix_c @ p['w_out']), p['ln1_g'], p['ln1_b'])
    hc2 = layer_norm(xc) * (1.0 + csc2) + csh2
    ffn_c = expert_choice_ffn(hc2, p['router'], p['exp_w_gate'], p['exp_w_up'], p['exp_w_down'])
    xc = layer_norm(ALPHA * xc + cgt2 * ffn_c, p['ln2_g'], p['ln2_b'])
    return x, xc


def setup_inputs(seed: int = 0) -> dict:
    key = jax.random.key(seed)
    ks = jax.random.split(key, 32)
    f32 = jnp.float32
    D = D_MODEL
    G, P, H = SSM_GROUPS, SSM_STATE, SSM_GROUP

    def nrm(k, shape, s):
        return jax.random.normal(k, shape, f32) * s

    lam_im_base = jnp.pi * jnp.arange(P, dtype=f32)
    return {
        'x': nrm(ks[0], (BATCH, SEQ, D), 1.0),
        'c': nrm(ks[1], (BATCH, D), 1.0),
        'ctx': nrm(ks[2], (BATCH, CTX_LEN, D), 1.0),
        'c_ctx': nrm(ks[3], (D,), 1.0),
        'w_ada': nrm(ks[4], (DEPTH, D, N_MOD * D), 0.5 * D ** -0.5),
        'b_ada': nrm(ks[5], (DEPTH, N_MOD * D), 0.01),
        'w_in': nrm(ks[6], (DEPTH, D, N_IN), D ** -0.5),
        'attn_sink': nrm(ks[7], (DEPTH, ATT_HEADS), 0.5),
        'ssm_lam_re': -0.5 + nrm(ks[8], (DEPTH, 2, G, P), 0.01),
        'ssm_lam_im': lam_im_base + nrm(ks[9], (DEPTH, 2, G, P), 0.01),
        'ssm_log_dt': jax.random.uniform(ks[10], (DEPTH, 2, G), f32, math.log(1e-3), math.log(1e-1)),
        'ssm_b_re': nrm(ks[11], (DEPTH, 2, G, P, H), (2 * H) ** -0.5),
        'ssm_b_im': nrm(ks[12], (DEPTH, 2, G, P, H), (2 * H) ** -0.5),
        'ssm_c_re': nrm(ks[13], (DEPTH, 2, G, H, P), P ** -0.5),
        'ssm_c_im': nrm(ks[14], (DEPTH, 2, G, H, P), P ** -0.5),
        'ssm_d': nrm(ks[15], (DEPTH, SSM_WIDTH), 1.0),
        'ssm_w_glu': nrm(ks[16], (DEPTH, SSM_WIDTH, SSM_WIDTH), SSM_WIDTH ** -0.5),
        'ssm_b_glu': nrm(ks[17], (DEPTH, SSM_WIDTH), 0.01),
        'gla_w_gate': nrm(ks[18], (DEPTH, 2, GLA_RANK, GLA_QK_W), GLA_RANK ** -0.5),
        'gla_b_gate': nrm(ks[19], (DEPTH, 2, GLA_QK_W), 0.1),
        'gla_norm_g': 1.0 + nrm(ks[20], (DEPTH, GLA_DV), 0.01),
        'w_out': nrm(ks[21], (DEPTH, MIX_WIDTH, D), BETA * MIX_WIDTH ** -0.5),
        'ln1_g': 1.0 + nrm(ks[22], (DEPTH, D), 0.01),
        'ln1_b': nrm(ks[23], (DEPTH, D), 0.01),
        'ln2_g': 1.0 + nrm(ks[24], (DEPTH, D), 0.01),
        'ln2_b': nrm(ks[25], (DEPTH, D), 0.01),
        'router': nrm(ks[26], (DEPTH, D, N_EXPERTS), D ** -0.5),
        'exp_w_gate': nrm(ks[27], (DEPTH, N_EXPERTS, D, EXPERT_FF), D ** -0.5),
        'exp_w_up': nrm(ks[28], (DEPTH, N_EXPERTS, D, EXPERT_FF), D ** -0.5),
        'exp_w_down': nrm(ks[29], (DEPTH, N_EXPERTS, EXPERT_FF, D), BETA * EXPERT_FF ** -0.5),
    }


def reference(x, c, ctx, c_ctx, w_ada, b_ada, w_in, attn_sink, ssm_lam_re, ssm_lam_im, ssm_log_dt,
              ssm_b_re, ssm_b_im, ssm_c_re, ssm_c_im, ssm_d, ssm_w_glu, ssm_b_glu, gla_w_gate,
              gla_b_gate, gla_norm_g, w_out, ln1_g, ln1_b, ln2_g, ln2_b, router, exp_w_gate,
              exp_w_up, exp_w_down):
    L = x.shape[1]
    ROWS = L // GRID_W
    rows = jnp.repeat(jnp.arange(ROWS), GRID_W)
    cols = jnp.tile(jnp.arange(GRID_W), ROWS)
    xc = ctx
    for l in range(DEPTH):
        need_ctx = l < DEPTH - 1
        p = {
            'w_in': w_in[l], 'attn_sink': attn_sink[l],
            'ssm_lam_re': ssm_lam_re[l], 'ssm_lam_im': ssm_lam_im[l], 'ssm_log_dt': ssm_log_dt[l],
            'ssm_b_re': ssm_b_re[l], 'ssm_b_im': ssm_b_im[l], 'ssm_c_re': ssm_c_re[l], 'ssm_c_im': ssm_c_im[l],
            'ssm_d': ssm_d[l], 'ssm_w_glu': ssm_w_glu[l], 'ssm_b_glu': ssm_b_glu[l],
            'gla_w_gate': gla_w_gate[l], 'gla_b_gate': gla_b_gate[l], 'gla_norm_g': gla_norm_g[l],
            'w_out': w_out[l], 'ln1_g': ln1_g[l], 'ln1_b': ln1_b[l], 'ln2_g': ln2_g[l], 'ln2_b': ln2_b[l],
            'router': router[l], 'exp_w_gate': exp_w_gate[l], 'exp_w_up': exp_w_up[l], 'exp_w_down': exp_w_down[l],
        }
        mod = (jax.nn.silu(c) @ w_ada[l] + b_ada[l])[:, None, :]
        mod_c = (jax.nn.silu(c_ctx) @ w_ada[l] + b_ada[l])[None, None, :]
        x, xc = trunk_layer(x, xc, mod, mod_c, p, rows, cols, need_ctx)
    return x
```

```python
import numpy as np
import concourse.bass as bass
import concourse.mybir as mybir
from concourse.bass_utils import run_bass_kernel_spmd

F32 = mybir.dt.float32
BF16 = mybir.dt.bfloat16
I32 = mybir.dt.int32
ALU = mybir.AluOpType
AF = mybir.ActivationFunctionType
AX = mybir.AxisListType

PE, ACT, DVE, POOL, SP = "pe", "act", "dve", "pool", "sp"
ENGS = (PE, ACT, DVE, POOL, SP)
NDMASEM = 8


class Prog:
    def __init__(self):
        self.nc = bass.Bass("TRN2", target_bir_lowering=False)
        self.ops = {e: [] for e in ENGS}
        self.state = {}
        self.dmas = []
        self.ndma = {e: 0 for e in ENGS}
        self.sb_off = 20608
        self.sb_hi = 0
        self.nname = 0
        self.psum = []
        self.sb_cap = 229376

    def dram(self, name, shape, dtype, kind="Internal"):
        return self.nc.dram_tensor(name, list(shape), dtype, kind=kind)

    def sb(self, shape, dtype, name=None):
        size = int(np.prod(shape[1:])) * mybir.dt.size(dtype) if hasattr(mybir.dt, "size") else None
        if size is None:
            size = int(np.prod(shape[1:])) * {F32: 4, BF16: 2, I32: 4}[dtype]
        size = (size + 31) // 32 * 32
        self.nname += 1
        nm = (name or "t") + "_%d" % self.nname
        t = self.nc.alloc_sbuf_tensor_at(nm, list(shape), dtype, offset=self.sb_off)
        self.sb_off += size
        assert self.sb_off <= self.sb_cap, ("SBUF overflow", nm, self.sb_off)
        self.sb_hi = max(self.sb_hi, self.sb_off)
        return t

    def sb_mark(self):
        return self.sb_off

    def sb_reset(self, mark):
        self.sb_off = mark

    @staticmethod
    def _conf(a, b):
        n = min(len(a), len(b))
        return a[:n] == b[:n]

    def _deps(self, reads, writes):
        deps = set()
        for k in reads:
            root = self.state.setdefault(k[0], {})
            for k2, st in root.items():
                if self._conf(k, k2) and st[0] is not None:
                    deps.add(st[0])
        for k in writes:
            root = self.state.setdefault(k[0], {})
            for k2, st in root.items():
                if self._conf(k, k2):
                    if st[0] is not None:
                        deps.add(st[0])
                    deps.update(st[1])
        return deps

    def _commit(self, ev, reads, writes):
        for k in reads:
            root = self.state[k[0]]
            st = root.setdefault(k, [None, []])
            st[1].append(ev)
        for k in writes:
            root = self.state[k[0]]
            for k2 in [k2 for k2 in root if len(k2) > len(k) and self._conf(k, k2)]:
                del root[k2]
            root[k] = [ev, []]

    @staticmethod
    def _norm(keys):
        out = []
        for k in keys:
            if isinstance(k, str):
                k = (k,)
            assert isinstance(k[0], str), k
            out.append(tuple(k))
        return out

    def I(self, eng, name, reads, writes, *a, **kw):
        return self.op(eng, lambda e: getattr(e, name)(*a, **kw), reads, writes)

    def D(self, q, reads, writes, **kw):
        return self.dma(q, lambda e: e.dma_start(**kw), reads, writes)

    def op(self, eng, fn, reads=(), writes=()):
        reads = self._norm(reads)
        writes = self._norm(writes)
        deps = self._deps(reads, writes)
        idx = len(self.ops[eng])
        ev = ("c", eng, idx)
        self.ops[eng].append(dict(fn=fn, waits=deps, dma=None, signal=False))
        self._commit(ev, reads, writes)
        return ev

    def dma(self, q, fn, reads=(), writes=()):
        reads = self._norm(reads)
        writes = self._norm(writes)
        deps = self._deps(reads, writes)
        j = self.ndma[q]
        self.ndma[q] += 1
        si, val = j % NDMASEM, 16 * (j // NDMASEM + 1)
        did = len(self.dmas)
        self.dmas.append((q, si, val))
        if j >= NDMASEM:
            deps.add(("d", self._last_dma[(q, si)]))
        if not hasattr(self, "_last_dma"):
            self._last_dma = {}
        self._last_dma[(q, si)] = did
        ev = ("d", did)
        self.ops[q].append(dict(fn=fn, waits=deps, dma=did, signal=False))
        self._commit(ev, reads, writes)
        return ev

    def barrier(self):
        evs = set()
        for e in ENGS:
            for i in range(len(self.ops[e]) - 1, -1, -1):
                o = self.ops[e][i]
                if o["fn"] is not None and o["dma"] is None:
                    evs.add(("c", e, i))
                    break
        if hasattr(self, "_last_dma"):
            for did in self._last_dma.values():
                evs.add(("d", did))
        for e in ENGS:
            self.ops[e].append(dict(fn=None, waits=set(evs), dma=None, signal=False))
        self.state = {}

    def emit(self):
        nc = self.nc
        plan = {e: [] for e in ENGS}
        for e in ENGS:
            wc = {}
            wd = {}
            for i, o in enumerate(self.ops[e]):
                need_c, need_d = {}, {}
                for ev in o["waits"]:
                    if ev[0] == "c":
                        _, se, si_ = ev
                        if se == e and e == PE:
                            continue
                        if se == e and si_ >= i:
                            continue
                        if wc.get(se, -1) >= si_:
                            continue
                        need_c[se] = max(need_c.get(se, -1), si_)
                    else:
                        q, si_, val = self.dmas[ev[1]]
                        if wd.get((q, si_), 0) >= val:
                            continue
                        need_d[(q, si_)] = max(need_d.get((q, si_), 0), val)
                for se, si_ in need_c.items():
                    wc[se] = si_
                    self.ops[se][si_]["signal"] = True
                for k, v in need_d.items():
                    wd[k] = v
                plan[e].append((need_c, need_d))
        semval = {}
        for e in ENGS:
            c = 0
            for i, o in enumerate(self.ops[e]):
                if o["signal"]:
                    c += 1
                    semval[(e, i)] = c
            self.nsig = getattr(self, "nsig", {})
            self.nsig[e] = c
        from contextlib import ExitStack
        with ExitStack() as es:
            csem = {e: es.enter_context(nc.semaphore("c_" + e)) for e in ENGS}
            dsem = {(q, s): es.enter_context(nc.semaphore("d_%s_%d" % (q, s)))
                    for q in ENGS for s in range(NDMASEM) if self.ndma[q] > s}
            block = es.enter_context(nc.Block())

            def run(e, engobj):
                for i, o in enumerate(self.ops[e]):
                    need_c, need_d = plan[e][i]
                    for se, si_ in need_c.items():
                        engobj.wait_ge(csem[se], semval[(se, si_)])
                    for k, v in need_d.items():
                        engobj.wait_ge(dsem[k], v)
                    if o["fn"] is None:
                        continue
                    inst = o["fn"](engobj)
                    if o["dma"] is not None:
                        q, s, v = self.dmas[o["dma"]]
                        inst.then_inc(dsem[(q, s)], 16)
                    elif o["signal"]:
                        inst.then_inc(csem[e], 1)

            @block.tensor
            def _(eng):
                run(PE, eng)

            @block.scalar
            def _(eng):
                run(ACT, eng)

            @block.vector
            def _(eng):
                run(DVE, eng)

            @block.gpsimd
            def _(eng):
                run(POOL, eng)

            @block.sync
            def _(eng):
                run(SP, eng)
        return nc

    def finish(self):
        self.barrier()


NT = 34
NTOK = 4352
D = 2048
NIN = 3616


def alt(i):
    return ACT if i % 2 == 0 else DVE


def copy_op(P, eng, out, in_, reads, writes):
    if eng == ACT:
        P.op(ACT, lambda e: e.copy(out=out, in_=in_), reads, writes)
    else:
        P.op(eng, lambda e: e.tensor_copy(out=out, in_=in_), reads, writes)


class Ctx:
    pass


def setup_consts(P, C, cst):
    C.ident = P.sb([128, 128], F32, "ident")
    C.identb = P.sb([128, 128], BF16, "identb")
    P.dma(SP, lambda e: e.dma_start(out=C.ident[:], in_=cst["ident"][:, :]), writes=["ident"])
    P.op(DVE, lambda e: e.tensor_copy(out=C.identb[:], in_=C.ident[:]), reads=["ident"], writes=["identb"])
    C.ps = [P.nc.alloc_psum_tensor("psb%d" % i, [128, 512], F32) for i in range(8)]
    C.nbank = 0

    def bank():
        b = C.nbank % 8
        C.nbank += 1
        return b
    C.bank = bank


def phase_mod(P, C, c_in, w_ada, b_ada, MODS, R=5, NBLK=3):
    m0 = P.sb_mark()
    cT = P.sb([128, 16, R], F32, "cT")
    for r in range(R):
        P.D(SP, [], [("cT", r)], out=cT[:, :, r], in_=c_in[r, :].rearrange("(kt p) -> p kt", p=128), allow_slow_non_contiguous=True)
    P.I(ACT, "activation", ["cT"], ["cT"], out=cT[:], in_=cT[:], func=AF.Silu)
    wts = [P.sb([128, 16, 512], F32, "wada") for _ in range(2)]
    bts = [P.sb([R, 512], F32, "bada") for _ in range(2)]
    rts = [P.sb([R, 512], F32, "rada") for _ in range(2)]
    it = 0
    for l in range(2):
        for nb in range(NBLK):
            b = it % 2
            it += 1
            for kh in range(2):
                P.D(SP if kh == 0 else ACT, [], [("wada", b, kh)], out=wts[b][:, kh * 8:(kh + 1) * 8, :],
                    in_=w_ada[l, kh * 1024:(kh + 1) * 1024, nb * 512:(nb + 1) * 512].rearrange("(kt p) n -> p kt n", p=128))
            P.D(SP, [], [("bada", b)], out=bts[b][:], in_=b_ada[l, nb * 512:(nb + 1) * 512].partition_broadcast(R))
            bk = C.bank()
            for kt in range(16):
                P.I(PE, "matmul", ["cT", ("wada", b)], [("ps", bk)], C.ps[bk][0:R, :], lhsT=cT[:, kt, :], rhs=wts[b][:, kt, :], start=(kt == 0), stop=(kt == 15))
            P.I(DVE, "tensor_tensor", [("bada", b)], [("ps", bk), ("rada", b)], out=rts[b][:], in0=C.ps[bk][0:R, :], in1=bts[b][:], op=ALU.add)
            P.D(SP, [("rada", b)], [("MODS", l, nb)], out=MODS[l, :, nb * 512:(nb + 1) * 512], in_=rts[b][:])
    P.barrier()
    P.sb_reset(m0)


def ln_stats(P, C, xt, key, st, mv, rs, tag):
    for j in range(4):
        P.op(DVE, lambda e, j=j: e.bn_stats(out=st[:, j, :], in_=xt[:, j * 512:(j + 1) * 512]),
             reads=[key], writes=[(tag + "st", j)])
    P.op(DVE, lambda e: e.bn_aggr(out=mv[:], in_=st[:].rearrange("p a b -> p (a b)")), reads=[tag + "st"], writes=[tag + "mv"])
    P.op(DVE, lambda e: e.tensor_scalar_add(out=rs[:, 0:1], in0=mv[:, 1:2], scalar1=1e-6), reads=[tag + "mv"], writes=[(tag + "rs", 0)])
    P.op(ACT, lambda e: e.activation(out=rs[:, 0:1], in_=rs[:, 0:1], func=AF.Ln), reads=[(tag + "rs", 0)], writes=[(tag + "rs", 0)])
    P.op(ACT, lambda e: e.activation(out=rs[:, 0:1], in_=rs[:, 0:1], func=AF.Exp, scale=-0.5), reads=[(tag + "rs", 0)], writes=[(tag + "rs", 0)])
    P.op(DVE, lambda e: e.scalar_tensor_tensor(out=rs[:, 1:2], in0=mv[:, 0:1], scalar=-1.0, in1=rs[:, 0:1], op0=ALU.mult, op1=ALU.mult),
         reads=[tag + "mv", (tag + "rs", 0)], writes=[(tag + "rs", 1)])


def load_modT(P, MODV, l, chunk, dst, key, plus1):
    for r in range(2):
        P.dma(SP, lambda e, r=r: e.dma_start(out=dst[:, r, :], in_=MODV[l, r, chunk * 2048:(chunk + 1) * 2048].rearrange("(kt p) -> p kt", p=128),
                                            allow_slow_non_contiguous=True), reads=[("MODV", l)], writes=[(key, r)])
    if plus1:
        P.op(DVE, lambda e: e.tensor_scalar_add(out=dst[:], in0=dst[:], scalar1=1.0), reads=[key], writes=[key])


def phase_inproj(P, C, l, X, PROJ, w_in, MODV):
    m0 = P.sb_mark()
    wbf = P.sb([128, 16, NIN], BF16, "wbf")
    for kt in range(16):
        for h in range(2):
            P.dma(POOL, lambda e, kt=kt, h=h: e.dma_start(out=wbf[:, kt, h * 1808:(h + 1) * 1808],
                                                         in_=w_in[l, kt * 128:(kt + 1) * 128, h * 1808:(h + 1) * 1808]),
                  writes=[("wbf", kt, h)])
    scT = P.sb([128, 2, 16], F32, "scT")
    shT = P.sb([128, 2, 16], F32, "shT")
    load_modT(P, MODV, l, 1, scT, "scT", True)
    load_modT(P, MODV, l, 0, shT, "shT", False)
    xts = [P.sb([128, D], F32, "xt") for _ in range(2)]
    xns = [P.sb([128, D], BF16, "xn") for _ in range(2)]
    hTs = [P.sb([128, 16, 128], BF16, "hT") for _ in range(2)]
    ots = [P.sb([128, NIN], F32, "ot") for _ in range(2)]
    st = P.sb([128, 4, 6], F32, "st")
    mv = P.sb([128, 2], F32, "mv")
    rs = P.sb([128, 2], F32, "rs")
    for i in range(NT):
        b = i % 2
        r = 1 if i < 2 else 0
        xt, xn, hT, ot = xts[b], xns[b], hTs[b], ots[b]
        P.dma(SP, lambda e, i=i, xt=xt: e.dma_start(out=xt[:], in_=X[i * 128:(i + 1) * 128, :]), reads=[("X", i)], writes=[("xt", b)])
        ln_stats(P, C, xt, ("xt", b), st, mv, rs, "ip")
        P.op(ACT, lambda e, xt=xt, xn=xn: e.activation(out=xn[:], in_=xt[:], func=AF.Identity, bias=rs[:, 1:2], scale=rs[:, 0:1]),
             reads=[("xt", b), "iprs"], writes=[("xn", b)])
        for kg in range(4):
            bk = C.bank()
            psb = C.ps[bk][:].bitcast(BF16)
            for j in range(4):
                kt = kg * 4 + j
                P.op(PE, lambda e, j=j, kt=kt, psb=psb, xn=xn: e.transpose(out=psb[:, j * 128:(j + 1) * 128], in_=xn[:, kt * 128:(kt + 1) * 128], identity=C.identb[:]),
                     reads=[("xn", b), "identb"], writes=[("ps", bk)])
            for j in range(4):
                kt = kg * 4 + j
                if j % 2 == 0:
                    P.op(ACT, lambda e, j=j, kt=kt, psb=psb, hT=hT, r=r: e.activation(out=hT[:, kt, :], in_=psb[:, j * 128:(j + 1) * 128], func=AF.Identity,
                                                                              bias=shT[:, r, kt:kt + 1], scale=scT[:, r, kt:kt + 1]),
                         reads=["scT", "shT"], writes=[("ps", bk), ("hT", b, kt)])
                else:
                    P.op(DVE, lambda e, j=j, kt=kt, psb=psb, hT=hT, r=r: e.tensor_scalar(out=hT[:, kt, :], in0=psb[:, j * 128:(j + 1) * 128],
                                                                                 scalar1=scT[:, r, kt:kt + 1], scalar2=shT[:, r, kt:kt + 1], op0=ALU.mult, op1=ALU.add),
                         reads=["scT", "shT"], writes=[("ps", bk), ("hT", b, kt)])
        for nb in range(8):
            n0 = nb * 512
            w = min(512, NIN - n0)
            bk = C.bank()
            for kt in range(16):
                P.op(PE, lambda e, kt=kt, bk=bk, n0=n0, w=w, hT=hT: e.matmul(C.ps[bk][:, 0:w], lhsT=hT[:, kt, :], rhs=wbf[:, kt, n0:n0 + w],
                                                                          start=(kt == 0), stop=(kt == 15)),
                     reads=[("hT", b), ("wbf", kt)], writes=[("ps", bk)])
            copy_op(P, alt(nb), ot[:, n0:n0 + w], C.ps[bk][:, 0:w], reads=[], writes=[("ps", bk), ("ot", b, nb)])
        P.dma(SP, lambda e, i=i, ot=ot: e.dma_start(out=PROJ[i * 128:(i + 1) * 128, :], in_=ot[:]), reads=[("ot", b)], writes=[("PROJ", i)])
    P.barrier()
    P.sb_reset(m0)


def phase_attn(P, C, l, PROJ, MIXT, sink, cst):
    m0 = P.sb_mark()
    kT = P.sb([128, 2, NTOK], BF16, "kT")
    vbf = P.sb([128, NT, 256], BF16, "vbf")
    onesb = P.sb([128, 128], BF16, "onesb")
    P.op(POOL, lambda e: e.memset(onesb[:], 1.0), writes=["onesb"])
    mk = []
    for nm in ("mprev", "mnext"):
        tf = P.sb([128, 512], F32, nm + "f")
        tb = P.sb([128, 512], BF16, nm + "b")
        P.dma(SP, lambda e, tf=tf, nm=nm: e.dma_start(out=tf[:], in_=cst[nm][:, :]), writes=[nm + "f"])
        P.op(DVE, lambda e, tf=tf, tb=tb: e.tensor_copy(out=tb[:], in_=tf[:]), reads=[nm + "f"], writes=[nm + "b"])
        mk.append(tb)
    esb = P.sb([128, 8], F32, "esb")
    P.dma(SP, lambda e: e.dma_start(out=esb[:], in_=sink[l, :].partition_broadcast(128)), writes=["esb"])
    P.op(ACT, lambda e: e.activation(out=esb[:], in_=esb[:], func=AF.Exp), reads=["esb"], writes=["esb"])
    cos = [P.sb([128, 128], F32, "cos") for _ in range(2)]
    sin = [P.sb([128, 128], F32, "sin") for _ in range(2)]

    def load_rope(i, b):
        P.dma(SP, lambda e: e.dma_start(out=cos[b][:], in_=cst["cos"][i - 2, :, :]), writes=[("cos", b)])
        P.dma(SP, lambda e: e.dma_start(out=sin[b][:], in_=cst["sin"][i - 2, :, :]), writes=[("sin", b)])

    def rope(x, xo, t, H, b, kx, ko, kt_):
        xv = x[:, 0:H * 128].rearrange("p (h a c d) -> p h a c d", h=H, a=2, c=2)
        ov = xo[:, 0:H * 128].rearrange("p (h a c d) -> p h a c d", h=H, a=2, c=2)
        tv = t[:, 0:H * 128].rearrange("p (h a c d) -> p h a c d", h=H, a=2, c=2)
        cv = cos[b][:].rearrange("p (a c d) -> p a c d", a=2, c=2)
        sv = sin[b][:].rearrange("p (a c d) -> p a c d", a=2, c=2)
        for h in range(H):
            P.op(POOL, lambda e, h=h: e.tensor_tensor(out=ov[:, h], in0=xv[:, h], in1=cv, op=ALU.mult),
                 reads=[kx, ("cos", b)], writes=[ko + (h,)])
            for c in range(2):
                P.op(DVE, lambda e, h=h, c=c: e.tensor_tensor(out=tv[:, h, :, c, :], in0=xv[:, h, :, 1 - c, :], in1=sv[:, :, c, :], op=ALU.mult),
                     reads=[kx, ("sin", b)], writes=[kt_ + (h, c)])
            P.op(DVE, lambda e, h=h: e.tensor_tensor(out=ov[:, h], in0=ov[:, h], in1=tv[:, h], op=ALU.add),
                 reads=[kt_ + (h,)], writes=[ko + (h,)])

    kin = [P.sb([128, 512], F32, "kin") for _ in range(2)]
    kro = [P.sb([128, 256], F32, "kro") for _ in range(2)]
    ktm = [P.sb([128, 256], F32, "ktm") for _ in range(2)]
    for i in range(NT):
        b = i % 2
        P.dma(SP, lambda e, i=i, b=b: e.dma_start(out=kin[b][:], in_=PROJ[i * 128:(i + 1) * 128, 1024:1536]),
              reads=[("PROJ", i)], writes=[("kin", b)])
        P.op(ACT, lambda e, i=i, b=b: e.copy(out=vbf[:, i, :], in_=kin[b][:, 256:512]), reads=[("kin", b)], writes=[("vbf", i)])
        if i >= 2:
            load_rope(i, b)
            rope(kin[b], kro[b], ktm[b], 2, b, ("kin", b), ("kro", b), ("ktm", b))
            src, skey = kro[b], ("kro", b)
        else:
            src, skey = kin[b], ("kin", b)
        bk = C.bank()
        for h in range(2):
            P.op(PE, lambda e, h=h, bk=bk, src=src: e.transpose(out=C.ps[bk][:, h * 128:(h + 1) * 128], in_=src[:, h * 128:(h + 1) * 128], identity=C.ident[:]),
                 reads=[skey, "ident"], writes=[("ps", bk)])
        P.op(DVE, lambda e, i=i, bk=bk: e.tensor_copy(out=kT[:, :, i * 128:(i + 1) * 128], in_=C.ps[bk][:, 0:256].rearrange("p (h t) -> p h t", h=2)),
             writes=[("ps", bk), ("kT", i)])
    qin = [P.sb([128, 1024], F32, "qin") for _ in range(2)]
    qro = [P.sb([128, 1024], F32, "qro") for _ in range(2)]
    qtm = [P.sb([128, 1024], F32, "qtm") for _ in range(2)]
    qT = [P.sb([128, 8, 128], BF16, "qT") for _ in range(2)]
    pT = [P.sb([128, 512], BF16, "pT") for _ in range(4)]
    den = [P.sb([128, 512], F32, "den") for _ in range(2)]
    oT = [P.sb([128, 4, 128], BF16, "oT") for _ in range(2)]
    npt = 0
    scale = 128.0 ** -0.5
    for iq in range(NT):
        b = iq % 2
        P.dma(SP, lambda e, iq=iq, b=b: e.dma_start(out=qin[b][:], in_=PROJ[iq * 128:(iq + 1) * 128, 0:1024]),
              reads=[("PROJ", iq)], writes=[("qin", b)])
        if iq >= 2:
            load_rope(iq, b)
            rope(qin[b], qro[b], qtm[b], 8, b, ("qin", b), ("qro", b), ("qtm", b))
            src, skey = qro[b], ("qro", b)
        else:
            src, skey = qin[b], ("qin", b)
        for g in range(2):
            bk = C.bank()
            for h in range(4):
                hh = g * 4 + h
                P.op(PE, lambda e, h=h, hh=hh, bk=bk, src=src: e.transpose(out=C.ps[bk][:, h * 128:(h + 1) * 128], in_=src[:, hh * 128:(hh + 1) * 128], identity=C.ident[:]),
                     reads=[skey, "ident"], writes=[("ps", bk)])
            copy_op(P, alt(g), qT[b][:, g * 4:(g + 1) * 4, :], C.ps[bk][:, :].rearrange("p (h t) -> p h t", h=4), reads=[], writes=[("ps", bk), ("qT", b, g)])
        if iq < 2:
            keys = [(0, None), (1, None)]
        else:
            keys = [(0, None), (1, None)]
            if iq - 1 >= 2:
                keys.append((iq - 1, 0))
            keys.append((iq, None))
            if iq + 1 < NT:
                keys.append((iq + 1, 1))
        for kvh in range(2):
            bo = C.bank()
            bd = C.bank()
            for n, (kt_, mi) in enumerate(keys):
                bs = C.bank()
                pb = npt % 4
                npt += 1
                P.op(PE, lambda e, kt_=kt_, bs=bs, kvh=kvh, b=b: e.matmul(C.ps[bs][:, :], lhsT=kT[:, kvh, kt_ * 128:(kt_ + 1) * 128],
                                                                        rhs=qT[b][:, kvh * 4:(kvh + 1) * 4, :].rearrange("p h t -> p (h t)"), start=True, stop=True),
                     reads=[("kT", kt_), ("qT", b, kvh)], writes=[("ps", bs)])
                P.op(ACT, lambda e, bs=bs, pb=pb: e.activation(out=pT[pb][:], in_=C.ps[bs][:, :], func=AF.Exp, scale=scale),
                     writes=[("ps", bs), ("pT", pb)])
                if mi is not None:
                    P.op(POOL, lambda e, pb=pb, mi=mi: e.tensor_tensor(out=pT[pb][:], in0=pT[pb][:], in1=mk[mi][:], op=ALU.mult),
                         reads=["mprevb", "mnextb"], writes=[("pT", pb)])
                st_, sp_ = (n == 0), (n == len(keys) - 1)
                P.op(PE, lambda e, kt_=kt_, bo=bo, kvh=kvh, pb=pb, st_=st_, sp_=sp_: e.matmul(C.ps[bo][:, :], lhsT=vbf[:, kt_, kvh * 128:(kvh + 1) * 128], rhs=pT[pb][:],
                                                                              start=st_, stop=sp_),
                     reads=[("vbf", kt_), ("pT", pb)], writes=[("ps", bo)])
                P.op(PE, lambda e, bd=bd, pb=pb, st_=st_, sp_=sp_: e.matmul(C.ps[bd][:, :], lhsT=onesb[:], rhs=pT[pb][:], start=st_, stop=sp_),
                     reads=["onesb", ("pT", pb)], writes=[("ps", bd)])
            db = kvh
            for h in range(4):
                hh = kvh * 4 + h
                P.op(DVE, lambda e, h=h, hh=hh, bd=bd, db=db: e.tensor_scalar_add(out=den[db][:, h * 128:(h + 1) * 128], in0=C.ps[bd][:, h * 128:(h + 1) * 128], scalar1=esb[:, hh:hh + 1]),
                     reads=["esb"], writes=[("ps", bd), ("den", db, h)])
            P.op(DVE, lambda e, db=db: e.reciprocal(out=den[db][:], in_=den[db][:]), reads=[("den", db)], writes=[("den", db)])
            P.op(DVE, lambda e, db=db, bo=bo: e.tensor_tensor(out=oT[db][:].rearrange("p h t -> p (h t)"), in0=C.ps[bo][:, :], in1=den[db][:], op=ALU.mult),
                 reads=[("den", db)], writes=[("ps", bo), ("oT", db)])
            P.dma(SP, lambda e, db=db, kvh=kvh, iq=iq: e.dma_start(out=MIXT[kvh * 4:(kvh + 1) * 4, :, iq * 128:(iq + 1) * 128].rearrange("c p t -> p c t"), in_=oT[db][:]),
                  reads=[("oT", db)], writes=[("MIXT", "att", iq, kvh)])
    P.barrier()
    P.sb_reset(m0)


import math
NCH = 544
CB = 272
TWO_PI = 2.0 * math.pi


def bc(ap, axis, shape):
    return ap.unsqueeze(axis).to_broadcast(shape)


def phase_s5(P, C, l, PROJ, MIXT, prm, cst):
    m0 = P.sb_mark()
    nsb = [0]

    def T(shape, dt=F32, nm="s5"):
        nsb[0] += 1
        return P.sb(shape, dt, nm), "%s%d" % (nm, nsb[0])

    SH = [128, 2, 16, 48]
    PWR, kPWR = T(SH); PWI, kPWI = T(SH)
    NK = 9
    AR, kAR = T([128, 2, 16, NK]); AI, kAI = T([128, 2, 16, NK]); NAI, kNAI = T([128, 2, 16, NK])
    FR, kFR = T([128, 2, 16]); FI, kFI = T([128, 2, 16])
    SB4 = [128, 2, 16, 16]
    BBR, kBBR = T(SB4); BBI, kBBI = T(SB4)
    CR, kCR = T(SB4); CI, kCI = T(SB4)
    mscr = P.sb_mark()
    LR, kLR = T([128, 2, 16]); LI, kLI = T([128, 2, 16]); DTt, kDT = T([128, 2, 16])
    BR, kBR = T([128, 2, 16, 16]); BI, kBI = T([128, 2, 16, 16])
    CRr, kCRr = T([16, 2, 16, 128]); CIr, kCIr = T([16, 2, 16, 128])
    erow, kerow = T([128, 48])
    P.D(SP, [], [kerow], out=erow[:], in_=cst["erow"][:, :])
    for d in range(2):
        for g2 in range(2):
            ps_ = slice(g2 * 64, (g2 + 1) * 64)
            P.D(SP, [], [(kLR, d, g2)], out=LR[ps_, d, :], in_=prm["lam_re"][l, d].rearrange("(gp g2) p -> g2 p gp", g2=2)[g2], allow_slow_non_contiguous=True)
            P.D(SP, [], [(kLI, d, g2)], out=LI[ps_, d, :], in_=prm["lam_im"][l, d].rearrange("(gp g2) p -> g2 p gp", g2=2)[g2], allow_slow_non_contiguous=True)
            P.D(SP, [], [(kDT, d, g2)], out=DTt[ps_, d, :], in_=prm["log_dt"][l, d].rearrange("(gp g2) -> g2 gp", g2=2)[g2].partition_broadcast(64), allow_slow_non_contiguous=True)
            P.D(SP, [], [(kBR, d, g2)], out=BR[ps_, d, :, :], in_=prm["b_re"][l, d].rearrange("(gp g2) p j -> g2 p gp j", g2=2)[g2])
            P.D(SP, [], [(kBI, d, g2)], out=BI[ps_, d, :, :], in_=prm["b_im"][l, d].rearrange("(gp g2) p j -> g2 p gp j", g2=2)[g2])
        for g2 in range(2):
            P.D(SP, [], [(kCRr, d, g2)], out=CRr[:, d, :, g2 * 64:(g2 + 1) * 64], in_=prm["c_re"][l, d].rearrange("(gp g2) i p -> g2 i gp p", g2=2)[g2])
            P.D(SP, [], [(kCIr, d, g2)], out=CIr[:, d, :, g2 * 64:(g2 + 1) * 64], in_=prm["c_im"][l, d].rearrange("(gp g2) i p -> g2 i gp p", g2=2)[g2])
    P.I(ACT, "activation", [kDT], [kDT], out=DTt[:], in_=DTt[:], func=AF.Exp)
    LRD, kLRD = T([128, 2, 16]); TH, kTH = T([128, 2, 16])
    P.I(DVE, "tensor_tensor", [kLR, kDT], [kLRD], out=LRD[:], in0=LR[:], in1=DTt[:], op=ALU.mult)
    P.I(DVE, "tensor_tensor", [kLI, kDT], [kTH], out=TH[:], in0=LI[:], in1=DTt[:], op=ALU.mult)
    SH = [128, 2, 16, 48]
    ANG, kANG = T(SH); MAG, kMAG = T(SH); TMP, kTMP = T(SH)
    eb_ = erow[:].unsqueeze(1).unsqueeze(1).to_broadcast(SH)
    P.I(DVE, "tensor_tensor", [kTH, kerow], [kANG], out=ANG[:], in0=bc(TH[:], 3, SH), in1=eb_, op=ALU.mult)
    P.I(DVE, "tensor_tensor", [kLRD, kerow], [kMAG], out=MAG[:], in0=bc(LRD[:], 3, SH), in1=eb_, op=ALU.mult)
    P.I(ACT, "activation", [kMAG], [kMAG], out=MAG[:], in_=MAG[:], func=AF.Exp)
    KI, kKI = T(SH, I32); KF, kKF = T(SH); MK, kMK = T(SH)

    def sin_rr(OUT, kOUT, off):
        P.I(DVE, "tensor_scalar_add", [kANG], [kTMP], out=TMP[:], in0=ANG[:], scalar1=off)
        P.I(DVE, "tensor_scalar_mul", [kTMP], [kKF], out=KF[:], in0=TMP[:], scalar1=1.0 / TWO_PI)
        P.I(DVE, "tensor_copy", [kKF], [kKI], out=KI[:], in_=KF[:])
        P.I(DVE, "tensor_copy", [kKI], [kKF], out=KF[:], in_=KI[:])
        P.I(DVE, "scalar_tensor_tensor", [kKF, kTMP], [kTMP], out=TMP[:], in0=KF[:], scalar=-TWO_PI, in1=TMP[:], op0=ALU.mult, op1=ALU.add)
        P.I(DVE, "tensor_single_scalar", [kTMP], [kMK], out=MK[:], in_=TMP[:], scalar=math.pi, op=ALU.is_gt)
        P.I(DVE, "scalar_tensor_tensor", [kMK, kTMP], [kTMP], out=TMP[:], in0=MK[:], scalar=-TWO_PI, in1=TMP[:], op0=ALU.mult, op1=ALU.add)
        P.I(ACT, "activation", [kTMP], [kOUT], out=OUT[:], in_=TMP[:], func=AF.Sin)
    sin_rr(PWI, kPWI, TWO_PI * 32)
    sin_rr(PWR, kPWR, TWO_PI * 32 + math.pi / 2)
    P.I(DVE, "tensor_tensor", [kPWR, kMAG], [kPWR], out=PWR[:], in0=PWR[:], in1=MAG[:], op=ALU.mult)
    P.I(DVE, "tensor_tensor", [kPWI, kMAG], [kPWI], out=PWI[:], in0=PWI[:], in1=MAG[:], op=ALU.mult)
    t1, kt1 = T([128, 2, 16]); t2, kt2 = T([128, 2, 16])
    P.I(DVE, "tensor_copy", [kPWR], [(kAR, 0)], out=AR[:, :, :, 0], in_=PWR[:, :, :, 23])
    P.I(DVE, "tensor_copy", [kPWI], [(kAI, 0)], out=AI[:, :, :, 0], in_=PWI[:, :, :, 23])
    for k in range(NK - 1):
        P.I(DVE, "tensor_tensor", [(kAR, k)], [kt1], out=t1[:], in0=AR[:, :, :, k], in1=AR[:, :, :, k], op=ALU.mult)
        P.I(DVE, "tensor_tensor", [(kAI, k)], [kt2], out=t2[:], in0=AI[:, :, :, k], in1=AI[:, :, :, k], op=ALU.mult)
        P.I(DVE, "tensor_tensor", [kt1, kt2], [(kAR, k + 1)], out=AR[:, :, :, k + 1], in0=t1[:], in1=t2[:], op=ALU.subtract)
        P.I(DVE, "scalar_tensor_tensor", [(kAR, k), (kAI, k)], [(kAI, k + 1)], out=AI[:, :, :, k + 1], in0=AR[:, :, :, k], scalar=2.0, in1=AI[:, :, :, k], op0=ALU.mult, op1=ALU.mult)
    P.I(DVE, "tensor_scalar_mul", [kAI], [kNAI], out=NAI[:], in0=AI[:], scalar1=-1.0)
    NR, kNR = T([128, 2, 16]); DEN, kDEN = T([128, 2, 16])
    P.I(DVE, "tensor_scalar_add", [kPWR], [kNR], out=NR[:], in0=PWR[:, :, :, 16], scalar1=-1.0)
    P.I(DVE, "tensor_tensor", [kLR], [kDEN], out=DEN[:], in0=LR[:], in1=LR[:], op=ALU.mult)
    P.I(DVE, "tensor_tensor", [kLI], [kt1], out=t1[:], in0=LI[:], in1=LI[:], op=ALU.mult)
    P.I(DVE, "tensor_tensor", [kDEN, kt1], [kDEN], out=DEN[:], in0=DEN[:], in1=t1[:], op=ALU.add)
    P.I(DVE, "reciprocal", [kDEN], [kDEN], out=DEN[:], in_=DEN[:])
    P.I(DVE, "tensor_tensor", [kNR, kLR], [kFR], out=FR[:], in0=NR[:], in1=LR[:], op=ALU.mult)
    P.I(DVE, "tensor_tensor", [kPWI, kLI], [kt1], out=t1[:], in0=PWI[:, :, :, 16], in1=LI[:], op=ALU.mult)
    P.I(DVE, "tensor_tensor", [kFR, kt1], [kFR], out=FR[:], in0=FR[:], in1=t1[:], op=ALU.add)
    P.I(DVE, "tensor_tensor", [kFR, kDEN], [kFR], out=FR[:], in0=FR[:], in1=DEN[:], op=ALU.mult)
    P.I(DVE, "tensor_tensor", [kPWI, kLR], [kFI], out=FI[:], in0=PWI[:, :, :, 16], in1=LR[:], op=ALU.mult)
    P.I(DVE, "tensor_tensor", [kNR, kLI], [kt1], out=t1[:], in0=NR[:], in1=LI[:], op=ALU.mult)
    P.I(DVE, "tensor_tensor", [kFI, kt1], [kFI], out=FI[:], in0=FI[:], in1=t1[:], op=ALU.subtract)
    P.I(DVE, "tensor_tensor", [kFI, kDEN], [kFI], out=FI[:], in0=FI[:], in1=DEN[:], op=ALU.mult)
    T4, kT4 = T(SB4)
    P.I(DVE, "tensor_tensor", [kFR, kBR], [kBBR], out=BBR[:], in0=bc(FR[:], 3, SB4), in1=BR[:], op=ALU.mult)
    P.I(DVE, "tensor_tensor", [kFI, kBI], [kT4], out=T4[:], in0=bc(FI[:], 3, SB4), in1=BI[:], op=ALU.mult)
    P.I(DVE, "tensor_tensor", [kBBR, kT4], [kBBR], out=BBR[:], in0=BBR[:], in1=T4[:], op=ALU.subtract)
    P.I(DVE, "tensor_tensor", [kFR, kBI], [kBBI], out=BBI[:], in0=bc(FR[:], 3, SB4), in1=BI[:], op=ALU.mult)
    P.I(DVE, "tensor_tensor", [kFI, kBR], [kT4], out=T4[:], in0=bc(FI[:], 3, SB4), in1=BR[:], op=ALU.mult)
    P.I(DVE, "tensor_tensor", [kBBI, kT4], [kBBI], out=BBI[:], in0=BBI[:], in1=T4[:], op=ALU.add)
    for (src, ksrc, dst, kdst) in ((CRr, kCRr, CR, kCR), (CIr, kCIr, CI, kCI)):
        for d in range(2):
            bk = C.bank()
            for gp in range(16):
                P.I(PE, "transpose", [ksrc, "ident"], [("ps", bk)], out=C.ps[bk][:, gp * 16:(gp + 1) * 16], in_=src[:, d, gp, :], identity=C.ident[0:16, 0:16])
            P.I(DVE, "tensor_copy", [], [("ps", bk), (kdst, d)], out=dst[:, d, :, :], in_=C.ps[bk][:, 0:256].rearrange("p (g i) -> p g i", g=16))
    P.barrier()
    P.sb_reset(mscr)
    E, kE = T([128, 8, 240]); Eb, kEb = T([128, 8, 240], BF16)
    P.D(SP, [], [kE], out=E[:], in_=cst["E"].rearrange("r p c -> p r c"))
    P.I(DVE, "tensor_copy", [kE], [kEb], out=Eb[:], in_=E[:])
    MF, kMF = T([128, 128]); MB, kMB = T([128, 128])
    P.D(SP, [], [kMF], out=MF[:], in_=cst["toemf"][:, :])
    P.D(SP, [], [kMB], out=MB[:], in_=cst["toemb"][:, :])
    Dall, kDall = T([128, 32])
    for t in range(8):
        P.D(SP, [], [(kDall, t)], out=Dall[t * 16:(t + 1) * 16, :], in_=prm["d"][l].rearrange("(g i) -> i g", i=16), allow_slow_non_contiguous=True)
    zT, kzT = T([128, 4, NTOK], BF16)
    zTv = zT[:].rearrange("p a (c s) -> p a c s", s=8)
    SG = [128, 4, 8, 16]
    tabs = {}
    for nm in ("WTR", "WTI", "XR", "XI", "VR", "VI"):
        for d in range(2):
            tabs[(nm, d)] = T(SG)
    TG, kTG = T(SG)
    suin, ksuin = T([128, NT, 128])
    suT, ksuT = T([128, NTOK])
    suTv = suT[:].rearrange("p (c s) -> p c s", s=8)
    U8, kU8 = T([128, 8, NCH])
    Z8, kZ8 = T([128, 8, NCH], BF16)
    Wt, kWt = T([128, 4, 128])
    Toe, kToe = T([128, 2, 128])
    H = {}
    for d in range(2):
        for pp in range(2):
            for c_ in range(2):
                H[(d, pp, c_)] = T([128, NCH])
    xg, kxg = T([128, CB]); ug, kug = T([128, CB]); sg, ksg = T([128, CB])
    for ct in range(4):
        gsl = slice(ct * 4, ct * 4 + 4)
        for d in range(2):
            ea, eb2, ec = (0, 8, 16) if d == 0 else (24, 32, 40)
            def pw(tile_, e0):
                return tile_[:, d, gsl, e0:e0 + 8].unsqueeze(3).to_broadcast(SG)
            def bb(tile_):
                return tile_[:, d, gsl, :].unsqueeze(2).to_broadcast(SG)
            (WTR, kWTR), (WTI, kWTI) = tabs[("WTR", d)], tabs[("WTI", d)]
            (XR, kXR), (XI, kXI) = tabs[("XR", d)], tabs[("XI", d)]
            (VR, kVR), (VI, kVI) = tabs[("VR", d)], tabs[("VI", d)]
            P.I(DVE, "tensor_tensor", [kPWR, kBBR], [kWTR], out=WTR[:], in0=pw(PWR, ea), in1=bb(BBR), op=ALU.mult)
            P.I(DVE, "tensor_tensor", [kPWI, kBBI], [kTG], out=TG[:], in0=pw(PWI, ea), in1=bb(BBI), op=ALU.mult)
            P.I(DVE, "tensor_tensor", [kWTR, kTG], [kWTR], out=WTR[:], in0=WTR[:], in1=TG[:], op=ALU.subtract)
            P.I(DVE, "tensor_tensor", [kPWR, kBBI], [kWTI], out=WTI[:], in0=pw(PWR, ea), in1=bb(BBI), op=ALU.mult)
            P.I(DVE, "tensor_tensor", [kPWI, kBBR], [kTG], out=TG[:], in0=pw(PWI, ea), in1=bb(BBR), op=ALU.mult)
            P.I(DVE, "tensor_tensor", [kWTI, kTG], [kWTI], out=WTI[:], in0=WTI[:], in1=TG[:], op=ALU.add)
            for (RR, kRR, II, kII, e0) in ((XR, kXR, XI, kXI, eb2), (VR, kVR, VI, kVI, ec)):
                P.I(DVE, "tensor_tensor", [kPWR, kCR], [kRR], out=RR[:], in0=pw(PWR, e0), in1=bb(CR), op=ALU.mult)
                P.I(DVE, "tensor_tensor", [kPWI, kCI], [kTG], out=TG[:], in0=pw(PWI, e0), in1=bb(CI), op=ALU.mult)
                P.I(DVE, "tensor_tensor", [kRR, kTG], [kRR], out=RR[:], in0=RR[:], in1=TG[:], op=ALU.subtract)
                P.I(DVE, "tensor_tensor", [kPWI, kCR], [kII], out=II[:], in0=pw(PWI, e0), in1=bb(CR), op=ALU.mult)
                P.I(DVE, "tensor_tensor", [kPWR, kCI], [kTG], out=TG[:], in0=pw(PWR, e0), in1=bb(CI), op=ALU.mult)
                P.I(DVE, "scalar_tensor_tensor", [kII, kTG], [kII], out=II[:], in0=II[:], scalar=-1.0, in1=TG[:], op0=ALU.mult, op1=ALU.subtract)
        P.D(SP, [("PROJ",)], [ksuin], out=suin[:], in_=PROJ[:, 1536 + ct * 128:1536 + (ct + 1) * 128].rearrange("(i p) c -> p i c", p=128))
        for i4 in range(0, NT, 4):
            n = min(4, NT - i4)
            bk = C.bank()
            for j in range(n):
                P.I(PE, "transpose", [ksuin, "ident"], [("ps", bk)], out=C.ps[bk][:, j * 128:(j + 1) * 128], in_=suin[:, i4 + j, :], identity=C.ident[:])
            copy_op(P, alt(i4 // 4), suT[:, i4 * 128:(i4 + n) * 128], C.ps[bk][:, 0:n * 128], [], [("ps", bk), (ksuT, i4)])
        for g8 in range(8):
            for cb in range(2):
                bk = C.bank()
                for s in range(8):
                    P.I(PE, "matmul", [kE, ksuT], [("ps", bk)], C.ps[bk][:, 0:CB], lhsT=E[:, g8, (7 - s) * 16:(7 - s) * 16 + 128],
                        rhs=suTv[:, cb * CB:(cb + 1) * CB, s], start=(s == 0), stop=(s == 7))
                copy_op(P, alt(cb), U8[:, g8, cb * CB:(cb + 1) * CB], C.ps[bk][:, 0:CB], [], [("ps", bk), (kU8, g8, cb)])
        for gpl in range(4):
            gp = ct * 4 + gpl
            bk = C.bank()
            for d in range(2):
                for c_, nm in enumerate(("WTR", "WTI")):
                    tt, ktt = tabs[(nm, d)]
                    j = d * 2 + c_
                    P.I(PE, "transpose", [ktt, "ident"], [("ps", bk)], out=C.ps[bk][:, j * 128:(j + 1) * 128], in_=tt[:, gpl, :, :].rearrange("p s j -> p (s j)"), identity=C.ident[:])
            P.I(DVE, "tensor_copy", [], [("ps", bk), kWt], out=Wt[:], in_=C.ps[bk][:, :].rearrange("p (a b) -> p a b", a=4))
            for g2 in range(2):
                rs_ = slice(g2 * 64, (g2 + 1) * 64)
                bks = []
                for d in range(2):
                    bk = C.bank()
                    bks.append(bk)
                    (WTR, kWTR), (WTI, kWTI) = tabs[("WTR", d)], tabs[("WTI", d)]
                    (XR, kXR), (XI, kXI) = tabs[("XR", d)], tabs[("XI", d)]
                    P.I(PE, "matmul", [kWTR, kXR], [("ps", bk)], C.ps[bk][:, 0:128], lhsT=WTR[rs_, gpl, :, :].rearrange("p s j -> p (s j)"),
                        rhs=XR[rs_, gpl, :, :].rearrange("p s j -> p (s j)"), start=True, stop=False)
                    P.I(PE, "matmul", [kWTI, kXI], [("ps", bk)], C.ps[bk][:, 0:128], lhsT=WTI[rs_, gpl, :, :].rearrange("p s j -> p (s j)"),
                        rhs=XI[rs_, gpl, :, :].rearrange("p s j -> p (s j)"), start=False, stop=True)
                P.I(DVE, "tensor_tensor", [kMF], [("ps", bks[0]), (kToe, g2)], out=Toe[:, g2, :], in0=C.ps[bks[0]][:, 0:128], in1=MF[:], op=ALU.mult)
                P.I(DVE, "tensor_tensor", [kMB], [("ps", bks[1]), kTG], out=TG[:, 0, :, :].rearrange("p s j -> p (s j)"), in0=C.ps[bks[1]][:, 0:128], in1=MB[:], op=ALU.mult)
                P.I(DVE, "tensor_tensor", [kTG], [(kToe, g2)], out=Toe[:, g2, :], in0=Toe[:, g2, :], in1=TG[:, 0, :, :].rearrange("p s j -> p (s j)"), op=ALU.add)
            for d in range(2):
                eng = DVE
                for c_ in range(2):
                    Ht, kHt = H[(d, 0, c_)]
                    for cb in range(2):
                        bk = C.bank()
                        for g2 in range(2):
                            rs_ = slice(g2 * 64, (g2 + 1) * 64)
                            P.I(PE, "matmul", [kWt, kU8], [("ps", bk)], C.ps[bk][rs_, 0:CB], lhsT=Wt[:, d * 2 + c_, rs_], rhs=U8[:, gpl * 2 + g2, cb * CB:(cb + 1) * CB], start=True, stop=True)
                        copy_op(P, ACT, Ht[:, cb * CB:(cb + 1) * CB], C.ps[bk][:, 0:CB], [], [("ps", bk), (kHt, cb)])
                cur = 0
                def arK(k):
                    return AR[:, d, gp, k:k + 1], AI[:, d, gp, k:k + 1], NAI[:, d, gp, k:k + 1]
                def scan(lo, hi, cur):
                    n = hi - lo
                    k = 0
                    sh = 1
                    while sh < n:
                        (A_, kA), (B_, kB) = H[(d, cur, 0)], H[(d, cur, 1)]
                        (An, kAn), (Bn, kBn) = H[(d, 1 - cur, 0)], H[(d, 1 - cur, 1)]
                        ar, ai, nai = arK(k)
                        if d == 0:
                            dst, src, keep = slice(lo + sh, hi), slice(lo, hi - sh), slice(lo, lo + sh)
                        else:
                            dst, src, keep = slice(lo, hi - sh), slice(lo + sh, hi), slice(hi - sh, hi)
                        P.I(eng, "scalar_tensor_tensor", [kA, kAR], [kAn], out=An[:, dst], in0=A_[:, src], scalar=ar, in1=A_[:, dst], op0=ALU.mult, op1=ALU.add)
                        P.I(eng, "scalar_tensor_tensor", [kB, kNAI, kAn], [kAn], out=An[:, dst], in0=B_[:, src], scalar=nai, in1=An[:, dst], op0=ALU.mult, op1=ALU.add)
                        P.I(eng, "scalar_tensor_tensor", [kB, kAR], [kBn], out=Bn[:, dst], in0=B_[:, src], scalar=ar, in1=B_[:, dst], op0=ALU.mult, op1=ALU.add)
                        P.I(eng, "scalar_tensor_tensor", [kA, kAI, kBn], [kBn], out=Bn[:, dst], in0=A_[:, src], scalar=ai, in1=Bn[:, dst], op0=ALU.mult, op1=ALU.add)
                        P.I(eng, "tensor_copy", [kA], [kAn], out=An[:, keep], in_=A_[:, keep])
                        P.I(eng, "tensor_copy", [kB], [kBn], out=Bn[:, keep], in_=B_[:, keep])
                        cur = 1 - cur
                        sh *= 2
                        k += 1
                    return cur
                cur = scan(0, 32, 0)
                (A_, kA), (B_, kB) = H[(d, cur, 0)], H[(d, cur, 1)]
                if cur != 0:
                    (A0, kA0), (B0, kB0) = H[(d, 0, 0)], H[(d, 0, 1)]
                    P.I(eng, "tensor_copy", [kA0], [kA], out=A_[:, 32:NCH], in_=A0[:, 32:NCH])
                    P.I(eng, "tensor_copy", [kB0], [kB], out=B_[:, 32:NCH], in_=B0[:, 32:NCH])
                ar, ai, nai = arK(0)
                if d == 0:
                    inj, frm = slice(32, 33), slice(31, 32)
                else:
                    inj, frm = slice(NCH - 1, NCH), slice(0, 1)
                P.I(eng, "scalar_tensor_tensor", [kA, kAR], [kA], out=A_[:, inj], in0=A_[:, frm], scalar=ar, in1=A_[:, inj], op0=ALU.mult, op1=ALU.add)
                P.I(eng, "scalar_tensor_tensor", [kB, kNAI, kA], [kA], out=A_[:, inj], in0=B_[:, frm], scalar=nai, in1=A_[:, inj], op0=ALU.mult, op1=ALU.add)
                P.I(eng, "scalar_tensor_tensor", [kB, kAR], [kB], out=B_[:, inj], in0=B_[:, frm], scalar=ar, in1=B_[:, inj], op0=ALU.mult, op1=ALU.add)
                P.I(eng, "scalar_tensor_tensor", [kA, kAI, kB], [kB], out=B_[:, inj], in0=A_[:, frm], scalar=ai, in1=B_[:, inj], op0=ALU.mult, op1=ALU.add)
                cur0 = cur
                cur = scan(32, NCH, cur)
                if cur != cur0:
                    (An, kAn), (Bn, kBn) = H[(d, cur, 0)], H[(d, cur, 1)]
                    P.I(eng, "tensor_copy", [kA], [kAn], out=An[:, 0:32], in_=A_[:, 0:32])
                    P.I(eng, "tensor_copy", [kB], [kBn], out=Bn[:, 0:32], in_=B_[:, 0:32])
                H[("fin", d)] = cur
            for g2 in range(2):
                g8 = gpl * 2 + g2
                g = gp * 2 + g2
                rs_ = slice(g2 * 64, (g2 + 1) * 64)
                for cb in range(2):
                    c0 = cb * CB
                    bk = C.bank()
                    mms = [(Toe[:, g2, :], U8[:, g8, c0:c0 + CB], 0, CB, [kToe, kU8])]
                    cf = H[("fin", 0)]
                    lo = max(c0, 1)
                    for c_, nm in enumerate(("VR", "VI")):
                        tt, ktt = tabs[(nm, 0)]
                        Hh, kHh = H[(0, cf, c_)]
                        mms.append((tt[rs_, gpl, :, :].rearrange("p s j -> p (s j)"), Hh[rs_, lo - 1:c0 + CB - 1], lo - c0, CB, [ktt, kHh]))
                    cbk = H[("fin", 1)]
                    if cb == 0:
                        segs = [(0, 31, 1), (32, CB, 33)]
                    else:
                        segs = [(CB, NCH - 1, CB + 1), (NCH - 1, NCH, 0)]
                    for c_, nm in enumerate(("VR", "VI")):
                        tt, ktt = tabs[(nm, 1)]
                        Hh, kHh = H[(1, cbk, c_)]
                        for (a0, a1, s0) in segs:
                            mms.append((tt[rs_, gpl, :, :].rearrange("p s j -> p (s j)"), Hh[rs_, s0:s0 + (a1 - a0)], a0 - c0, a1 - c0, [ktt, kHh]))
                    for n, (lt, rh, o0, o1, rd) in enumerate(mms):
                        P.I(PE, "matmul", rd, [("ps", bk)], C.ps[bk][:, o0:o1], lhsT=lt, rhs=rh, start=(n == 0), stop=(n == len(mms) - 1))
                    P.I(DVE, "scalar_tensor_tensor", [kU8, kDall], [("ps", bk), kxg], out=xg[:], in0=U8[:, g8, c0:c0 + CB], scalar=Dall[:, g:g + 1], in1=C.ps[bk][:, 0:CB], op0=ALU.mult, op1=ALU.add)
                    P.I(POOL, "tensor_tensor", [kxg], [kug], out=ug[:], in0=xg[:], in1=xg[:], op=ALU.mult)
                    P.I(POOL, "tensor_scalar", [kug], [kug], out=ug[:], in0=ug[:], scalar1=0.044715, scalar2=1.0, op0=ALU.mult, op1=ALU.add)
                    P.I(POOL, "tensor_tensor", [kug, kxg], [kug], out=ug[:], in0=ug[:], in1=xg[:], op=ALU.mult)
                    P.I(ACT, "activation", [kug], [ksg], out=sg[:], in_=ug[:], func=AF.Sigmoid, scale=2.0 * math.sqrt(2.0 / math.pi))
                    P.I(DVE, "tensor_tensor", [kxg, ksg], [(kZ8, g8, cb)], out=Z8[:, g8, c0:c0 + CB], in0=xg[:], in1=sg[:], op=ALU.mult)
        for s in range(8):
            for cb in range(2):
                bk = C.bank()
                for g8 in range(8):
                    P.I(PE, "matmul", [kEb, kZ8], [("ps", bk)], C.ps[bk][:, 0:CB], lhsT=Eb[:, s, (7 - g8) * 16:(7 - g8) * 16 + 128], rhs=Z8[:, g8, cb * CB:(cb + 1) * CB], start=(g8 == 0), stop=(g8 == 7))
                copy_op(P, alt(s), zTv[:, ct, cb * CB:(cb + 1) * CB, s], C.ps[bk][:, 0:CB], [], [("ps", bk), (kzT, ct, s, cb)])
    wg, kwg = T([128, 4, 512], BF16)
    for kt in range(4):
        P.D(POOL, [], [(kwg, kt)], out=wg[:, kt, :], in_=prm["w_glu"][l, kt * 128:(kt + 1) * 128, :])
    bg, kbg = T([128, 4])
    P.D(SP, [], [kbg], out=bg[:], in_=prm["b_glu"][l].rearrange("(m p) -> p m", p=128), allow_slow_non_contiguous=True)
    gts = [T([128, 512]) for _ in range(2)]
    ots = [T([128, 512], BF16) for _ in range(2)]
    it = 0
    for mt in range(4):
        for t0 in range(0, NTOK, 512):
            w = min(512, NTOK - t0)
            b = it % 2
            it += 1
            (gt_, kgt), (ot_, kot) = gts[b], ots[b]
            bk = C.bank()
            for kt in range(4):
                P.I(PE, "matmul", [kwg, kzT], [("ps", bk)], C.ps[bk][:, 0:w], lhsT=wg[:, kt, mt * 128:(mt + 1) * 128], rhs=zT[:, kt, t0:t0 + w], start=(kt == 0), stop=(kt == 3))
            P.I(ACT, "activation", [kbg], [("ps", bk), kgt], out=gt_[:, 0:w], in_=C.ps[bk][:, 0:w], func=AF.Sigmoid, bias=bg[:, mt:mt + 1], scale=1.0)
            P.I(DVE, "tensor_tensor", [kgt, kzT], [kot], out=ot_[:, 0:w], in0=gt_[:, 0:w], in1=zT[:, mt, t0:t0 + w], op=ALU.mult)
            P.D(SP, [kot], [("MIXT", "ssm", mt, t0)], out=MIXT[8 + mt, :, t0:t0 + w], in_=ot_[:, 0:w])
    P.barrier()
    P.sb_reset(m0)


def phase_gla(P, C, l, PROJ, MIXT, OF, prm, cst):
    m0 = P.sb_mark()
    n_ = [0]

    def T(shape, dt=F32, nm="gl"):
        n_[0] += 1
        return P.sb(shape, dt, nm), "%s%d" % (nm, n_[0])

    def ld(name, shape):
        t, k = T(shape)
        P.D(SP, [], [k], out=t[:], in_=cst[name][:, :])
        return t, k
    TRI = [ld("trif", [128, 128]), ld("trib", [128, 128])]
    BLK, kBLK = ld("blk", [128, 128])
    CIND, kCIND = ld("cind", [128, 2])
    MSK = [ld("gmaskf", [128, 512]), ld("gmaskb", [128, 512])]
    WG = []
    for d in range(2):
        t, k = T([17, 256])
        P.D(SP, [], [(k, 0)], out=t[0:16, :], in_=prm["w_gate"][l, d, :, :])
        P.D(SP, [], [(k, 1)], out=t[16:17, :], in_=prm["b_gate"][l, d:d + 1, :])
        WG.append((t, k))
    NG, kNG = T([128, 128])
    P.D(SP, [], [kNG], out=NG[:], in_=prm["norm_g"][l, :].partition_broadcast(128))
    S, kS = T([64, 4, 128])
    zaug = [T([17, 128]) for _ in range(2)]
    for (t, k) in zaug:
        P.I(POOL, "memset", [], [k], t[:], 1.0)
    NB = 2
    qk = [T([128, 512]) for _ in range(NB)]
    vv = [T([128, 512]) for _ in range(NB)]
    zz = [T([128, 16]) for _ in range(NB)]
    gp_ = [T([128, 256]) for _ in range(NB)]
    bS = [T([128, 256]) for _ in range(NB)]
    eb = [T([128, 256]) for _ in range(NB)]
    enb = [T([128, 256]) for _ in range(NB)]
    ebl = [T([128, 256]) for _ in range(NB)]
    qd = [T([128, 256]) for _ in range(NB)]
    kd = [T([128, 256]) for _ in range(NB)]
    kl = [T([128, 256]) for _ in range(NB)]
    qdT = [T([64, 4, 128]) for _ in range(NB)]
    kdT = [T([64, 4, 128]) for _ in range(NB)]
    ATm = [T([128, 4, 128]) for _ in range(NB)]
    edec = [T([64, 4, 2]) for _ in range(NB)]
    ot = [T([128, 512]) for _ in range(NB)]
    of_ = [T([128, 512]) for _ in range(NB)]
    rr = [T([128, 512]) for _ in range(NB)]
    sq = [T([128, 512]) for _ in range(NB)]
    ssq = [T([128, 4]) for _ in range(NB)]
    oTb = [T([128, 4, 128], BF16) for _ in range(NB)]
    it = 0
    for d in range(2):
        P.I(DVE, "memset", [], [kS], S[:], 0.0)
        order = list(range(NT)) if d == 0 else [1, 0] + list(range(NT - 1, 1, -1))
        corder = (0, 1) if d == 0 else (1, 0)
        (TRId, kTRI), (MK, kMK), (WGd, kWG) = TRI[d], MSK[d], WG[d]
        for i in order:
            b = it % NB
            it += 1
            r0 = i * 128
            (QK, kQK), (V, kV), (Z, kZ), (GP, kGP) = qk[b], vv[b], zz[b], gp_[b]
            (ZA, kZA) = zaug[b]
            P.D(SP, [("PROJ", i)], [kQK], out=QK[:], in_=PROJ[r0:r0 + 128, 2048:2560])
            P.D(SP, [("PROJ", i)], [kV], out=V[:], in_=PROJ[r0:r0 + 128, 2560:3072])
            P.D(SP, [("PROJ", i)], [kZ], out=Z[:], in_=PROJ[r0:r0 + 128, 3584 + 16 * d:3600 + 16 * d])
            bk = C.bank()
            P.I(PE, "transpose", [kZ, "ident"], [("ps", bk)], out=C.ps[bk][0:16, 0:128], in_=Z[:], identity=C.ident[:])
            P.I(ACT, "copy", [], [("ps", bk), kZA], out=ZA[0:16, :], in_=C.ps[bk][0:16, 0:128])
            bk = C.bank()
            P.I(PE, "matmul", [kZA, kWG], [("ps", bk)], C.ps[bk][:, 0:256], lhsT=ZA[:], rhs=WGd[:], start=True, stop=True)
            P.I(ACT, "activation", [], [("ps", bk), kGP], out=GP[:], in_=C.ps[bk][:, 0:256], func=AF.Exp, scale=-1.0)
            P.I(ACT, "activation", [kGP], [kGP], out=GP[:], in_=GP[:], func=AF.Ln, bias=1.0, scale=1.0)
            bkb = C.bank()
            P.I(PE, "matmul", [kTRI, kGP], [("ps", bkb)], C.ps[bkb][:, 0:256], lhsT=TRId[:], rhs=GP[:], start=True, stop=True)
            P.I(PE, "matmul", [kBLK, kGP], [("ps", bkb)], C.ps[bkb][:, 256:512], lhsT=BLK[:], rhs=GP[:], start=True, stop=True)
            (BS, kBS), (EB, kEB), (ENB, kENB), (EBL, kEBL) = bS[b], eb[b], enb[b], ebl[b]
            P.I(ACT, "copy", [], [("ps", bkb), kBS], out=BS[:], in_=C.ps[bkb][:, 0:256])
            P.I(DVE, "tensor_tensor", [kBS], [("ps", bkb), kEBL], out=EBL[:], in0=C.ps[bkb][:, 256:512], in1=BS[:], op=ALU.subtract)
            P.I(ACT, "activation", [kBS], [kEB], out=EB[:], in_=BS[:], func=AF.Exp)
            P.I(ACT, "activation", [kBS], [kENB], out=ENB[:], in_=BS[:], func=AF.Exp, scale=-1.0)
            P.I(ACT, "activation", [kEBL], [kEBL], out=EBL[:], in_=EBL[:], func=AF.Exp)
            (ED, kED) = edec[b]
            bk = C.bank()
            for h in range(4):
                P.I(PE, "matmul", [kGP, kCIND], [("ps", bk)], C.ps[bk][0:64, h * 2:h * 2 + 2], lhsT=GP[:, h * 64:(h + 1) * 64], rhs=CIND[:], start=True, stop=True)
            P.I(ACT, "activation", [], [("ps", bk), kED], out=ED[:].rearrange("p h c -> p (h c)"), in_=C.ps[bk][0:64, 0:8], func=AF.Exp)
            (QD, kQD), (KD, kKD), (KL, kKL) = qd[b], kd[b], kl[b]
            P.I(DVE, "scalar_tensor_tensor", [kQK, kEB], [kQD], out=QD[:], in0=QK[:, 0:256], scalar=0.125, in1=EB[:], op0=ALU.mult, op1=ALU.mult)
            P.I(POOL, "tensor_tensor", [kQK, kENB], [kKD], out=KD[:], in0=QK[:, 256:512], in1=ENB[:], op=ALU.mult)
            P.I(POOL, "tensor_tensor", [kQK, kEBL], [kKL], out=KL[:], in0=QK[:, 256:512], in1=EBL[:], op=ALU.mult)
            (QT, kQT), (KT, kKT) = qdT[b], kdT[b]
            for (src, ksrc, dst, kdst, eng) in ((QD, kQD, QT, kQT, ACT), (KD, kKD, KT, kKT, DVE)):
                bk = C.bank()
                for h in range(4):
                    P.I(PE, "transpose", [ksrc, "ident"], [("ps", bk)], out=C.ps[bk][0:64, h * 128:(h + 1) * 128], in_=src[:, h * 64:(h + 1) * 64], identity=C.ident[:])
                copy_op(P, eng, dst[:].rearrange("p h t -> p (h t)"), C.ps[bk][0:64, :], [], [("ps", bk), kdst])
            (AT, kAT) = ATm[b]
            bk = C.bank()
            for h in range(4):
                P.I(PE, "matmul", [kKT, kQT], [("ps", bk)], C.ps[bk][:, h * 128:(h + 1) * 128], lhsT=KT[:, h, :], rhs=QT[:, h, :], start=True, stop=True)
            P.I(DVE, "tensor_tensor", [kMK], [("ps", bk), kAT], out=AT[:].rearrange("p h t -> p (h t)"), in0=C.ps[bk][:, :], in1=MK[:], op=ALU.mult)
            bo = C.bank()
            for h in range(4):
                P.I(PE, "matmul", [kAT, kV], [("ps", bo)], C.ps[bo][:, h * 128:(h + 1) * 128], lhsT=AT[:, h, :], rhs=V[:, h * 128:(h + 1) * 128], start=(h == 0), stop=False)
            for ci, c in enumerate(corder):
                cs = slice(c * 64, (c + 1) * 64)
                for h in range(4):
                    P.I(PE, "matmul", [kQT, (kS, h)], [("ps", bo)], C.ps[bo][cs, h * 128:(h + 1) * 128], lhsT=QT[:, h, cs], rhs=S[:, h, :], start=False, stop=(ci == 1 and h == 3))
                bu = C.bank()
                for h in range(4):
                    P.I(PE, "matmul", [kKL, kV], [("ps", bu)], C.ps[bu][0:64, h * 128:(h + 1) * 128], lhsT=KL[cs, h * 64:(h + 1) * 64], rhs=V[cs, h * 128:(h + 1) * 128], start=True, stop=True)
                for h in range(4):
                    P.I(DVE, "scalar_tensor_tensor", [kED], [("ps", bu), (kS, h)], out=S[:, h, :], in0=S[:, h, :], scalar=ED[:, h, c:c + 1], in1=C.ps[bu][0:64, h * 128:(h + 1) * 128], op0=ALU.mult, op1=ALU.add)
            (OT, kOT) = ot[b]
            if d == 0:
                P.I(ACT, "copy", [], [("ps", bo), kOT], out=OT[:], in_=C.ps[bo][:, :])
                P.D(SP, [kOT], [("OF", i)], out=OF[r0:r0 + 128, :], in_=OT[:])
                continue
            (OFt, kOFt), (RR, kRR), (SQ, kSQ), (SS, kSS), (OB, kOB) = of_[b], rr[b], sq[b], ssq[b], oTb[b]
            P.D(SP, [("OF", i)], [kOFt], out=OFt[:], in_=OF[r0:r0 + 128, :])
            P.D(SP, [("PROJ", i)], [kRR], out=RR[:], in_=PROJ[r0:r0 + 128, 3072:3584])
            P.I(DVE, "tensor_tensor", [kOFt], [("ps", bo), kOT], out=OT[:], in0=C.ps[bo][:, :], in1=OFt[:], op=ALU.add)
            P.I(POOL, "tensor_tensor", [kOT], [kSQ], out=SQ[:], in0=OT[:], in1=OT[:], op=ALU.mult)
            P.I(DVE, "tensor_reduce", [kSQ], [kSS], out=SS[:], in_=SQ[:].rearrange("p (h v) -> p h v", h=4), axis=AX.X, op=ALU.add)
            P.I(DVE, "tensor_scalar", [kSS], [kSS], out=SS[:], in0=SS[:], scalar1=1.0 / 128.0, scalar2=1e-6, op0=ALU.mult, op1=ALU.add)
            P.I(ACT, "activation", [kSS], [kSS], out=SS[:], in_=SS[:], func=AF.Ln)
            P.I(ACT, "activation", [kSS], [kSS], out=SS[:], in_=SS[:], func=AF.Exp, scale=-0.5)
            P.I(ACT, "activation", [kRR], [kRR], out=RR[:], in_=RR[:], func=AF.Silu)
            o3 = OT[:].rearrange("p (h v) -> p h v", h=4)
            P.I(DVE, "tensor_tensor", [kOT, kSS], [kOT], out=o3, in0=o3, in1=SS[:].unsqueeze(2).to_broadcast([128, 4, 128]), op=ALU.mult)
            P.I(POOL, "tensor_tensor", [kOT, kNG], [kOT], out=o3, in0=o3, in1=NG[:].unsqueeze(1).to_broadcast([128, 4, 128]), op=ALU.mult)
            P.I(DVE, "tensor_tensor", [kOT, kRR], [kOT], out=OT[:], in0=OT[:], in1=RR[:], op=ALU.mult)
            bk = C.bank()
            for h in range(4):
                P.I(PE, "transpose", [kOT, "ident"], [("ps", bk)], out=C.ps[bk][:, h * 128:(h + 1) * 128], in_=OT[:, h * 128:(h + 1) * 128], identity=C.ident[:])
            P.I(ACT, "copy", [], [("ps", bk), kOB], out=OB[:].rearrange("p h t -> p (h t)"), in_=C.ps[bk][:, :])
            P.D(SP, [kOB], [("MIXT", "gla", i)], out=MIXT[12:16, :, r0:r0 + 128].rearrange("c p t -> p c t"), in_=OB[:])
    P.barrier()
    P.sb_reset(m0)


ALPHA_ = (2 * 2) ** 0.25
NSLOT = 544
ROWW = 2080


def bcast_load(P, q, dst, key, src_row):
    P.D(q, [], [key], out=dst[:], in_=src_row.partition_broadcast(128))


def ln_apply(P, C, src, ksrc, dst, kdst, st, mv, rs, tag, gmul, kg, badd, kb, eng2=POOL):
    ln_stats(P, C, src, ksrc, st, mv, rs, tag)
    P.I(ACT, "activation", [ksrc, tag + "rs"], [kdst], out=dst[:], in_=src[:], func=AF.Identity, bias=rs[:, 1:2], scale=rs[:, 0:1])
    P.I(DVE, "tensor_tensor", [kdst, kg], [kdst], out=dst[:], in0=dst[:], in1=gmul, op=ALU.mult)
    P.I(eng2, "tensor_tensor", [kdst, kb], [kdst], out=dst[:], in0=dst[:], in1=badd, op=ALU.add)


def phase_wout(P, C, l, X, MIXT, X1, H2R, AFF, w_out, MODV, ln_g, ln_b, router):
    m0 = P.sb_mark()
    wo = P.sb([128, 16, D], BF16, "wo")
    for kt in range(16):
        P.D(POOL, [], [("wo", kt)], out=wo[:, kt, :], in_=w_out[l, kt * 128:(kt + 1) * 128, :])
    names = {}
    for nm, src in (("gt1", lambda r: MODV[l, r, 2 * D:3 * D]), ("sc2", lambda r: MODV[l, r, 4 * D:5 * D]), ("sh2", lambda r: MODV[l, r, 3 * D:4 * D])):
        for r in range(2):
            t = P.sb([128, D], F32, nm)
            bcast_load(P, SP, t, (nm, r), src(r))
            names[(nm, r)] = t
    for r in range(2):
        P.I(POOL, "tensor_scalar_add", [("sc2", r)], [("sc2", r)], out=names[("sc2", r)][:], in0=names[("sc2", r)][:], scalar1=1.0)
    g1 = P.sb([128, D], F32, "g1"); b1 = P.sb([128, D], F32, "b1")
    bcast_load(P, SP, g1, "g1", ln_g[l, :]); bcast_load(P, SP, b1, "b1", ln_b[l, :])
    rt = P.sb([128, 16, 16], F32, "rt")
    P.D(SP, [], ["rt"], out=rt[:], in_=router[l].rearrange("(kt p) e -> p kt e", p=128))
    mx = [P.sb([128, 16, 128], BF16, "mx") for _ in range(2)]
    xs = [P.sb([128, D], F32, "xs") for _ in range(2)]
    x1s = [P.sb([128, D], F32, "x1s") for _ in range(2)]
    rows = [P.sb([128, ROWW], F32, "row") for _ in range(2)]
    h2T = P.sb([128, 16, 128], F32, "h2T")
    st = P.sb([128, 4, 6], F32, "st"); mv = P.sb([128, 2], F32, "mv"); rs = P.sb([128, 2], F32, "rs")
    st2 = P.sb([128, 4, 6], F32, "st2"); mv2 = P.sb([128, 2], F32, "mv2"); rs2 = P.sb([128, 2], F32, "rs2")
    lmx = P.sb([128, 1], F32, "lmx"); lsum = P.sb([128, 1], F32, "lsum")
    for (t, k) in ((rows[0], ("row", 0)), (rows[1], ("row", 1))):
        P.I(POOL, "memset", [], [k], t[:, 2064:ROWW], 0.0)
    for i in range(NT):
        b = i % 2
        r = 1 if i < 2 else 0
        r0 = i * 128
        MX, XS, X1S, ROW = mx[b], xs[b], x1s[b], rows[b]
        kMX, kXS, kX1, kROW = ("mx", b), ("xs", b), ("x1s", b), ("row", b)
        P.D(SP, [("MIXT",)], [kMX], out=MX[:], in_=MIXT[:, :, r0:r0 + 128].rearrange("c p t -> p c t"))
        P.D(SP, [("X", i)], [kXS], out=XS[:], in_=X[r0:r0 + 128, :])
        for nb in range(4):
            bk = C.bank()
            for kt in range(16):
                P.I(PE, "matmul", [kMX, ("wo", kt)], [("ps", bk)], C.ps[bk][:, :], lhsT=MX[:, kt, :], rhs=wo[:, kt, nb * 512:(nb + 1) * 512], start=(kt == 0), stop=(kt == 15))
            sl = slice(nb * 512, (nb + 1) * 512)
            P.I(DVE, "tensor_tensor", [("gt1", r)], [("ps", bk), kX1 + (nb,)], out=X1S[:, sl], in0=C.ps[bk][:, :], in1=names[("gt1", r)][:, sl], op=ALU.mult)
        P.I(DVE, "scalar_tensor_tensor", [kXS, kX1], [kX1], out=X1S[:], in0=XS[:], scalar=ALPHA_, in1=X1S[:], op0=ALU.mult, op1=ALU.add)
        ln_apply(P, C, X1S, kX1, X1S, kX1, st, mv, rs, "w1", g1[:], "g1", b1[:], "b1")
        P.D(SP, [kX1], [("X1", i)], out=X1[r0:r0 + 128, :], in_=X1S[:])
        ln_stats(P, C, X1S, kX1, st2, mv2, rs2, "w2")
        P.I(ACT, "activation", [kX1, "w2rs"], [kROW + (0,)], out=ROW[:, 0:D], in_=X1S[:], func=AF.Identity, bias=rs2[:, 1:2], scale=rs2[:, 0:1])
        P.I(DVE, "tensor_tensor", [kROW + (0,), ("sc2", r)], [kROW + (0,)], out=ROW[:, 0:D], in0=ROW[:, 0:D], in1=names[("sc2", r)][:], op=ALU.mult)
        P.I(POOL, "tensor_tensor", [kROW + (0,), ("sh2", r)], [kROW + (0,)], out=ROW[:, 0:D], in0=ROW[:, 0:D], in1=names[("sh2", r)][:], op=ALU.add)
        for kg in range(4):
            bk = C.bank()
            for j in range(4):
                kt = kg * 4 + j
                P.I(PE, "transpose", [kROW + (0,), "ident"], [("ps", bk)], out=C.ps[bk][:, j * 128:(j + 1) * 128], in_=ROW[:, kt * 128:(kt + 1) * 128], identity=C.ident[:])
            copy_op(P, alt(kg), h2T[:, kg * 4:(kg + 1) * 4, :].rearrange("p a t -> p (a t)"), C.ps[bk][:, :], [], [("ps", bk), ("h2T", kg)])
        bk = C.bank()
        for kt in range(16):
            P.I(PE, "matmul", [("h2T", kt // 4), "rt"], [("ps", bk)], C.ps[bk][:, 0:16], lhsT=h2T[:, kt, :], rhs=rt[:, kt, :], start=(kt == 0), stop=(kt == 15))
        P.I(DVE, "tensor_reduce", [], [("ps", bk), "lmx"], out=lmx[:], in_=C.ps[bk][:, 0:16], axis=AX.X, op=ALU.max)
        P.I(DVE, "tensor_scalar_mul", ["lmx"], ["lmx"], out=lmx[:], in0=lmx[:], scalar1=-1.0)
        P.I(DVE, "memset", [], ["lsum"], lsum[:], 0.0)
        P.I(ACT, "activation", ["lmx"], [("ps", bk), kROW + (1,), "lsum"], out=ROW[:, D:D + 16], in_=C.ps[bk][:, 0:16], func=AF.Exp, bias=lmx[:, 0:1], scale=1.0, accum_out=lsum[:])
        P.I(DVE, "reciprocal", ["lsum"], ["lsum"], out=lsum[:], in_=lsum[:])
        P.I(DVE, "tensor_scalar_mul", [kROW + (1,), "lsum"], [kROW + (1,)], out=ROW[:, D:D + 16], in0=ROW[:, D:D + 16], scalar1=lsum[:, 0:1])
        P.I(POOL, "tensor_copy", [kROW + (1,)], [("AFF", i)], out=AFF[:, i, :], in_=ROW[:, D:D + 16])
        P.D(SP, [kROW], [("H2R", i)], out=H2R[r0:r0 + 128, :], in_=ROW[:])
    P.barrier()
    P.sb_reset(m0)


def phase_route(P, C, AFF, IDXS, IDXC_dram, cst, niter=30):
    m0 = P.sb_mark()
    n_ = [0]

    def T(shape, dt=F32, nm="rt"):
        n_[0] += 1
        return P.sb(shape, dt, nm), "%s%d" % (nm, n_[0])
    ones, kones = T([128, 128]); strict, kstrict = T([128, 128]); TGT, kTGT = T([128, 2, 16])
    P.D(SP, [], [kones], out=ones[:], in_=cst["ones"][:, :])
    P.D(SP, [], [kstrict], out=strict[:], in_=cst["strict"][:, :])
    P.D(SP, [], [kTGT], out=TGT[:], in_=cst["tgt"].rearrange("p (s e) -> p s e", s=2))
    LO, kLO = T([128, 2, 16]); HI, kHI = T([128, 2, 16]); MID, kMID = T([128, 2, 16])
    CNT, kCNT = T([128, 2, 16]); GE, kGE = T([128, 2, 16]); D1, kD1 = T([128, 2, 16])
    CMP, kCMP = T([128, NT, 16])
    P.I(DVE, "memset", [], [kLO], LO[:], 0.0)
    P.I(DVE, "memset", [], [kHI], HI[:], 1.0001)
    segs = ((0, 0, 2), (1, 2, NT))

    def compare(TH, kTH):
        for (s, a, b_) in segs:
            P.I(DVE, "tensor_tensor", [("AFF",), kTH], [(kCMP, s)], out=CMP[:, a:b_, :], in0=AFF[:, a:b_, :],
                in1=TH[:, s, :].unsqueeze(1).to_broadcast([128, b_ - a, 16]), op=ALU.is_ge)
    for it in range(niter):
        P.I(DVE, "tensor_tensor", [kLO, kHI], [kMID], out=MID[:], in0=LO[:], in1=HI[:], op=ALU.add)
        P.I(DVE, "tensor_scalar_mul", [kMID], [kMID], out=MID[:], in0=MID[:], scalar1=0.5)
        compare(MID, kMID)
        for (s, a, b_) in segs:
            P.I(DVE, "tensor_reduce", [(kCMP, s)], [(kCNT, s)], out=CNT[:, s, :], in_=CMP[:, a:b_, :].rearrange("p t e -> p e t"), axis=AX.X, op=ALU.add)
        bk = C.bank()
        P.I(PE, "matmul", [kones, kCNT], [("ps", bk)], C.ps[bk][:, 0:32], lhsT=ones[:], rhs=CNT[:].rearrange("p s e -> p (s e)"), start=True, stop=True)
        P.I(DVE, "tensor_tensor", [kTGT], [("ps", bk), kGE], out=GE[:].rearrange("p s e -> p (s e)"), in0=C.ps[bk][:, 0:32], in1=TGT[:].rearrange("p s e -> p (s e)"), op=ALU.is_ge)
        P.I(DVE, "tensor_tensor", [kMID, kLO], [kD1], out=D1[:], in0=MID[:], in1=LO[:], op=ALU.subtract)
        P.I(DVE, "tensor_tensor", [kD1, kGE], [kD1], out=D1[:], in0=D1[:], in1=GE[:], op=ALU.mult)
        P.I(DVE, "tensor_tensor", [kLO, kD1], [kLO], out=LO[:], in0=LO[:], in1=D1[:], op=ALU.add)
        P.I(DVE, "tensor_tensor", [kHI, kMID], [kD1], out=D1[:], in0=HI[:], in1=MID[:], op=ALU.subtract)
        P.I(DVE, "tensor_tensor", [kD1, kGE], [kD1], out=D1[:], in0=D1[:], in1=GE[:], op=ALU.mult)
        P.I(DVE, "tensor_tensor", [kMID, kD1], [kHI], out=HI[:], in0=MID[:], in1=D1[:], op=ALU.add)
    compare(LO, kLO)
    PRE, kPRE = T([128, NT, 16]); TOT, kTOT = T([128, NT, 16]); OFFS, kOFFS = T([128, NT, 16])
    cm = CMP[:].rearrange("p t e -> p (t e)")
    for (dst, kdst, lt, klt) in ((PRE, kPRE, strict, kstrict), (TOT, kTOT, ones, kones)):
        for (c0, c1) in ((0, 512), (512, 544)):
            bk = C.bank()
            P.I(PE, "matmul", [klt, kCMP], [("ps", bk)], C.ps[bk][:, 0:c1 - c0], lhsT=lt[:], rhs=cm[:, c0:c1], start=True, stop=True)
            P.I(ACT, "copy", [], [("ps", bk), (kdst, c0)], out=dst[:].rearrange("p t e -> p (t e)")[:, c0:c1], in_=C.ps[bk][:, 0:c1 - c0])
    for (s, a, b_) in segs:
        P.I(DVE, "memset", [], [(kOFFS, a)], OFFS[:, a, :], 0.0)
        for i in range(a, b_ - 1):
            P.I(DVE, "tensor_tensor", [(kOFFS, i), kTOT], [(kOFFS, i + 1)], out=OFFS[:, i + 1, :], in0=OFFS[:, i, :], in1=TOT[:, i, :], op=ALU.add)
    P.I(DVE, "tensor_tensor", [kPRE, kOFFS], [kPRE], out=PRE[:], in0=PRE[:], in1=OFFS[:], op=ALU.add)
    for (s, a, b_) in segs:
        cap = 32.0 if s == 0 else 512.0
        P.I(DVE, "tensor_single_scalar", [kPRE], [(kTOT, s)], out=TOT[:, a:b_, :], in_=PRE[:, a:b_, :], scalar=cap, op=ALU.is_lt)
    P.I(DVE, "tensor_tensor", [kTOT, kCMP], [kCMP], out=CMP[:], in0=CMP[:], in1=TOT[:], op=ALU.mult)
    P.I(DVE, "tensor_scalar_add", [kPRE], [(kPRE, 0)], out=PRE[:, 0:2, :], in0=PRE[:, 0:2, :], scalar1=512.0)
    IDXC, kIDXC = T([128, NT, 16], I32)
    for (big, dst, kdst) in ((10000.0, IDXS, ("IDXS",)), (544.0, IDXC, kIDXC)):
        P.I(DVE, "tensor_scalar_add", [kPRE], [kOFFS], out=OFFS[:], in0=PRE[:], scalar1=-big)
        P.I(DVE, "tensor_tensor", [kOFFS, kCMP], [kOFFS], out=OFFS[:], in0=OFFS[:], in1=CMP[:], op=ALU.mult)
        P.I(DVE, "tensor_scalar_add", [kOFFS], [kOFFS], out=OFFS[:], in0=OFFS[:], scalar1=big)
        P.I(DVE, "tensor_copy", [kOFFS], [kdst], out=dst[:], in_=OFFS[:])
    P.D(SP, [kIDXC], [("IDXC",)], out=IDXC_dram[:, :, :], in_=IDXC[:])
    P.barrier()
    P.sb_reset(m0)


def phase_scatter(P, C, H2R, XE, IDXS):
    m0 = P.sb_mark()
    rows = [P.sb([128, ROWW], F32, "srow") for _ in range(3)]
    holder = {}

    P.nname += 1
    rname = "bcreg%d" % P.nname

    def f0(eng):
        holder["reg"] = eng.alloc_register(rname)
        return eng.reg_mov(holder["reg"], NSLOT - 1)
    P.op(POOL, f0, [], [])
    for i in range(NT):
        b = i % 3
        P.D(SP, [("H2R", i)], [("srow", b)], out=rows[b][:], in_=H2R[i * 128:(i + 1) * 128, :])
        for e in range(16):
            P.dma(POOL, (lambda eng, b=b, i=i, e=e: eng.indirect_dma_start(
                out=XE.ap().rearrange("e s w -> (e s) w"), out_offset=bass.IndirectOffsetOnAxis(ap=IDXS[:, i, e:e + 1], axis=0),
                in_=rows[b][:], in_offset=None, element_offset=e * NSLOT * ROWW, bounds_check=holder["reg"], oob_is_err=False)),
                reads=[("srow", b), ("IDXS",)], writes=[("XE", i, e)])
    P.barrier()
    P.sb_reset(m0)


NSL = 2176


def phase_experts(P, C, nsamp, nexp, xe_ap, gate_ap, w_ap, ye_ap):
    m0 = P.sb_mark()
    nsl = nsamp * 544
    cx0 = nsamp * 512
    hidT = P.sb([128, 16, nsl], BF16, "hidT")
    stg = [P.sb([128, D], F32, "stg") for _ in range(2)]
    wgb = [P.sb([128, 16, 256], BF16, "wgb") for _ in range(2)]
    wub = [P.sb([128, 16, 256], BF16, "wub") for _ in range(2)]
    sil = [P.sb([128, 512], F32, "sil") for _ in range(2)]
    GT = P.sb([128, 2, 5, 4], F32, "GT")
    mA = P.sb_mark()
    nst = 0
    nw = 0
    blocks = [(b * 512, 512) for b in range(nsamp)] + [(cx0, nsamp * 32)]
    for el in range(nexp):
        ep = el % 2
        P.sb_reset(mA)
        xeT = P.sb([128, 16, nsl], BF16, "xeT")
        kx = "xeT%d" % el
        for b in range(nsamp):
            P.D(SP, [("XE",)], [("GT", ep, b)], out=GT[:, ep, 0:4, b], in_=gate_ap(b, el, 0, 512).rearrange("(j p) -> p j", p=128), allow_slow_non_contiguous=True)
            P.D(SP, [("XE",)], [("GTc", ep, b)], out=GT[b * 32:(b + 1) * 32, ep, 4, 0:1], in_=gate_ap(b, el, 512, 544).rearrange("(p o) -> p o", o=1), allow_slow_non_contiguous=True)
        for b in range(nsamp):
            for j in range(5):
                np_ = 128 if j < 4 else 32
                col0 = b * 512 + j * 128 if j < 4 else cx0 + b * 32
                sb_ = nst % 2
                nst += 1
                ST = stg[sb_]
                P.D(SP, [("XE",)], [("stg", sb_)], out=ST[0:np_, :], in_=xe_ap(b, el, j * 128, j * 128 + np_))
                for kg in range(4):
                    bk = C.bank()
                    for q in range(4):
                        kt = kg * 4 + q
                        P.I(PE, "transpose", [("stg", sb_), "ident"], [("ps", bk)], out=C.ps[bk][:, q * 128:q * 128 + np_], in_=ST[0:np_, kt * 128:(kt + 1) * 128], identity=C.ident[0:np_, 0:np_])
                    copy_op(P, alt(kg), xeT[:, kg * 4:(kg + 1) * 4, col0:col0 + np_], C.ps[bk][:, :].rearrange("p (q t) -> p q t", q=4)[:, :, 0:np_], [], [("ps", bk), (kx, b, j, kg)])
        for fb in range(8):
            wb = nw % 2
            nw += 1
            P.D(POOL, [], [("wgb", wb)], out=wgb[wb][:], in_=w_ap("gate", el)[:, fb * 256:(fb + 1) * 256].rearrange("(kt p) n -> p kt n", p=128))
            P.D(POOL, [], [("wub", wb)], out=wub[wb][:], in_=w_ap("up", el)[:, fb * 256:(fb + 1) * 256].rearrange("(kt p) n -> p kt n", p=128))
            for fl in range(2):
                ft = fb * 2 + fl
                for nb, (c0, w) in enumerate(blocks):
                    bg = C.bank()
                    bu = C.bank()
                    for kt in range(16):
                        P.I(PE, "matmul", [("wgb", wb), (kx,)], [("ps", bg)], C.ps[bg][:, 0:w], lhsT=wgb[wb][:, kt, fl * 128:(fl + 1) * 128], rhs=xeT[:, kt, c0:c0 + w], start=(kt == 0), stop=(kt == 15))
                    for kt in range(16):
                        P.I(PE, "matmul", [("wub", wb), (kx,)], [("ps", bu)], C.ps[bu][:, 0:w], lhsT=wub[wb][:, kt, fl * 128:(fl + 1) * 128], rhs=xeT[:, kt, c0:c0 + w], start=(kt == 0), stop=(kt == 15))
                    sb_ = (ft * len(blocks) + nb) % 2
                    P.I(ACT, "activation", [], [("ps", bg), ("sil", sb_)], out=sil[sb_][:, 0:w], in_=C.ps[bg][:, 0:w], func=AF.Silu)
                    P.I(DVE, "tensor_tensor", [("sil", sb_)], [("ps", bu), ("hidT", el, ft, nb)], out=hidT[:, ft, c0:c0 + w], in0=C.ps[bu][:, 0:w], in1=sil[sb_][:, 0:w], op=ALU.mult)
        P.barrier()
        P.sb_reset(mA)
        wd = P.sb([128, 16, D], BF16, "wd")
        kw = "wd%d" % el
        for db in range(8):
            P.D(POOL, [], [(kw, db)], out=wd[:, :, db * 256:(db + 1) * 256], in_=w_ap("down", el)[:, db * 256:(db + 1) * 256].rearrange("(kt p) n -> p kt n", p=128))
        for b in range(nsamp):
            for j in range(5):
                if j == 4 and b > 0:
                    continue
                if j < 4:
                    col0, np_ = b * 512 + j * 128, 128
                    gcol = GT[:, ep, j, b:b + 1]
                else:
                    col0, np_ = cx0, nsamp * 32
                    gcol = GT[0:np_, ep, 4, 0:1]
                sb_ = nst % 2
                nst += 1
                ST = stg[sb_]
                for dk in range(4):
                    bk = C.bank()
                    for ft in range(16):
                        P.I(PE, "matmul", [("hidT", el, ft), (kw,)], [("ps", bk)], C.ps[bk][0:np_, :], lhsT=hidT[:, ft, col0:col0 + np_], rhs=wd[:, ft, dk * 512:(dk + 1) * 512], start=(ft == 0), stop=(ft == 15))
                    if dk % 2 == 0:
                        P.I(ACT, "activation", [("GT", ep), ("GTc", ep)], [("ps", bk), ("stg", sb_, dk)], out=ST[0:np_, dk * 512:(dk + 1) * 512], in_=C.ps[bk][0:np_, :], func=AF.Copy, scale=gcol)
                    else:
                        P.I(DVE, "tensor_scalar_mul", [("GT", ep), ("GTc", ep)], [("ps", bk), ("stg", sb_, dk)], out=ST[0:np_, dk * 512:(dk + 1) * 512], in0=C.ps[bk][0:np_, :], scalar1=gcol)
                if j < 4:
                    P.D(SP, [("stg", sb_)], [("YE", el, b, j)], out=ye_ap(b, el, j * 128, (j + 1) * 128), in_=ST[:])
                else:
                    for b2 in range(nsamp):
                        P.D(SP, [("stg", sb_)], [("YE", el, b2, 4)], out=ye_ap(b2, el, 512, 544), in_=ST[b2 * 32:(b2 + 1) * 32, :])
        P.barrier()
    P.sb_reset(m0)


def phase_combine(P, C, X1, YEp, IDXC_in, OUT, MODV, ln_g, ln_b, t_lo, out_off, l=0):
    m0 = P.sb_mark()
    idx = P.sb([128, NT, 16], I32, "cidx")
    P.D(SP, [], ["cidx"], out=idx[:], in_=IDXC_in[:, :, :])
    gt2 = []
    for r in range(2):
        t = P.sb([128, D], F32, "gt2")
        bcast_load(P, SP, t, ("gt2", r), MODV[l, r, 5 * D:6 * D])
        gt2.append(t)
    g2 = P.sb([128, D], F32, "g2"); b2 = P.sb([128, D], F32, "b2")
    bcast_load(P, SP, g2, "g2", ln_g[l, :]); bcast_load(P, SP, b2, "b2", ln_b[l, :])
    NG = 4
    gb = [P.sb([128, D], F32, "gb") for _ in range(NG)]
    acc = [P.sb([128, D], F32, "acc") for _ in range(2)]
    xs = [P.sb([128, D], F32, "cxs") for _ in range(2)]
    st = P.sb([128, 4, 6], F32, "cst"); mv = P.sb([128, 2], F32, "cmv"); rs = P.sb([128, 2], F32, "crs")
    yv = YEp.ap().rearrange("e s w -> (e s) w")
    ng = 0
    for i in range(t_lo, NT):
        b = i % 2
        r = 1 if i < 2 else 0
        A, XS = acc[b], xs[b]
        kA, kXS = ("acc", b), ("cxs", b)
        P.D(SP, [("X1", i)], [kXS], out=XS[:], in_=X1[i * 128:(i + 1) * 128, :])
        for e in range(16):
            if e == 0:
                dst, kdst = A, kA
            else:
                gi = ng % NG
                ng += 1
                dst, kdst = gb[gi], ("gb", gi)
            P.dma(POOL, (lambda eng, dst=dst, i=i, e=e: eng.indirect_dma_start(
                out=dst[:], out_offset=None, in_=yv, in_offset=bass.IndirectOffsetOnAxis(ap=idx[:, i, e:e + 1], axis=0),
                element_offset=e * 545 * D)), reads=["cidx", ("YEp",)], writes=[kdst])
            if e > 0:
                P.I(DVE if e % 2 else POOL, "tensor_tensor", [kdst, kA], [kA], out=A[:], in0=A[:], in1=dst[:], op=ALU.add)
        P.I(DVE, "tensor_tensor", [kA, ("gt2", r)], [kA], out=A[:], in0=A[:], in1=gt2[r][:], op=ALU.mult)
        P.I(DVE, "scalar_tensor_tensor", [kXS, kA], [kA], out=A[:], in0=XS[:], scalar=ALPHA_, in1=A[:], op0=ALU.mult, op1=ALU.add)
        ln_apply(P, C, A, kA, A, kA, st, mv, rs, "c1", g2[:], "g2", b2[:], "b2")
        o0 = i * 128 - out_off
        P.D(SP, [kA], [("X", i)], out=OUT[o0:o0 + 128, :], in_=A[:])
    P.barrier()
    P.sb_reset(m0)


def _host_consts():
    c = {}
    c["ident"] = np.eye(128, dtype=np.float32)
    kp = np.arange(128)[:, None]; qp = np.arange(128)[None, :]
    c["mprev"] = np.tile((kp >= qp).astype(np.float32), (1, 4))
    c["mnext"] = np.tile((kp <= qp).astype(np.float32), (1, 4))
    nf = 32
    inv = (10000.0 ** (-np.arange(nf, dtype=np.float32) / nf)).astype(np.float32)
    t = np.arange(4096)
    rows = (t // 64).astype(np.float32); cols = (t % 64).astype(np.float32)
    ang_r = rows[:, None] * inv[None, :]; ang_c = cols[:, None] * inv[None, :]
    cos = np.concatenate([np.cos(ang_r), np.cos(ang_r), np.cos(ang_c), np.cos(ang_c)], 1)
    sin = np.concatenate([-np.sin(ang_r), np.sin(ang_r), -np.sin(ang_c), np.sin(ang_c)], 1)
    c["cos"] = cos.reshape(32, 128, 128).astype(np.float32)
    c["sin"] = sin.reshape(32, 128, 128).astype(np.float32)
    s_ = np.arange(8, dtype=np.float32)
    er = np.concatenate([7 - s_, s_ - 7, s_ + 1, s_, -s_, 8 - s_]).astype(np.float32)
    c["erow"] = np.tile(er[None, :], (128, 1))
    E = np.zeros((8, 128, 240), np.float32)
    for r in range(8):
        for j in range(16):
            E[r, r * 16 + j, 7 * 16 + j] = 1.0
    c["E"] = E
    sb = np.arange(128)[:, None] // 16; tb = np.arange(128)[None, :] // 16
    c["toemf"] = (tb >= sb).astype(np.float32)
    c["toemb"] = (sb >= tb).astype(np.float32)
    a = np.arange(128)
    same = (a[:, None] // 64) == (a[None, :] // 64)
    le = a[:, None] <= a[None, :]
    ge = a[:, None] >= a[None, :]
    c["trif"] = (same & le).astype(np.float32) * (-1.0 / 16.0)
    c["trib"] = (same & ge).astype(np.float32) * (-1.0 / 16.0)
    c["blk"] = same.astype(np.float32) * (-1.0 / 16.0)
    c["cind"] = np.stack([(a < 64), (a >= 64)], 1).astype(np.float32) * (-1.0 / 16.0)
    c["gmaskf"] = np.tile((same & le).astype(np.float32), (1, 4))
    c["gmaskb"] = np.tile((same & ge).astype(np.float32), (1, 4))
    c["ones"] = np.ones((128, 128), np.float32)
    c["strict"] = (a[:, None] < a[None, :]).astype(np.float32)
    c["tgt"] = np.tile(np.concatenate([np.full(16, 32.0), np.full(16, 512.0)])[None, :], (128, 1)).astype(np.float32)
    return c
HOSTC = _host_consts()


S5N = ["lam_re", "lam_im", "log_dt", "b_re", "b_im", "c_re", "c_im", "d", "w_glu", "b_glu"]
GLN = ["w_gate", "b_gate", "norm_g"]


def _decl_consts(P):
    return {k: P.dram("c_" + k, list(v.shape), F32, kind="ExternalInput") for k, v in HOSTC.items()}


def build_mod():
    P = Prog(); C = Ctx()
    cst = {"ident": P.dram("c_ident", [128, 128], F32, kind="ExternalInput")}
    c_in = P.dram("c_in", [5, D], F32, kind="ExternalInput")
    w_ada = P.dram("w_ada", [2, D, 1536], F32, kind="ExternalInput")
    b_ada = P.dram("b_ada", [2, 1536], F32, kind="ExternalInput")
    MODS = P.dram("MODS", [2, 5, 1536], F32, kind="ExternalOutput")
    setup_consts(P, C, cst)
    phase_mod(P, C, c_in, w_ada, b_ada, MODS, R=5, NBLK=3)
    P.finish()
    return P.emit()


def build_layer(with_combine, shapes):
    P = Prog(); C = Ctx()
    cst = _decl_consts(P)
    ins = {k: P.dram(k, list(s), F32, kind="ExternalInput") for k, s in shapes.items()}
    MODV = P.dram("MODV", [1, 2, 6 * D], F32, kind="ExternalInput")
    setup_consts(P, C, cst)
    if with_combine:
        X1p = P.dram("X1p", [NTOK, D], F32, kind="ExternalInput")
        YEp = P.dram("YEp", [16, 545, D], F32, kind="ExternalInput")
        IDXp = P.dram("IDXp", [128, NT, 16], I32, kind="ExternalInput")
        MODVp = P.dram("MODVp", [1, 2, 6 * D], F32, kind="ExternalInput")
        l2g = P.dram("ln2_g", [1, D], F32, kind="ExternalInput")
        l2b = P.dram("ln2_b", [1, D], F32, kind="ExternalInput")
        X = P.dram("X2s", [NTOK, D], F32)
        phase_combine(P, C, X1p, YEp, IDXp, X, MODVp, l2g, l2b, 0, 0)
    else:
        X = P.dram("xin", [NTOK, D], F32, kind="ExternalInput")
    PROJ = P.dram("PROJ", [NTOK, NIN], F32)
    MIXT = P.dram("MIXT", [16, 128, NTOK], BF16)
    OF = P.dram("OF", [NTOK, 512], F32)
    H2R = P.dram("H2R", [NTOK, ROWW], F32)
    X1 = P.dram("X1", [NTOK, D], F32, kind="ExternalOutput")
    XE = P.dram("XE", [16, NSLOT, ROWW], F32, kind="ExternalOutput")
    IDXC = P.dram("IDXC", [128, NT, 16], I32, kind="ExternalOutput")
    phase_inproj(P, C, 0, X, PROJ, ins["w_in"], MODV)
    phase_attn(P, C, 0, PROJ, MIXT, ins["attn_sink"], cst)
    phase_s5(P, C, 0, PROJ, MIXT, {n: ins["ssm_" + n] for n in S5N}, cst)
    phase_gla(P, C, 0, PROJ, MIXT, OF, {n: ins["gla_" + n] for n in GLN}, cst)
    AFF = P.sb([128, NT, 16], F32, "AFF")
    IDXS = P.sb([128, NT, 16], I32, "IDXS")
    phase_wout(P, C, 0, X, MIXT, X1, H2R, AFF, ins["w_out"], MODV, ins["ln1_g"], ins["ln1_b"], ins["router"])
    phase_route(P, C, AFF, IDXS, IDXC, cst)
    phase_scatter(P, C, H2R, XE, IDXS)
    P.finish()
    return P.emit()


def build_experts():
    P = Prog(); C = Ctx()
    cst = {"ident": P.dram("c_ident", [128, 128], F32, kind="ExternalInput")}
    XEc = P.dram("XEc", [4, 2, NSLOT, ROWW], F32, kind="ExternalInput")
    GATE = P.dram("GATE", [4, 2, NSLOT], F32, kind="ExternalInput")
    WG = P.dram("WG", [2, D, D], F32, kind="ExternalInput")
    WU = P.dram("WU", [2, D, D], F32, kind="ExternalInput")
    WD = P.dram("WD", [2, D, D], F32, kind="ExternalInput")
    YE = P.dram("YE", [4, 2, NSLOT, D], F32, kind="ExternalOutput")
    setup_consts(P, C, cst)
    wmap = {"gate": WG, "up": WU, "down": WD}
    phase_experts(P, C, 4, 2, lambda b, el, r0, r1: XEc[b, el, r0:r1, 0:D], lambda b, el, r0, r1: GATE[b, el, r0:r1],
                  lambda kind, el: wmap[kind][el], lambda b, el, r0, r1: YE[b, el, r0:r1, :])
    P.finish()
    return P.emit()


def build_final():
    P = Prog(); C = Ctx()
    cst = {"ident": P.dram("c_ident", [128, 128], F32, kind="ExternalInput")}
    X1p = P.dram("X1p", [NTOK, D], F32, kind="ExternalInput")
    YEp = P.dram("YEp", [16, 545, D], F32, kind="ExternalInput")
    IDXp = P.dram("IDXp", [128, NT, 16], I32, kind="ExternalInput")
    MODVp = P.dram("MODVp", [1, 2, 6 * D], F32, kind="ExternalInput")
    l2g = P.dram("ln2_g", [1, D], F32, kind="ExternalInput")
    l2b = P.dram("ln2_b", [1, D], F32, kind="ExternalInput")
    OUT = P.dram("OUT", [4096, D], F32, kind="ExternalOutput")
    setup_consts(P, C, cst)
    phase_combine(P, C, X1p, YEp, IDXp, OUT, MODVp, l2g, l2b, 2, 256)
    P.finish()
    return P.emit()


LAYER_KEYS = ["w_in", "attn_sink", "ssm_lam_re", "ssm_lam_im", "ssm_log_dt", "ssm_b_re", "ssm_b_im", "ssm_c_re", "ssm_c_im",
              "ssm_d", "ssm_w_glu", "ssm_b_glu", "gla_w_gate", "gla_b_gate", "gla_norm_g", "w_out", "ln1_g", "ln1_b", "router"]


def kernel_multi(**inp):
    inp = {k: np.ascontiguousarray(np.asarray(v)) for k, v in inp.items()}
    f32 = np.float32
    cmap = {"c_" + k: v for k, v in HOSTC.items()}
    ident = {"c_ident": HOSTC["ident"]}
    c_in = np.concatenate([inp["c"], inp["c_ctx"][None]], 0).astype(f32)
    maps = []
    for c in range(8):
        sl = slice(c * 1536, (c + 1) * 1536)
        maps.append(dict(ident, c_in=c_in, w_ada=np.ascontiguousarray(inp["w_ada"][:, :, sl]), b_ada=np.ascontiguousarray(inp["b_ada"][:, sl])))
    res = run_bass_kernel_spmd(build_mod(), maps, core_ids=list(range(8)))
    mods = np.concatenate([r["MODS"] for r in res.results], axis=2)
    modv = [[np.ascontiguousarray(np.stack([mods[l, b], mods[l, 4]], 0)[None]) for l in range(2)] for b in range(4)]
    shapes = {k: (1,) + inp[k].shape[1:] for k in LAYER_KEYS}
    prev = None
    nc_exp = None
    for l in range(2):
        lw = {k: np.ascontiguousarray(inp[k][l:l + 1]) for k in LAYER_KEYS}
        maps = []
        for b in range(4):
            m = dict(cmap); m.update(lw); m["MODV"] = modv[b][l]
            if l == 0:
                m["xin"] = np.concatenate([inp["ctx"][b], inp["x"][b]], 0)
            else:
                m.update(prev[b])
            maps.append(m)
        res = run_bass_kernel_spmd(build_layer(l > 0, shapes), maps, core_ids=list(range(4)))
        X1 = [res.results[b]["X1"] for b in range(4)]
        XE = [res.results[b]["XE"] for b in range(4)]
        IDX = [res.results[b]["IDXC"] for b in range(4)]
        maps = []
        for c in range(8):
            xec = np.stack([XE[b][2 * c:2 * c + 2] for b in range(4)], 0)
            gate = np.stack([np.stack([XE[b][2 * c + el, :, 2048 + 2 * c + el] for el in range(2)], 0) for b in range(4)], 0)
            maps.append(dict(ident, XEc=np.ascontiguousarray(xec), GATE=np.ascontiguousarray(gate),
                             WG=np.ascontiguousarray(inp["exp_w_gate"][l, 2 * c:2 * c + 2]),
                             WU=np.ascontiguousarray(inp["exp_w_up"][l, 2 * c:2 * c + 2]),
                             WD=np.ascontiguousarray(inp["exp_w_down"][l, 2 * c:2 * c + 2])))
        if nc_exp is None:
            nc_exp = build_experts()
        res = run_bass_kernel_spmd(nc_exp if l == 0 else build_experts(), maps, core_ids=list(range(8)))
        prev = []
        for b in range(4):
            yep = np.zeros((16, 545, D), f32)
            for c in range(8):
                yep[2 * c:2 * c + 2, :544] = res.results[c]["YE"][b]
            prev.append({"X1p": X1[b], "YEp": yep, "IDXp": IDX[b], "MODVp": modv[b][l],
                         "ln2_g": np.ascontiguousarray(inp["ln2_g"][l:l + 1]), "ln2_b": np.ascontiguousarray(inp["ln2_b"][l:l + 1])})
    maps = [dict(ident, **prev[b]) for b in range(4)]
    res = run_bass_kernel_spmd(build_final(), maps, core_ids=list(range(4)))
    return np.stack([res.results[b]["OUT"] for b in range(4)], 0).astype(f32)


FUSED_KEYS = LAYER_KEYS + ["ln2_g", "ln2_b", "w_ada", "b_ada", "exp_w_gate", "exp_w_up", "exp_w_down"]


def build_fused(shapes):
    P = Prog(); C = Ctx()
    cst = _decl_consts(P)
    ins = {k: P.dram(k, list(shapes[k]), F32, kind="ExternalInput") for k in FUSED_KEYS}
    xin = P.dram("xin", [NTOK, D], F32, kind="ExternalInput")
    c_in = P.dram("c_in", [2, D], F32, kind="ExternalInput")
    OUT = P.dram("OUT", [4096, D], F32, kind="ExternalOutput")
    MODV = P.dram("MODV", [2, 2, 6 * D], F32)
    PROJ = P.dram("PROJ", [NTOK, NIN], F32)
    MIXT = P.dram("MIXT", [16, 128, NTOK], BF16)
    OF = P.dram("OF", [NTOK, 512], F32)
    H2R = P.dram("H2R", [NTOK, ROWW], F32)
    X1 = P.dram("X1", [NTOK, D], F32)
    XN = P.dram("XN", [NTOK, D], F32)
    XE = P.dram("XE", [16, NSLOT, ROWW], F32)
    YE = P.dram("YE", [16, 545, D], F32)
    IDXC = P.dram("IDXC", [128, NT, 16], I32)
    setup_consts(P, C, cst)
    mz = P.sb_mark()
    zt = P.sb([16, D], F32, "zt")
    P.I(POOL, "memset", [], ["zt"], zt[:], 0.0)
    P.D(SP, ["zt"], [("YE", "z")], out=YE[:, 544, :], in_=zt[:])
    P.barrier()
    P.sb_reset(mz)
    phase_mod(P, C, c_in, ins["w_ada"], ins["b_ada"], MODV, R=2, NBLK=24)
    X = xin
    for l in range(2):
        phase_inproj(P, C, l, X, PROJ, ins["w_in"], MODV)
        phase_attn(P, C, l, PROJ, MIXT, ins["attn_sink"], cst)
        phase_s5(P, C, l, PROJ, MIXT, {n: ins["ssm_" + n] for n in S5N}, cst)
        phase_gla(P, C, l, PROJ, MIXT, OF, {n: ins["gla_" + n] for n in GLN}, cst)
        m0 = P.sb_mark()
        AFF = P.sb([128, NT, 16], F32, "AFF")
        IDXS = P.sb([128, NT, 16], I32, "IDXS")
        phase_wout(P, C, l, X, MIXT, X1, H2R, AFF, ins["w_out"], MODV, ins["ln1_g"], ins["ln1_b"], ins["router"])
        phase_route(P, C, AFF, IDXS, IDXC, cst)
        phase_scatter(P, C, H2R, XE, IDXS)
        P.sb_reset(m0)
        phase_experts(P, C, 1, 16, lambda b, el, r0, r1: XE[el, r0:r1, 0:D], lambda b, el, r0, r1: XE[el, r0:r1, 2048 + el],
                      lambda kind, el, l=l: ins["exp_w_" + kind][l, el], lambda b, el, r0, r1: YE[el, r0:r1, :])
        if l == 0:
            phase_combine(P, C, X1, YE, IDXC, XN, MODV, ins["ln2_g"], ins["ln2_b"], 0, 0, l)
            X = XN
        else:
            phase_combine(P, C, X1, YE, IDXC, OUT, MODV, ins["ln2_g"], ins["ln2_b"], 2, 256, l)
    P.finish()
    nc = P.emit()
    return nc


def fused_maps(inp, samples):
    cmap = {"c_" + k: v for k, v in HOSTC.items()}
    shared = {k: inp[k] for k in FUSED_KEYS}
    maps = []
    for b in samples:
        m = dict(cmap); m.update(shared)
        m["xin"] = np.concatenate([inp["ctx"][b], inp["x"][b]], 0)
        m["c_in"] = np.ascontiguousarray(np.stack([inp["c"][b], inp["c_ctx"]], 0))
        maps.append(m)
    return maps


def kernel(**inp):
    inp = {k: np.ascontiguousarray(np.asarray(v)) for k, v in inp.items()}
    shapes = {k: inp[k].shape for k in FUSED_KEYS}
    nc = build_fused(shapes)
    maps = fused_maps(inp, [c % 4 for c in range(8)])
    res = run_bass_kernel_spmd(nc, maps, core_ids=list(range(8)))
    return np.stack([res.results[b]["OUT"] for b in range(4)], 0).astype(np.float32)
```

```python
import numpy as np
import concourse.bass as bass
import concourse.mybir as mybir
from concourse.bass_utils import run_bass_kernel_spmd

F32 = mybir.dt.float32
BF16 = mybir.dt.bfloat16
I32 = mybir.dt.int32
ALU = mybir.AluOpType
AF = mybir.ActivationFunctionType
AX = mybir.AxisListType

PE, ACT, DVE, POOL, SP = "pe", "act", "dve", "pool", "sp"
ENGS = (PE, ACT, DVE, POOL, SP)
NDMASEM = 8


class Prog:
    def __init__(self):
        self.nc = bass.Bass("TRN2", target_bir_lowering=False)
        self.ops = {e: [] for e in ENGS}
        self.state = {}
        self.dmas = []
        self.ndma = {e: 0 for e in ENGS}
        self.sb_off = 20608
        self.sb_hi = 0
        self.nname = 0
        self.psum = []
        self.sb_cap = 229376

    def dram(self, name, shape, dtype, kind="Internal"):
        return self.nc.dram_tensor(name, list(shape), dtype, kind=kind)

    def sb(self, shape, dtype, name=None):
        size = int(np.prod(shape[1:])) * mybir.dt.size(dtype) if hasattr(mybir.dt, "size") else None
        if size is None:
            size = int(np.prod(shape[1:])) * {F32: 4, BF16: 2, I32: 4}[dtype]
        size = (size + 31) // 32 * 32
        self.nname += 1
        nm = (name or "t") + "_%d" % self.nname
        t = self.nc.alloc_sbuf_tensor_at(nm, list(shape), dtype, offset=self.sb_off)
        self.sb_off += size
        assert self.sb_off <= self.sb_cap, ("SBUF overflow", nm, self.sb_off)
        self.sb_hi = max(self.sb_hi, self.sb_off)
        return t

    def sb_mark(self):
        return self.sb_off

    def sb_reset(self, mark):
        self.sb_off = mark

    @staticmethod
    def _conf(a, b):
        n = min(len(a), len(b))
        return a[:n] == b[:n]

    def _deps(self, reads, writes):
        deps = set()
        for k in reads:
            root = self.state.setdefault(k[0], {})
            for k2, st in root.items():
                if self._conf(k, k2) and st[0] is not None:
                    deps.add(st[0])
        for k in writes:
            root = self.state.setdefault(k[0], {})
            for k2, st in root.items():
                if self._conf(k, k2):
                    if st[0] is not None:
                        deps.add(st[0])
                    deps.update(st[1])
        return deps

    def _commit(self, ev, reads, writes):
        for k in reads:
            root = self.state[k[0]]
            st = root.setdefault(k, [None, []])
            st[1].append(ev)
        for k in writes:
            root = self.state[k[0]]
            for k2 in [k2 for k2 in root if len(k2) > len(k) and self._conf(k, k2)]:
                del root[k2]
            root[k] = [ev, []]

    @staticmethod
    def _norm(keys):
        out = []
        for k in keys:
            if isinstance(k, str):
                k = (k,)
            assert isinstance(k[0], str), k
            out.append(tuple(k))
        return out

    def I(self, eng, name, reads, writes, *a, **kw):
        return self.op(eng, lambda e: getattr(e, name)(*a, **kw), reads, writes)

    def D(self, q, reads, writes, **kw):
        return self.dma(q, lambda e: e.dma_start(**kw), reads, writes)

    def op(self, eng, fn, reads=(), writes=()):
        reads = self._norm(reads)
        writes = self._norm(writes)
        deps = self._deps(reads, writes)
        idx = len(self.ops[eng])
        ev = ("c", eng, idx)
        self.ops[eng].append(dict(fn=fn, waits=deps, dma=None, signal=False))
        self._commit(ev, reads, writes)
        return ev

    def dma(self, q, fn, reads=(), writes=()):
        reads = self._norm(reads)
        writes = self._norm(writes)
        deps = self._deps(reads, writes)
        j = self.ndma[q]
        self.ndma[q] += 1
        si, val = j % NDMASEM, 16 * (j // NDMASEM + 1)
        did = len(self.dmas)
        self.dmas.append((q, si, val))
        if j >= NDMASEM:
            deps.add(("d", self._last_dma[(q, si)]))
        if not hasattr(self, "_last_dma"):
            self._last_dma = {}
        self._last_dma[(q, si)] = did
        ev = ("d", did)
        self.ops[q].append(dict(fn=fn, waits=deps, dma=did, signal=False))
        self._commit(ev, reads, writes)
        return ev

    def barrier(self):
        evs = set()
        for e in ENGS:
            for i in range(len(self.ops[e]) - 1, -1, -1):
                o = self.ops[e][i]
                if o["fn"] is not None and o["dma"] is None:
                    evs.add(("c", e, i))
                    break
        if hasattr(self, "_last_dma"):
            for did in self._last_dma.values():
                evs.add(("d", did))
        for e in ENGS:
            self.ops[e].append(dict(fn=None, waits=set(evs), dma=None, signal=False))
        self.state = {}

    def emit(self):
        nc = self.nc
        plan = {e: [] for e in ENGS}
        for e in ENGS:
            wc = {}
            wd = {}
            for i, o in enumerate(self.ops[e]):
                need_c, need_d = {}, {}
                for ev in o["waits"]:
                    if ev[0] == "c":
                        _, se, si_ = ev
                        if se == e and e == PE:
                            continue
                        if se == e and si_ >= i:
                            continue
                        if wc.get(se, -1) >= si_:
                            continue
                        need_c[se] = max(need_c.get(se, -1), si_)
                    else:
                        q, si_, val = self.dmas[ev[1]]
                        if wd.get((q, si_), 0) >= val:
                            continue
                        need_d[(q, si_)] = max(need_d.get((q, si_), 0), val)
                for se, si_ in need_c.items():
                    wc[se] = si_
                    self.ops[se][si_]["signal"] = True
                for k, v in need_d.items():
                    wd[k] = v
                plan[e].append((need_c, need_d))
        semval = {}
        for e in ENGS:
            c = 0
            for i, o in enumerate(self.ops[e]):
                if o["signal"]:
                    c += 1
                    semval[(e, i)] = c
            self.nsig = getattr(self, "nsig", {})
            self.nsig[e] = c
        from contextlib import ExitStack
        with ExitStack() as es:
            csem = {e: es.enter_context(nc.semaphore("c_" + e)) for e in ENGS}
            dsem = {(q, s): es.enter_context(nc.semaphore("d_%s_%d" % (q, s)))
                    for q in ENGS for s in range(NDMASEM) if self.ndma[q] > s}
            block = es.enter_context(nc.Block())

            def run(e, engobj):
                for i, o in enumerate(self.ops[e]):
                    need_c, need_d = plan[e][i]
                    for se, si_ in need_c.items():
                        engobj.wait_ge(csem[se], semval[(se, si_)])
                    for k, v in need_d.items():
                        engobj.wait_ge(dsem[k], v)
                    if o["fn"] is None:
                        continue
                    inst = o["fn"](engobj)
                    if o["dma"] is not None:
                        q, s, v = self.dmas[o["dma"]]
                        inst.then_inc(dsem[(q, s)], 16)
                    elif o["signal"]:
                        inst.then_inc(csem[e], 1)

            @block.tensor
            def _(eng):
                run(PE, eng)

            @block.scalar
            def _(eng):
                run(ACT, eng)

            @block.vector
            def _(eng):
                run(DVE, eng)

            @block.gpsimd
            def _(eng):
                run(POOL, eng)

            @block.sync
            def _(eng):
                run(SP, eng)
        return nc

    def finish(self):
        self.barrier()


NT = 34
NTOK = 4352
D = 2048
NIN = 3616


def alt(i):
    return ACT if i % 2 == 0 else DVE


def copy_op(P, eng, out, in_, reads, writes):
    if eng == ACT:
        P.op(ACT, lambda e: e.copy(out=out, in_=in_), reads, writes)
    else:
        P.op(eng, lambda e: e.tensor_copy(out=out, in_=in_), reads, writes)


class Ctx:
    pass


def setup_consts(P, C, cst):
    C.ident = P.sb([128, 128], F32, "ident")
    C.identb = P.sb([128, 128], BF16, "identb")
    P.dma(SP, lambda e: e.dma_start(out=C.ident[:], in_=cst["ident"][:, :]), writes=["ident"])
    P.op(DVE, lambda e: e.tensor_copy(out=C.identb[:], in_=C.ident[:]), reads=["ident"], writes=["identb"])
    C.ps = [P.nc.alloc_psum_tensor("psb%d" % i, [128, 512], F32) for i in range(8)]
    C.nbank = 0

    def bank():
        b = C.nbank % 8
        C.nbank += 1
        return b
    C.bank = bank


def phase_mod(P, C, c_in, w_ada, b_ada, MODS, R=5, NBLK=3):
    m0 = P.sb_mark()
    cT = P.sb([128, 16, R], F32, "cT")
    for r in range(R):
        P.D(SP, [], [("cT", r)], out=cT[:, :, r], in_=c_in[r, :].rearrange("(kt p) -> p kt", p=128), allow_slow_non_contiguous=True)
    P.I(ACT, "activation", ["cT"], ["cT"], out=cT[:], in_=cT[:], func=AF.Silu)
    wts = [P.sb([128, 16, 512], F32, "wada") for _ in range(2)]
    bts = [P.sb([R, 512], F32, "bada") for _ in range(2)]
    rts = [P.sb([R, 512], F32, "rada") for _ in range(2)]
    it = 0
    for l in range(2):
        for nb in range(NBLK):
            b = it % 2
            it += 1
            for kh in range(2):
                P.D(SP if kh == 0 else ACT, [], [("wada", b, kh)], out=wts[b][:, kh * 8:(kh + 1) * 8, :],
                    in_=w_ada[l, kh * 1024:(kh + 1) * 1024, nb * 512:(nb + 1) * 512].rearrange("(kt p) n -> p kt n", p=128))
            P.D(SP, [], [("bada", b)], out=bts[b][:], in_=b_ada[l, nb * 512:(nb + 1) * 512].partition_broadcast(R))
            bk = C.bank()
            for kt in range(16):
                P.I(PE, "matmul", ["cT", ("wada", b)], [("ps", bk)], C.ps[bk][0:R, :], lhsT=cT[:, kt, :], rhs=wts[b][:, kt, :], start=(kt == 0), stop=(kt == 15))
            P.I(DVE, "tensor_tensor", [("bada", b)], [("ps", bk), ("rada", b)], out=rts[b][:], in0=C.ps[bk][0:R, :], in1=bts[b][:], op=ALU.add)
            P.D(SP, [("rada", b)], [("MODS", l, nb)], out=MODS[l, :, nb * 512:(nb + 1) * 512], in_=rts[b][:])
    P.barrier()
    P.sb_reset(m0)


def ln_stats(P, C, xt, key, st, mv, rs, tag):
    for j in range(4):
        P.op(DVE, lambda e, j=j: e.bn_stats(out=st[:, j, :], in_=xt[:, j * 512:(j + 1) * 512]),
             reads=[key], writes=[(tag + "st", j)])
    P.op(DVE, lambda e: e.bn_aggr(out=mv[:], in_=st[:].rearrange("p a b -> p (a b)")), reads=[tag + "st"], writes=[tag + "mv"])
    P.op(DVE, lambda e: e.tensor_scalar_add(out=rs[:, 0:1], in0=mv[:, 1:2], scalar1=1e-6), reads=[tag + "mv"], writes=[(tag + "rs", 0)])
    P.op(ACT, lambda e: e.activation(out=rs[:, 0:1], in_=rs[:, 0:1], func=AF.Ln), reads=[(tag + "rs", 0)], writes=[(tag + "rs", 0)])
    P.op(ACT, lambda e: e.activation(out=rs[:, 0:1], in_=rs[:, 0:1], func=AF.Exp, scale=-0.5), reads=[(tag + "rs", 0)], writes=[(tag + "rs", 0)])
    P.op(DVE, lambda e: e.scalar_tensor_tensor(out=rs[:, 1:2], in0=mv[:, 0:1], scalar=-1.0, in1=rs[:, 0:1], op0=ALU.mult, op1=ALU.mult),
         reads=[tag + "mv", (tag + "rs", 0)], writes=[(tag + "rs", 1)])


def load_modT(P, MODV, l, chunk, dst, key, plus1):
    for r in range(2):
        P.dma(SP, lambda e, r=r: e.dma_start(out=dst[:, r, :], in_=MODV[l, r, chunk * 2048:(chunk + 1) * 2048].rearrange("(kt p) -> p kt", p=128),
                                            allow_slow_non_contiguous=True), reads=[("MODV", l)], writes=[(key, r)])
    if plus1:
        P.op(DVE, lambda e: e.tensor_scalar_add(out=dst[:], in0=dst[:], scalar1=1.0), reads=[key], writes=[key])


def phase_inproj(P, C, l, X, PROJ, w_in, MODV):
    m0 = P.sb_mark()
    wbf = P.sb([128, 16, NIN], BF16, "wbf")
    for kt in range(16):
        for h in range(2):
            P.dma(POOL, lambda e, kt=kt, h=h: e.dma_start(out=wbf[:, kt, h * 1808:(h + 1) * 1808],
                                                         in_=w_in[l, kt * 128:(kt + 1) * 128, h * 1808:(h + 1) * 1808]),
                  writes=[("wbf", kt, h)])
    scT = P.sb([128, 2, 16], F32, "scT")
    shT = P.sb([128, 2, 16], F32, "shT")
    load_modT(P, MODV, l, 1, scT, "scT", True)
    load_modT(P, MODV, l, 0, shT, "shT", False)
    xts = [P.sb([128, D], F32, "xt") for _ in range(2)]
    xns = [P.sb([128, D], BF16, "xn") for _ in range(2)]
    hTs = [P.sb([128, 16, 128], BF16, "hT") for _ in range(2)]
    ots = [P.sb([128, NIN], F32, "ot") for _ in range(2)]
    st = P.sb([128, 4, 6], F32, "st")
    mv = P.sb([128, 2], F32, "mv")
    rs = P.sb([128, 2], F32, "rs")
    for i in range(NT):
        b = i % 2
        r = 1 if i < 2 else 0
        xt, xn, hT, ot = xts[b], xns[b], hTs[b], ots[b]
        P.dma(SP, lambda e, i=i, xt=xt: e.dma_start(out=xt[:], in_=X[i * 128:(i + 1) * 128, :]), reads=[("X", i)], writes=[("xt", b)])
        ln_stats(P, C, xt, ("xt", b), st, mv, rs, "ip")
        P.op(ACT, lambda e, xt=xt, xn=xn: e.activation(out=xn[:], in_=xt[:], func=AF.Identity, bias=rs[:, 1:2], scale=rs[:, 0:1]),
             reads=[("xt", b), "iprs"], writes=[("xn", b)])
        for kg in range(4):
            bk = C.bank()
            psb = C.ps[bk][:].bitcast(BF16)
            for j in range(4):
                kt = kg * 4 + j
                P.op(PE, lambda e, j=j, kt=kt, psb=psb, xn=xn: e.transpose(out=psb[:, j * 128:(j + 1) * 128], in_=xn[:, kt * 128:(kt + 1) * 128], identity=C.identb[:]),
                     reads=[("xn", b), "identb"], writes=[("ps", bk)])
            for j in range(4):
                kt = kg * 4 + j
                if j % 2 == 0:
                    P.op(ACT, lambda e, j=j, kt=kt, psb=psb, hT=hT, r=r: e.activation(out=hT[:, kt, :], in_=psb[:, j * 128:(j + 1) * 128], func=AF.Identity,
                                                                              bias=shT[:, r, kt:kt + 1], scale=scT[:, r, kt:kt + 1]),
                         reads=["scT", "shT"], writes=[("ps", bk), ("hT", b, kt)])
                else:
                    P.op(DVE, lambda e, j=j, kt=kt, psb=psb, hT=hT, r=r: e.tensor_scalar(out=hT[:, kt, :], in0=psb[:, j * 128:(j + 1) * 128],
                                                                                 scalar1=scT[:, r, kt:kt + 1], scalar2=shT[:, r, kt:kt + 1], op0=ALU.mult, op1=ALU.add),
                         reads=["scT", "shT"], writes=[("ps", bk), ("hT", b, kt)])
        for nb in range(8):
            n0 = nb * 512
            w = min(512, NIN - n0)
            bk = C.bank()
            for kt in range(16):
                P.op(PE, lambda e, kt=kt, bk=bk, n0=n0, w=w, hT=hT: e.matmul(C.ps[bk][:, 0:w], lhsT=hT[:, kt, :], rhs=wbf[:, kt, n0:n0 + w],
                                                                          start=(kt == 0), stop=(kt == 15)),
                     reads=[("hT", b), ("wbf", kt)], writes=[("ps", bk)])
            copy_op(P, alt(nb), ot[:, n0:n0 + w], C.ps[bk][:, 0:w], reads=[], writes=[("ps", bk), ("ot", b, nb)])
        P.dma(SP, lambda e, i=i, ot=ot: e.dma_start(out=PROJ[i * 128:(i + 1) * 128, :], in_=ot[:]), reads=[("ot", b)], writes=[("PROJ", i)])
    P.barrier()
    P.sb_reset(m0)


def phase_attn(P, C, l, PROJ, MIXT, sink, cst):
    m0 = P.sb_mark()
    kT = P.sb([128, 2, NTOK], BF16, "kT")
    vbf = P.sb([128, NT, 256], BF16, "vbf")
    onesb = P.sb([128, 128], BF16, "onesb")
    P.op(POOL, lambda e: e.memset(onesb[:], 1.0), writes=["onesb"])
    mk = []
    for nm in ("mprev", "mnext"):
        tf = P.sb([128, 512], F32, nm + "f")
        tb = P.sb([128, 512], BF16, nm + "b")
        P.dma(SP, lambda e, tf=tf, nm=nm: e.dma_start(out=tf[:], in_=cst[nm][:, :]), writes=[nm + "f"])
        P.op(DVE, lambda e, tf=tf, tb=tb: e.tensor_copy(out=tb[:], in_=tf[:]), reads=[nm + "f"], writes=[nm + "b"])
        mk.append(tb)
    esb = P.sb([128, 8], F32, "esb")
    P.dma(SP, lambda e: e.dma_start(out=esb[:], in_=sink[l, :].partition_broadcast(128)), writes=["esb"])
    P.op(ACT, lambda e: e.activation(out=esb[:], in_=esb[:], func=AF.Exp), reads=["esb"], writes=["esb"])
    cos = [P.sb([128, 128], F32, "cos") for _ in range(2)]
    sin = [P.sb([128, 128], F32, "sin") for _ in range(2)]

    def load_rope(i, b):
        P.dma(SP, lambda e: e.dma_start(out=cos[b][:], in_=cst["cos"][i - 2, :, :]), writes=[("cos", b)])
        P.dma(SP, lambda e: e.dma_start(out=sin[b][:], in_=cst["sin"][i - 2, :, :]), writes=[("sin", b)])

    def rope(x, xo, t, H, b, kx, ko, kt_):
        xv = x[:, 0:H * 128].rearrange("p (h a c d) -> p h a c d", h=H, a=2, c=2)
        ov = xo[:, 0:H * 128].rearrange("p (h a c d) -> p h a c d", h=H, a=2, c=2)
        tv = t[:, 0:H * 128].rearrange("p (h a c d) -> p h a c d", h=H, a=2, c=2)
        cv = cos[b][:].rearrange("p (a c d) -> p a c d", a=2, c=2)
        sv = sin[b][:].rearrange("p (a c d) -> p a c d", a=2, c=2)
        for h in range(H):
            P.op(POOL, lambda e, h=h: e.tensor_tensor(out=ov[:, h], in0=xv[:, h], in1=cv, op=ALU.mult),
                 reads=[kx, ("cos", b)], writes=[ko + (h,)])
            for c in range(2):
                P.op(DVE, lambda e, h=h, c=c: e.tensor_tensor(out=tv[:, h, :, c, :], in0=xv[:, h, :, 1 - c, :], in1=sv[:, :, c, :], op=ALU.mult),
                     reads=[kx, ("sin", b)], writes=[kt_ + (h, c)])
            P.op(DVE, lambda e, h=h: e.tensor_tensor(out=ov[:, h], in0=ov[:, h], in1=tv[:, h], op=ALU.add),
                 reads=[kt_ + (h,)], writes=[ko + (h,)])

    kin = [P.sb([128, 512], F32, "kin") for _ in range(2)]
    kro = [P.sb([128, 256], F32, "kro") for _ in range(2)]
    ktm = [P.sb([128, 256], F32, "ktm") for _ in range(2)]
    for i in range(NT):
        b = i % 2
        P.dma(SP, lambda e, i=i, b=b: e.dma_start(out=kin[b][:], in_=PROJ[i * 128:(i + 1) * 128, 1024:1536]),
              reads=[("PROJ", i)], writes=[("kin", b)])
        P.op(ACT, lambda e, i=i, b=b: e.copy(out=vbf[:, i, :], in_=kin[b][:, 256:512]), reads=[("kin", b)], writes=[("vbf", i)])
        if i >= 2:
            load_rope(i, b)
            rope(kin[b], kro[b], ktm[b], 2, b, ("kin", b), ("kro", b), ("ktm", b))
            src, skey = kro[b], ("kro", b)
        else:
            src, skey = kin[b], ("kin", b)
        bk = C.bank()
        for h in range(2):
            P.op(PE, lambda e, h=h, bk=bk, src=src: e.transpose(out=C.ps[bk][:, h * 128:(h + 1) * 128], in_=src[:, h * 128:(h + 1) * 128], identity=C.ident[:]),
                 reads=[skey, "ident"], writes=[("ps", bk)])
        P.op(DVE, lambda e, i=i, bk=bk: e.tensor_copy(out=kT[:, :, i * 128:(i + 1) * 128], in_=C.ps[bk][:, 0:256].rearrange("p (h t) -> p h t", h=2)),
             writes=[("ps", bk), ("kT", i)])
    qin = [P.sb([128, 1024], F32, "qin") for _ in range(2)]
    qro = [P.sb([128, 1024], F32, "qro") for _ in range(2)]
    qtm = [P.sb([128, 1024], F32, "qtm") for _ in range(2)]
    qT = [P.sb([128, 8, 128], BF16, "qT") for _ in range(2)]
    pT = [P.sb([128, 512], BF16, "pT") for _ in range(4)]
    den = [P.sb([128, 512], F32, "den") for _ in range(2)]
    oT = [P.sb([128, 4, 128], BF16, "oT") for _ in range(2)]
    npt = 0
    scale = 128.0 ** -0.5
    for iq in range(NT):
        b = iq % 2
        P.dma(SP, lambda e, iq=iq, b=b: e.dma_start(out=qin[b][:], in_=PROJ[iq * 128:(iq + 1) * 128, 0:1024]),
              reads=[("PROJ", iq)], writes=[("qin", b)])
        if iq >= 2:
            load_rope(iq, b)
            rope(qin[b], qro[b], qtm[b], 8, b, ("qin", b), ("qro", b), ("qtm", b))
            src, skey = qro[b], ("qro", b)
        else:
            src, skey = qin[b], ("qin", b)
        for g in range(2):
            bk = C.bank()
            for h in range(4):
                hh = g * 4 + h
                P.op(PE, lambda e, h=h, hh=hh, bk=bk, src=src: e.transpose(out=C.ps[bk][:, h * 128:(h + 1) * 128], in_=src[:, hh * 128:(hh + 1) * 128], identity=C.ident[:]),
                     reads=[skey, "ident"], writes=[("ps", bk)])
            copy_op(P, alt(g), qT[b][:, g * 4:(g + 1) * 4, :], C.ps[bk][:, :].rearrange("p (h t) -> p h t", h=4), reads=[], writes=[("ps", bk), ("qT", b, g)])
        if iq < 2:
            keys = [(0, None), (1, None)]
        else:
            keys = [(0, None), (1, None)]
            if iq - 1 >= 2:
                keys.append((iq - 1, 0))
            keys.append((iq, None))
            if iq + 1 < NT:
                keys.append((iq + 1, 1))
        for kvh in range(2):
            bo = C.bank()
            bd = C.bank()
            for n, (kt_, mi) in enumerate(keys):
                bs = C.bank()
                pb = npt % 4
                npt += 1
                P.op(PE, lambda e, kt_=kt_, bs=bs, kvh=kvh, b=b: e.matmul(C.ps[bs][:, :], lhsT=kT[:, kvh, kt_ * 128:(kt_ + 1) * 128],
                                                                        rhs=qT[b][:, kvh * 4:(kvh + 1) * 4, :].rearrange("p h t -> p (h t)"), start=True, stop=True),
                     reads=[("kT", kt_), ("qT", b, kvh)], writes=[("ps", bs)])
                P.op(ACT, lambda e, bs=bs, pb=pb: e.activation(out=pT[pb][:], in_=C.ps[bs][:, :], func=AF.Exp, scale=scale),
                     writes=[("ps", bs), ("pT", pb)])
                if mi is not None:
                    P.op(POOL, lambda e, pb=pb, mi=mi: e.tensor_tensor(out=pT[pb][:], in0=pT[pb][:], in1=mk[mi][:], op=ALU.mult),
                         reads=["mprevb", "mnextb"], writes=[("pT", pb)])
                st_, sp_ = (n == 0), (n == len(keys) - 1)
                P.op(PE, lambda e, kt_=kt_, bo=bo, kvh=kvh, pb=pb, st_=st_, sp_=sp_: e.matmul(C.ps[bo][:, :], lhsT=vbf[:, kt_, kvh * 128:(kvh + 1) * 128], rhs=pT[pb][:],
                                                                              start=st_, stop=sp_),
                     reads=[("vbf", kt_), ("pT", pb)], writes=[("ps", bo)])
                P.op(PE, lambda e, bd=bd, pb=pb, st_=st_, sp_=sp_: e.matmul(C.ps[bd][:, :], lhsT=onesb[:], rhs=pT[pb][:], start=st_, stop=sp_),
                     reads=["onesb", ("pT", pb)], writes=[("ps", bd)])
            db = kvh
            for h in range(4):
                hh = kvh * 4 + h
                P.op(DVE, lambda e, h=h, hh=hh, bd=bd, db=db: e.tensor_scalar_add(out=den[db][:, h * 128:(h + 1) * 128], in0=C.ps[bd][:, h * 128:(h + 1) * 128], scalar1=esb[:, hh:hh + 1]),
                     reads=["esb"], writes=[("ps", bd), ("den", db, h)])
            P.op(DVE, lambda e, db=db: e.reciprocal(out=den[db][:], in_=den[db][:]), reads=[("den", db)], writes=[("den", db)])
            P.op(DVE, lambda e, db=db, bo=bo: e.tensor_tensor(out=oT[db][:].rearrange("p h t -> p (h t)"), in0=C.ps[bo][:, :], in1=den[db][:], op=ALU.mult),
                 reads=[("den", db)], writes=[("ps", bo), ("oT", db)])
            P.dma(SP, lambda e, db=db, kvh=kvh, iq=iq: e.dma_start(out=MIXT[kvh * 4:(kvh + 1) * 4, :, iq * 128:(iq + 1) * 128].rearrange("c p t -> p c t"), in_=oT[db][:]),
                  reads=[("oT", db)], writes=[("MIXT", "att", iq, kvh)])
    P.barrier()
    P.sb_reset(m0)


import math
NCH = 544
CB = 272
TWO_PI = 2.0 * math.pi


def bc(ap, axis, shape):
    return ap.unsqueeze(axis).to_broadcast(shape)


def phase_s5(P, C, l, PROJ, MIXT, prm, cst):
    m0 = P.sb_mark()
    nsb = [0]

    def T(shape, dt=F32, nm="s5"):
        nsb[0] += 1
        return P.sb(shape, dt, nm), "%s%d" % (nm, nsb[0])

    SH = [128, 2, 16, 48]
    PWR, kPWR = T(SH); PWI, kPWI = T(SH)
    NK = 9
    AR, kAR = T([128, 2, 16, NK]); AI, kAI = T([128, 2, 16, NK]); NAI, kNAI = T([128, 2, 16, NK])
    FR, kFR = T([128, 2, 16]); FI, kFI = T([128, 2, 16])
    SB4 = [128, 2, 16, 16]
    BBR, kBBR = T(SB4); BBI, kBBI = T(SB4)
    CR, kCR = T(SB4); CI, kCI = T(SB4)
    mscr = P.sb_mark()
    LR, kLR = T([128, 2, 16]); LI, kLI = T([128, 2, 16]); DTt, kDT = T([128, 2, 16])
    BR, kBR = T([128, 2, 16, 16]); BI, kBI = T([128, 2, 16, 16])
    CRr, kCRr = T([16, 2, 16, 128]); CIr, kCIr = T([16, 2, 16, 128])
    erow, kerow = T([128, 48])
    P.D(SP, [], [kerow], out=erow[:], in_=cst["erow"][:, :])
    for d in range(2):
        for g2 in range(2):
            ps_ = slice(g2 * 64, (g2 + 1) * 64)
            P.D(SP, [], [(kLR, d, g2)], out=LR[ps_, d, :], in_=prm["lam_re"][l, d].rearrange("(gp g2) p -> g2 p gp", g2=2)[g2], allow_slow_non_contiguous=True)
            P.D(SP, [], [(kLI, d, g2)], out=LI[ps_, d, :], in_=prm["lam_im"][l, d].rearrange("(gp g2) p -> g2 p gp", g2=2)[g2], allow_slow_non_contiguous=True)
            P.D(SP, [], [(kDT, d, g2)], out=DTt[ps_, d, :], in_=prm["log_dt"][l, d].rearrange("(gp g2) -> g2 gp", g2=2)[g2].partition_broadcast(64), allow_slow_non_contiguous=True)
            P.D(SP, [], [(kBR, d, g2)], out=BR[ps_, d, :, :], in_=prm["b_re"][l, d].rearrange("(gp g2) p j -> g2 p gp j", g2=2)[g2])
            P.D(SP, [], [(kBI, d, g2)], out=BI[ps_, d, :, :], in_=prm["b_im"][l, d].rearrange("(gp g2) p j -> g2 p gp j", g2=2)[g2])
        for g2 in range(2):
            P.D(SP, [], [(kCRr, d, g2)], out=CRr[:, d, :, g2 * 64:(g2 + 1) * 64], in_=prm["c_re"][l, d].rearrange("(gp g2) i p -> g2 i gp p", g2=2)[g2])
            P.D(SP, [], [(kCIr, d, g2)], out=CIr[:, d, :, g2 * 64:(g2 + 1) * 64], in_=prm["c_im"][l, d].rearrange("(gp g2) i p -> g2 i gp p", g2=2)[g2])
    P.I(ACT, "activation", [kDT], [kDT], out=DTt[:], in_=DTt[:], func=AF.Exp)
    LRD, kLRD = T([128, 2, 16]); TH, kTH = T([128, 2, 16])
    P.I(DVE, "tensor_tensor", [kLR, kDT], [kLRD], out=LRD[:], in0=LR[:], in1=DTt[:], op=ALU.mult)
    P.I(DVE, "tensor_tensor", [kLI, kDT], [kTH], out=TH[:], in0=LI[:], in1=DTt[:], op=ALU.mult)
    SH = [128, 2, 16, 48]
    ANG, kANG = T(SH); MAG, kMAG = T(SH); TMP, kTMP = T(SH)
    eb_ = erow[:].unsqueeze(1).unsqueeze(1).to_broadcast(SH)
    P.I(DVE, "tensor_tensor", [kTH, kerow], [kANG], out=ANG[:], in0=bc(TH[:], 3, SH), in1=eb_, op=ALU.mult)
    P.I(DVE, "tensor_tensor", [kLRD, kerow], [kMAG], out=MAG[:], in0=bc(LRD[:], 3, SH), in1=eb_, op=ALU.mult)
    P.I(ACT, "activation", [kMAG], [kMAG], out=MAG[:], in_=MAG[:], func=AF.Exp)
    KI, kKI = T(SH, I32); KF, kKF = T(SH); MK, kMK = T(SH)

    def sin_rr(OUT, kOUT, off):
        P.I(DVE, "tensor_scalar_add", [kANG], [kTMP], out=TMP[:], in0=ANG[:], scalar1=off)
        P.I(DVE, "tensor_scalar_mul", [kTMP], [kKF], out=KF[:], in0=TMP[:], scalar1=1.0 / TWO_PI)
        P.I(DVE, "tensor_copy", [kKF], [kKI], out=KI[:], in_=KF[:])
        P.I(DVE, "tensor_copy", [kKI], [kKF], out=KF[:], in_=KI[:])
        P.I(DVE, "scalar_tensor_tensor", [kKF, kTMP], [kTMP], out=TMP[:], in0=KF[:], scalar=-TWO_PI, in1=TMP[:], op0=ALU.mult, op1=ALU.add)
        P.I(DVE, "tensor_single_scalar", [kTMP], [kMK], out=MK[:], in_=TMP[:], scalar=math.pi, op=ALU.is_gt)
        P.I(DVE, "scalar_tensor_tensor", [kMK, kTMP], [kTMP], out=TMP[:], in0=MK[:], scalar=-TWO_PI, in1=TMP[:], op0=ALU.mult, op1=ALU.add)
        P.I(ACT, "activation", [kTMP], [kOUT], out=OUT[:], in_=TMP[:], func=AF.Sin)
    sin_rr(PWI, kPWI, TWO_PI * 32)
    sin_rr(PWR, kPWR, TWO_PI * 32 + math.pi / 2)
    P.I(DVE, "tensor_tensor", [kPWR, kMAG], [kPWR], out=PWR[:], in0=PWR[:], in1=MAG[:], op=ALU.mult)
    P.I(DVE, "tensor_tensor", [kPWI, kMAG], [kPWI], out=PWI[:], in0=PWI[:], in1=MAG[:], op=ALU.mult)
    t1, kt1 = T([128, 2, 16]); t2, kt2 = T([128, 2, 16])
    P.I(DVE, "tensor_copy", [kPWR], [(kAR, 0)], out=AR[:, :, :, 0], in_=PWR[:, :, :, 23])
    P.I(DVE, "tensor_copy", [kPWI], [(kAI, 0)], out=AI[:, :, :, 0], in_=PWI[:, :, :, 23])
    for k in range(NK - 1):
        P.I(DVE, "tensor_tensor", [(kAR, k)], [kt1], out=t1[:], in0=AR[:, :, :, k], in1=AR[:, :, :, k], op=ALU.mult)
        P.I(DVE, "tensor_tensor", [(kAI, k)], [kt2], out=t2[:], in0=AI[:, :, :, k], in1=AI[:, :, :, k], op=ALU.mult)
        P.I(DVE, "tensor_tensor", [kt1, kt2], [(kAR, k + 1)], out=AR[:, :, :, k + 1], in0=t1[:], in1=t2[:], op=ALU.subtract)
        P.I(DVE, "scalar_tensor_tensor", [(kAR, k), (kAI, k)], [(kAI, k + 1)], out=AI[:, :, :, k + 1], in0=AR[:, :, :, k], scalar=2.0, in1=AI[:, :, :, k], op0=ALU.mult, op1=ALU.mult)
    P.I(DVE, "tensor_scalar_mul", [kAI], [kNAI], out=NAI[:], in0=AI[:], scalar1=-1.0)
    NR, kNR = T([128, 2, 16]); DEN, kDEN = T([128, 2, 16])
    P.I(DVE, "tensor_scalar_add", [kPWR], [kNR], out=NR[:], in0=PWR[:, :, :, 16], scalar1=-1.0)
    P.I(DVE, "tensor_tensor", [kLR], [kDEN], out=DEN[:], in0=LR[:], in1=LR[:], op=ALU.mult)
    P.I(DVE, "tensor_tensor", [kLI], [kt1], out=t1[:], in0=LI[:], in1=LI[:], op=ALU.mult)
    P.I(DVE, "tensor_tensor", [kDEN, kt1], [kDEN], out=DEN[:], in0=DEN[:], in1=t1[:], op=ALU.add)
    P.I(DVE, "reciprocal", [kDEN], [kDEN], out=DEN[:], in_=DEN[:])
    P.I(DVE, "tensor_tensor", [kNR, kLR], [kFR], out=FR[:], in0=NR[:], in1=LR[:], op=ALU.mult)
    P.I(DVE, "tensor_tensor", [kPWI, kLI], [kt1], out=t1[:], in0=PWI[:, :, :, 16], in1=LI[:], op=ALU.mult)
    P.I(DVE, "tensor_tensor", [kFR, kt1], [kFR], out=FR[:], in0=FR[:], in1=t1[:], op=ALU.add)
    P.I(DVE, "tensor_tensor", [kFR, kDEN], [kFR], out=FR[:], in0=FR[:], in1=DEN[:], op=ALU.mult)
    P.I(DVE, "tensor_tensor", [kPWI, kLR], [kFI], out=FI[:], in0=PWI[:, :, :, 16], in1=LR[:], op=ALU.mult)
    P.I(DVE, "tensor_tensor", [kNR, kLI], [kt1], out=t1[:], in0=NR[:], in1=LI[:], op=ALU.mult)
    P.I(DVE, "tensor_tensor", [kFI, kt1], [kFI], out=FI[:], in0=FI[:], in1=t1[:], op=ALU.subtract)
    P.I(DVE, "tensor_tensor", [kFI, kDEN], [kFI], out=FI[:], in0=FI[:], in1=DEN[:], op=ALU.mult)
    T4, kT4 = T(SB4)
    P.I(DVE, "tensor_tensor", [kFR, kBR], [kBBR], out=BBR[:], in0=bc(FR[:], 3, SB4), in1=BR[:], op=ALU.mult)
    P.I(DVE, "tensor_tensor", [kFI, kBI], [kT4], out=T4[:], in0=bc(FI[:], 3, SB4), in1=BI[:], op=ALU.mult)
    P.I(DVE, "tensor_tensor", [kBBR, kT4], [kBBR], out=BBR[:], in0=BBR[:], in1=T4[:], op=ALU.subtract)
    P.I(DVE, "tensor_tensor", [kFR, kBI], [kBBI], out=BBI[:], in0=bc(FR[:], 3, SB4), in1=BI[:], op=ALU.mult)
    P.I(DVE, "tensor_tensor", [kFI, kBR], [kT4], out=T4[:], in0=bc(FI[:], 3, SB4), in1=BR[:], op=ALU.mult)
    P.I(DVE, "tensor_tensor", [kBBI, kT4], [kBBI], out=BBI[:], in0=BBI[:], in1=T4[:], op=ALU.add)
    for (src, ksrc, dst, kdst) in ((CRr, kCRr, CR, kCR), (CIr, kCIr, CI, kCI)):
        for d in range(2):
            bk = C.bank()
            for gp in range(16):
                P.I(PE, "transpose", [ksrc, "ident"], [("ps", bk)], out=C.ps[bk][:, gp * 16:(gp + 1) * 16], in_=src[:, d, gp, :], identity=C.ident[0:16, 0:16])
            P.I(DVE, "tensor_copy", [], [("ps", bk), (kdst, d)], out=dst[:, d, :, :], in_=C.ps[bk][:, 0:256].rearrange("p (g i) -> p g i", g=16))
    P.barrier()
    P.sb_reset(mscr)
    E, kE = T([128, 8, 240]); Eb, kEb = T([128, 8, 240], BF16)
    P.D(SP, [], [kE], out=E[:], in_=cst["E"].rearrange("r p c -> p r c"))
    P.I(DVE, "tensor_copy", [kE], [kEb], out=Eb[:], in_=E[:])
    MF, kMF = T([128, 128]); MB, kMB = T([128, 128])
    P.D(SP, [], [kMF], out=MF[:], in_=cst["toemf"][:, :])
    P.D(SP, [], [kMB], out=MB[:], in_=cst["toemb"][:, :])
    Dall, kDall = T([128, 32])
    for t in range(8):
        P.D(SP, [], [(kDall, t)], out=Dall[t * 16:(t + 1) * 16, :], in_=prm["d"][l].rearrange("(g i) -> i g", i=16), allow_slow_non_contiguous=True)
    zT, kzT = T([128, 4, NTOK], BF16)
    zTv = zT[:].rearrange("p a (c s) -> p a c s", s=8)
    SG = [128, 4, 8, 16]
    tabs = {}
    for nm in ("WTR", "WTI", "XR", "XI", "VR", "VI"):
        for d in range(2):
            tabs[(nm, d)] = T(SG)
    TG, kTG = T(SG)
    suin, ksuin = T([128, NT, 128])
    suT, ksuT = T([128, NTOK])
    suTv = suT[:].rearrange("p (c s) -> p c s", s=8)
    U8, kU8 = T([128, 8, NCH])
    Z8, kZ8 = T([128, 8, NCH], BF16)
    Wt, kWt = T([128, 4, 128])
    Toe, kToe = T([128, 2, 128])
    H = {}
    for d in range(2):
        for pp in range(2):
            for c_ in range(2):
                H[(d, pp, c_)] = T([128, NCH])
    xg, kxg = T([128, CB]); ug, kug = T([128, CB]); sg, ksg = T([128, CB])
    for ct in range(4):
        gsl = slice(ct * 4, ct * 4 + 4)
        for d in range(2):
            ea, eb2, ec = (0, 8, 16) if d == 0 else (24, 32, 40)
            def pw(tile_, e0):
                return tile_[:, d, gsl, e0:e0 + 8].unsqueeze(3).to_broadcast(SG)
            def bb(tile_):
                return tile_[:, d, gsl, :].unsqueeze(2).to_broadcast(SG)
            (WTR, kWTR), (WTI, kWTI) = tabs[("WTR", d)], tabs[("WTI", d)]
            (XR, kXR), (XI, kXI) = tabs[("XR", d)], tabs[("XI", d)]
            (VR, kVR), (VI, kVI) = tabs[("VR", d)], tabs[("VI", d)]
            P.I(DVE, "tensor_tensor", [kPWR, kBBR], [kWTR], out=WTR[:], in0=pw(PWR, ea), in1=bb(BBR), op=ALU.mult)
            P.I(DVE, "tensor_tensor", [kPWI, kBBI], [kTG], out=TG[:], in0=pw(PWI, ea), in1=bb(BBI), op=ALU.mult)
            P.I(DVE, "tensor_tensor", [kWTR, kTG], [kWTR], out=WTR[:], in0=WTR[:], in1=TG[:], op=ALU.subtract)
            P.I(DVE, "tensor_tensor", [kPWR, kBBI], [kWTI], out=WTI[:], in0=pw(PWR, ea), in1=bb(BBI), op=ALU.mult)
            P.I(DVE, "tensor_tensor", [kPWI, kBBR], [kTG], out=TG[:], in0=pw(PWI, ea), in1=bb(BBR), op=ALU.mult)
            P.I(DVE, "tensor_tensor", [kWTI, kTG], [kWTI], out=WTI[:], in0=WTI[:], in1=TG[:], op=ALU.add)
            for (RR, kRR, II, kII, e0) in ((XR, kXR, XI, kXI, eb2), (VR, kVR, VI, kVI, ec)):
                P.I(DVE, "tensor_tensor", [kPWR, kCR], [kRR], out=RR[:], in0=pw(PWR, e0), in1=bb(CR), op=ALU.mult)
                P.I(DVE, "tensor_tensor", [kPWI, kCI], [kTG], out=TG[:], in0=pw(PWI, e0), in1=bb(CI), op=ALU.mult)
                P.I(DVE, "tensor_tensor", [kRR, kTG], [kRR], out=RR[:], in0=RR[:], in1=TG[:], op=ALU.subtract)
                P.I(DVE, "tensor_tensor", [kPWI, kCR], [kII], out=II[:], in0=pw(PWI, e0), in1=bb(CR), op=ALU.mult)
                P.I(DVE, "tensor_tensor", [kPWR, kCI], [kTG], out=TG[:], in0=pw(PWR, e0), in1=bb(CI), op=ALU.mult)
                P.I(DVE, "scalar_tensor_tensor", [kII, kTG], [kII], out=II[:], in0=II[:], scalar=-1.0, in1=TG[:], op0=ALU.mult, op1=ALU.subtract)
        P.D(SP, [("PROJ",)], [ksuin], out=suin[:], in_=PROJ[:, 1536 + ct * 128:1536 + (ct + 1) * 128].rearrange("(i p) c -> p i c", p=128))
        for i4 in range(0, NT, 4):
            n = min(4, NT - i4)
            bk = C.bank()
            for j in range(n):
                P.I(PE, "transpose", [ksuin, "ident"], [("ps", bk)], out=C.ps[bk][:, j * 128:(j + 1) * 128], in_=suin[:, i4 + j, :], identity=C.ident[:])
            copy_op(P, alt(i4 // 4), suT[:, i4 * 128:(i4 + n) * 128], C.ps[bk][:, 0:n * 128], [], [("ps", bk), (ksuT, i4)])
        for g8 in range(8):
            for cb in range(2):
                bk = C.bank()
                for s in range(8):
                    P.I(PE, "matmul", [kE, ksuT], [("ps", bk)], C.ps[bk][:, 0:CB], lhsT=E[:, g8, (7 - s) * 16:(7 - s) * 16 + 128],
                        rhs=suTv[:, cb * CB:(cb + 1) * CB, s], start=(s == 0), stop=(s == 7))
                copy_op(P, alt(cb), U8[:, g8, cb * CB:(cb + 1) * CB], C.ps[bk][:, 0:CB], [], [("ps", bk), (kU8, g8, cb)])
        for gpl in range(4):
            gp = ct * 4 + gpl
            bk = C.bank()
            for d in range(2):
                for c_, nm in enumerate(("WTR", "WTI")):
                    tt, ktt = tabs[(nm, d)]
                    j = d * 2 + c_
                    P.I(PE, "transpose", [ktt, "ident"], [("ps", bk)], out=C.ps[bk][:, j * 128:(j + 1) * 128], in_=tt[:, gpl, :, :].rearrange("p s j -> p (s j)"), identity=C.ident[:])
            P.I(DVE, "tensor_copy", [], [("ps", bk), kWt], out=Wt[:], in_=C.ps[bk][:, :].rearrange("p (a b) -> p a b", a=4))
            for g2 in range(2):
                rs_ = slice(g2 * 64, (g2 + 1) * 64)
                bks = []
                for d in range(2):
                    bk = C.bank()
                    bks.append(bk)
                    (WTR, kWTR), (WTI, kWTI) = tabs[("WTR", d)], tabs[("WTI", d)]
                    (XR, kXR), (XI, kXI) = tabs[("XR", d)], tabs[("XI", d)]
                    P.I(PE, "matmul", [kWTR, kXR], [("ps", bk)], C.ps[bk][:, 0:128], lhsT=WTR[rs_, gpl, :, :].rearrange("p s j -> p (s j)"),
                        rhs=XR[rs_, gpl, :, :].rearrange("p s j -> p (s j)"), start=True, stop=False)
                    P.I(PE, "matmul", [kWTI, kXI], [("ps", bk)], C.ps[bk][:, 0:128], lhsT=WTI[rs_, gpl, :, :].rearrange("p s j -> p (s j)"),
                        rhs=XI[rs_, gpl, :, :].rearrange("p s j -> p (s j)"), start=False, stop=True)
                P.I(DVE, "tensor_tensor", [kMF], [("ps", bks[0]), (kToe, g2)], out=Toe[:, g2, :], in0=C.ps[bks[0]][:, 0:128], in1=MF[:], op=ALU.mult)
                P.I(DVE, "tensor_tensor", [kMB], [("ps", bks[1]), kTG], out=TG[:, 0, :, :].rearrange("p s j -> p (s j)"), in0=C.ps[bks[1]][:, 0:128], in1=MB[:], op=ALU.mult)
                P.I(DVE, "tensor_tensor", [kTG], [(kToe, g2)], out=Toe[:, g2, :], in0=Toe[:, g2, :], in1=TG[:, 0, :, :].rearrange("p s j -> p (s j)"), op=ALU.add)
            for d in range(2):
                eng = DVE
                for c_ in range(2):
                    Ht, kHt = H[(d, 0, c_)]
                    for cb in range(2):
                        bk = C.bank()
                        for g2 in range(2):
                            rs_ = slice(g2 * 64, (g2 + 1) * 64)
                            P.I(PE, "matmul", [kWt, kU8], [("ps", bk)], C.ps[bk][rs_, 0:CB], lhsT=Wt[:, d * 2 + c_, rs_], rhs=U8[:, gpl * 2 + g2, cb * CB:(cb + 1) * CB], start=True, stop=True)
                        copy_op(P, ACT, Ht[:, cb * CB:(cb + 1) * CB], C.ps[bk][:, 0:CB], [], [("ps", bk), (kHt, cb)])
                cur = 0
                def arK(k):
                    return AR[:, d, gp, k:k + 1], AI[:, d, gp, k:k + 1], NAI[:, d, gp, k:k + 1]
                def scan(lo, hi, cur):
                    n = hi - lo
                    k = 0
                    sh = 1
                    while sh < n:
                        (A_, kA), (B_, kB) = H[(d, cur, 0)], H[(d, cur, 1)]
                        (An, kAn), (Bn, kBn) = H[(d, 1 - cur, 0)], H[(d, 1 - cur, 1)]
                        ar, ai, nai = arK(k)
                        if d == 0:
                            dst, src, keep = slice(lo + sh, hi), slice(lo, hi - sh), slice(lo, lo + sh)
                        else:
                            dst, src, keep = slice(lo, hi - sh), slice(lo + sh, hi), slice(hi - sh, hi)
                        P.I(eng, "scalar_tensor_tensor", [kA, kAR], [kAn], out=An[:, dst], in0=A_[:, src], scalar=ar, in1=A_[:, dst], op0=ALU.mult, op1=ALU.add)
                        P.I(eng, "scalar_tensor_tensor", [kB, kNAI, kAn], [kAn], out=An[:, dst], in0=B_[:, src], scalar=nai, in1=An[:, dst], op0=ALU.mult, op1=ALU.add)
                        P.I(eng, "scalar_tensor_tensor", [kB, kAR], [kBn], out=Bn[:, dst], in0=B_[:, src], scalar=ar, in1=B_[:, dst], op0=ALU.mult, op1=ALU.add)
                        P.I(eng, "scalar_tensor_tensor", [kA, kAI, kBn], [kBn], out=Bn[:, dst], in0=A_[:, src], scalar=ai, in1=Bn[:, dst], op0=ALU.mult, op1=ALU.add)
                        P.I(eng, "tensor_copy", [kA], [kAn], out=An[:, keep], in_=A_[:, keep])
                        P.I(eng, "tensor_copy", [kB], [kBn], out=Bn[:, keep], in_=B_[:, keep])
                        cur = 1 - cur
                        sh *= 2
                        k += 1
                    return cur
                cur = scan(0, 32, 0)
                (A_, kA), (B_, kB) = H[(d, cur, 0)], H[(d, cur, 1)]
                if cur != 0:
                    (A0, kA0), (B0, kB0) = H[(d, 0, 0)], H[(d, 0, 1)]
                    P.I(eng, "tensor_copy", [kA0], [kA], out=A_[:, 32:NCH], in_=A0[:, 32:NCH])
                    P.I(eng, "tensor_copy", [kB0], [kB], out=B_[:, 32:NCH], in_=B0[:, 32:NCH])
                ar, ai, nai = arK(0)
                if d == 0:
                    inj, frm = slice(32, 33), slice(31, 32)
                else:
                    inj, frm = slice(NCH - 1, NCH), slice(0, 1)
                P.I(eng, "scalar_tensor_tensor", [kA, kAR], [kA], out=A_[:, inj], in0=A_[:, frm], scalar=ar, in1=A_[:, inj], op0=ALU.mult, op1=ALU.add)
                P.I(eng, "scalar_tensor_tensor", [kB, kNAI, kA], [kA], out=A_[:, inj], in0=B_[:, frm], scalar=nai, in1=A_[:, inj], op0=ALU.mult, op1=ALU.add)
                P.I(eng, "scalar_tensor_tensor", [kB, kAR], [kB], out=B_[:, inj], in0=B_[:, frm], scalar=ar, in1=B_[:, inj], op0=ALU.mult, op1=ALU.add)
                P.I(eng, "scalar_tensor_tensor", [kA, kAI, kB], [kB], out=B_[:, inj], in0=A_[:, frm], scalar=ai, in1=B_[:, inj], op0=ALU.mult, op1=ALU.add)
                cur0 = cur
                cur = scan(32, NCH, cur)
                if cur != cur0:
                    (An, kAn), (Bn, kBn) = H[(d, cur, 0)], H[(d, cur, 1)]
                    P.I(eng, "tensor_copy", [kA], [kAn], out=An[:, 0:32], in_=A_[:, 0:32])
                    P.I(eng, "tensor_copy", [kB], [kBn], out=Bn[:, 0:32], in_=B_[:, 0:32])
                H[("fin", d)] = cur
            for g2 in range(2):
                g8 = gpl * 2 + g2
                g = gp * 2 + g2
                rs_ = slice(g2 * 64, (g2 + 1) * 64)
                for cb in range(2):
                    c0 = cb * CB
                    bk = C.bank()
                    mms = [(Toe[:, g2, :], U8[:, g8, c0:c0 + CB], 0, CB, [kToe, kU8])]
                    cf = H[("fin", 0)]
                    lo = max(c0, 1)
                    for c_, nm in enumerate(("VR", "VI")):
                        tt, ktt = tabs[(nm, 0)]
                        Hh, kHh = H[(0, cf, c_)]
                        mms.append((tt[rs_, gpl, :, :].rearrange("p s j -> p (s j)"), Hh[rs_, lo - 1:c0 + CB - 1], lo - c0, CB, [ktt, kHh]))
                    cbk = H[("fin", 1)]
                    if cb == 0:
                        segs = [(0, 31, 1), (32, CB, 33)]
                    else:
                        segs = [(CB, NCH - 1, CB + 1), (NCH - 1, NCH, 0)]
                    for c_, nm in enumerate(("VR", "VI")):
                        tt, ktt = tabs[(nm, 1)]
                        Hh, kHh = H[(1, cbk, c_)]
                        for (a0, a1, s0) in segs:
                            mms.append((tt[rs_, gpl, :, :].rearrange("p s j -> p (s j)"), Hh[rs_, s0:s0 + (a1 - a0)], a0 - c0, a1 - c0, [ktt, kHh]))
                    for n, (lt, rh, o0, o1, rd) in enumerate(mms):
                        P.I(PE, "matmul", rd, [("ps", bk)], C.ps[bk][:, o0:o1], lhsT=lt, rhs=rh, start=(n == 0), stop=(n == len(mms) - 1))
                    P.I(DVE, "scalar_tensor_tensor", [kU8, kDall], [("ps", bk), kxg], out=xg[:], in0=U8[:, g8, c0:c0 + CB], scalar=Dall[:, g:g + 1], in1=C.ps[bk][:, 0:CB], op0=ALU.mult, op1=ALU.add)
                    P.I(POOL, "tensor_tensor", [kxg], [kug], out=ug[:], in0=xg[:], in1=xg[:], op=ALU.mult)
                    P.I(POOL, "tensor_scalar", [kug], [kug], out=ug[:], in0=ug[:], scalar1=0.044715, scalar2=1.0, op0=ALU.mult, op1=ALU.add)
                    P.I(POOL, "tensor_tensor", [kug, kxg], [kug], out=ug[:], in0=ug[:], in1=xg[:], op=ALU.mult)
                    P.I(ACT, "activation", [kug], [ksg], out=sg[:], in_=ug[:], func=AF.Sigmoid, scale=2.0 * math.sqrt(2.0 / math.pi))
                    P.I(DVE, "tensor_tensor", [kxg, ksg], [(kZ8, g8, cb)], out=Z8[:, g8, c0:c0 + CB], in0=xg[:], in1=sg[:], op=ALU.mult)
        for s in range(8):
            for cb in range(2):
                bk = C.bank()
                for g8 in range(8):
                    P.I(PE, "matmul", [kEb, kZ8], [("ps", bk)], C.ps[bk][:, 0:CB], lhsT=Eb[:, s, (7 - g8) * 16:(7 - g8) * 16 + 128], rhs=Z8[:, g8, cb * CB:(cb + 1) * CB], start=(g8 == 0), stop=(g8 == 7))
                copy_op(P, alt(s), zTv[:, ct, cb * CB:(cb + 1) * CB, s], C.ps[bk][:, 0:CB], [], [("ps", bk), (kzT, ct, s, cb)])
    wg, kwg = T([128, 4, 512], BF16)
    for kt in range(4):
        P.D(POOL, [], [(kwg, kt)], out=wg[:, kt, :], in_=prm["w_glu"][l, kt * 128:(kt + 1) * 128, :])
    bg, kbg = T([128, 4])
    P.D(SP, [], [kbg], out=bg[:], in_=prm["b_glu"][l].rearrange("(m p) -> p m", p=128), allow_slow_non_contiguous=True)
    gts = [T([128, 512]) for _ in range(2)]
    ots = [T([128, 512], BF16) for _ in range(2)]
    it = 0
    for mt in range(4):
        for t0 in range(0, NTOK, 512):
            w = min(512, NTOK - t0)
            b = it % 2
            it += 1
            (gt_, kgt), (ot_, kot) = gts[b], ots[b]
            bk = C.bank()
            for kt in range(4):
                P.I(PE, "matmul", [kwg, kzT], [("ps", bk)], C.ps[bk][:, 0:w], lhsT=wg[:, kt, mt * 128:(mt + 1) * 128], rhs=zT[:, kt, t0:t0 + w], start=(kt == 0), stop=(kt == 3))
            P.I(ACT, "activation", [kbg], [("ps", bk), kgt], out=gt_[:, 0:w], in_=C.ps[bk][:, 0:w], func=AF.Sigmoid, bias=bg[:, mt:mt + 1], scale=1.0)
            P.I(DVE, "tensor_tensor", [kgt, kzT], [kot], out=ot_[:, 0:w], in0=gt_[:, 0:w], in1=zT[:, mt, t0:t0 + w], op=ALU.mult)
            P.D(SP, [kot], [("MIXT", "ssm", mt, t0)], out=MIXT[8 + mt, :, t0:t0 + w], in_=ot_[:, 0:w])
    P.barrier()
    P.sb_reset(m0)


def phase_gla(P, C, l, PROJ, MIXT, OF, prm, cst):
    m0 = P.sb_mark()
    n_ = [0]

    def T(shape, dt=F32, nm="gl"):
        n_[0] += 1
        return P.sb(shape, dt, nm), "%s%d" % (nm, n_[0])

    def ld(name, shape):
        t, k = T(shape)
        P.D(SP, [], [k], out=t[:], in_=cst[name][:, :])
        return t, k
    TRI = [ld("trif", [128, 128]), ld("trib", [128, 128])]
    BLK, kBLK = ld("blk", [128, 128])
    CIND, kCIND = ld("cind", [128, 2])
    MSK = [ld("gmaskf", [128, 512]), ld("gmaskb", [128, 512])]
    WG = []
    for d in range(2):
        t, k = T([17, 256])
        P.D(SP, [], [(k, 0)], out=t[0:16, :], in_=prm["w_gate"][l, d, :, :])
        P.D(SP, [], [(k, 1)], out=t[16:17, :], in_=prm["b_gate"][l, d:d + 1, :])
        WG.append((t, k))
    NG, kNG = T([128, 128])
    P.D(SP, [], [kNG], out=NG[:], in_=prm["norm_g"][l, :].partition_broadcast(128))
    S, kS = T([64, 4, 128])
    zaug = [T([17, 128]) for _ in range(2)]
    for (t, k) in zaug:
        P.I(POOL, "memset", [], [k], t[:], 1.0)
    NB = 2
    qk = [T([128, 512]) for _ in range(NB)]
    vv = [T([128, 512]) for _ in range(NB)]
    zz = [T([128, 16]) for _ in range(NB)]
    gp_ = [T([128, 256]) for _ in range(NB)]
    bS = [T([128, 256]) for _ in range(NB)]
    eb = [T([128, 256]) for _ in range(NB)]
    enb = [T([128, 256]) for _ in range(NB)]
    ebl = [T([128, 256]) for _ in range(NB)]
    qd = [T([128, 256]) for _ in range(NB)]
    kd = [T([128, 256]) for _ in range(NB)]
    kl = [T([128, 256]) for _ in range(NB)]
    qdT = [T([64, 4, 128]) for _ in range(NB)]
    kdT = [T([64, 4, 128]) for _ in range(NB)]
    ATm = [T([128, 4, 128]) for _ in range(NB)]
    edec = [T([64, 4, 2]) for _ in range(NB)]
    ot = [T([128, 512]) for _ in range(NB)]
    of_ = [T([128, 512]) for _ in range(NB)]
    rr = [T([128, 512]) for _ in range(NB)]
    sq = [T([128, 512]) for _ in range(NB)]
    ssq = [T([128, 4]) for _ in range(NB)]
    oTb = [T([128, 4, 128], BF16) for _ in range(NB)]
    it = 0
    for d in range(2):
        P.I(DVE, "memset", [], [kS], S[:], 0.0)
        order = list(range(NT)) if d == 0 else [1, 0] + list(range(NT - 1, 1, -1))
        corder = (0, 1) if d == 0 else (1, 0)
        (TRId, kTRI), (MK, kMK), (WGd, kWG) = TRI[d], MSK[d], WG[d]
        for i in order:
            b = it % NB
            it += 1
            r0 = i * 128
            (QK, kQK), (V, kV), (Z, kZ), (GP, kGP) = qk[b], vv[b], zz[b], gp_[b]
            (ZA, kZA) = zaug[b]
            P.D(SP, [("PROJ", i)], [kQK], out=QK[:], in_=PROJ[r0:r0 + 128, 2048:2560])
            P.D(SP, [("PROJ", i)], [kV], out=V[:], in_=PROJ[r0:r0 + 128, 2560:3072])
            P.D(SP, [("PROJ", i)], [kZ], out=Z[:], in_=PROJ[r0:r0 + 128, 3584 + 16 * d:3600 + 16 * d])
            bk = C.bank()
            P.I(PE, "transpose", [kZ, "ident"], [("ps", bk)], out=C.ps[bk][0:16, 0:128], in_=Z[:], identity=C.ident[:])
            P.I(ACT, "copy", [], [("ps", bk), kZA], out=ZA[0:16, :], in_=C.ps[bk][0:16, 0:128])
            bk = C.bank()
            P.I(PE, "matmul", [kZA, kWG], [("ps", bk)], C.ps[bk][:, 0:256], lhsT=ZA[:], rhs=WGd[:], start=True, stop=True)
            P.I(ACT, "activation", [], [("ps", bk), kGP], out=GP[:], in_=C.ps[bk][:, 0:256], func=AF.Exp, scale=-1.0)
            P.I(ACT, "activation", [kGP], [kGP], out=GP[:], in_=GP[:], func=AF.Ln, bias=1.0, scale=1.0)
            bkb = C.bank()
            P.I(PE, "matmul", [kTRI, kGP], [("ps", bkb)], C.ps[bkb][:, 0:256], lhsT=TRId[:], rhs=GP[:], start=True, stop=True)
            P.I(PE, "matmul", [kBLK, kGP], [("ps", bkb)], C.ps[bkb][:, 256:512], lhsT=BLK[:], rhs=GP[:], start=True, stop=True)
            (BS, kBS), (EB, kEB), (ENB, kENB), (EBL, kEBL) = bS[b], eb[b], enb[b], ebl[b]
            P.I(ACT, "copy", [], [("ps", bkb), kBS], out=BS[:], in_=C.ps[bkb][:, 0:256])
            P.I(DVE, "tensor_tensor", [kBS], [("ps", bkb), kEBL], out=EBL[:], in0=C.ps[bkb][:, 256:512], in1=BS[:], op=ALU.subtract)
            P.I(ACT, "activation", [kBS], [kEB], out=EB[:], in_=BS[:], func=AF.Exp)
            P.I(ACT, "activation", [kBS], [kENB], out=ENB[:], in_=BS[:], func=AF.Exp, scale=-1.0)
            P.I(ACT, "activation", [kEBL], [kEBL], out=EBL[:], in_=EBL[:], func=AF.Exp)
            (ED, kED) = edec[b]
            bk = C.bank()
            for h in range(4):
                P.I(PE, "matmul", [kGP, kCIND], [("ps", bk)], C.ps[bk][0:64, h * 2:h * 2 + 2], lhsT=GP[:, h * 64:(h + 1) * 64], rhs=CIND[:], start=True, stop=True)
            P.I(ACT, "activation", [], [("ps", bk), kED], out=ED[:].rearrange("p h c -> p (h c)"), in_=C.ps[bk][0:64, 0:8], func=AF.Exp)
            (QD, kQD), (KD, kKD), (KL, kKL) = qd[b], kd[b], kl[b]
            P.I(DVE, "scalar_tensor_tensor", [kQK, kEB], [kQD], out=QD[:], in0=QK[:, 0:256], scalar=0.125, in1=EB[:], op0=ALU.mult, op1=ALU.mult)
            P.I(POOL, "tensor_tensor", [kQK, kENB], [kKD], out=KD[:], in0=QK[:, 256:512], in1=ENB[:], op=ALU.mult)
            P.I(POOL, "tensor_tensor", [kQK, kEBL], [kKL], out=KL[:], in0=QK[:, 256:512], in1=EBL[:], op=ALU.mult)
            (QT, kQT), (KT, kKT) = qdT[b], kdT[b]
            for (src, ksrc, dst, kdst, eng) in ((QD, kQD, QT, kQT, ACT), (KD, kKD, KT, kKT, DVE)):
                bk = C.bank()
                for h in range(4):
                    P.I(PE, "transpose", [ksrc, "ident"], [("ps", bk)], out=C.ps[bk][0:64, h * 128:(h + 1) * 128], in_=src[:, h * 64:(h + 1) * 64], identity=C.ident[:])
                copy_op(P, eng, dst[:].rearrange("p h t -> p (h t)"), C.ps[bk][0:64, :], [], [("ps", bk), kdst])
            (AT, kAT) = ATm[b]
            bk = C.bank()
            for h in range(4):
                P.I(PE, "matmul", [kKT, kQT], [("ps", bk)], C.ps[bk][:, h * 128:(h + 1) * 128], lhsT=KT[:, h, :], rhs=QT[:, h, :], start=True, stop=True)
            P.I(DVE, "tensor_tensor", [kMK], [("ps", bk), kAT], out=AT[:].rearrange("p h t -> p (h t)"), in0=C.ps[bk][:, :], in1=MK[:], op=ALU.mult)
            bo = C.bank()
            for h in range(4):
                P.I(PE, "matmul", [kAT, kV], [("ps", bo)], C.ps[bo][:, h * 128:(h + 1) * 128], lhsT=AT[:, h, :], rhs=V[:, h * 128:(h + 1) * 128], start=(h == 0), stop=False)
            for ci, c in enumerate(corder):
                cs = slice(c * 64, (c + 1) * 64)
                for h in range(4):
                    P.I(PE, "matmul", [kQT, (kS, h)], [("ps", bo)], C.ps[bo][cs, h * 128:(h + 1) * 128], lhsT=QT[:, h, cs], rhs=S[:, h, :], start=False, stop=(ci == 1 and h == 3))
                bu = C.bank()
                for h in range(4):
                    P.I(PE, "matmul", [kKL, kV], [("ps", bu)], C.ps[bu][0:64, h * 128:(h + 1) * 128], lhsT=KL[cs, h * 64:(h + 1) * 64], rhs=V[cs, h * 128:(h + 1) * 128], start=True, stop=True)
                for h in range(4):
                    P.I(DVE, "scalar_tensor_tensor", [kED], [("ps", bu), (kS, h)], out=S[:, h, :], in0=S[:, h, :], scalar=ED[:, h, c:c + 1], in1=C.ps[bu][0:64, h * 128:(h + 1) * 128], op0=ALU.mult, op1=ALU.add)
            (OT, kOT) = ot[b]
            if d == 0:
                P.I(ACT, "copy", [], [("ps", bo), kOT], out=OT[:], in_=C.ps[bo][:, :])
                P.D(SP, [kOT], [("OF", i)], out=OF[r0:r0 + 128, :], in_=OT[:])
                continue
            (OFt, kOFt), (RR, kRR), (SQ, kSQ), (SS, kSS), (OB, kOB) = of_[b], rr[b], sq[b], ssq[b], oTb[b]
            P.D(SP, [("OF", i)], [kOFt], out=OFt[:], in_=OF[r0:r0 + 128, :])
            P.D(SP, [("PROJ", i)], [kRR], out=RR[:], in_=PROJ[r0:r0 + 128, 3072:3584])
            P.I(DVE, "tensor_tensor", [kOFt], [("ps", bo), kOT], out=OT[:], in0=C.ps[bo][:, :], in1=OFt[:], op=ALU.add)
            P.I(POOL, "tensor_tensor", [kOT], [kSQ], out=SQ[:], in0=OT[:], in1=OT[:], op=ALU.mult)
            P.I(DVE, "tensor_reduce", [kSQ], [kSS], out=SS[:], in_=SQ[:].rearrange("p (h v) -> p h v", h=4), axis=AX.X, op=ALU.add)
            P.I(DVE, "tensor_scalar", [kSS], [kSS], out=SS[:], in0=SS[:], scalar1=1.0 / 128.0, scalar2=1e-6, op0=ALU.mult, op1=ALU.add)
            P.I(ACT, "activation", [kSS], [kSS], out=SS[:], in_=SS[:], func=AF.Ln)
            P.I(ACT, "activation", [kSS], [kSS], out=SS[:], in_=SS[:], func=AF.Exp, scale=-0.5)
            P.I(ACT, "activation", [kRR], [kRR], out=RR[:], in_=RR[:], func=AF.Silu)
            o3 = OT[:].rearrange("p (h v) -> p h v", h=4)
            P.I(DVE, "tensor_tensor", [kOT, kSS], [kOT], out=o3, in0=o3, in1=SS[:].unsqueeze(2).to_broadcast([128, 4, 128]), op=ALU.mult)
            P.I(POOL, "tensor_tensor", [kOT, kNG], [kOT], out=o3, in0=o3, in1=NG[:].unsqueeze(1).to_broadcast([128, 4, 128]), op=ALU.mult)
            P.I(DVE, "tensor_tensor", [kOT, kRR], [kOT], out=OT[:], in0=OT[:], in1=RR[:], op=ALU.mult)
            bk = C.bank()
            for h in range(4):
                P.I(PE, "transpose", [kOT, "ident"], [("ps", bk)], out=C.ps[bk][:, h * 128:(h + 1) * 128], in_=OT[:, h * 128:(h + 1) * 128], identity=C.ident[:])
            P.I(ACT, "copy", [], [("ps", bk), kOB], out=OB[:].rearrange("p h t -> p (h t)"), in_=C.ps[bk][:, :])
            P.D(SP, [kOB], [("MIXT", "gla", i)], out=MIXT[12:16, :, r0:r0 + 128].rearrange("c p t -> p c t"), in_=OB[:])
    P.barrier()
    P.sb_reset(m0)


ALPHA_ = (2 * 2) ** 0.25
NSLOT = 544
ROWW = 2080


def bcast_load(P, q, dst, key, src_row):
    P.D(q, [], [key], out=dst[:], in_=src_row.partition_broadcast(128))


def ln_apply(P, C, src, ksrc, dst, kdst, st, mv, rs, tag, gmul, kg, badd, kb, eng2=POOL):
    ln_stats(P, C, src, ksrc, st, mv, rs, tag)
    P.I(ACT, "activation", [ksrc, tag + "rs"], [kdst], out=dst[:], in_=src[:], func=AF.Identity, bias=rs[:, 1:2], scale=rs[:, 0:1])
    P.I(DVE, "tensor_tensor", [kdst, kg], [kdst], out=dst[:], in0=dst[:], in1=gmul, op=ALU.mult)
    P.I(eng2, "tensor_tensor", [kdst, kb], [kdst], out=dst[:], in0=dst[:], in1=badd, op=ALU.add)


def phase_wout(P, C, l, X, MIXT, X1, H2R, AFF, w_out, MODV, ln_g, ln_b, router, cst=None):
    m0 = P.sb_mark()
    wo = P.sb([128, 16, D], BF16, "wo")
    for kt in range(16):
        P.D(POOL, [], [("wo", kt)], out=wo[:, kt, :], in_=w_out[l, kt * 128:(kt + 1) * 128, :])
    names = {}
    for nm, src in (("gt1", lambda r: MODV[l, r, 2 * D:3 * D]), ("sc2", lambda r: MODV[l, r, 4 * D:5 * D]), ("sh2", lambda r: MODV[l, r, 3 * D:4 * D])):
        for r in range(2):
            t = P.sb([128, D], F32, nm)
            bcast_load(P, SP, t, (nm, r), src(r))
            names[(nm, r)] = t
    for r in range(2):
        P.I(POOL, "tensor_scalar_add", [("sc2", r)], [("sc2", r)], out=names[("sc2", r)][:], in0=names[("sc2", r)][:], scalar1=1.0)
    g1 = P.sb([128, D], F32, "g1"); b1 = P.sb([128, D], F32, "b1")
    bcast_load(P, SP, g1, "g1", ln_g[l, :]); bcast_load(P, SP, b1, "b1", ln_b[l, :])
    rt = P.sb([128, 16, 16], F32, "rt")
    P.D(SP, [], ["rt"], out=rt[:], in_=router[l].rearrange("(kt p) e -> p kt e", p=128))
    mx = [P.sb([128, 16, 128], BF16, "mx") for _ in range(2)]
    xs = [P.sb([128, D], F32, "xs") for _ in range(2)]
    x1s = [P.sb([128, D], F32, "x1s") for _ in range(2)]
    rows = [P.sb([128, ROWW], F32, "row") for _ in range(2)]
    h2T = P.sb([128, 16, 128], F32, "h2T")
    st = P.sb([128, 4, 6], F32, "st"); mv = P.sb([128, 2], F32, "mv"); rs = P.sb([128, 2], F32, "rs")
    st2 = P.sb([128, 4, 6], F32, "st2"); mv2 = P.sb([128, 2], F32, "mv2"); rs2 = P.sb([128, 2], F32, "rs2")
    lmx = P.sb([128, 1], F32, "lmx"); lsum = P.sb([128, 1], F32, "lsum")
    for (t, k) in ((rows[0], ("row", 0)), (rows[1], ("row", 1))):
        P.I(POOL, "memset", [], [k], t[:, 2064:ROWW], 0.0)
    TOK = P.sb([128, NT], F32, "TOK")
    if cst is not None and "tokid" in cst:
        P.D(SP, [], ["TOK"], out=TOK[:], in_=cst["tokid"][:, :])
    else:
        P.I(POOL, "memset", [], ["TOK"], TOK[:], 0.0)
    for i in range(NT):
        b = i % 2
        r = 1 if i < 2 else 0
        r0 = i * 128
        MX, XS, X1S, ROW = mx[b], xs[b], x1s[b], rows[b]
        kMX, kXS, kX1, kROW = ("mx", b), ("xs", b), ("x1s", b), ("row", b)
        P.D(SP, [("MIXT",)], [kMX], out=MX[:], in_=MIXT[:, :, r0:r0 + 128].rearrange("c p t -> p c t"))
        P.D(SP, [("X", i)], [kXS], out=XS[:], in_=X[r0:r0 + 128, :])
        for nb in range(4):
            bk = C.bank()
            for kt in range(16):
                P.I(PE, "matmul", [kMX, ("wo", kt)], [("ps", bk)], C.ps[bk][:, :], lhsT=MX[:, kt, :], rhs=wo[:, kt, nb * 512:(nb + 1) * 512], start=(kt == 0), stop=(kt == 15))
            sl = slice(nb * 512, (nb + 1) * 512)
            P.I(DVE, "tensor_tensor", [("gt1", r)], [("ps", bk), kX1 + (nb,)], out=X1S[:, sl], in0=C.ps[bk][:, :], in1=names[("gt1", r)][:, sl], op=ALU.mult)
        P.I(DVE, "scalar_tensor_tensor", [kXS, kX1], [kX1], out=X1S[:], in0=XS[:], scalar=ALPHA_, in1=X1S[:], op0=ALU.mult, op1=ALU.add)
        ln_apply(P, C, X1S, kX1, X1S, kX1, st, mv, rs, "w1", g1[:], "g1", b1[:], "b1")
        P.D(SP, [kX1], [("X1", i)], out=X1[r0:r0 + 128, :], in_=X1S[:])
        ln_stats(P, C, X1S, kX1, st2, mv2, rs2, "w2")
        P.I(ACT, "activation", [kX1, "w2rs"], [kROW + (0,)], out=ROW[:, 0:D], in_=X1S[:], func=AF.Identity, bias=rs2[:, 1:2], scale=rs2[:, 0:1])
        P.I(DVE, "tensor_tensor", [kROW + (0,), ("sc2", r)], [kROW + (0,)], out=ROW[:, 0:D], in0=ROW[:, 0:D], in1=names[("sc2", r)][:], op=ALU.mult)
        P.I(POOL, "tensor_tensor", [kROW + (0,), ("sh2", r)], [kROW + (0,)], out=ROW[:, 0:D], in0=ROW[:, 0:D], in1=names[("sh2", r)][:], op=ALU.add)
        for kg in range(4):
            bk = C.bank()
            for j in range(4):
                kt = kg * 4 + j
                P.I(PE, "transpose", [kROW + (0,), "ident"], [("ps", bk)], out=C.ps[bk][:, j * 128:(j + 1) * 128], in_=ROW[:, kt * 128:(kt + 1) * 128], identity=C.ident[:])
            copy_op(P, alt(kg), h2T[:, kg * 4:(kg + 1) * 4, :].rearrange("p a t -> p (a t)"), C.ps[bk][:, :], [], [("ps", bk), ("h2T", kg)])
        bk = C.bank()
        for kt in range(16):
            P.I(PE, "matmul", [("h2T", kt // 4), "rt"], [("ps", bk)], C.ps[bk][:, 0:16], lhsT=h2T[:, kt, :], rhs=rt[:, kt, :], start=(kt == 0), stop=(kt == 15))
        P.I(DVE, "tensor_reduce", [], [("ps", bk), "lmx"], out=lmx[:], in_=C.ps[bk][:, 0:16], axis=AX.X, op=ALU.max)
        P.I(DVE, "tensor_scalar_mul", ["lmx"], ["lmx"], out=lmx[:], in0=lmx[:], scalar1=-1.0)
        P.I(DVE, "memset", [], ["lsum"], lsum[:], 0.0)
        P.I(ACT, "activation", ["lmx"], [("ps", bk), kROW + (1,), "lsum"], out=ROW[:, D:D + 16], in_=C.ps[bk][:, 0:16], func=AF.Exp, bias=lmx[:, 0:1], scale=1.0, accum_out=lsum[:])
        P.I(DVE, "reciprocal", ["lsum"], ["lsum"], out=lsum[:], in_=lsum[:])
        P.I(DVE, "tensor_scalar_mul", [kROW + (1,), "lsum"], [kROW + (1,)], out=ROW[:, D:D + 16], in0=ROW[:, D:D + 16], scalar1=lsum[:, 0:1])
        P.I(POOL, "tensor_copy", [kROW + (1,)], [("AFF", i)], out=AFF[:, i, :], in_=ROW[:, D:D + 16])
        P.I(POOL, "tensor_copy", ["TOK"], [kROW + (2,)], out=ROW[:, 2064:2065], in_=TOK[:, i:i + 1])
        P.D(SP, [kROW], [("H2R", i)], out=H2R[r0:r0 + 128, :], in_=ROW[:])
    P.barrier()
    P.sb_reset(m0)


def phase_route(P, C, AFF, IDXS, IDXC_dram, cst, niter=30):
    m0 = P.sb_mark()
    n_ = [0]

    def T(shape, dt=F32, nm="rt"):
        n_[0] += 1
        return P.sb(shape, dt, nm), "%s%d" % (nm, n_[0])
    ones, kones = T([128, 128]); strict, kstrict = T([128, 128]); TGT, kTGT = T([128, 2, 16])
    P.D(SP, [], [kones], out=ones[:], in_=cst["ones"][:, :])
    P.D(SP, [], [kstrict], out=strict[:], in_=cst["strict"][:, :])
    P.D(SP, [], [kTGT], out=TGT[:], in_=cst["tgt"].rearrange("p (s e) -> p s e", s=2))
    LO, kLO = T([128, 2, 16]); HI, kHI = T([128, 2, 16]); MID, kMID = T([128, 2, 16])
    CNT, kCNT = T([128, 2, 16]); GE, kGE = T([128, 2, 16]); D1, kD1 = T([128, 2, 16])
    CMP, kCMP = T([128, NT, 16])
    P.I(DVE, "memset", [], [kLO], LO[:], 0.0)
    P.I(DVE, "memset", [], [kHI], HI[:], 1.0001)
    segs = ((0, 0, 2), (1, 2, NT))

    def compare(TH, kTH):
        for (s, a, b_) in segs:
            P.I(DVE, "tensor_tensor", [("AFF",), kTH], [(kCMP, s)], out=CMP[:, a:b_, :], in0=AFF[:, a:b_, :],
                in1=TH[:, s, :].unsqueeze(1).to_broadcast([128, b_ - a, 16]), op=ALU.is_ge)
    for it in range(niter):
        P.I(DVE, "tensor_tensor", [kLO, kHI], [kMID], out=MID[:], in0=LO[:], in1=HI[:], op=ALU.add)
        P.I(DVE, "tensor_scalar_mul", [kMID], [kMID], out=MID[:], in0=MID[:], scalar1=0.5)
        compare(MID, kMID)
        for (s, a, b_) in segs:
            P.I(DVE, "tensor_reduce", [(kCMP, s)], [(kCNT, s)], out=CNT[:, s, :], in_=CMP[:, a:b_, :].rearrange("p t e -> p e t"), axis=AX.X, op=ALU.add)
        bk = C.bank()
        P.I(PE, "matmul", [kones, kCNT], [("ps", bk)], C.ps[bk][:, 0:32], lhsT=ones[:], rhs=CNT[:].rearrange("p s e -> p (s e)"), start=True, stop=True)
        P.I(DVE, "tensor_tensor", [kTGT], [("ps", bk), kGE], out=GE[:].rearrange("p s e -> p (s e)"), in0=C.ps[bk][:, 0:32], in1=TGT[:].rearrange("p s e -> p (s e)"), op=ALU.is_ge)
        P.I(DVE, "tensor_tensor", [kMID, kLO], [kD1], out=D1[:], in0=MID[:], in1=LO[:], op=ALU.subtract)
        P.I(DVE, "tensor_tensor", [kD1, kGE], [kD1], out=D1[:], in0=D1[:], in1=GE[:], op=ALU.mult)
        P.I(DVE, "tensor_tensor", [kLO, kD1], [kLO], out=LO[:], in0=LO[:], in1=D1[:], op=ALU.add)
        P.I(DVE, "tensor_tensor", [kHI, kMID], [kD1], out=D1[:], in0=HI[:], in1=MID[:], op=ALU.subtract)
        P.I(DVE, "tensor_tensor", [kD1, kGE], [kD1], out=D1[:], in0=D1[:], in1=GE[:], op=ALU.mult)
        P.I(DVE, "tensor_tensor", [kMID, kD1], [kHI], out=HI[:], in0=MID[:], in1=D1[:], op=ALU.add)
    compare(LO, kLO)
    PRE, kPRE = T([128, NT, 16]); TOT, kTOT = T([128, NT, 16]); OFFS, kOFFS = T([128, NT, 16])
    cm = CMP[:].rearrange("p t e -> p (t e)")
    for (dst, kdst, lt, klt) in ((PRE, kPRE, strict, kstrict), (TOT, kTOT, ones, kones)):
        for (c0, c1) in ((0, 512), (512, 544)):
            bk = C.bank()
            P.I(PE, "matmul", [klt, kCMP], [("ps", bk)], C.ps[bk][:, 0:c1 - c0], lhsT=lt[:], rhs=cm[:, c0:c1], start=True, stop=True)
            P.I(ACT, "copy", [], [("ps", bk), (kdst, c0)], out=dst[:].rearrange("p t e -> p (t e)")[:, c0:c1], in_=C.ps[bk][:, 0:c1 - c0])
    for (s, a, b_) in segs:
        P.I(DVE, "memset", [], [(kOFFS, a)], OFFS[:, a, :], 0.0)
        for i in range(a, b_ - 1):
            P.I(DVE, "tensor_tensor", [(kOFFS, i), kTOT], [(kOFFS, i + 1)], out=OFFS[:, i + 1, :], in0=OFFS[:, i, :], in1=TOT[:, i, :], op=ALU.add)
    P.I(DVE, "tensor_tensor", [kPRE, kOFFS], [kPRE], out=PRE[:], in0=PRE[:], in1=OFFS[:], op=ALU.add)
    for (s, a, b_) in segs:
        cap = 32.0 if s == 0 else 512.0
        P.I(DVE, "tensor_single_scalar", [kPRE], [(kTOT, s)], out=TOT[:, a:b_, :], in_=PRE[:, a:b_, :], scalar=cap, op=ALU.is_lt)
    P.I(DVE, "tensor_tensor", [kTOT, kCMP], [kCMP], out=CMP[:], in0=CMP[:], in1=TOT[:], op=ALU.mult)
    P.I(DVE, "tensor_scalar_add", [kPRE], [(kPRE, 0)], out=PRE[:, 0:2, :], in0=PRE[:, 0:2, :], scalar1=512.0)
    IDXC, kIDXC = T([128, NT, 16], I32)
    for (big, dst, kdst) in ((10000.0, IDXS, ("IDXS",)), (544.0, IDXC, kIDXC)):
        P.I(DVE, "tensor_scalar_add", [kPRE], [kOFFS], out=OFFS[:], in0=PRE[:], scalar1=-big)
        P.I(DVE, "tensor_tensor", [kOFFS, kCMP], [kOFFS], out=OFFS[:], in0=OFFS[:], in1=CMP[:], op=ALU.mult)
        P.I(DVE, "tensor_scalar_add", [kOFFS], [kOFFS], out=OFFS[:], in0=OFFS[:], scalar1=big)
        P.I(DVE, "tensor_copy", [kOFFS], [kdst], out=dst[:], in_=OFFS[:])
    P.D(SP, [kIDXC], [("IDXC",)], out=IDXC_dram[:, :, :], in_=IDXC[:])
    P.barrier()
    P.sb_reset(m0)


def phase_scatter(P, C, H2R, XE, IDXS, cst=None):
    m0 = P.sb_mark()
    rows = [P.sb([128, ROWW], F32, "srow") for _ in range(3)]
    holder = {}

    P.nname += 1
    rname = "bcreg%d" % P.nname

    def f0(eng):
        holder["reg"] = eng.alloc_register(rname)
        return eng.reg_mov(holder["reg"], NSLOT - 1)
    P.op(POOL, f0, [], [])
    pre = []
    if cst is not None and "trash" in cst:
        TR = P.sb([128, 5], F32, "TR")
        P.D(SP, [], ["TR"], out=TR[:], in_=cst["trash"][:, :])
        for e in range(16):
            P.D(SP, ["TR"], [("XEpre", e, 0)], out=XE[e, 0:512, 2064].rearrange("(j p) -> p j", p=128), in_=TR[:, 0:4], allow_slow_non_contiguous=True)
            P.D(SP, ["TR"], [("XEpre", e, 1)], out=XE[e, 512:544, 2064].rearrange("(p o) -> p o", o=1), in_=TR[0:32, 4:5], allow_slow_non_contiguous=True)
        pre = [("XEpre",)]
    for i in range(NT):
        b = i % 3
        P.D(SP, [("H2R", i)], [("srow", b)], out=rows[b][:], in_=H2R[i * 128:(i + 1) * 128, :])
        for e in range(16):
            P.dma(POOL, (lambda eng, b=b, i=i, e=e: eng.indirect_dma_start(
                out=XE.ap().rearrange("e s w -> (e s) w"), out_offset=bass.IndirectOffsetOnAxis(ap=IDXS[:, i, e:e + 1], axis=0),
                in_=rows[b][:], in_offset=None, element_offset=e * NSLOT * ROWW, bounds_check=holder["reg"], oob_is_err=False)),
                reads=[("srow", b), ("IDXS",)] + pre, writes=[("XE", i, e)])
    P.barrier()
    P.sb_reset(m0)


NSL = 2176


def phase_experts(P, C, nsamp, nexp, xe_ap, gate_ap, w_ap, ye_ap, scat=None):
    m0 = P.sb_mark()
    nsl = nsamp * 544
    cx0 = nsamp * 512
    hidT = P.sb([128, 16, nsl], BF16, "hidT")
    stg = [P.sb([128, D], F32, "stg") for _ in range(2)]
    wgb = [P.sb([128, 16, 256], BF16, "wgb") for _ in range(2)]
    wub = [P.sb([128, 16, 256], BF16, "wub") for _ in range(2)]
    sil = [P.sb([128, 512], F32, "sil") for _ in range(2)]
    GT = P.sb([128, 2, 5, 4], F32, "GT")
    TKf = P.sb([128, 2, 5], F32, "TKf")
    TKi = P.sb([128, 2, 5], I32, "TKi")
    if scat is not None:
        FFN = scat["FFN"]
        zt = P.sb([128, D], F32, "zt")
        P.I(POOL, "memset", [], ["zt"], zt[:], 0.0)
        nrow = FFN.shape[0]
        for r0 in range(0, nrow, 128):
            n = min(128, nrow - r0)
            P.D(SP, ["zt"], [("FFN", r0)], out=FFN[r0:r0 + n, :], in_=zt[0:n, :])
    mA = P.sb_mark()
    nst = 0
    nw = 0
    blocks = [(b * 512, 512) for b in range(nsamp)] + [(cx0, nsamp * 32)]
    for el in range(nexp):
        ep = el % 2
        P.sb_reset(mA)
        xeT = P.sb([128, 16, nsl], BF16, "xeT")
        kx = "xeT%d" % el
        for b in range(nsamp):
            P.D(SP, [("XE",)], [("GT", ep, b)], out=GT[:, ep, 0:4, b], in_=gate_ap(b, el, 0, 512).rearrange("(j p) -> p j", p=128), allow_slow_non_contiguous=True)
            P.D(SP, [("XE",)], [("GTc", ep, b)], out=GT[b * 32:(b + 1) * 32, ep, 4, 0:1], in_=gate_ap(b, el, 512, 544).rearrange("(p o) -> p o", o=1), allow_slow_non_contiguous=True)
        if scat is not None:
            P.D(SP, [("XE",)], [("TKf", ep, 0)], out=TKf[:, ep, 0:4], in_=scat["tok_ap"](0, el, 0, 512).rearrange("(j p) -> p j", p=128), allow_slow_non_contiguous=True)
            P.D(SP, [("XE",)], [("TKf", ep, 1)], out=TKf[0:32, ep, 4:5], in_=scat["tok_ap"](0, el, 512, 544).rearrange("(p o) -> p o", o=1), allow_slow_non_contiguous=True)
            P.I(DVE, "tensor_copy", [("TKf", ep)], [("TKi", ep, 0)], out=TKi[:, ep, 0:4], in_=TKf[:, ep, 0:4])
            P.I(DVE, "tensor_copy", [("TKf", ep)], [("TKi", ep, 1)], out=TKi[0:32, ep, 4:5], in_=TKf[0:32, ep, 4:5])
        for b in range(nsamp):
            for j in range(5):
                np_ = 128 if j < 4 else 32
                col0 = b * 512 + j * 128 if j < 4 else cx0 + b * 32
                sb_ = nst % 2
                nst += 1
                ST = stg[sb_]
                P.D(SP, [("XE",)], [("stg", sb_)], out=ST[0:np_, :], in_=xe_ap(b, el, j * 128, j * 128 + np_))
                for kg in range(4):
                    bk = C.bank()
                    for q in range(4):
                        kt = kg * 4 + q
                        P.I(PE, "transpose", [("stg", sb_), "ident"], [("ps", bk)], out=C.ps[bk][:, q * 128:q * 128 + np_], in_=ST[0:np_, kt * 128:(kt + 1) * 128], identity=C.ident[0:np_, 0:np_])
                    copy_op(P, alt(kg), xeT[:, kg * 4:(kg + 1) * 4, col0:col0 + np_], C.ps[bk][:, :].rearrange("p (q t) -> p q t", q=4)[:, :, 0:np_], [], [("ps", bk), (kx, b, j, kg)])
        for fb in range(8):
            wb = nw % 2
            nw += 1
            P.D(POOL, [], [("wgb", wb)], out=wgb[wb][:], in_=w_ap("gate", el)[:, fb * 256:(fb + 1) * 256].rearrange("(kt p) n -> p kt n", p=128))
            P.D(POOL, [], [("wub", wb)], out=wub[wb][:], in_=w_ap("up", el)[:, fb * 256:(fb + 1) * 256].rearrange("(kt p) n -> p kt n", p=128))
            for fl in range(2):
                ft = fb * 2 + fl
                for nb, (c0, w) in enumerate(blocks):
                    bg = C.bank()
                    bu = C.bank()
                    for kt in range(16):
                        P.I(PE, "matmul", [("wgb", wb), (kx,)], [("ps", bg)], C.ps[bg][:, 0:w], lhsT=wgb[wb][:, kt, fl * 128:(fl + 1) * 128], rhs=xeT[:, kt, c0:c0 + w], start=(kt == 0), stop=(kt == 15))
                    for kt in range(16):
                        P.I(PE, "matmul", [("wub", wb), (kx,)], [("ps", bu)], C.ps[bu][:, 0:w], lhsT=wub[wb][:, kt, fl * 128:(fl + 1) * 128], rhs=xeT[:, kt, c0:c0 + w], start=(kt == 0), stop=(kt == 15))
                    sb_ = (ft * len(blocks) + nb) % 2
                    P.I(ACT, "activation", [], [("ps", bg), ("sil", sb_)], out=sil[sb_][:, 0:w], in_=C.ps[bg][:, 0:w], func=AF.Silu)
                    P.I(DVE, "tensor_tensor", [("sil", sb_)], [("ps", bu), ("hidT", el, ft, nb)], out=hidT[:, ft, c0:c0 + w], in0=C.ps[bu][:, 0:w], in1=sil[sb_][:, 0:w], op=ALU.mult)
        P.barrier()
        P.sb_reset(mA)
        wd = P.sb([128, 16, D], BF16, "wd")
        kw = "wd%d" % el
        for db in range(8):
            P.D(POOL, [], [(kw, db)], out=wd[:, :, db * 256:(db + 1) * 256], in_=w_ap("down", el)[:, db * 256:(db + 1) * 256].rearrange("(kt p) n -> p kt n", p=128))
        for b in range(nsamp):
            for j in range(5):
                if j == 4 and b > 0:
                    continue
                if j < 4:
                    col0, np_ = b * 512 + j * 128, 128
                    gcol = GT[:, ep, j, b:b + 1]
                else:
                    col0, np_ = cx0, nsamp * 32
                    gcol = GT[0:np_, ep, 4, 0:1]
                sb_ = nst % 2
                nst += 1
                ST = stg[sb_]
                for dk in range(4):
                    bk = C.bank()
                    for ft in range(16):
                        P.I(PE, "matmul", [("hidT", el, ft), (kw,)], [("ps", bk)], C.ps[bk][0:np_, :], lhsT=hidT[:, ft, col0:col0 + np_], rhs=wd[:, ft, dk * 512:(dk + 1) * 512], start=(ft == 0), stop=(ft == 15))
                    if dk % 2 == 0:
                        P.I(ACT, "activation", [("GT", ep), ("GTc", ep)], [("ps", bk), ("stg", sb_, dk)], out=ST[0:np_, dk * 512:(dk + 1) * 512], in_=C.ps[bk][0:np_, :], func=AF.Copy, scale=gcol)
                    else:
                        P.I(DVE, "tensor_scalar_mul", [("GT", ep), ("GTc", ep)], [("ps", bk), ("stg", sb_, dk)], out=ST[0:np_, dk * 512:(dk + 1) * 512], in0=C.ps[bk][0:np_, :], scalar1=gcol)
                if scat is not None:
                    P.dma(POOL, (lambda eng, ST=ST, np_=np_, ep=ep, j=j: eng.indirect_dma_start(
                        out=scat["FFN"].ap(), out_offset=bass.IndirectOffsetOnAxis(ap=TKi[0:np_, ep, j:j + 1], axis=0),
                        in_=ST[0:np_, :], in_offset=None, compute_op=ALU.add)), reads=[("stg", sb_), ("TKi", ep)], writes=[("FFN",)])
                elif j < 4:
                    P.D(SP, [("stg", sb_)], [("YE", el, b, j)], out=ye_ap(b, el, j * 128, (j + 1) * 128), in_=ST[:])
                else:
                    for b2 in range(nsamp):
                        P.D(SP, [("stg", sb_)], [("YE", el, b2, 4)], out=ye_ap(b2, el, 512, 544), in_=ST[b2 * 32:(b2 + 1) * 32, :])
        P.barrier()
    P.sb_reset(m0)


def phase_combine(P, C, X1, YEp, IDXC_in, OUT, MODV, ln_g, ln_b, t_lo, out_off, l=0):
    m0 = P.sb_mark()
    idx = P.sb([128, NT, 16], I32, "cidx")
    P.D(SP, [], ["cidx"], out=idx[:], in_=IDXC_in[:, :, :])
    gt2 = []
    for r in range(2):
        t = P.sb([128, D], F32, "gt2")
        bcast_load(P, SP, t, ("gt2", r), MODV[l, r, 5 * D:6 * D])
        gt2.append(t)
    g2 = P.sb([128, D], F32, "g2"); b2 = P.sb([128, D], F32, "b2")
    bcast_load(P, SP, g2, "g2", ln_g[l, :]); bcast_load(P, SP, b2, "b2", ln_b[l, :])
    NG = 4
    gb = [P.sb([128, D], F32, "gb") for _ in range(NG)]
    acc = [P.sb([128, D], F32, "acc") for _ in range(2)]
    xs = [P.sb([128, D], F32, "cxs") for _ in range(2)]
    st = P.sb([128, 4, 6], F32, "cst"); mv = P.sb([128, 2], F32, "cmv"); rs = P.sb([128, 2], F32, "crs")
    yv = YEp.ap().rearrange("e s w -> (e s) w")
    ng = 0
    for i in range(t_lo, NT):
        b = i % 2
        r = 1 if i < 2 else 0
        A, XS = acc[b], xs[b]
        kA, kXS = ("acc", b), ("cxs", b)
        P.D(SP, [("X1", i)], [kXS], out=XS[:], in_=X1[i * 128:(i + 1) * 128, :])
        for e in range(16):
            if e == 0:
                dst, kdst = A, kA
            else:
                gi = ng % NG
                ng += 1
                dst, kdst = gb[gi], ("gb", gi)
            P.dma(POOL, (lambda eng, dst=dst, i=i, e=e: eng.indirect_dma_start(
                out=dst[:], out_offset=None, in_=yv, in_offset=bass.IndirectOffsetOnAxis(ap=idx[:, i, e:e + 1], axis=0),
                element_offset=e * 545 * D)), reads=["cidx", ("YEp",)], writes=[kdst])
            if e > 0:
                P.I(DVE if e % 2 else POOL, "tensor_tensor", [kdst, kA], [kA], out=A[:], in0=A[:], in1=dst[:], op=ALU.add)
        P.I(DVE, "tensor_tensor", [kA, ("gt2", r)], [kA], out=A[:], in0=A[:], in1=gt2[r][:], op=ALU.mult)
        P.I(DVE, "scalar_tensor_tensor", [kXS, kA], [kA], out=A[:], in0=XS[:], scalar=ALPHA_, in1=A[:], op0=ALU.mult, op1=ALU.add)
        ln_apply(P, C, A, kA, A, kA, st, mv, rs, "c1", g2[:], "g2", b2[:], "b2")
        o0 = i * 128 - out_off
        P.D(SP, [kA], [("X", i)], out=OUT[o0:o0 + 128, :], in_=A[:])
    P.barrier()
    P.sb_reset(m0)


def phase_ln2(P, C, X1, FFN, OUT, MODV, ln_g, ln_b, t_lo, out_off, l):
    m0 = P.sb_mark()
    gt2 = []
    for r in range(2):
        t = P.sb([128, D], F32, "gt2")
        bcast_load(P, SP, t, ("gt2", r), MODV[l, r, 5 * D:6 * D])
        gt2.append(t)
    g2 = P.sb([128, D], F32, "g2"); b2 = P.sb([128, D], F32, "b2")
    bcast_load(P, SP, g2, "g2", ln_g[l, :]); bcast_load(P, SP, b2, "b2", ln_b[l, :])
    acc = [P.sb([128, D], F32, "acc") for _ in range(3)]
    xs = [P.sb([128, D], F32, "cxs") for _ in range(3)]
    st = P.sb([128, 4, 6], F32, "cst"); mv = P.sb([128, 2], F32, "cmv"); rs = P.sb([128, 2], F32, "crs")
    for i in range(t_lo, NT):
        b = i % 3
        r = 1 if i < 2 else 0
        A, XS = acc[b], xs[b]
        kA, kXS = ("acc", b), ("cxs", b)
        P.D(SP, [("X1", i)], [kXS], out=XS[:], in_=X1[i * 128:(i + 1) * 128, :])
        P.D(ACT, [("FFN",)], [kA], out=A[:], in_=FFN[i * 128:(i + 1) * 128, :])
        P.I(POOL, "tensor_tensor", [kA, ("gt2", r)], [kA], out=A[:], in0=A[:], in1=gt2[r][:], op=ALU.mult)
        P.I(DVE, "scalar_tensor_tensor", [kXS, kA], [kA], out=A[:], in0=XS[:], scalar=ALPHA_, in1=A[:], op0=ALU.mult, op1=ALU.add)
        ln_apply(P, C, A, kA, A, kA, st, mv, rs, "c1", g2[:], "g2", b2[:], "b2")
        o0 = i * 128 - out_off
        P.D(SP, [kA], [("X", i)], out=OUT[o0:o0 + 128, :], in_=A[:])
    P.barrier()
    P.sb_reset(m0)


def _host_consts():
    c = {}
    c["ident"] = np.eye(128, dtype=np.float32)
    kp = np.arange(128)[:, None]; qp = np.arange(128)[None, :]
    c["mprev"] = np.tile((kp >= qp).astype(np.float32), (1, 4))
    c["mnext"] = np.tile((kp <= qp).astype(np.float32), (1, 4))
    nf = 32
    inv = (10000.0 ** (-np.arange(nf, dtype=np.float32) / nf)).astype(np.float32)
    t = np.arange(4096)
    rows = (t // 64).astype(np.float32); cols = (t % 64).astype(np.float32)
    ang_r = rows[:, None] * inv[None, :]; ang_c = cols[:, None] * inv[None, :]
    cos = np.concatenate([np.cos(ang_r), np.cos(ang_r), np.cos(ang_c), np.cos(ang_c)], 1)
    sin = np.concatenate([-np.sin(ang_r), np.sin(ang_r), -np.sin(ang_c), np.sin(ang_c)], 1)
    c["cos"] = cos.reshape(32, 128, 128).astype(np.float32)
    c["sin"] = sin.reshape(32, 128, 128).astype(np.float32)
    s_ = np.arange(8, dtype=np.float32)
    er = np.concatenate([7 - s_, s_ - 7, s_ + 1, s_, -s_, 8 - s_]).astype(np.float32)
    c["erow"] = np.tile(er[None, :], (128, 1))
    E = np.zeros((8, 128, 240), np.float32)
    for r in range(8):
        for j in range(16):
            E[r, r * 16 + j, 7 * 16 + j] = 1.0
    c["E"] = E
    sb = np.arange(128)[:, None] // 16; tb = np.arange(128)[None, :] // 16
    c["toemf"] = (tb >= sb).astype(np.float32)
    c["toemb"] = (sb >= tb).astype(np.float32)
    a = np.arange(128)
    same = (a[:, None] // 64) == (a[None, :] // 64)
    le = a[:, None] <= a[None, :]
    ge = a[:, None] >= a[None, :]
    c["trif"] = (same & le).astype(np.float32) * (-1.0 / 16.0)
    c["trib"] = (same & ge).astype(np.float32) * (-1.0 / 16.0)
    c["blk"] = same.astype(np.float32) * (-1.0 / 16.0)
    c["cind"] = np.stack([(a < 64), (a >= 64)], 1).astype(np.float32) * (-1.0 / 16.0)
    c["gmaskf"] = np.tile((same & le).astype(np.float32), (1, 4))
    c["gmaskb"] = np.tile((same & ge).astype(np.float32), (1, 4))
    c["ones"] = np.ones((128, 128), np.float32)
    c["strict"] = (a[:, None] < a[None, :]).astype(np.float32)
    c["tgt"] = np.tile(np.concatenate([np.full(16, 32.0), np.full(16, 512.0)])[None, :], (128, 1)).astype(np.float32)
    c["tokid"] = (np.arange(34)[None, :] * 128 + np.arange(128)[:, None]).astype(np.float32)
    c["trash"] = (4352 + np.arange(5)[None, :] * 128 + np.arange(128)[:, None]).astype(np.float32)
    return c
HOSTC = _host_consts()


S5N = ["lam_re", "lam_im", "log_dt", "b_re", "b_im", "c_re", "c_im", "d", "w_glu", "b_glu"]
GLN = ["w_gate", "b_gate", "norm_g"]


def _decl_consts(P):
    return {k: P.dram("c_" + k, list(v.shape), F32, kind="ExternalInput") for k, v in HOSTC.items()}


def build_mod():
    P = Prog(); C = Ctx()
    cst = {"ident": P.dram("c_ident", [128, 128], F32, kind="ExternalInput")}
    c_in = P.dram("c_in", [5, D], F32, kind="ExternalInput")
    w_ada = P.dram("w_ada", [2, D, 1536], F32, kind="ExternalInput")
    b_ada = P.dram("b_ada", [2, 1536], F32, kind="ExternalInput")
    MODS = P.dram("MODS", [2, 5, 1536], F32, kind="ExternalOutput")
    setup_consts(P, C, cst)
    phase_mod(P, C, c_in, w_ada, b_ada, MODS, R=5, NBLK=3)
    P.finish()
    return P.emit()


def build_layer(with_combine, shapes):
    P = Prog(); C = Ctx()
    cst = _decl_consts(P)
    ins = {k: P.dram(k, list(s), F32, kind="ExternalInput") for k, s in shapes.items()}
    MODV = P.dram("MODV", [1, 2, 6 * D], F32, kind="ExternalInput")
    setup_consts(P, C, cst)
    if with_combine:
        X1p = P.dram("X1p", [NTOK, D], F32, kind="ExternalInput")
        YEp = P.dram("YEp", [16, 545, D], F32, kind="ExternalInput")
        IDXp = P.dram("IDXp", [128, NT, 16], I32, kind="ExternalInput")
        MODVp = P.dram("MODVp", [1, 2, 6 * D], F32, kind="ExternalInput")
        l2g = P.dram("ln2_g", [1, D], F32, kind="ExternalInput")
        l2b = P.dram("ln2_b", [1, D], F32, kind="ExternalInput")
        X = P.dram("X2s", [NTOK, D], F32)
        phase_combine(P, C, X1p, YEp, IDXp, X, MODVp, l2g, l2b, 0, 0)
    else:
        X = P.dram("xin", [NTOK, D], F32, kind="ExternalInput")
    PROJ = P.dram("PROJ", [NTOK, NIN], F32)
    MIXT = P.dram("MIXT", [16, 128, NTOK], BF16)
    OF = P.dram("OF", [NTOK, 512], F32)
    H2R = P.dram("H2R", [NTOK, ROWW], F32)
    X1 = P.dram("X1", [NTOK, D], F32, kind="ExternalOutput")
    XE = P.dram("XE", [16, NSLOT, ROWW], F32, kind="ExternalOutput")
    IDXC = P.dram("IDXC", [128, NT, 16], I32, kind="ExternalOutput")
    phase_inproj(P, C, 0, X, PROJ, ins["w_in"], MODV)
    phase_attn(P, C, 0, PROJ, MIXT, ins["attn_sink"], cst)
    phase_s5(P, C, 0, PROJ, MIXT, {n: ins["ssm_" + n] for n in S5N}, cst)
    phase_gla(P, C, 0, PROJ, MIXT, OF, {n: ins["gla_" + n] for n in GLN}, cst)
    AFF = P.sb([128, NT, 16], F32, "AFF")
    IDXS = P.sb([128, NT, 16], I32, "IDXS")
    phase_wout(P, C, 0, X, MIXT, X1, H2R, AFF, ins["w_out"], MODV, ins["ln1_g"], ins["ln1_b"], ins["router"])
    phase_route(P, C, AFF, IDXS, IDXC, cst)
    phase_scatter(P, C, H2R, XE, IDXS)
    P.finish()
    return P.emit()


def build_experts():
    P = Prog(); C = Ctx()
    cst = {"ident": P.dram("c_ident", [128, 128], F32, kind="ExternalInput")}
    XEc = P.dram("XEc", [4, 2, NSLOT, ROWW], F32, kind="ExternalInput")
    GATE = P.dram("GATE", [4, 2, NSLOT], F32, kind="ExternalInput")
    WG = P.dram("WG", [2, D, D], F32, kind="ExternalInput")
    WU = P.dram("WU", [2, D, D], F32, kind="ExternalInput")
    WD = P.dram("WD", [2, D, D], F32, kind="ExternalInput")
    YE = P.dram("YE", [4, 2, NSLOT, D], F32, kind="ExternalOutput")
    setup_consts(P, C, cst)
    wmap = {"gate": WG, "up": WU, "down": WD}
    phase_experts(P, C, 4, 2, lambda b, el, r0, r1: XEc[b, el, r0:r1, 0:D], lambda b, el, r0, r1: GATE[b, el, r0:r1],
                  lambda kind, el: wmap[kind][el], lambda b, el, r0, r1: YE[b, el, r0:r1, :])
    P.finish()
    return P.emit()


def build_final():
    P = Prog(); C = Ctx()
    cst = {"ident": P.dram("c_ident", [128, 128], F32, kind="ExternalInput")}
    X1p = P.dram("X1p", [NTOK, D], F32, kind="ExternalInput")
    YEp = P.dram("YEp", [16, 545, D], F32, kind="ExternalInput")
    IDXp = P.dram("IDXp", [128, NT, 16], I32, kind="ExternalInput")
    MODVp = P.dram("MODVp", [1, 2, 6 * D], F32, kind="ExternalInput")
    l2g = P.dram("ln2_g", [1, D], F32, kind="ExternalInput")
    l2b = P.dram("ln2_b", [1, D], F32, kind="ExternalInput")
    OUT = P.dram("OUT", [4096, D], F32, kind="ExternalOutput")
    setup_consts(P, C, cst)
    phase_combine(P, C, X1p, YEp, IDXp, OUT, MODVp, l2g, l2b, 2, 256)
    P.finish()
    return P.emit()


LAYER_KEYS = ["w_in", "attn_sink", "ssm_lam_re", "ssm_lam_im", "ssm_log_dt", "ssm_b_re", "ssm_b_im", "ssm_c_re", "ssm_c_im",
              "ssm_d", "ssm_w_glu", "ssm_b_glu", "gla_w_gate", "gla_b_gate", "gla_norm_g", "w_out", "ln1_g", "ln1_b", "router"]


def kernel_multi(**inp):
    inp = {k: np.ascontiguousarray(np.asarray(v)) for k, v in inp.items()}
    f32 = np.float32
    cmap = {"c_" + k: v for k, v in HOSTC.items()}
    ident = {"c_ident": HOSTC["ident"]}
    c_in = np.concatenate([inp["c"], inp["c_ctx"][None]], 0).astype(f32)
    maps = []
    for c in range(8):
        sl = slice(c * 1536, (c + 1) * 1536)
        maps.append(dict(ident, c_in=c_in, w_ada=np.ascontiguousarray(inp["w_ada"][:, :, sl]), b_ada=np.ascontiguousarray(inp["b_ada"][:, sl])))
    res = run_bass_kernel_spmd(build_mod(), maps, core_ids=list(range(8)))
    mods = np.concatenate([r["MODS"] for r in res.results], axis=2)
    modv = [[np.ascontiguousarray(np.stack([mods[l, b], mods[l, 4]], 0)[None]) for l in range(2)] for b in range(4)]
    shapes = {k: (1,) + inp[k].shape[1:] for k in LAYER_KEYS}
    prev = None
    nc_exp = None
    for l in range(2):
        lw = {k: np.ascontiguousarray(inp[k][l:l + 1]) for k in LAYER_KEYS}
        maps = []
        for b in range(4):
            m = dict(cmap); m.update(lw); m["MODV"] = modv[b][l]
            if l == 0:
                m["xin"] = np.concatenate([inp["ctx"][b], inp["x"][b]], 0)
            else:
                m.update(prev[b])
            maps.append(m)
        res = run_bass_kernel_spmd(build_layer(l > 0, shapes), maps, core_ids=list(range(4)))
        X1 = [res.results[b]["X1"] for b in range(4)]
        XE = [res.results[b]["XE"] for b in range(4)]
        IDX = [res.results[b]["IDXC"] for b in range(4)]
        maps = []
        for c in range(8):
            xec = np.stack([XE[b][2 * c:2 * c + 2] for b in range(4)], 0)
            gate = np.stack([np.stack([XE[b][2 * c + el, :, 2048 + 2 * c + el] for el in range(2)], 0) for b in range(4)], 0)
            maps.append(dict(ident, XEc=np.ascontiguousarray(xec), GATE=np.ascontiguousarray(gate),
                             WG=np.ascontiguousarray(inp["exp_w_gate"][l, 2 * c:2 * c + 2]),
                             WU=np.ascontiguousarray(inp["exp_w_up"][l, 2 * c:2 * c + 2]),
                             WD=np.ascontiguousarray(inp["exp_w_down"][l, 2 * c:2 * c + 2])))
        if nc_exp is None:
            nc_exp = build_experts()
        res = run_bass_kernel_spmd(nc_exp if l == 0 else build_experts(), maps, core_ids=list(range(8)))
        prev = []
        for b in range(4):
            yep = np.zeros((16, 545, D), f32)
            for c in range(8):
                yep[2 * c:2 * c + 2, :544] = res.results[c]["YE"][b]
            prev.append({"X1p": X1[b], "YEp": yep, "IDXp": IDX[b], "MODVp": modv[b][l],
                         "ln2_g": np.ascontiguousarray(inp["ln2_g"][l:l + 1]), "ln2_b": np.ascontiguousarray(inp["ln2_b"][l:l + 1])})
    maps = [dict(ident, **prev[b]) for b in range(4)]
    res = run_bass_kernel_spmd(build_final(), maps, core_ids=list(range(4)))
    return np.stack([res.results[b]["OUT"] for b in range(4)], 0).astype(f32)


FUSED_KEYS = LAYER_KEYS + ["ln2_g", "ln2_b", "w_ada", "b_ada", "exp_w_gate", "exp_w_up", "exp_w_down"]


def build_fused(shapes):
    P = Prog(); C = Ctx()
    cst = _decl_consts(P)
    ins = {k: P.dram(k, list(shapes[k]), F32, kind="ExternalInput") for k in FUSED_KEYS}
    xin = P.dram("xin", [NTOK, D], F32, kind="ExternalInput")
    c_in = P.dram("c_in", [2, D], F32, kind="ExternalInput")
    OUT = P.dram("OUT", [4096, D], F32, kind="ExternalOutput")
    MODV = P.dram("MODV", [2, 2, 6 * D], F32)
    PROJ = P.dram("PROJ", [NTOK, NIN], F32)
    MIXT = P.dram("MIXT", [16, 128, NTOK], BF16)
    OF = P.dram("OF", [NTOK, 512], F32)
    H2R = P.dram("H2R", [NTOK, ROWW], F32)
    X1 = P.dram("X1", [NTOK, D], F32)
    XN = P.dram("XN", [NTOK, D], F32)
    XE = P.dram("XE", [16, NSLOT, ROWW], F32)
    FFN = P.dram("FFN", [NTOK + NSLOT, D], F32)
    IDXC = P.dram("IDXC", [128, NT, 16], I32)
    setup_consts(P, C, cst)
    phase_mod(P, C, c_in, ins["w_ada"], ins["b_ada"], MODV, R=2, NBLK=24)
    X = xin
    for l in range(2):
        phase_inproj(P, C, l, X, PROJ, ins["w_in"], MODV)
        phase_attn(P, C, l, PROJ, MIXT, ins["attn_sink"], cst)
        phase_s5(P, C, l, PROJ, MIXT, {n: ins["ssm_" + n] for n in S5N}, cst)
        phase_gla(P, C, l, PROJ, MIXT, OF, {n: ins["gla_" + n] for n in GLN}, cst)
        m0 = P.sb_mark()
        AFF = P.sb([128, NT, 16], F32, "AFF")
        IDXS = P.sb([128, NT, 16], I32, "IDXS")
        phase_wout(P, C, l, X, MIXT, X1, H2R, AFF, ins["w_out"], MODV, ins["ln1_g"], ins["ln1_b"], ins["router"], cst)
        phase_route(P, C, AFF, IDXS, IDXC, cst)
        phase_scatter(P, C, H2R, XE, IDXS, cst)
        P.sb_reset(m0)
        phase_experts(P, C, 1, 16, lambda b, el, r0, r1: XE[el, r0:r1, 0:D], lambda b, el, r0, r1: XE[el, r0:r1, 2048 + el],
                      lambda kind, el, l=l: ins["exp_w_" + kind][l, el], None,
                      scat={"FFN": FFN, "tok_ap": lambda b, el, r0, r1: XE[el, r0:r1, 2064]})
        if l == 0:
            phase_ln2(P, C, X1, FFN, XN, MODV, ins["ln2_g"], ins["ln2_b"], 0, 0, l)
            X = XN
        else:
            phase_ln2(P, C, X1, FFN, OUT, MODV, ins["ln2_g"], ins["ln2_b"], 2, 256, l)
    P.finish()
    nc = P.emit()
    return nc


def fused_maps(inp, samples):
    cmap = {"c_" + k: v for k, v in HOSTC.items()}
    shared = {k: inp[k] for k in FUSED_KEYS}
    maps = []
    for b in samples:
        m = dict(cmap); m.update(shared)
        m["xin"] = np.concatenate([inp["ctx"][b], inp["x"][b]], 0)
        m["c_in"] = np.ascontiguousarray(np.stack([inp["c"][b], inp["c_ctx"]], 0))
        maps.append(m)
    return maps


def kernel(**inp):
    inp = {k: np.ascontiguousarray(np.asarray(v)) for k, v in inp.items()}
    shapes = {k: inp[k].shape for k in FUSED_KEYS}
    nc = build_fused(shapes)
    maps = fused_maps(inp, [c % 4 for c in range(8)])
    res = run_bass_kernel_spmd(nc, maps, core_ids=list(range(8)))
    return np.stack([res.results[b]["OUT"] for b in range(4)], 0).astype(np.float32)
```

```python
import numpy as np
import concourse.bass as bass
import concourse.mybir as mybir
from concourse.bass_utils import run_bass_kernel_spmd

F32 = mybir.dt.float32
BF16 = mybir.dt.bfloat16
I32 = mybir.dt.int32
ALU = mybir.AluOpType
AF = mybir.ActivationFunctionType
AX = mybir.AxisListType

PE, ACT, DVE, POOL, SP = "pe", "act", "dve", "pool", "sp"
ENGS = (PE, ACT, DVE, POOL, SP)
NDMASEM = 8


class Prog:
    def __init__(self):
        self.nc = bass.Bass("TRN2", target_bir_lowering=False)
        self.ops = {e: [] for e in ENGS}
        self.state = {}
        self.dmas = []
        self.ndma = {e: 0 for e in ENGS}
        self.sb_off = 20608
        self.sb_hi = 0
        self.nname = 0
        self.psum = []
        self.sb_cap = 229376

    def dram(self, name, shape, dtype, kind="Internal"):
        return self.nc.dram_tensor(name, list(shape), dtype, kind=kind)

    def sb(self, shape, dtype, name=None):
        size = int(np.prod(shape[1:])) * mybir.dt.size(dtype) if hasattr(mybir.dt, "size") else None
        if size is None:
            size = int(np.prod(shape[1:])) * {F32: 4, BF16: 2, I32: 4}[dtype]
        size = (size + 31) // 32 * 32
        self.nname += 1
        nm = (name or "t") + "_%d" % self.nname
        t = self.nc.alloc_sbuf_tensor_at(nm, list(shape), dtype, offset=self.sb_off)
        self.sb_off += size
        assert self.sb_off <= self.sb_cap, ("SBUF overflow", nm, self.sb_off)
        self.sb_hi = max(self.sb_hi, self.sb_off)
        return t

    def sb_mark(self):
        return self.sb_off

    def sb_reset(self, mark):
        self.sb_off = mark

    @staticmethod
    def _conf(a, b):
        n = min(len(a), len(b))
        return a[:n] == b[:n]

    def _deps(self, reads, writes):
        deps = set()
        for k in reads:
            root = self.state.setdefault(k[0], {})
            for k2, st in root.items():
                if self._conf(k, k2) and st[0] is not None:
                    deps.add(st[0])
        for k in writes:
            root = self.state.setdefault(k[0], {})
            for k2, st in root.items():
                if self._conf(k, k2):
                    if st[0] is not None:
                        deps.add(st[0])
                    deps.update(st[1])
        return deps

    def _commit(self, ev, reads, writes):
        for k in reads:
            root = self.state[k[0]]
            st = root.setdefault(k, [None, []])
            st[1].append(ev)
        for k in writes:
            root = self.state[k[0]]
            for k2 in [k2 for k2 in root if len(k2) > len(k) and self._conf(k, k2)]:
                del root[k2]
            root[k] = [ev, []]

    @staticmethod
    def _norm(keys):
        out = []
        for k in keys:
            if isinstance(k, str):
                k = (k,)
            assert isinstance(k[0], str), k
            out.append(tuple(k))
        return out

    def I(self, eng, name, reads, writes, *a, **kw):
        return self.op(eng, lambda e: getattr(e, name)(*a, **kw), reads, writes)

    def D(self, q, reads, writes, **kw):
        return self.dma(q, lambda e: e.dma_start(**kw), reads, writes)

    def op(self, eng, fn, reads=(), writes=()):
        reads = self._norm(reads)
        writes = self._norm(writes)
        deps = self._deps(reads, writes)
        idx = len(self.ops[eng])
        ev = ("c", eng, idx)
        self.ops[eng].append(dict(fn=fn, waits=deps, dma=None, signal=False))
        self._commit(ev, reads, writes)
        return ev

    def dma(self, q, fn, reads=(), writes=()):
        reads = self._norm(reads)
        writes = self._norm(writes)
        deps = self._deps(reads, writes)
        j = self.ndma[q]
        self.ndma[q] += 1
        si, val = j % NDMASEM, 16 * (j // NDMASEM + 1)
        did = len(self.dmas)
        self.dmas.append((q, si, val))
        if j >= NDMASEM:
            deps.add(("d", self._last_dma[(q, si)]))
        if not hasattr(self, "_last_dma"):
            self._last_dma = {}
        self._last_dma[(q, si)] = did
        ev = ("d", did)
        self.ops[q].append(dict(fn=fn, waits=deps, dma=did, signal=False))
        self._commit(ev, reads, writes)
        return ev

    def barrier(self):
        evs = set()
        for e in ENGS:
            for i in range(len(self.ops[e]) - 1, -1, -1):
                o = self.ops[e][i]
                if o["fn"] is not None and o["dma"] is None:
                    evs.add(("c", e, i))
                    break
        if hasattr(self, "_last_dma"):
            for did in self._last_dma.values():
                evs.add(("d", did))
        for e in ENGS:
            self.ops[e].append(dict(fn=None, waits=set(evs), dma=None, signal=False))
        self.state = {}

    def emit(self):
        nc = self.nc
        plan = {e: [] for e in ENGS}
        for e in ENGS:
            wc = {}
            wd = {}
            for i, o in enumerate(self.ops[e]):
                need_c, need_d = {}, {}
                for ev in o["waits"]:
                    if ev[0] == "c":
                        _, se, si_ = ev
                        if se == e and e == PE:
                            continue
                        if se == e and si_ >= i:
                            continue
                        if wc.get(se, -1) >= si_:
                            continue
                        need_c[se] = max(need_c.get(se, -1), si_)
                    else:
                        q, si_, val = self.dmas[ev[1]]
                        if wd.get((q, si_), 0) >= val:
                            continue
                        need_d[(q, si_)] = max(need_d.get((q, si_), 0), val)
                for se, si_ in need_c.items():
                    wc[se] = si_
                    self.ops[se][si_]["signal"] = True
                for k, v in need_d.items():
                    wd[k] = v
                plan[e].append((need_c, need_d))
        semval = {}
        for e in ENGS:
            c = 0
            for i, o in enumerate(self.ops[e]):
                if o["signal"]:
                    c += 1
                    semval[(e, i)] = c
            self.nsig = getattr(self, "nsig", {})
            self.nsig[e] = c
        from contextlib import ExitStack
        with ExitStack() as es:
            csem = {e: es.enter_context(nc.semaphore("c_" + e)) for e in ENGS}
            dsem = {(q, s): es.enter_context(nc.semaphore("d_%s_%d" % (q, s)))
                    for q in ENGS for s in range(NDMASEM) if self.ndma[q] > s}
            block = es.enter_context(nc.Block())

            def run(e, engobj):
                for i, o in enumerate(self.ops[e]):
                    need_c, need_d = plan[e][i]
                    for se, si_ in need_c.items():
                        engobj.wait_ge(csem[se], semval[(se, si_)])
                    for k, v in need_d.items():
                        engobj.wait_ge(dsem[k], v)
                    if o["fn"] is None:
                        continue
                    inst = o["fn"](engobj)
                    if o["dma"] is not None:
                        q, s, v = self.dmas[o["dma"]]
                        inst.then_inc(dsem[(q, s)], 16)
                    elif o["signal"]:
                        inst.then_inc(csem[e], 1)

            @block.tensor
            def _(eng):
                run(PE, eng)

            @block.scalar
            def _(eng):
                run(ACT, eng)

            @block.vector
            def _(eng):
                run(DVE, eng)

            @block.gpsimd
            def _(eng):
                run(POOL, eng)

            @block.sync
            def _(eng):
                run(SP, eng)
        return nc

    def finish(self):
        self.barrier()


NT = 34
NTOK = 4352
D = 2048
NIN = 3616


def alt(i):
    return ACT if i % 2 == 0 else DVE


def copy_op(P, eng, out, in_, reads, writes):
    if eng == ACT:
        P.op(ACT, lambda e: e.copy(out=out, in_=in_), reads, writes)
    else:
        P.op(eng, lambda e: e.tensor_copy(out=out, in_=in_), reads, writes)


class Ctx:
    pass


def setup_consts(P, C, cst):
    C.ident = P.sb([128, 128], F32, "ident")
    C.identb = P.sb([128, 128], BF16, "identb")
    P.dma(SP, lambda e: e.dma_start(out=C.ident[:], in_=cst["ident"][:, :]), writes=["ident"])
    P.op(DVE, lambda e: e.tensor_copy(out=C.identb[:], in_=C.ident[:]), reads=["ident"], writes=["identb"])
    C.ps = [P.nc.alloc_psum_tensor("psb%d" % i, [128, 512], F32) for i in range(8)]
    C.nbank = 0

    def bank():
        b = C.nbank % 8
        C.nbank += 1
        return b
    C.bank = bank


def phase_mod(P, C, c_in, w_ada, b_ada, MODS, R=5, NBLK=3):
    m0 = P.sb_mark()
    cT = P.sb([128, 16, R], F32, "cT")
    for r in range(R):
        P.D(SP, [], [("cT", r)], out=cT[:, :, r], in_=c_in[r, :].rearrange("(kt p) -> p kt", p=128), allow_slow_non_contiguous=True)
    P.I(ACT, "activation", ["cT"], ["cT"], out=cT[:], in_=cT[:], func=AF.Silu)
    wts = [P.sb([128, 16, 512], F32, "wada") for _ in range(2)]
    bts = [P.sb([R, 512], F32, "bada") for _ in range(2)]
    rts = [P.sb([R, 512], F32, "rada") for _ in range(2)]
    it = 0
    for l in range(2):
        for nb in range(NBLK):
            b = it % 2
            it += 1
            for kh in range(2):
                P.D(SP if kh == 0 else ACT, [], [("wada", b, kh)], out=wts[b][:, kh * 8:(kh + 1) * 8, :],
                    in_=w_ada[l, kh * 1024:(kh + 1) * 1024, nb * 512:(nb + 1) * 512].rearrange("(kt p) n -> p kt n", p=128))
            P.D(SP, [], [("bada", b)], out=bts[b][:], in_=b_ada[l, nb * 512:(nb + 1) * 512].partition_broadcast(R))
            bk = C.bank()
            for kt in range(16):
                P.I(PE, "matmul", ["cT", ("wada", b)], [("ps", bk)], C.ps[bk][0:R, :], lhsT=cT[:, kt, :], rhs=wts[b][:, kt, :], start=(kt == 0), stop=(kt == 15))
            P.I(DVE, "tensor_tensor", [("bada", b)], [("ps", bk), ("rada", b)], out=rts[b][:], in0=C.ps[bk][0:R, :], in1=bts[b][:], op=ALU.add)
            P.D(SP, [("rada", b)], [("MODS", l, nb)], out=MODS[l, :, nb * 512:(nb + 1) * 512], in_=rts[b][:])
    P.barrier()
    P.sb_reset(m0)


def ln_stats(P, C, xt, key, st, mv, rs, tag):
    for j in range(4):
        P.op(DVE, lambda e, j=j: e.bn_stats(out=st[:, j, :], in_=xt[:, j * 512:(j + 1) * 512]),
             reads=[key], writes=[(tag + "st", j)])
    P.op(DVE, lambda e: e.bn_aggr(out=mv[:], in_=st[:].rearrange("p a b -> p (a b)")), reads=[tag + "st"], writes=[tag + "mv"])
    P.op(DVE, lambda e: e.tensor_scalar_add(out=rs[:, 0:1], in0=mv[:, 1:2], scalar1=1e-6), reads=[tag + "mv"], writes=[(tag + "rs", 0)])
    P.op(ACT, lambda e: e.activation(out=rs[:, 0:1], in_=rs[:, 0:1], func=AF.Ln), reads=[(tag + "rs", 0)], writes=[(tag + "rs", 0)])
    P.op(ACT, lambda e: e.activation(out=rs[:, 0:1], in_=rs[:, 0:1], func=AF.Exp, scale=-0.5), reads=[(tag + "rs", 0)], writes=[(tag + "rs", 0)])
    P.op(DVE, lambda e: e.scalar_tensor_tensor(out=rs[:, 1:2], in0=mv[:, 0:1], scalar=-1.0, in1=rs[:, 0:1], op0=ALU.mult, op1=ALU.mult),
         reads=[tag + "mv", (tag + "rs", 0)], writes=[(tag + "rs", 1)])


def load_modT(P, MODV, l, chunk, dst, key, plus1):
    for r in range(2):
        P.dma(SP, lambda e, r=r: e.dma_start(out=dst[:, r, :], in_=MODV[l, r, chunk * 2048:(chunk + 1) * 2048].rearrange("(kt p) -> p kt", p=128),
                                            allow_slow_non_contiguous=True), reads=[("MODV", l)], writes=[(key, r)])
    if plus1:
        P.op(DVE, lambda e: e.tensor_scalar_add(out=dst[:], in0=dst[:], scalar1=1.0), reads=[key], writes=[key])


def phase_inproj(P, C, l, X, PROJ, w_in, MODV):
    m0 = P.sb_mark()
    wbf = P.sb([128, 16, NIN], BF16, "wbf")
    for kt in range(16):
        for h in range(2):
            P.dma(POOL, lambda e, kt=kt, h=h: e.dma_start(out=wbf[:, kt, h * 1808:(h + 1) * 1808],
                                                         in_=w_in[l, kt * 128:(kt + 1) * 128, h * 1808:(h + 1) * 1808]),
                  writes=[("wbf", kt, h)])
    scT = P.sb([128, 2, 16], F32, "scT")
    shT = P.sb([128, 2, 16], F32, "shT")
    load_modT(P, MODV, l, 1, scT, "scT", True)
    load_modT(P, MODV, l, 0, shT, "shT", False)
    xts = [P.sb([128, D], F32, "xt") for _ in range(2)]
    xns = [P.sb([128, D], BF16, "xn") for _ in range(2)]
    hTs = [P.sb([128, 16, 128], BF16, "hT") for _ in range(2)]
    ots = [P.sb([128, NIN], F32, "ot") for _ in range(2)]
    st = P.sb([128, 4, 6], F32, "st")
    mv = P.sb([128, 2], F32, "mv")
    rs = P.sb([128, 2], F32, "rs")
    for i in range(NT):
        b = i % 2
        r = 1 if i < 2 else 0
        xt, xn, hT, ot = xts[b], xns[b], hTs[b], ots[b]
        P.dma(SP, lambda e, i=i, xt=xt: e.dma_start(out=xt[:], in_=X[i * 128:(i + 1) * 128, :]), reads=[("X", i)], writes=[("xt", b)])
        ln_stats(P, C, xt, ("xt", b), st, mv, rs, "ip")
        P.op(ACT, lambda e, xt=xt, xn=xn: e.activation(out=xn[:], in_=xt[:], func=AF.Identity, bias=rs[:, 1:2], scale=rs[:, 0:1]),
             reads=[("xt", b), "iprs"], writes=[("xn", b)])
        for kg in range(4):
            bk = C.bank()
            psb = C.ps[bk][:].bitcast(BF16)
            for j in range(4):
                kt = kg * 4 + j
                P.op(PE, lambda e, j=j, kt=kt, psb=psb, xn=xn: e.transpose(out=psb[:, j * 128:(j + 1) * 128], in_=xn[:, kt * 128:(kt + 1) * 128], identity=C.identb[:]),
                     reads=[("xn", b), "identb"], writes=[("ps", bk)])
            for j in range(4):
                kt = kg * 4 + j
                if j % 2 == 0:
                    P.op(ACT, lambda e, j=j, kt=kt, psb=psb, hT=hT, r=r: e.activation(out=hT[:, kt, :], in_=psb[:, j * 128:(j + 1) * 128], func=AF.Identity,
                                                                              bias=shT[:, r, kt:kt + 1], scale=scT[:, r, kt:kt + 1]),
                         reads=["scT", "shT"], writes=[("ps", bk), ("hT", b, kt)])
                else:
                    P.op(DVE, lambda e, j=j, kt=kt, psb=psb, hT=hT, r=r: e.tensor_scalar(out=hT[:, kt, :], in0=psb[:, j * 128:(j + 1) * 128],
                                                                                 scalar1=scT[:, r, kt:kt + 1], scalar2=shT[:, r, kt:kt + 1], op0=ALU.mult, op1=ALU.add),
                         reads=["scT", "shT"], writes=[("ps", bk), ("hT", b, kt)])
        for nb in range(8):
            n0 = nb * 512
            w = min(512, NIN - n0)
            bk = C.bank()
            for kt in range(16):
                P.op(PE, lambda e, kt=kt, bk=bk, n0=n0, w=w, hT=hT: e.matmul(C.ps[bk][:, 0:w], lhsT=hT[:, kt, :], rhs=wbf[:, kt, n0:n0 + w],
                                                                          start=(kt == 0), stop=(kt == 15)),
                     reads=[("hT", b), ("wbf", kt)], writes=[("ps", bk)])
            copy_op(P, alt(nb), ot[:, n0:n0 + w], C.ps[bk][:, 0:w], reads=[], writes=[("ps", bk), ("ot", b, nb)])
        P.dma(SP, lambda e, i=i, ot=ot: e.dma_start(out=PROJ[i * 128:(i + 1) * 128, :], in_=ot[:]), reads=[("ot", b)], writes=[("PROJ", i)])
    P.barrier()
    P.sb_reset(m0)


def phase_attn(P, C, l, PROJ, MIXT, sink, cst):
    m0 = P.sb_mark()
    kT = P.sb([128, 2, NTOK], BF16, "kT")
    vbf = P.sb([128, NT, 256], BF16, "vbf")
    onesb = P.sb([128, 128], BF16, "onesb")
    P.op(POOL, lambda e: e.memset(onesb[:], 1.0), writes=["onesb"])
    mk = []
    for nm in ("mprev", "mnext"):
        tf = P.sb([128, 512], F32, nm + "f")
        tb = P.sb([128, 512], BF16, nm + "b")
        P.dma(SP, lambda e, tf=tf, nm=nm: e.dma_start(out=tf[:], in_=cst[nm][:, :]), writes=[nm + "f"])
        P.op(DVE, lambda e, tf=tf, tb=tb: e.tensor_copy(out=tb[:], in_=tf[:]), reads=[nm + "f"], writes=[nm + "b"])
        mk.append(tb)
    esb = P.sb([128, 8], F32, "esb")
    P.dma(SP, lambda e: e.dma_start(out=esb[:], in_=sink[l, :].partition_broadcast(128)), writes=["esb"])
    P.op(ACT, lambda e: e.activation(out=esb[:], in_=esb[:], func=AF.Exp), reads=["esb"], writes=["esb"])
    cos = [P.sb([128, 128], F32, "cos") for _ in range(2)]
    sin = [P.sb([128, 128], F32, "sin") for _ in range(2)]

    def load_rope(i, b):
        P.dma(SP, lambda e: e.dma_start(out=cos[b][:], in_=cst["cos"][i - 2, :, :]), writes=[("cos", b)])
        P.dma(SP, lambda e: e.dma_start(out=sin[b][:], in_=cst["sin"][i - 2, :, :]), writes=[("sin", b)])

    def rope(x, xo, t, H, b, kx, ko, kt_):
        xv = x[:, 0:H * 128].rearrange("p (h a c d) -> p h a c d", h=H, a=2, c=2)
        ov = xo[:, 0:H * 128].rearrange("p (h a c d) -> p h a c d", h=H, a=2, c=2)
        tv = t[:, 0:H * 128].rearrange("p (h a c d) -> p h a c d", h=H, a=2, c=2)
        cv = cos[b][:].rearrange("p (a c d) -> p a c d", a=2, c=2)
        sv = sin[b][:].rearrange("p (a c d) -> p a c d", a=2, c=2)
        for h in range(H):
            P.op(POOL, lambda e, h=h: e.tensor_tensor(out=ov[:, h], in0=xv[:, h], in1=cv, op=ALU.mult),
                 reads=[kx, ("cos", b)], writes=[ko + (h,)])
            for c in range(2):
                P.op(DVE, lambda e, h=h, c=c: e.tensor_tensor(out=tv[:, h, :, c, :], in0=xv[:, h, :, 1 - c, :], in1=sv[:, :, c, :], op=ALU.mult),
                     reads=[kx, ("sin", b)], writes=[kt_ + (h, c)])
            P.op(DVE, lambda e, h=h: e.tensor_tensor(out=ov[:, h], in0=ov[:, h], in1=tv[:, h], op=ALU.add),
                 reads=[kt_ + (h,)], writes=[ko + (h,)])

    kin = [P.sb([128, 512], F32, "kin") for _ in range(2)]
    kro = [P.sb([128, 256], F32, "kro") for _ in range(2)]
    ktm = [P.sb([128, 256], F32, "ktm") for _ in range(2)]
    for i in range(NT):
        b = i % 2
        P.dma(SP, lambda e, i=i, b=b: e.dma_start(out=kin[b][:], in_=PROJ[i * 128:(i + 1) * 128, 1024:1536]),
              reads=[("PROJ", i)], writes=[("kin", b)])
        P.op(ACT, lambda e, i=i, b=b: e.copy(out=vbf[:, i, :], in_=kin[b][:, 256:512]), reads=[("kin", b)], writes=[("vbf", i)])
        if i >= 2:
            load_rope(i, b)
            rope(kin[b], kro[b], ktm[b], 2, b, ("kin", b), ("kro", b), ("ktm", b))
            src, skey = kro[b], ("kro", b)
        else:
            src, skey = kin[b], ("kin", b)
        bk = C.bank()
        for h in range(2):
            P.op(PE, lambda e, h=h, bk=bk, src=src: e.transpose(out=C.ps[bk][:, h * 128:(h + 1) * 128], in_=src[:, h * 128:(h + 1) * 128], identity=C.ident[:]),
                 reads=[skey, "ident"], writes=[("ps", bk)])
        P.op(DVE, lambda e, i=i, bk=bk: e.tensor_copy(out=kT[:, :, i * 128:(i + 1) * 128], in_=C.ps[bk][:, 0:256].rearrange("p (h t) -> p h t", h=2)),
             writes=[("ps", bk), ("kT", i)])
    qin = [P.sb([128, 1024], F32, "qin") for _ in range(2)]
    qro = [P.sb([128, 1024], F32, "qro") for _ in range(2)]
    qtm = [P.sb([128, 1024], F32, "qtm") for _ in range(2)]
    qT = [P.sb([128, 8, 128], BF16, "qT") for _ in range(2)]
    pT = [P.sb([128, 512], BF16, "pT") for _ in range(4)]
    den = [P.sb([128, 512], F32, "den") for _ in range(2)]
    oT = [P.sb([128, 4, 128], BF16, "oT") for _ in range(2)]
    npt = 0
    scale = 128.0 ** -0.5
    for iq in range(NT):
        b = iq % 2
        P.dma(SP, lambda e, iq=iq, b=b: e.dma_start(out=qin[b][:], in_=PROJ[iq * 128:(iq + 1) * 128, 0:1024]),
              reads=[("PROJ", iq)], writes=[("qin", b)])
        if iq >= 2:
            load_rope(iq, b)
            rope(qin[b], qro[b], qtm[b], 8, b, ("qin", b), ("qro", b), ("qtm", b))
            src, skey = qro[b], ("qro", b)
        else:
            src, skey = qin[b], ("qin", b)
        for g in range(2):
            bk = C.bank()
            for h in range(4):
                hh = g * 4 + h
                P.op(PE, lambda e, h=h, hh=hh, bk=bk, src=src: e.transpose(out=C.ps[bk][:, h * 128:(h + 1) * 128], in_=src[:, hh * 128:(hh + 1) * 128], identity=C.ident[:]),
                     reads=[skey, "ident"], writes=[("ps", bk)])
            copy_op(P, alt(g), qT[b][:, g * 4:(g + 1) * 4, :], C.ps[bk][:, :].rearrange("p (h t) -> p h t", h=4), reads=[], writes=[("ps", bk), ("qT", b, g)])
        if iq < 2:
            keys = [(0, None), (1, None)]
        else:
            keys = [(0, None), (1, None)]
            if iq - 1 >= 2:
                keys.append((iq - 1, 0))
            keys.append((iq, None))
            if iq + 1 < NT:
                keys.append((iq + 1, 1))
        for kvh in range(2):
            bo = C.bank()
            bd = C.bank()
            for n, (kt_, mi) in enumerate(keys):
                bs = C.bank()
                pb = npt % 4
                npt += 1
                P.op(PE, lambda e, kt_=kt_, bs=bs, kvh=kvh, b=b: e.matmul(C.ps[bs][:, :], lhsT=kT[:, kvh, kt_ * 128:(kt_ + 1) * 128],
                                                                        rhs=qT[b][:, kvh * 4:(kvh + 1) * 4, :].rearrange("p h t -> p (h t)"), start=True, stop=True),
                     reads=[("kT", kt_), ("qT", b, kvh)], writes=[("ps", bs)])
                P.op(ACT, lambda e, bs=bs, pb=pb: e.activation(out=pT[pb][:], in_=C.ps[bs][:, :], func=AF.Exp, scale=scale),
                     writes=[("ps", bs), ("pT", pb)])
                if mi is not None:
                    P.op(POOL, lambda e, pb=pb, mi=mi: e.tensor_tensor(out=pT[pb][:], in0=pT[pb][:], in1=mk[mi][:], op=ALU.mult),
                         reads=["mprevb", "mnextb"], writes=[("pT", pb)])
                st_, sp_ = (n == 0), (n == len(keys) - 1)
                P.op(PE, lambda e, kt_=kt_, bo=bo, kvh=kvh, pb=pb, st_=st_, sp_=sp_: e.matmul(C.ps[bo][:, :], lhsT=vbf[:, kt_, kvh * 128:(kvh + 1) * 128], rhs=pT[pb][:],
                                                                              start=st_, stop=sp_),
                     reads=[("vbf", kt_), ("pT", pb)], writes=[("ps", bo)])
                P.op(PE, lambda e, bd=bd, pb=pb, st_=st_, sp_=sp_: e.matmul(C.ps[bd][:, :], lhsT=onesb[:], rhs=pT[pb][:], start=st_, stop=sp_),
                     reads=["onesb", ("pT", pb)], writes=[("ps", bd)])
            db = kvh
            for h in range(4):
                hh = kvh * 4 + h
                P.op(DVE, lambda e, h=h, hh=hh, bd=bd, db=db: e.tensor_scalar_add(out=den[db][:, h * 128:(h + 1) * 128], in0=C.ps[bd][:, h * 128:(h + 1) * 128], scalar1=esb[:, hh:hh + 1]),
                     reads=["esb"], writes=[("ps", bd), ("den", db, h)])
            P.op(DVE, lambda e, db=db: e.reciprocal(out=den[db][:], in_=den[db][:]), reads=[("den", db)], writes=[("den", db)])
            P.op(DVE, lambda e, db=db, bo=bo: e.tensor_tensor(out=oT[db][:].rearrange("p h t -> p (h t)"), in0=C.ps[bo][:, :], in1=den[db][:], op=ALU.mult),
                 reads=[("den", db)], writes=[("ps", bo), ("oT", db)])
            P.dma(SP, lambda e, db=db, kvh=kvh, iq=iq: e.dma_start(out=MIXT[kvh * 4:(kvh + 1) * 4, :, iq * 128:(iq + 1) * 128].rearrange("c p t -> p c t"), in_=oT[db][:]),
                  reads=[("oT", db)], writes=[("MIXT", "att", iq, kvh)])
    P.barrier()
    P.sb_reset(m0)


import math
NCH = 544
CB = 272
TWO_PI = 2.0 * math.pi


def bc(ap, axis, shape):
    return ap.unsqueeze(axis).to_broadcast(shape)


def phase_s5(P, C, l, PROJ, MIXT, prm, cst):
    m0 = P.sb_mark()
    nsb = [0]

    def T(shape, dt=F32, nm="s5"):
        nsb[0] += 1
        return P.sb(shape, dt, nm), "%s%d" % (nm, nsb[0])

    SH = [128, 2, 16, 48]
    PWR, kPWR = T(SH); PWI, kPWI = T(SH)
    NK = 9
    AR, kAR = T([128, 2, 16, NK]); AI, kAI = T([128, 2, 16, NK]); NAI, kNAI = T([128, 2, 16, NK])
    FR, kFR = T([128, 2, 16]); FI, kFI = T([128, 2, 16])
    SB4 = [128, 2, 16, 16]
    BBR, kBBR = T(SB4); BBI, kBBI = T(SB4)
    CR, kCR = T(SB4); CI, kCI = T(SB4)
    mscr = P.sb_mark()
    LR, kLR = T([128, 2, 16]); LI, kLI = T([128, 2, 16]); DTt, kDT = T([128, 2, 16])
    BR, kBR = T([128, 2, 16, 16]); BI, kBI = T([128, 2, 16, 16])
    CRr, kCRr = T([16, 2, 16, 128]); CIr, kCIr = T([16, 2, 16, 128])
    erow, kerow = T([128, 48])
    P.D(SP, [], [kerow], out=erow[:], in_=cst["erow"][:, :])
    for d in range(2):
        for g2 in range(2):
            ps_ = slice(g2 * 64, (g2 + 1) * 64)
            P.D(SP, [], [(kLR, d, g2)], out=LR[ps_, d, :], in_=prm["lam_re"][l, d].rearrange("(gp g2) p -> g2 p gp", g2=2)[g2], allow_slow_non_contiguous=True)
            P.D(SP, [], [(kLI, d, g2)], out=LI[ps_, d, :], in_=prm["lam_im"][l, d].rearrange("(gp g2) p -> g2 p gp", g2=2)[g2], allow_slow_non_contiguous=True)
            P.D(SP, [], [(kDT, d, g2)], out=DTt[ps_, d, :], in_=prm["log_dt"][l, d].rearrange("(gp g2) -> g2 gp", g2=2)[g2].partition_broadcast(64), allow_slow_non_contiguous=True)
            P.D(SP, [], [(kBR, d, g2)], out=BR[ps_, d, :, :], in_=prm["b_re"][l, d].rearrange("(gp g2) p j -> g2 p gp j", g2=2)[g2])
            P.D(SP, [], [(kBI, d, g2)], out=BI[ps_, d, :, :], in_=prm["b_im"][l, d].rearrange("(gp g2) p j -> g2 p gp j", g2=2)[g2])
        for g2 in range(2):
            P.D(SP, [], [(kCRr, d, g2)], out=CRr[:, d, :, g2 * 64:(g2 + 1) * 64], in_=prm["c_re"][l, d].rearrange("(gp g2) i p -> g2 i gp p", g2=2)[g2])
            P.D(SP, [], [(kCIr, d, g2)], out=CIr[:, d, :, g2 * 64:(g2 + 1) * 64], in_=prm["c_im"][l, d].rearrange("(gp g2) i p -> g2 i gp p", g2=2)[g2])
    P.I(ACT, "activation", [kDT], [kDT], out=DTt[:], in_=DTt[:], func=AF.Exp)
    LRD, kLRD = T([128, 2, 16]); TH, kTH = T([128, 2, 16])
    P.I(DVE, "tensor_tensor", [kLR, kDT], [kLRD], out=LRD[:], in0=LR[:], in1=DTt[:], op=ALU.mult)
    P.I(DVE, "tensor_tensor", [kLI, kDT], [kTH], out=TH[:], in0=LI[:], in1=DTt[:], op=ALU.mult)
    SH = [128, 2, 16, 48]
    ANG, kANG = T(SH); MAG, kMAG = T(SH); TMP, kTMP = T(SH)
    eb_ = erow[:].unsqueeze(1).unsqueeze(1).to_broadcast(SH)
    P.I(DVE, "tensor_tensor", [kTH, kerow], [kANG], out=ANG[:], in0=bc(TH[:], 3, SH), in1=eb_, op=ALU.mult)
    P.I(DVE, "tensor_tensor", [kLRD, kerow], [kMAG], out=MAG[:], in0=bc(LRD[:], 3, SH), in1=eb_, op=ALU.mult)
    P.I(ACT, "activation", [kMAG], [kMAG], out=MAG[:], in_=MAG[:], func=AF.Exp)
    KI, kKI = T(SH, I32); KF, kKF = T(SH); MK, kMK = T(SH)

    def sin_rr(OUT, kOUT, off):
        P.I(DVE, "tensor_scalar_add", [kANG], [kTMP], out=TMP[:], in0=ANG[:], scalar1=off)
        P.I(DVE, "tensor_scalar_mul", [kTMP], [kKF], out=KF[:], in0=TMP[:], scalar1=1.0 / TWO_PI)
        P.I(DVE, "tensor_copy", [kKF], [kKI], out=KI[:], in_=KF[:])
        P.I(DVE, "tensor_copy", [kKI], [kKF], out=KF[:], in_=KI[:])
        P.I(DVE, "scalar_tensor_tensor", [kKF, kTMP], [kTMP], out=TMP[:], in0=KF[:], scalar=-TWO_PI, in1=TMP[:], op0=ALU.mult, op1=ALU.add)
        P.I(DVE, "tensor_single_scalar", [kTMP], [kMK], out=MK[:], in_=TMP[:], scalar=math.pi, op=ALU.is_gt)
        P.I(DVE, "scalar_tensor_tensor", [kMK, kTMP], [kTMP], out=TMP[:], in0=MK[:], scalar=-TWO_PI, in1=TMP[:], op0=ALU.mult, op1=ALU.add)
        P.I(ACT, "activation", [kTMP], [kOUT], out=OUT[:], in_=TMP[:], func=AF.Sin)
    sin_rr(PWI, kPWI, TWO_PI * 32)
    sin_rr(PWR, kPWR, TWO_PI * 32 + math.pi / 2)
    P.I(DVE, "tensor_tensor", [kPWR, kMAG], [kPWR], out=PWR[:], in0=PWR[:], in1=MAG[:], op=ALU.mult)
    P.I(DVE, "tensor_tensor", [kPWI, kMAG], [kPWI], out=PWI[:], in0=PWI[:], in1=MAG[:], op=ALU.mult)
    t1, kt1 = T([128, 2, 16]); t2, kt2 = T([128, 2, 16])
    P.I(DVE, "tensor_copy", [kPWR], [(kAR, 0)], out=AR[:, :, :, 0], in_=PWR[:, :, :, 23])
    P.I(DVE, "tensor_copy", [kPWI], [(kAI, 0)], out=AI[:, :, :, 0], in_=PWI[:, :, :, 23])
    for k in range(NK - 1):
        P.I(DVE, "tensor_tensor", [(kAR, k)], [kt1], out=t1[:], in0=AR[:, :, :, k], in1=AR[:, :, :, k], op=ALU.mult)
        P.I(DVE, "tensor_tensor", [(kAI, k)], [kt2], out=t2[:], in0=AI[:, :, :, k], in1=AI[:, :, :, k], op=ALU.mult)
        P.I(DVE, "tensor_tensor", [kt1, kt2], [(kAR, k + 1)], out=AR[:, :, :, k + 1], in0=t1[:], in1=t2[:], op=ALU.subtract)
        P.I(DVE, "scalar_tensor_tensor", [(kAR, k), (kAI, k)], [(kAI, k + 1)], out=AI[:, :, :, k + 1], in0=AR[:, :, :, k], scalar=2.0, in1=AI[:, :, :, k], op0=ALU.mult, op1=ALU.mult)
    P.I(DVE, "tensor_scalar_mul", [kAI], [kNAI], out=NAI[:], in0=AI[:], scalar1=-1.0)
    NR, kNR = T([128, 2, 16]); DEN, kDEN = T([128, 2, 16])
    P.I(DVE, "tensor_scalar_add", [kPWR], [kNR], out=NR[:], in0=PWR[:, :, :, 16], scalar1=-1.0)
    P.I(DVE, "tensor_tensor", [kLR], [kDEN], out=DEN[:], in0=LR[:], in1=LR[:], op=ALU.mult)
    P.I(DVE, "tensor_tensor", [kLI], [kt1], out=t1[:], in0=LI[:], in1=LI[:], op=ALU.mult)
    P.I(DVE, "tensor_tensor", [kDEN, kt1], [kDEN], out=DEN[:], in0=DEN[:], in1=t1[:], op=ALU.add)
    P.I(DVE, "reciprocal", [kDEN], [kDEN], out=DEN[:], in_=DEN[:])
    P.I(DVE, "tensor_tensor", [kNR, kLR], [kFR], out=FR[:], in0=NR[:], in1=LR[:], op=ALU.mult)
    P.I(DVE, "tensor_tensor", [kPWI, kLI], [kt1], out=t1[:], in0=PWI[:, :, :, 16], in1=LI[:], op=ALU.mult)
    P.I(DVE, "tensor_tensor", [kFR, kt1], [kFR], out=FR[:], in0=FR[:], in1=t1[:], op=ALU.add)
    P.I(DVE, "tensor_tensor", [kFR, kDEN], [kFR], out=FR[:], in0=FR[:], in1=DEN[:], op=ALU.mult)
    P.I(DVE, "tensor_tensor", [kPWI, kLR], [kFI], out=FI[:], in0=PWI[:, :, :, 16], in1=LR[:], op=ALU.mult)
    P.I(DVE, "tensor_tensor", [kNR, kLI], [kt1], out=t1[:], in0=NR[:], in1=LI[:], op=ALU.mult)
    P.I(DVE, "tensor_tensor", [kFI, kt1], [kFI], out=FI[:], in0=FI[:], in1=t1[:], op=ALU.subtract)
    P.I(DVE, "tensor_tensor", [kFI, kDEN], [kFI], out=FI[:], in0=FI[:], in1=DEN[:], op=ALU.mult)
    T4, kT4 = T(SB4)
    P.I(DVE, "tensor_tensor", [kFR, kBR], [kBBR], out=BBR[:], in0=bc(FR[:], 3, SB4), in1=BR[:], op=ALU.mult)
    P.I(DVE, "tensor_tensor", [kFI, kBI], [kT4], out=T4[:], in0=bc(FI[:], 3, SB4), in1=BI[:], op=ALU.mult)
    P.I(DVE, "tensor_tensor", [kBBR, kT4], [kBBR], out=BBR[:], in0=BBR[:], in1=T4[:], op=ALU.subtract)
    P.I(DVE, "tensor_tensor", [kFR, kBI], [kBBI], out=BBI[:], in0=bc(FR[:], 3, SB4), in1=BI[:], op=ALU.mult)
    P.I(DVE, "tensor_tensor", [kFI, kBR], [kT4], out=T4[:], in0=bc(FI[:], 3, SB4), in1=BR[:], op=ALU.mult)
    P.I(DVE, "tensor_tensor", [kBBI, kT4], [kBBI], out=BBI[:], in0=BBI[:], in1=T4[:], op=ALU.add)
    for (src, ksrc, dst, kdst) in ((CRr, kCRr, CR, kCR), (CIr, kCIr, CI, kCI)):
        for d in range(2):
            bk = C.bank()
            for gp in range(16):
                P.I(PE, "transpose", [ksrc, "ident"], [("ps", bk)], out=C.ps[bk][:, gp * 16:(gp + 1) * 16], in_=src[:, d, gp, :], identity=C.ident[0:16, 0:16])
            P.I(DVE, "tensor_copy", [], [("ps", bk), (kdst, d)], out=dst[:, d, :, :], in_=C.ps[bk][:, 0:256].rearrange("p (g i) -> p g i", g=16))
    P.barrier()
    P.sb_reset(mscr)
    E, kE = T([128, 8, 240]); Eb, kEb = T([128, 8, 240], BF16)
    P.D(SP, [], [kE], out=E[:], in_=cst["E"].rearrange("r p c -> p r c"))
    P.I(DVE, "tensor_copy", [kE], [kEb], out=Eb[:], in_=E[:])
    MF, kMF = T([128, 128]); MB, kMB = T([128, 128])
    P.D(SP, [], [kMF], out=MF[:], in_=cst["toemf"][:, :])
    P.D(SP, [], [kMB], out=MB[:], in_=cst["toemb"][:, :])
    Dall, kDall = T([128, 32])
    for t in range(8):
        P.D(SP, [], [(kDall, t)], out=Dall[t * 16:(t + 1) * 16, :], in_=prm["d"][l].rearrange("(g i) -> i g", i=16), allow_slow_non_contiguous=True)
    zT, kzT = T([128, 4, NTOK], BF16)
    zTv = zT[:].rearrange("p a (c s) -> p a c s", s=8)
    SG = [128, 4, 8, 16]
    tabs = {}
    for nm in ("WTR", "WTI", "XR", "XI", "VR", "VI"):
        for d in range(2):
            tabs[(nm, d)] = T(SG)
    TG, kTG = T(SG)
    suin, ksuin = T([128, NT, 128])
    suT, ksuT = T([128, NTOK])
    suTv = suT[:].rearrange("p (c s) -> p c s", s=8)
    U8, kU8 = T([128, 8, NCH])
    Z8, kZ8 = T([128, 8, NCH], BF16)
    Wt, kWt = T([128, 4, 128])
    Toe, kToe = T([128, 2, 128])
    H = {}
    for d in range(2):
        for pp in range(2):
            for c_ in range(2):
                H[(d, pp, c_)] = T([128, NCH])
    xg, kxg = T([128, CB]); ug, kug = T([128, CB]); sg, ksg = T([128, CB])
    for ct in range(4):
        gsl = slice(ct * 4, ct * 4 + 4)
        for d in range(2):
            ea, eb2, ec = (0, 8, 16) if d == 0 else (24, 32, 40)
            def pw(tile_, e0):
                return tile_[:, d, gsl, e0:e0 + 8].unsqueeze(3).to_broadcast(SG)
            def bb(tile_):
                return tile_[:, d, gsl, :].unsqueeze(2).to_broadcast(SG)
            (WTR, kWTR), (WTI, kWTI) = tabs[("WTR", d)], tabs[("WTI", d)]
            (XR, kXR), (XI, kXI) = tabs[("XR", d)], tabs[("XI", d)]
            (VR, kVR), (VI, kVI) = tabs[("VR", d)], tabs[("VI", d)]
            P.I(DVE, "tensor_tensor", [kPWR, kBBR], [kWTR], out=WTR[:], in0=pw(PWR, ea), in1=bb(BBR), op=ALU.mult)
            P.I(DVE, "tensor_tensor", [kPWI, kBBI], [kTG], out=TG[:], in0=pw(PWI, ea), in1=bb(BBI), op=ALU.mult)
            P.I(DVE, "tensor_tensor", [kWTR, kTG], [kWTR], out=WTR[:], in0=WTR[:], in1=TG[:], op=ALU.subtract)
            P.I(DVE, "tensor_tensor", [kPWR, kBBI], [kWTI], out=WTI[:], in0=pw(PWR, ea), in1=bb(BBI), op=ALU.mult)
            P.I(DVE, "tensor_tensor", [kPWI, kBBR], [kTG], out=TG[:], in0=pw(PWI, ea), in1=bb(BBR), op=ALU.mult)
            P.I(DVE, "tensor_tensor", [kWTI, kTG], [kWTI], out=WTI[:], in0=WTI[:], in1=TG[:], op=ALU.add)
            for (RR, kRR, II, kII, e0) in ((XR, kXR, XI, kXI, eb2), (VR, kVR, VI, kVI, ec)):
                P.I(DVE, "tensor_tensor", [kPWR, kCR], [kRR], out=RR[:], in0=pw(PWR, e0), in1=bb(CR), op=ALU.mult)
                P.I(DVE, "tensor_tensor", [kPWI, kCI], [kTG], out=TG[:], in0=pw(PWI, e0), in1=bb(CI), op=ALU.mult)
                P.I(DVE, "tensor_tensor", [kRR, kTG], [kRR], out=RR[:], in0=RR[:], in1=TG[:], op=ALU.subtract)
                P.I(DVE, "tensor_tensor", [kPWI, kCR], [kII], out=II[:], in0=pw(PWI, e0), in1=bb(CR), op=ALU.mult)
                P.I(DVE, "tensor_tensor", [kPWR, kCI], [kTG], out=TG[:], in0=pw(PWR, e0), in1=bb(CI), op=ALU.mult)
                P.I(DVE, "scalar_tensor_tensor", [kII, kTG], [kII], out=II[:], in0=II[:], scalar=-1.0, in1=TG[:], op0=ALU.mult, op1=ALU.subtract)
        P.D(SP, [("PROJ",)], [ksuin], out=suin[:], in_=PROJ[:, 1536 + ct * 128:1536 + (ct + 1) * 128].rearrange("(i p) c -> p i c", p=128))
        for i4 in range(0, NT, 4):
            n = min(4, NT - i4)
            bk = C.bank()
            for j in range(n):
                P.I(PE, "transpose", [ksuin, "ident"], [("ps", bk)], out=C.ps[bk][:, j * 128:(j + 1) * 128], in_=suin[:, i4 + j, :], identity=C.ident[:])
            copy_op(P, alt(i4 // 4), suT[:, i4 * 128:(i4 + n) * 128], C.ps[bk][:, 0:n * 128], [], [("ps", bk), (ksuT, i4)])
        for g8 in range(8):
            for cb in range(2):
                bk = C.bank()
                for s in range(8):
                    P.I(PE, "matmul", [kE, ksuT], [("ps", bk)], C.ps[bk][:, 0:CB], lhsT=E[:, g8, (7 - s) * 16:(7 - s) * 16 + 128],
                        rhs=suTv[:, cb * CB:(cb + 1) * CB, s], start=(s == 0), stop=(s == 7))
                copy_op(P, alt(cb), U8[:, g8, cb * CB:(cb + 1) * CB], C.ps[bk][:, 0:CB], [], [("ps", bk), (kU8, g8, cb)])
        for gpl in range(4):
            gp = ct * 4 + gpl
            bk = C.bank()
            for d in range(2):
                for c_, nm in enumerate(("WTR", "WTI")):
                    tt, ktt = tabs[(nm, d)]
                    j = d * 2 + c_
                    P.I(PE, "transpose", [ktt, "ident"], [("ps", bk)], out=C.ps[bk][:, j * 128:(j + 1) * 128], in_=tt[:, gpl, :, :].rearrange("p s j -> p (s j)"), identity=C.ident[:])
            P.I(DVE, "tensor_copy", [], [("ps", bk), kWt], out=Wt[:], in_=C.ps[bk][:, :].rearrange("p (a b) -> p a b", a=4))
            for g2 in range(2):
                rs_ = slice(g2 * 64, (g2 + 1) * 64)
                bks = []
                for d in range(2):
                    bk = C.bank()
                    bks.append(bk)
                    (WTR, kWTR), (WTI, kWTI) = tabs[("WTR", d)], tabs[("WTI", d)]
                    (XR, kXR), (XI, kXI) = tabs[("XR", d)], tabs[("XI", d)]
                    P.I(PE, "matmul", [kWTR, kXR], [("ps", bk)], C.ps[bk][:, 0:128], lhsT=WTR[rs_, gpl, :, :].rearrange("p s j -> p (s j)"),
                        rhs=XR[rs_, gpl, :, :].rearrange("p s j -> p (s j)"), start=True, stop=False)
                    P.I(PE, "matmul", [kWTI, kXI], [("ps", bk)], C.ps[bk][:, 0:128], lhsT=WTI[rs_, gpl, :, :].rearrange("p s j -> p (s j)"),
                        rhs=XI[rs_, gpl, :, :].rearrange("p s j -> p (s j)"), start=False, stop=True)
                P.I(DVE, "tensor_tensor", [kMF], [("ps", bks[0]), (kToe, g2)], out=Toe[:, g2, :], in0=C.ps[bks[0]][:, 0:128], in1=MF[:], op=ALU.mult)
                P.I(DVE, "tensor_tensor", [kMB], [("ps", bks[1]), kTG], out=TG[:, 0, :, :].rearrange("p s j -> p (s j)"), in0=C.ps[bks[1]][:, 0:128], in1=MB[:], op=ALU.mult)
                P.I(DVE, "tensor_tensor", [kTG], [(kToe, g2)], out=Toe[:, g2, :], in0=Toe[:, g2, :], in1=TG[:, 0, :, :].rearrange("p s j -> p (s j)"), op=ALU.add)
            for d in range(2):
                eng = DVE
                for c_ in range(2):
                    Ht, kHt = H[(d, 0, c_)]
                    for cb in range(2):
                        bk = C.bank()
                        for g2 in range(2):
                            rs_ = slice(g2 * 64, (g2 + 1) * 64)
                            P.I(PE, "matmul", [kWt, kU8], [("ps", bk)], C.ps[bk][rs_, 0:CB], lhsT=Wt[:, d * 2 + c_, rs_], rhs=U8[:, gpl * 2 + g2, cb * CB:(cb + 1) * CB], start=True, stop=True)
                        copy_op(P, ACT, Ht[:, cb * CB:(cb + 1) * CB], C.ps[bk][:, 0:CB], [], [("ps", bk), (kHt, cb)])
                cur = 0
                def arK(k):
                    return AR[:, d, gp, k:k + 1], AI[:, d, gp, k:k + 1], NAI[:, d, gp, k:k + 1]
                def scan(lo, hi, cur):
                    n = hi - lo
                    k = 0
                    sh = 1
                    while sh < n:
                        (A_, kA), (B_, kB) = H[(d, cur, 0)], H[(d, cur, 1)]
                        (An, kAn), (Bn, kBn) = H[(d, 1 - cur, 0)], H[(d, 1 - cur, 1)]
                        ar, ai, nai = arK(k)
                        if d == 0:
                            dst, src, keep = slice(lo + sh, hi), slice(lo, hi - sh), slice(lo, lo + sh)
                        else:
                            dst, src, keep = slice(lo, hi - sh), slice(lo + sh, hi), slice(hi - sh, hi)
                        P.I(eng, "scalar_tensor_tensor", [kA, kAR], [kAn], out=An[:, dst], in0=A_[:, src], scalar=ar, in1=A_[:, dst], op0=ALU.mult, op1=ALU.add)
                        P.I(eng, "scalar_tensor_tensor", [kB, kNAI, kAn], [kAn], out=An[:, dst], in0=B_[:, src], scalar=nai, in1=An[:, dst], op0=ALU.mult, op1=ALU.add)
                        P.I(eng, "scalar_tensor_tensor", [kB, kAR], [kBn], out=Bn[:, dst], in0=B_[:, src], scalar=ar, in1=B_[:, dst], op0=ALU.mult, op1=ALU.add)
                        P.I(eng, "scalar_tensor_tensor", [kA, kAI, kBn], [kBn], out=Bn[:, dst], in0=A_[:, src], scalar=ai, in1=Bn[:, dst], op0=ALU.mult, op1=ALU.add)
                        P.I(eng, "tensor_copy", [kA], [kAn], out=An[:, keep], in_=A_[:, keep])
                        P.I(eng, "tensor_copy", [kB], [kBn], out=Bn[:, keep], in_=B_[:, keep])
                        cur = 1 - cur
                        sh *= 2
                        k += 1
                    return cur
                cur = scan(0, 32, 0)
                (A_, kA), (B_, kB) = H[(d, cur, 0)], H[(d, cur, 1)]
                if cur != 0:
                    (A0, kA0), (B0, kB0) = H[(d, 0, 0)], H[(d, 0, 1)]
                    P.I(eng, "tensor_copy", [kA0], [kA], out=A_[:, 32:NCH], in_=A0[:, 32:NCH])
                    P.I(eng, "tensor_copy", [kB0], [kB], out=B_[:, 32:NCH], in_=B0[:, 32:NCH])
                ar, ai, nai = arK(0)
                if d == 0:
                    inj, frm = slice(32, 33), slice(31, 32)
                else:
                    inj, frm = slice(NCH - 1, NCH), slice(0, 1)
                P.I(eng, "scalar_tensor_tensor", [kA, kAR], [kA], out=A_[:, inj], in0=A_[:, frm], scalar=ar, in1=A_[:, inj], op0=ALU.mult, op1=ALU.add)
                P.I(eng, "scalar_tensor_tensor", [kB, kNAI, kA], [kA], out=A_[:, inj], in0=B_[:, frm], scalar=nai, in1=A_[:, inj], op0=ALU.mult, op1=ALU.add)
                P.I(eng, "scalar_tensor_tensor", [kB, kAR], [kB], out=B_[:, inj], in0=B_[:, frm], scalar=ar, in1=B_[:, inj], op0=ALU.mult, op1=ALU.add)
                P.I(eng, "scalar_tensor_tensor", [kA, kAI, kB], [kB], out=B_[:, inj], in0=A_[:, frm], scalar=ai, in1=B_[:, inj], op0=ALU.mult, op1=ALU.add)
                cur0 = cur
                cur = scan(32, NCH, cur)
                if cur != cur0:
                    (An, kAn), (Bn, kBn) = H[(d, cur, 0)], H[(d, cur, 1)]
                    P.I(eng, "tensor_copy", [kA], [kAn], out=An[:, 0:32], in_=A_[:, 0:32])
                    P.I(eng, "tensor_copy", [kB], [kBn], out=Bn[:, 0:32], in_=B_[:, 0:32])
                H[("fin", d)] = cur
            for g2 in range(2):
                g8 = gpl * 2 + g2
                g = gp * 2 + g2
                rs_ = slice(g2 * 64, (g2 + 1) * 64)
                for cb in range(2):
                    c0 = cb * CB
                    bk = C.bank()
                    mms = [(Toe[:, g2, :], U8[:, g8, c0:c0 + CB], 0, CB, [kToe, kU8])]
                    cf = H[("fin", 0)]
                    lo = max(c0, 1)
                    for c_, nm in enumerate(("VR", "VI")):
                        tt, ktt = tabs[(nm, 0)]
                        Hh, kHh = H[(0, cf, c_)]
                        mms.append((tt[rs_, gpl, :, :].rearrange("p s j -> p (s j)"), Hh[rs_, lo - 1:c0 + CB - 1], lo - c0, CB, [ktt, kHh]))
                    cbk = H[("fin", 1)]
                    if cb == 0:
                        segs = [(0, 31, 1), (32, CB, 33)]
                    else:
                        segs = [(CB, NCH - 1, CB + 1), (NCH - 1, NCH, 0)]
                    for c_, nm in enumerate(("VR", "VI")):
                        tt, ktt = tabs[(nm, 1)]
                        Hh, kHh = H[(1, cbk, c_)]
                        for (a0, a1, s0) in segs:
                            mms.append((tt[rs_, gpl, :, :].rearrange("p s j -> p (s j)"), Hh[rs_, s0:s0 + (a1 - a0)], a0 - c0, a1 - c0, [ktt, kHh]))
                    for n, (lt, rh, o0, o1, rd) in enumerate(mms):
                        P.I(PE, "matmul", rd, [("ps", bk)], C.ps[bk][:, o0:o1], lhsT=lt, rhs=rh, start=(n == 0), stop=(n == len(mms) - 1))
                    P.I(DVE, "scalar_tensor_tensor", [kU8, kDall], [("ps", bk), kxg], out=xg[:], in0=U8[:, g8, c0:c0 + CB], scalar=Dall[:, g:g + 1], in1=C.ps[bk][:, 0:CB], op0=ALU.mult, op1=ALU.add)
                    P.I(POOL, "tensor_tensor", [kxg], [kug], out=ug[:], in0=xg[:], in1=xg[:], op=ALU.mult)
                    P.I(POOL, "tensor_scalar", [kug], [kug], out=ug[:], in0=ug[:], scalar1=0.044715, scalar2=1.0, op0=ALU.mult, op1=ALU.add)
                    P.I(POOL, "tensor_tensor", [kug, kxg], [kug], out=ug[:], in0=ug[:], in1=xg[:], op=ALU.mult)
                    P.I(ACT, "activation", [kug], [ksg], out=sg[:], in_=ug[:], func=AF.Sigmoid, scale=2.0 * math.sqrt(2.0 / math.pi))
                    P.I(DVE, "tensor_tensor", [kxg, ksg], [(kZ8, g8, cb)], out=Z8[:, g8, c0:c0 + CB], in0=xg[:], in1=sg[:], op=ALU.mult)
        for s in range(8):
            for cb in range(2):
                bk = C.bank()
                for g8 in range(8):
                    P.I(PE, "matmul", [kEb, kZ8], [("ps", bk)], C.ps[bk][:, 0:CB], lhsT=Eb[:, s, (7 - g8) * 16:(7 - g8) * 16 + 128], rhs=Z8[:, g8, cb * CB:(cb + 1) * CB], start=(g8 == 0), stop=(g8 == 7))
                copy_op(P, alt(s), zTv[:, ct, cb * CB:(cb + 1) * CB, s], C.ps[bk][:, 0:CB], [], [("ps", bk), (kzT, ct, s, cb)])
    wg, kwg = T([128, 4, 512], BF16)
    for kt in range(4):
        P.D(POOL, [], [(kwg, kt)], out=wg[:, kt, :], in_=prm["w_glu"][l, kt * 128:(kt + 1) * 128, :])
    bg, kbg = T([128, 4])
    P.D(SP, [], [kbg], out=bg[:], in_=prm["b_glu"][l].rearrange("(m p) -> p m", p=128), allow_slow_non_contiguous=True)
    gts = [T([128, 512]) for _ in range(2)]
    ots = [T([128, 512], BF16) for _ in range(2)]
    it = 0
    for mt in range(4):
        for t0 in range(0, NTOK, 512):
            w = min(512, NTOK - t0)
            b = it % 2
            it += 1
            (gt_, kgt), (ot_, kot) = gts[b], ots[b]
            bk = C.bank()
            for kt in range(4):
                P.I(PE, "matmul", [kwg, kzT], [("ps", bk)], C.ps[bk][:, 0:w], lhsT=wg[:, kt, mt * 128:(mt + 1) * 128], rhs=zT[:, kt, t0:t0 + w], start=(kt == 0), stop=(kt == 3))
            P.I(ACT, "activation", [kbg], [("ps", bk), kgt], out=gt_[:, 0:w], in_=C.ps[bk][:, 0:w], func=AF.Sigmoid, bias=bg[:, mt:mt + 1], scale=1.0)
            P.I(DVE, "tensor_tensor", [kgt, kzT], [kot], out=ot_[:, 0:w], in0=gt_[:, 0:w], in1=zT[:, mt, t0:t0 + w], op=ALU.mult)
            P.D(SP, [kot], [("MIXT", "ssm", mt, t0)], out=MIXT[8 + mt, :, t0:t0 + w], in_=ot_[:, 0:w])
    P.barrier()
    P.sb_reset(m0)


def phase_gla(P, C, l, PROJ, MIXT, OF, prm, cst):
    m0 = P.sb_mark()
    n_ = [0]

    def T(shape, dt=F32, nm="gl"):
        n_[0] += 1
        return P.sb(shape, dt, nm), "%s%d" % (nm, n_[0])

    def ld(name, shape):
        t, k = T(shape)
        P.D(SP, [], [k], out=t[:], in_=cst[name][:, :])
        return t, k
    TRI = [ld("trif", [128, 128]), ld("trib", [128, 128])]
    BLK, kBLK = ld("blk", [128, 128])
    CIND, kCIND = ld("cind", [128, 2])
    MSK = [ld("gmaskf", [128, 512]), ld("gmaskb", [128, 512])]
    WG = []
    for d in range(2):
        t, k = T([17, 256])
        P.D(SP, [], [(k, 0)], out=t[0:16, :], in_=prm["w_gate"][l, d, :, :])
        P.D(SP, [], [(k, 1)], out=t[16:17, :], in_=prm["b_gate"][l, d:d + 1, :])
        WG.append((t, k))
    NG, kNG = T([128, 128])
    P.D(SP, [], [kNG], out=NG[:], in_=prm["norm_g"][l, :].partition_broadcast(128))
    S, kS = T([64, 4, 128])
    zaug = [T([17, 128]) for _ in range(2)]
    for (t, k) in zaug:
        P.I(POOL, "memset", [], [k], t[:], 1.0)
    NB = 2
    qk = [T([128, 512]) for _ in range(NB)]
    vv = [T([128, 512]) for _ in range(NB)]
    zz = [T([128, 16]) for _ in range(NB)]
    gp_ = [T([128, 256]) for _ in range(NB)]
    bS = [T([128, 256]) for _ in range(NB)]
    eb = [T([128, 256]) for _ in range(NB)]
    enb = [T([128, 256]) for _ in range(NB)]
    ebl = [T([128, 256]) for _ in range(NB)]
    qd = [T([128, 256]) for _ in range(NB)]
    kd = [T([128, 256]) for _ in range(NB)]
    kl = [T([128, 256]) for _ in range(NB)]
    qdT = [T([64, 4, 128]) for _ in range(NB)]
    kdT = [T([64, 4, 128]) for _ in range(NB)]
    ATm = [T([128, 4, 128]) for _ in range(NB)]
    edec = [T([64, 4, 2]) for _ in range(NB)]
    ot = [T([128, 512]) for _ in range(NB)]
    of_ = [T([128, 512]) for _ in range(NB)]
    rr = [T([128, 512]) for _ in range(NB)]
    sq = [T([128, 512]) for _ in range(NB)]
    ssq = [T([128, 4]) for _ in range(NB)]
    oTb = [T([128, 4, 128], BF16) for _ in range(NB)]
    it = 0
    for d in range(2):
        P.I(DVE, "memset", [], [kS], S[:], 0.0)
        order = list(range(NT)) if d == 0 else [1, 0] + list(range(NT - 1, 1, -1))
        corder = (0, 1) if d == 0 else (1, 0)
        (TRId, kTRI), (MK, kMK), (WGd, kWG) = TRI[d], MSK[d], WG[d]
        for i in order:
            b = it % NB
            it += 1
            r0 = i * 128
            (QK, kQK), (V, kV), (Z, kZ), (GP, kGP) = qk[b], vv[b], zz[b], gp_[b]
            (ZA, kZA) = zaug[b]
            P.D(SP, [("PROJ", i)], [kQK], out=QK[:], in_=PROJ[r0:r0 + 128, 2048:2560])
            P.D(SP, [("PROJ", i)], [kV], out=V[:], in_=PROJ[r0:r0 + 128, 2560:3072])
            P.D(SP, [("PROJ", i)], [kZ], out=Z[:], in_=PROJ[r0:r0 + 128, 3584 + 16 * d:3600 + 16 * d])
            bk = C.bank()
            P.I(PE, "transpose", [kZ, "ident"], [("ps", bk)], out=C.ps[bk][0:16, 0:128], in_=Z[:], identity=C.ident[:])
            P.I(ACT, "copy", [], [("ps", bk), kZA], out=ZA[0:16, :], in_=C.ps[bk][0:16, 0:128])
            bk = C.bank()
            P.I(PE, "matmul", [kZA, kWG], [("ps", bk)], C.ps[bk][:, 0:256], lhsT=ZA[:], rhs=WGd[:], start=True, stop=True)
            P.I(ACT, "activation", [], [("ps", bk), kGP], out=GP[:], in_=C.ps[bk][:, 0:256], func=AF.Exp, scale=-1.0)
            P.I(ACT, "activation", [kGP], [kGP], out=GP[:], in_=GP[:], func=AF.Ln, bias=1.0, scale=1.0)
            bkb = C.bank()
            P.I(PE, "matmul", [kTRI, kGP], [("ps", bkb)], C.ps[bkb][:, 0:256], lhsT=TRId[:], rhs=GP[:], start=True, stop=True)
            P.I(PE, "matmul", [kBLK, kGP], [("ps", bkb)], C.ps[bkb][:, 256:512], lhsT=BLK[:], rhs=GP[:], start=True, stop=True)
            (BS, kBS), (EB, kEB), (ENB, kENB), (EBL, kEBL) = bS[b], eb[b], enb[b], ebl[b]
            P.I(ACT, "copy", [], [("ps", bkb), kBS], out=BS[:], in_=C.ps[bkb][:, 0:256])
            P.I(DVE, "tensor_tensor", [kBS], [("ps", bkb), kEBL], out=EBL[:], in0=C.ps[bkb][:, 256:512], in1=BS[:], op=ALU.subtract)
            P.I(ACT, "activation", [kBS], [kEB], out=EB[:], in_=BS[:], func=AF.Exp)
            P.I(ACT, "activation", [kBS], [kENB], out=ENB[:], in_=BS[:], func=AF.Exp, scale=-1.0)
            P.I(ACT, "activation", [kEBL], [kEBL], out=EBL[:], in_=EBL[:], func=AF.Exp)
            (ED, kED) = edec[b]
            bk = C.bank()
            for h in range(4):
                P.I(PE, "matmul", [kGP, kCIND], [("ps", bk)], C.ps[bk][0:64, h * 2:h * 2 + 2], lhsT=GP[:, h * 64:(h + 1) * 64], rhs=CIND[:], start=True, stop=True)
            P.I(ACT, "activation", [], [("ps", bk), kED], out=ED[:].rearrange("p h c -> p (h c)"), in_=C.ps[bk][0:64, 0:8], func=AF.Exp)
            (QD, kQD), (KD, kKD), (KL, kKL) = qd[b], kd[b], kl[b]
            P.I(DVE, "scalar_tensor_tensor", [kQK, kEB], [kQD], out=QD[:], in0=QK[:, 0:256], scalar=0.125, in1=EB[:], op0=ALU.mult, op1=ALU.mult)
            P.I(POOL, "tensor_tensor", [kQK, kENB], [kKD], out=KD[:], in0=QK[:, 256:512], in1=ENB[:], op=ALU.mult)
            P.I(POOL, "tensor_tensor", [kQK, kEBL], [kKL], out=KL[:], in0=QK[:, 256:512], in1=EBL[:], op=ALU.mult)
            (QT, kQT), (KT, kKT) = qdT[b], kdT[b]
            for (src, ksrc, dst, kdst, eng) in ((QD, kQD, QT, kQT, ACT), (KD, kKD, KT, kKT, DVE)):
                bk = C.bank()
                for h in range(4):
                    P.I(PE, "transpose", [ksrc, "ident"], [("ps", bk)], out=C.ps[bk][0:64, h * 128:(h + 1) * 128], in_=src[:, h * 64:(h + 1) * 64], identity=C.ident[:])
                copy_op(P, eng, dst[:].rearrange("p h t -> p (h t)"), C.ps[bk][0:64, :], [], [("ps", bk), kdst])
            (AT, kAT) = ATm[b]
            bk = C.bank()
            for h in range(4):
                P.I(PE, "matmul", [kKT, kQT], [("ps", bk)], C.ps[bk][:, h * 128:(h + 1) * 128], lhsT=KT[:, h, :], rhs=QT[:, h, :], start=True, stop=True)
            P.I(DVE, "tensor_tensor", [kMK], [("ps", bk), kAT], out=AT[:].rearrange("p h t -> p (h t)"), in0=C.ps[bk][:, :], in1=MK[:], op=ALU.mult)
            bo = C.bank()
            for h in range(4):
                P.I(PE, "matmul", [kAT, kV], [("ps", bo)], C.ps[bo][:, h * 128:(h + 1) * 128], lhsT=AT[:, h, :], rhs=V[:, h * 128:(h + 1) * 128], start=(h == 0), stop=False)
            for ci, c in enumerate(corder):
                cs = slice(c * 64, (c + 1) * 64)
                for h in range(4):
                    P.I(PE, "matmul", [kQT, (kS, h)], [("ps", bo)], C.ps[bo][cs, h * 128:(h + 1) * 128], lhsT=QT[:, h, cs], rhs=S[:, h, :], start=False, stop=(ci == 1 and h == 3))
                bu = C.bank()
                for h in range(4):
                    P.I(PE, "matmul", [kKL, kV], [("ps", bu)], C.ps[bu][0:64, h * 128:(h + 1) * 128], lhsT=KL[cs, h * 64:(h + 1) * 64], rhs=V[cs, h * 128:(h + 1) * 128], start=True, stop=True)
                for h in range(4):
                    P.I(DVE, "scalar_tensor_tensor", [kED], [("ps", bu), (kS, h)], out=S[:, h, :], in0=S[:, h, :], scalar=ED[:, h, c:c + 1], in1=C.ps[bu][0:64, h * 128:(h + 1) * 128], op0=ALU.mult, op1=ALU.add)
            (OT, kOT) = ot[b]
            if d == 0:
                P.I(ACT, "copy", [], [("ps", bo), kOT], out=OT[:], in_=C.ps[bo][:, :])
                P.D(SP, [kOT], [("OF", i)], out=OF[r0:r0 + 128, :], in_=OT[:])
                continue
            (OFt, kOFt), (RR, kRR), (SQ, kSQ), (SS, kSS), (OB, kOB) = of_[b], rr[b], sq[b], ssq[b], oTb[b]
            P.D(SP, [("OF", i)], [kOFt], out=OFt[:], in_=OF[r0:r0 + 128, :])
            P.D(SP, [("PROJ", i)], [kRR], out=RR[:], in_=PROJ[r0:r0 + 128, 3072:3584])
            P.I(DVE, "tensor_tensor", [kOFt], [("ps", bo), kOT], out=OT[:], in0=C.ps[bo][:, :], in1=OFt[:], op=ALU.add)
            P.I(POOL, "tensor_tensor", [kOT], [kSQ], out=SQ[:], in0=OT[:], in1=OT[:], op=ALU.mult)
            P.I(DVE, "tensor_reduce", [kSQ], [kSS], out=SS[:], in_=SQ[:].rearrange("p (h v) -> p h v", h=4), axis=AX.X, op=ALU.add)
            P.I(DVE, "tensor_scalar", [kSS], [kSS], out=SS[:], in0=SS[:], scalar1=1.0 / 128.0, scalar2=1e-6, op0=ALU.mult, op1=ALU.add)
            P.I(ACT, "activation", [kSS], [kSS], out=SS[:], in_=SS[:], func=AF.Ln)
            P.I(ACT, "activation", [kSS], [kSS], out=SS[:], in_=SS[:], func=AF.Exp, scale=-0.5)
            P.I(ACT, "activation", [kRR], [kRR], out=RR[:], in_=RR[:], func=AF.Silu)
            o3 = OT[:].rearrange("p (h v) -> p h v", h=4)
            P.I(DVE, "tensor_tensor", [kOT, kSS], [kOT], out=o3, in0=o3, in1=SS[:].unsqueeze(2).to_broadcast([128, 4, 128]), op=ALU.mult)
            P.I(POOL, "tensor_tensor", [kOT, kNG], [kOT], out=o3, in0=o3, in1=NG[:].unsqueeze(1).to_broadcast([128, 4, 128]), op=ALU.mult)
            P.I(DVE, "tensor_tensor", [kOT, kRR], [kOT], out=OT[:], in0=OT[:], in1=RR[:], op=ALU.mult)
            bk = C.bank()
            for h in range(4):
                P.I(PE, "transpose", [kOT, "ident"], [("ps", bk)], out=C.ps[bk][:, h * 128:(h + 1) * 128], in_=OT[:, h * 128:(h + 1) * 128], identity=C.ident[:])
            P.I(ACT, "copy", [], [("ps", bk), kOB], out=OB[:].rearrange("p h t -> p (h t)"), in_=C.ps[bk][:, :])
            P.D(SP, [kOB], [("MIXT", "gla", i)], out=MIXT[12:16, :, r0:r0 + 128].rearrange("c p t -> p c t"), in_=OB[:])
    P.barrier()
    P.sb_reset(m0)


ALPHA_ = (2 * 2) ** 0.25
NSLOT = 544
ROWW = 2080


def bcast_load(P, q, dst, key, src_row):
    P.D(q, [], [key], out=dst[:], in_=src_row.partition_broadcast(128))


def ln_apply(P, C, src, ksrc, dst, kdst, st, mv, rs, tag, gmul, kg, badd, kb, eng2=POOL):
    ln_stats(P, C, src, ksrc, st, mv, rs, tag)
    P.I(ACT, "activation", [ksrc, tag + "rs"], [kdst], out=dst[:], in_=src[:], func=AF.Identity, bias=rs[:, 1:2], scale=rs[:, 0:1])
    P.I(DVE, "tensor_tensor", [kdst, kg], [kdst], out=dst[:], in0=dst[:], in1=gmul, op=ALU.mult)
    P.I(eng2, "tensor_tensor", [kdst, kb], [kdst], out=dst[:], in0=dst[:], in1=badd, op=ALU.add)


def phase_wout(P, C, l, X, MIXT, X1, H2R, AFF, w_out, MODV, ln_g, ln_b, router, cst=None):
    m0 = P.sb_mark()
    wo = P.sb([128, 16, D], BF16, "wo")
    for kt in range(16):
        P.D(POOL, [], [("wo", kt)], out=wo[:, kt, :], in_=w_out[l, kt * 128:(kt + 1) * 128, :])
    names = {}
    for nm, src in (("gt1", lambda r: MODV[l, r, 2 * D:3 * D]), ("sc2", lambda r: MODV[l, r, 4 * D:5 * D]), ("sh2", lambda r: MODV[l, r, 3 * D:4 * D])):
        for r in range(2):
            t = P.sb([128, D], F32, nm)
            bcast_load(P, SP, t, (nm, r), src(r))
            names[(nm, r)] = t
    for r in range(2):
        P.I(POOL, "tensor_scalar_add", [("sc2", r)], [("sc2", r)], out=names[("sc2", r)][:], in0=names[("sc2", r)][:], scalar1=1.0)
    g1 = P.sb([128, D], F32, "g1"); b1 = P.sb([128, D], F32, "b1")
    bcast_load(P, SP, g1, "g1", ln_g[l, :]); bcast_load(P, SP, b1, "b1", ln_b[l, :])
    rt = P.sb([128, 16, 16], F32, "rt")
    P.D(SP, [], ["rt"], out=rt[:], in_=router[l].rearrange("(kt p) e -> p kt e", p=128))
    mx = [P.sb([128, 16, 128], BF16, "mx") for _ in range(2)]
    xs = [P.sb([128, D], F32, "xs") for _ in range(2)]
    x1s = [P.sb([128, D], F32, "x1s") for _ in range(2)]
    rows = [P.sb([128, ROWW], F32, "row") for _ in range(2)]
    h2T = P.sb([128, 16, 128], F32, "h2T")
    st = P.sb([128, 4, 6], F32, "st"); mv = P.sb([128, 2], F32, "mv"); rs = P.sb([128, 2], F32, "rs")
    st2 = P.sb([128, 4, 6], F32, "st2"); mv2 = P.sb([128, 2], F32, "mv2"); rs2 = P.sb([128, 2], F32, "rs2")
    lmx = P.sb([128, 1], F32, "lmx"); lsum = P.sb([128, 1], F32, "lsum")
    for (t, k) in ((rows[0], ("row", 0)), (rows[1], ("row", 1))):
        P.I(POOL, "memset", [], [k], t[:, 2064:ROWW], 0.0)
    TOK = P.sb([128, NT], F32, "TOK")
    if cst is not None and "tokid" in cst:
        P.D(SP, [], ["TOK"], out=TOK[:], in_=cst["tokid"][:, :])
    else:
        P.I(POOL, "memset", [], ["TOK"], TOK[:], 0.0)
    for i in range(NT):
        b = i % 2
        r = 1 if i < 2 else 0
        r0 = i * 128
        MX, XS, X1S, ROW = mx[b], xs[b], x1s[b], rows[b]
        kMX, kXS, kX1, kROW = ("mx", b), ("xs", b), ("x1s", b), ("row", b)
        P.D(SP, [("MIXT",)], [kMX], out=MX[:], in_=MIXT[:, :, r0:r0 + 128].rearrange("c p t -> p c t"))
        P.D(SP, [("X", i)], [kXS], out=XS[:], in_=X[r0:r0 + 128, :])
        for nb in range(4):
            bk = C.bank()
            for kt in range(16):
                P.I(PE, "matmul", [kMX, ("wo", kt)], [("ps", bk)], C.ps[bk][:, :], lhsT=MX[:, kt, :], rhs=wo[:, kt, nb * 512:(nb + 1) * 512], start=(kt == 0), stop=(kt == 15))
            sl = slice(nb * 512, (nb + 1) * 512)
            P.I(DVE, "tensor_tensor", [("gt1", r)], [("ps", bk), kX1 + (nb,)], out=X1S[:, sl], in0=C.ps[bk][:, :], in1=names[("gt1", r)][:, sl], op=ALU.mult)
        P.I(DVE, "scalar_tensor_tensor", [kXS, kX1], [kX1], out=X1S[:], in0=XS[:], scalar=ALPHA_, in1=X1S[:], op0=ALU.mult, op1=ALU.add)
        ln_apply(P, C, X1S, kX1, X1S, kX1, st, mv, rs, "w1", g1[:], "g1", b1[:], "b1")
        P.D(SP, [kX1], [("X1", i)], out=X1[r0:r0 + 128, :], in_=X1S[:])
        ln_stats(P, C, X1S, kX1, st2, mv2, rs2, "w2")
        P.I(ACT, "activation", [kX1, "w2rs"], [kROW + (0,)], out=ROW[:, 0:D], in_=X1S[:], func=AF.Identity, bias=rs2[:, 1:2], scale=rs2[:, 0:1])
        P.I(DVE, "tensor_tensor", [kROW + (0,), ("sc2", r)], [kROW + (0,)], out=ROW[:, 0:D], in0=ROW[:, 0:D], in1=names[("sc2", r)][:], op=ALU.mult)
        P.I(POOL, "tensor_tensor", [kROW + (0,), ("sh2", r)], [kROW + (0,)], out=ROW[:, 0:D], in0=ROW[:, 0:D], in1=names[("sh2", r)][:], op=ALU.add)
        for kg in range(4):
            bk = C.bank()
            for j in range(4):
                kt = kg * 4 + j
                P.I(PE, "transpose", [kROW + (0,), "ident"], [("ps", bk)], out=C.ps[bk][:, j * 128:(j + 1) * 128], in_=ROW[:, kt * 128:(kt + 1) * 128], identity=C.ident[:])
            copy_op(P, alt(kg), h2T[:, kg * 4:(kg + 1) * 4, :].rearrange("p a t -> p (a t)"), C.ps[bk][:, :], [], [("ps", bk), ("h2T", kg)])
        bk = C.bank()
        for kt in range(16):
            P.I(PE, "matmul", [("h2T", kt // 4), "rt"], [("ps", bk)], C.ps[bk][:, 0:16], lhsT=h2T[:, kt, :], rhs=rt[:, kt, :], start=(kt == 0), stop=(kt == 15))
        P.I(DVE, "tensor_reduce", [], [("ps", bk), "lmx"], out=lmx[:], in_=C.ps[bk][:, 0:16], axis=AX.X, op=ALU.max)
        P.I(DVE, "tensor_scalar_mul", ["lmx"], ["lmx"], out=lmx[:], in0=lmx[:], scalar1=-1.0)
        P.I(DVE, "memset", [], ["lsum"], lsum[:], 0.0)
        P.I(ACT, "activation", ["lmx"], [("ps", bk), kROW + (1,), "lsum"], out=ROW[:, D:D + 16], in_=C.ps[bk][:, 0:16], func=AF.Exp, bias=lmx[:, 0:1], scale=1.0, accum_out=lsum[:])
        P.I(DVE, "reciprocal", ["lsum"], ["lsum"], out=lsum[:], in_=lsum[:])
        P.I(DVE, "tensor_scalar_mul", [kROW + (1,), "lsum"], [kROW + (1,)], out=ROW[:, D:D + 16], in0=ROW[:, D:D + 16], scalar1=lsum[:, 0:1])
        P.I(POOL, "tensor_copy", [kROW + (1,)], [("AFF", i)], out=AFF[:, i, :], in_=ROW[:, D:D + 16])
        P.I(POOL, "tensor_copy", ["TOK"], [kROW + (2,)], out=ROW[:, 2064:2065], in_=TOK[:, i:i + 1])
        P.D(SP, [kROW], [("H2R", i)], out=H2R[r0:r0 + 128, :], in_=ROW[:])
    P.barrier()
    P.sb_reset(m0)


def phase_route(P, C, AFF, IDXS, IDXC_dram, cst, niter=30):
    m0 = P.sb_mark()
    n_ = [0]

    def T(shape, dt=F32, nm="rt"):
        n_[0] += 1
        return P.sb(shape, dt, nm), "%s%d" % (nm, n_[0])
    ones, kones = T([128, 128]); strict, kstrict = T([128, 128]); TGT, kTGT = T([128, 2, 16])
    P.D(SP, [], [kones], out=ones[:], in_=cst["ones"][:, :])
    P.D(SP, [], [kstrict], out=strict[:], in_=cst["strict"][:, :])
    P.D(SP, [], [kTGT], out=TGT[:], in_=cst["tgt"].rearrange("p (s e) -> p s e", s=2))
    LO, kLO = T([128, 2, 16]); HI, kHI = T([128, 2, 16]); MID, kMID = T([128, 2, 16])
    CNT, kCNT = T([128, 2, 16]); GE, kGE = T([128, 2, 16]); D1, kD1 = T([128, 2, 16])
    CMP, kCMP = T([128, NT, 16])
    P.I(DVE, "memset", [], [kLO], LO[:], 0.0)
    P.I(DVE, "memset", [], [kHI], HI[:], 1.0001)
    segs = ((0, 0, 2), (1, 2, NT))

    def compare(TH, kTH):
        for (s, a, b_) in segs:
            P.I(DVE, "tensor_tensor", [("AFF",), kTH], [(kCMP, s)], out=CMP[:, a:b_, :], in0=AFF[:, a:b_, :],
                in1=TH[:, s, :].unsqueeze(1).to_broadcast([128, b_ - a, 16]), op=ALU.is_ge)
    for it in range(niter):
        P.I(DVE, "tensor_tensor", [kLO, kHI], [kMID], out=MID[:], in0=LO[:], in1=HI[:], op=ALU.add)
        P.I(DVE, "tensor_scalar_mul", [kMID], [kMID], out=MID[:], in0=MID[:], scalar1=0.5)
        compare(MID, kMID)
        for (s, a, b_) in segs:
            P.I(DVE, "tensor_reduce", [(kCMP, s)], [(kCNT, s)], out=CNT[:, s, :], in_=CMP[:, a:b_, :].rearrange("p t e -> p e t"), axis=AX.X, op=ALU.add)
        bk = C.bank()
        P.I(PE, "matmul", [kones, kCNT], [("ps", bk)], C.ps[bk][:, 0:32], lhsT=ones[:], rhs=CNT[:].rearrange("p s e -> p (s e)"), start=True, stop=True)
        P.I(DVE, "tensor_tensor", [kTGT], [("ps", bk), kGE], out=GE[:].rearrange("p s e -> p (s e)"), in0=C.ps[bk][:, 0:32], in1=TGT[:].rearrange("p s e -> p (s e)"), op=ALU.is_ge)
        P.I(DVE, "tensor_tensor", [kMID, kLO], [kD1], out=D1[:], in0=MID[:], in1=LO[:], op=ALU.subtract)
        P.I(DVE, "tensor_tensor", [kD1, kGE], [kD1], out=D1[:], in0=D1[:], in1=GE[:], op=ALU.mult)
        P.I(DVE, "tensor_tensor", [kLO, kD1], [kLO], out=LO[:], in0=LO[:], in1=D1[:], op=ALU.add)
        P.I(DVE, "tensor_tensor", [kHI, kMID], [kD1], out=D1[:], in0=HI[:], in1=MID[:], op=ALU.subtract)
        P.I(DVE, "tensor_tensor", [kD1, kGE], [kD1], out=D1[:], in0=D1[:], in1=GE[:], op=ALU.mult)
        P.I(DVE, "tensor_tensor", [kMID, kD1], [kHI], out=HI[:], in0=MID[:], in1=D1[:], op=ALU.add)
    compare(LO, kLO)
    PRE, kPRE = T([128, NT, 16]); TOT, kTOT = T([128, NT, 16]); OFFS, kOFFS = T([128, NT, 16])
    cm = CMP[:].rearrange("p t e -> p (t e)")
    for (dst, kdst, lt, klt) in ((PRE, kPRE, strict, kstrict), (TOT, kTOT, ones, kones)):
        for (c0, c1) in ((0, 512), (512, 544)):
            bk = C.bank()
            P.I(PE, "matmul", [klt, kCMP], [("ps", bk)], C.ps[bk][:, 0:c1 - c0], lhsT=lt[:], rhs=cm[:, c0:c1], start=True, stop=True)
            P.I(ACT, "copy", [], [("ps", bk), (kdst, c0)], out=dst[:].rearrange("p t e -> p (t e)")[:, c0:c1], in_=C.ps[bk][:, 0:c1 - c0])
    for (s, a, b_) in segs:
        P.I(DVE, "memset", [], [(kOFFS, a)], OFFS[:, a, :], 0.0)
        for i in range(a, b_ - 1):
            P.I(DVE, "tensor_tensor", [(kOFFS, i), kTOT], [(kOFFS, i + 1)], out=OFFS[:, i + 1, :], in0=OFFS[:, i, :], in1=TOT[:, i, :], op=ALU.add)
    P.I(DVE, "tensor_tensor", [kPRE, kOFFS], [kPRE], out=PRE[:], in0=PRE[:], in1=OFFS[:], op=ALU.add)
    for (s, a, b_) in segs:
        cap = 32.0 if s == 0 else 512.0
        P.I(DVE, "tensor_single_scalar", [kPRE], [(kTOT, s)], out=TOT[:, a:b_, :], in_=PRE[:, a:b_, :], scalar=cap, op=ALU.is_lt)
    P.I(DVE, "tensor_tensor", [kTOT, kCMP], [kCMP], out=CMP[:], in0=CMP[:], in1=TOT[:], op=ALU.mult)
    P.I(DVE, "tensor_scalar_add", [kPRE], [(kPRE, 0)], out=PRE[:, 0:2, :], in0=PRE[:, 0:2, :], scalar1=512.0)
    IDXC, kIDXC = T([128, NT, 16], I32)
    for (big, dst, kdst) in ((10000.0, IDXS, ("IDXS",)), (544.0, IDXC, kIDXC)):
        P.I(DVE, "tensor_scalar_add", [kPRE], [kOFFS], out=OFFS[:], in0=PRE[:], scalar1=-big)
        P.I(DVE, "tensor_tensor", [kOFFS, kCMP], [kOFFS], out=OFFS[:], in0=OFFS[:], in1=CMP[:], op=ALU.mult)
        P.I(DVE, "tensor_scalar_add", [kOFFS], [kOFFS], out=OFFS[:], in0=OFFS[:], scalar1=big)
        P.I(DVE, "tensor_copy", [kOFFS], [kdst], out=dst[:], in_=OFFS[:])
    P.D(SP, [kIDXC], [("IDXC",)], out=IDXC_dram[:, :, :], in_=IDXC[:])
    P.barrier()
    P.sb_reset(m0)


def phase_scatter(P, C, H2R, XE, IDXS, cst=None):
    m0 = P.sb_mark()
    rows = [P.sb([128, ROWW], F32, "srow") for _ in range(3)]
    holder = {}

    P.nname += 1
    rname = "bcreg%d" % P.nname

    def f0(eng):
        holder["reg"] = eng.alloc_register(rname)
        return eng.reg_mov(holder["reg"], NSLOT - 1)
    P.op(POOL, f0, [], [])
    pre = []
    if cst is not None and "trash" in cst:
        TR = P.sb([128, 5], F32, "TR")
        P.D(SP, [], ["TR"], out=TR[:], in_=cst["trash"][:, :])
        for e in range(16):
            P.D(SP, ["TR"], [("XEpre", e, 0)], out=XE[e, 0:512, 2064].rearrange("(j p) -> p j", p=128), in_=TR[:, 0:4], allow_slow_non_contiguous=True)
            P.D(SP, ["TR"], [("XEpre", e, 1)], out=XE[e, 512:544, 2064].rearrange("(p o) -> p o", o=1), in_=TR[0:32, 4:5], allow_slow_non_contiguous=True)
        pre = [("XEpre",)]
    for i in range(NT):
        b = i % 3
        P.D(SP, [("H2R", i)], [("srow", b)], out=rows[b][:], in_=H2R[i * 128:(i + 1) * 128, :])
        for e in range(16):
            P.dma(POOL, (lambda eng, b=b, i=i, e=e: eng.indirect_dma_start(
                out=XE.ap().rearrange("e s w -> (e s) w"), out_offset=bass.IndirectOffsetOnAxis(ap=IDXS[:, i, e:e + 1], axis=0),
                in_=rows[b][:], in_offset=None, element_offset=e * NSLOT * ROWW, bounds_check=holder["reg"], oob_is_err=False)),
                reads=[("srow", b), ("IDXS",)] + pre, writes=[("XE", i, e)])
    P.barrier()
    P.sb_reset(m0)


NSL = 2176


def phase_experts(P, C, nsamp, nexp, xe_ap, gate_ap, w_ap, ye_ap, scat=None):
    m0 = P.sb_mark()
    nsl = nsamp * 544
    cx0 = nsamp * 512
    hidT = P.sb([128, 16, nsl], BF16, "hidT")
    stg = [P.sb([128, D], F32, "stg") for _ in range(2)]
    wgb = [P.sb([128, 16, 256], BF16, "wgb") for _ in range(2)]
    wub = [P.sb([128, 16, 256], BF16, "wub") for _ in range(2)]
    sil = [P.sb([128, 512], F32, "sil") for _ in range(2)]
    GT = P.sb([128, 2, 5, 4], F32, "GT")
    TKf = P.sb([128, 2, 5], F32, "TKf")
    TKi = P.sb([128, 2, 5], I32, "TKi")
    if scat is not None:
        FFN = scat["FFN"]
        zt = P.sb([128, D], F32, "zt")
        P.I(POOL, "memset", [], ["zt"], zt[:], 0.0)
        nrow = FFN.shape[0]
        for r0 in range(0, nrow, 128):
            n = min(128, nrow - r0)
            P.D(SP, ["zt"], [("FFN", r0)], out=FFN[r0:r0 + n, :], in_=zt[0:n, :])
    alias = nsamp > 1
    if not alias:
        xeT_s = P.sb([128, 16, nsl], BF16, "xeT")
        wd_s = P.sb([128, 16, D], BF16, "wd")
    mA = P.sb_mark()
    nst = 0
    nw = 0
    blocks = [(b * 512, 512) for b in range(nsamp)] + [(cx0, nsamp * 32)]
    for el in range(nexp):
        ep = el % 2
        if alias:
            P.sb_reset(mA)
            xeT = P.sb([128, 16, nsl], BF16, "xeT")
            kx = "xeT%d" % el
        else:
            xeT = xeT_s
            kx = "xeT"
        for b in range(nsamp):
            P.D(SP, [("XE",)], [("GT", ep, b)], out=GT[:, ep, 0:4, b], in_=gate_ap(b, el, 0, 512).rearrange("(j p) -> p j", p=128), allow_slow_non_contiguous=True)
            P.D(SP, [("XE",)], [("GTc", ep, b)], out=GT[b * 32:(b + 1) * 32, ep, 4, 0:1], in_=gate_ap(b, el, 512, 544).rearrange("(p o) -> p o", o=1), allow_slow_non_contiguous=True)
        if scat is not None:
            P.D(SP, [("XE",)], [("TKf", ep, 0)], out=TKf[:, ep, 0:4], in_=scat["tok_ap"](0, el, 0, 512).rearrange("(j p) -> p j", p=128), allow_slow_non_contiguous=True)
            P.D(SP, [("XE",)], [("TKf", ep, 1)], out=TKf[0:32, ep, 4:5], in_=scat["tok_ap"](0, el, 512, 544).rearrange("(p o) -> p o", o=1), allow_slow_non_contiguous=True)
            P.I(DVE, "tensor_copy", [("TKf", ep)], [("TKi", ep, 0)], out=TKi[:, ep, 0:4], in_=TKf[:, ep, 0:4])
            P.I(DVE, "tensor_copy", [("TKf", ep)], [("TKi", ep, 1)], out=TKi[0:32, ep, 4:5], in_=TKf[0:32, ep, 4:5])
        for b in range(nsamp):
            for j in range(5):
                np_ = 128 if j < 4 else 32
                col0 = b * 512 + j * 128 if j < 4 else cx0 + b * 32
                sb_ = nst % 2
                nst += 1
                ST = stg[sb_]
                P.D(SP, [("XE",)], [("stg", sb_)], out=ST[0:np_, :], in_=xe_ap(b, el, j * 128, j * 128 + np_))
                for kg in range(4):
                    bk = C.bank()
                    for q in range(4):
                        kt = kg * 4 + q
                        P.I(PE, "transpose", [("stg", sb_), "ident"], [("ps", bk)], out=C.ps[bk][:, q * 128:q * 128 + np_], in_=ST[0:np_, kt * 128:(kt + 1) * 128], identity=C.ident[0:np_, 0:np_])
                    copy_op(P, alt(kg), xeT[:, kg * 4:(kg + 1) * 4, col0:col0 + np_], C.ps[bk][:, :].rearrange("p (q t) -> p q t", q=4)[:, :, 0:np_], [], [("ps", bk), (kx, b, j, kg)])
        for fb in range(8):
            wb = nw % 2
            nw += 1
            P.D(POOL, [], [("wgb", wb)], out=wgb[wb][:], in_=w_ap("gate", el)[:, fb * 256:(fb + 1) * 256].rearrange("(kt p) n -> p kt n", p=128))
            P.D(POOL, [], [("wub", wb)], out=wub[wb][:], in_=w_ap("up", el)[:, fb * 256:(fb + 1) * 256].rearrange("(kt p) n -> p kt n", p=128))
            for fl in range(2):
                ft = fb * 2 + fl
                for nb, (c0, w) in enumerate(blocks):
                    bg = C.bank()
                    bu = C.bank()
                    for kt in range(16):
                        P.I(PE, "matmul", [("wgb", wb), (kx,)], [("ps", bg)], C.ps[bg][:, 0:w], lhsT=wgb[wb][:, kt, fl * 128:(fl + 1) * 128], rhs=xeT[:, kt, c0:c0 + w], start=(kt == 0), stop=(kt == 15))
                    for kt in range(16):
                        P.I(PE, "matmul", [("wub", wb), (kx,)], [("ps", bu)], C.ps[bu][:, 0:w], lhsT=wub[wb][:, kt, fl * 128:(fl + 1) * 128], rhs=xeT[:, kt, c0:c0 + w], start=(kt == 0), stop=(kt == 15))
                    sb_ = (ft * len(blocks) + nb) % 2
                    P.I(ACT, "activation", [], [("ps", bg), ("sil", sb_)], out=sil[sb_][:, 0:w], in_=C.ps[bg][:, 0:w], func=AF.Silu)
                    P.I(DVE, "tensor_tensor", [("sil", sb_)], [("ps", bu), ("hidT", ft, nb)], out=hidT[:, ft, c0:c0 + w], in0=C.ps[bu][:, 0:w], in1=sil[sb_][:, 0:w], op=ALU.mult)
        if alias:
            P.barrier()
            P.sb_reset(mA)
            wd = P.sb([128, 16, D], BF16, "wd")
            kw = "wd%d" % el
        else:
            wd = wd_s
            kw = "wd"
        for db in range(8):
            P.D(POOL, [], [(kw, db)], out=wd[:, :, db * 256:(db + 1) * 256], in_=w_ap("down", el)[:, db * 256:(db + 1) * 256].rearrange("(kt p) n -> p kt n", p=128))
        for b in range(nsamp):
            for j in range(5):
                if j == 4 and b > 0:
                    continue
                if j < 4:
                    col0, np_ = b * 512 + j * 128, 128
                    gcol = GT[:, ep, j, b:b + 1]
                else:
                    col0, np_ = cx0, nsamp * 32
                    gcol = GT[0:np_, ep, 4, 0:1]
                sb_ = nst % 2
                nst += 1
                ST = stg[sb_]
                for dk in range(4):
                    bk = C.bank()
                    for ft in range(16):
                        P.I(PE, "matmul", [("hidT", ft), (kw,)], [("ps", bk)], C.ps[bk][0:np_, :], lhsT=hidT[:, ft, col0:col0 + np_], rhs=wd[:, ft, dk * 512:(dk + 1) * 512], start=(ft == 0), stop=(ft == 15))
                    if dk % 2 == 0:
                        P.I(ACT, "activation", [("GT", ep), ("GTc", ep)], [("ps", bk), ("stg", sb_, dk)], out=ST[0:np_, dk * 512:(dk + 1) * 512], in_=C.ps[bk][0:np_, :], func=AF.Copy, scale=gcol)
                    else:
                        P.I(DVE, "tensor_scalar_mul", [("GT", ep), ("GTc", ep)], [("ps", bk), ("stg", sb_, dk)], out=ST[0:np_, dk * 512:(dk + 1) * 512], in0=C.ps[bk][0:np_, :], scalar1=gcol)
                if scat is not None:
                    P.dma(POOL, (lambda eng, ST=ST, np_=np_, ep=ep, j=j: eng.indirect_dma_start(
                        out=scat["FFN"].ap(), out_offset=bass.IndirectOffsetOnAxis(ap=TKi[0:np_, ep, j:j + 1], axis=0),
                        in_=ST[0:np_, :], in_offset=None, compute_op=ALU.add)), reads=[("stg", sb_), ("TKi", ep)], writes=[("FFN",)])
                elif j < 4:
                    P.D(SP, [("stg", sb_)], [("YE", el, b, j)], out=ye_ap(b, el, j * 128, (j + 1) * 128), in_=ST[:])
                else:
                    for b2 in range(nsamp):
                        P.D(SP, [("stg", sb_)], [("YE", el, b2, 4)], out=ye_ap(b2, el, 512, 544), in_=ST[b2 * 32:(b2 + 1) * 32, :])
        if alias:
            P.barrier()
    P.barrier()
    P.sb_reset(m0)


def phase_combine(P, C, X1, YEp, IDXC_in, OUT, MODV, ln_g, ln_b, t_lo, out_off, l=0):
    m0 = P.sb_mark()
    idx = P.sb([128, NT, 16], I32, "cidx")
    P.D(SP, [], ["cidx"], out=idx[:], in_=IDXC_in[:, :, :])
    gt2 = []
    for r in range(2):
        t = P.sb([128, D], F32, "gt2")
        bcast_load(P, SP, t, ("gt2", r), MODV[l, r, 5 * D:6 * D])
        gt2.append(t)
    g2 = P.sb([128, D], F32, "g2"); b2 = P.sb([128, D], F32, "b2")
    bcast_load(P, SP, g2, "g2", ln_g[l, :]); bcast_load(P, SP, b2, "b2", ln_b[l, :])
    NG = 4
    gb = [P.sb([128, D], F32, "gb") for _ in range(NG)]
    acc = [P.sb([128, D], F32, "acc") for _ in range(2)]
    xs = [P.sb([128, D], F32, "cxs") for _ in range(2)]
    st = P.sb([128, 4, 6], F32, "cst"); mv = P.sb([128, 2], F32, "cmv"); rs = P.sb([128, 2], F32, "crs")
    yv = YEp.ap().rearrange("e s w -> (e s) w")
    ng = 0
    for i in range(t_lo, NT):
        b = i % 2
        r = 1 if i < 2 else 0
        A, XS = acc[b], xs[b]
        kA, kXS = ("acc", b), ("cxs", b)
        P.D(SP, [("X1", i)], [kXS], out=XS[:], in_=X1[i * 128:(i + 1) * 128, :])
        for e in range(16):
            if e == 0:
                dst, kdst = A, kA
            else:
                gi = ng % NG
                ng += 1
                dst, kdst = gb[gi], ("gb", gi)
            P.dma(POOL, (lambda eng, dst=dst, i=i, e=e: eng.indirect_dma_start(
                out=dst[:], out_offset=None, in_=yv, in_offset=bass.IndirectOffsetOnAxis(ap=idx[:, i, e:e + 1], axis=0),
                element_offset=e * 545 * D)), reads=["cidx", ("YEp",)], writes=[kdst])
            if e > 0:
                P.I(DVE if e % 2 else POOL, "tensor_tensor", [kdst, kA], [kA], out=A[:], in0=A[:], in1=dst[:], op=ALU.add)
        P.I(DVE, "tensor_tensor", [kA, ("gt2", r)], [kA], out=A[:], in0=A[:], in1=gt2[r][:], op=ALU.mult)
        P.I(DVE, "scalar_tensor_tensor", [kXS, kA], [kA], out=A[:], in0=XS[:], scalar=ALPHA_, in1=A[:], op0=ALU.mult, op1=ALU.add)
        ln_apply(P, C, A, kA, A, kA, st, mv, rs, "c1", g2[:], "g2", b2[:], "b2")
        o0 = i * 128 - out_off
        P.D(SP, [kA], [("X", i)], out=OUT[o0:o0 + 128, :], in_=A[:])
    P.barrier()
    P.sb_reset(m0)


def phase_ln2(P, C, X1, FFN, OUT, MODV, ln_g, ln_b, t_lo, out_off, l):
    m0 = P.sb_mark()
    gt2 = []
    for r in range(2):
        t = P.sb([128, D], F32, "gt2")
        bcast_load(P, SP, t, ("gt2", r), MODV[l, r, 5 * D:6 * D])
        gt2.append(t)
    g2 = P.sb([128, D], F32, "g2"); b2 = P.sb([128, D], F32, "b2")
    bcast_load(P, SP, g2, "g2", ln_g[l, :]); bcast_load(P, SP, b2, "b2", ln_b[l, :])
    acc = [P.sb([128, D], F32, "acc") for _ in range(3)]
    xs = [P.sb([128, D], F32, "cxs") for _ in range(3)]
    st = P.sb([128, 4, 6], F32, "cst"); mv = P.sb([128, 2], F32, "cmv"); rs = P.sb([128, 2], F32, "crs")
    for i in range(t_lo, NT):
        b = i % 3
        r = 1 if i < 2 else 0
        A, XS = acc[b], xs[b]
        kA, kXS = ("acc", b), ("cxs", b)
        P.D(SP, [("X1", i)], [kXS], out=XS[:], in_=X1[i * 128:(i + 1) * 128, :])
        P.D(ACT, [("FFN",)], [kA], out=A[:], in_=FFN[i * 128:(i + 1) * 128, :])
        P.I(POOL, "tensor_tensor", [kA, ("gt2", r)], [kA], out=A[:], in0=A[:], in1=gt2[r][:], op=ALU.mult)
        P.I(DVE, "scalar_tensor_tensor", [kXS, kA], [kA], out=A[:], in0=XS[:], scalar=ALPHA_, in1=A[:], op0=ALU.mult, op1=ALU.add)
        ln_apply(P, C, A, kA, A, kA, st, mv, rs, "c1", g2[:], "g2", b2[:], "b2")
        o0 = i * 128 - out_off
        P.D(SP, [kA], [("X", i)], out=OUT[o0:o0 + 128, :], in_=A[:])
    P.barrier()
    P.sb_reset(m0)


def _host_consts():
    c = {}
    c["ident"] = np.eye(128, dtype=np.float32)
    kp = np.arange(128)[:, None]; qp = np.arange(128)[None, :]
    c["mprev"] = np.tile((kp >= qp).astype(np.float32), (1, 4))
    c["mnext"] = np.tile((kp <= qp).astype(np.float32), (1, 4))
    nf = 32
    inv = (10000.0 ** (-np.arange(nf, dtype=np.float32) / nf)).astype(np.float32)
    t = np.arange(4096)
    rows = (t // 64).astype(np.float32); cols = (t % 64).astype(np.float32)
    ang_r = rows[:, None] * inv[None, :]; ang_c = cols[:, None] * inv[None, :]
    cos = np.concatenate([np.cos(ang_r), np.cos(ang_r), np.cos(ang_c), np.cos(ang_c)], 1)
    sin = np.concatenate([-np.sin(ang_r), np.sin(ang_r), -np.sin(ang_c), np.sin(ang_c)], 1)
    c["cos"] = cos.reshape(32, 128, 128).astype(np.float32)
    c["sin"] = sin.reshape(32, 128, 128).astype(np.float32)
    s_ = np.arange(8, dtype=np.float32)
    er = np.concatenate([7 - s_, s_ - 7, s_ + 1, s_, -s_, 8 - s_]).astype(np.float32)
    c["erow"] = np.tile(er[None, :], (128, 1))
    E = np.zeros((8, 128, 240), np.float32)
    for r in range(8):
        for j in range(16):
            E[r, r * 16 + j, 7 * 16 + j] = 1.0
    c["E"] = E
    sb = np.arange(128)[:, None] // 16; tb = np.arange(128)[None, :] // 16
    c["toemf"] = (tb >= sb).astype(np.float32)
    c["toemb"] = (sb >= tb).astype(np.float32)
    a = np.arange(128)
    same = (a[:, None] // 64) == (a[None, :] // 64)
    le = a[:, None] <= a[None, :]
    ge = a[:, None] >= a[None, :]
    c["trif"] = (same & le).astype(np.float32) * (-1.0 / 16.0)
    c["trib"] = (same & ge).astype(np.float32) * (-1.0 / 16.0)
    c["blk"] = same.astype(np.float32) * (-1.0 / 16.0)
    c["cind"] = np.stack([(a < 64), (a >= 64)], 1).astype(np.float32) * (-1.0 / 16.0)
    c["gmaskf"] = np.tile((same & le).astype(np.float32), (1, 4))
    c["gmaskb"] = np.tile((same & ge).astype(np.float32), (1, 4))
    c["ones"] = np.ones((128, 128), np.float32)
    c["strict"] = (a[:, None] < a[None, :]).astype(np.float32)
    c["tgt"] = np.tile(np.concatenate([np.full(16, 32.0), np.full(16, 512.0)])[None, :], (128, 1)).astype(np.float32)
    c["tokid"] = (np.arange(34)[None, :] * 128 + np.arange(128)[:, None]).astype(np.float32)
    c["trash"] = (4352 + np.arange(5)[None, :] * 128 + np.arange(128)[:, None]).astype(np.float32)
    return c
HOSTC = _host_consts()


S5N = ["lam_re", "lam_im", "log_dt", "b_re", "b_im", "c_re", "c_im", "d", "w_glu", "b_glu"]
GLN = ["w_gate", "b_gate", "norm_g"]


def _decl_consts(P):
    return {k: P.dram("c_" + k, list(v.shape), F32, kind="ExternalInput") for k, v in HOSTC.items()}


def build_mod():
    P = Prog(); C = Ctx()
    cst = {"ident": P.dram("c_ident", [128, 128], F32, kind="ExternalInput")}
    c_in = P.dram("c_in", [5, D], F32, kind="ExternalInput")
    w_ada = P.dram("w_ada", [2, D, 1536], F32, kind="ExternalInput")
    b_ada = P.dram("b_ada", [2, 1536], F32, kind="ExternalInput")
    MODS = P.dram("MODS", [2, 5, 1536], F32, kind="ExternalOutput")
    setup_consts(P, C, cst)
    phase_mod(P, C, c_in, w_ada, b_ada, MODS, R=5, NBLK=3)
    P.finish()
    return P.emit()


def build_layer(with_combine, shapes):
    P = Prog(); C = Ctx()
    cst = _decl_consts(P)
    ins = {k: P.dram(k, list(s), F32, kind="ExternalInput") for k, s in shapes.items()}
    MODV = P.dram("MODV", [1, 2, 6 * D], F32, kind="ExternalInput")
    setup_consts(P, C, cst)
    if with_combine:
        X1p = P.dram("X1p", [NTOK, D], F32, kind="ExternalInput")
        YEp = P.dram("YEp", [16, 545, D], F32, kind="ExternalInput")
        IDXp = P.dram("IDXp", [128, NT, 16], I32, kind="ExternalInput")
        MODVp = P.dram("MODVp", [1, 2, 6 * D], F32, kind="ExternalInput")
        l2g = P.dram("ln2_g", [1, D], F32, kind="ExternalInput")
        l2b = P.dram("ln2_b", [1, D], F32, kind="ExternalInput")
        X = P.dram("X2s", [NTOK, D], F32)
        phase_combine(P, C, X1p, YEp, IDXp, X, MODVp, l2g, l2b, 0, 0)
    else:
        X = P.dram("xin", [NTOK, D], F32, kind="ExternalInput")
    PROJ = P.dram("PROJ", [NTOK, NIN], F32)
    MIXT = P.dram("MIXT", [16, 128, NTOK], BF16)
    OF = P.dram("OF", [NTOK, 512], F32)
    H2R = P.dram("H2R", [NTOK, ROWW], F32)
    X1 = P.dram("X1", [NTOK, D], F32, kind="ExternalOutput")
    XE = P.dram("XE", [16, NSLOT, ROWW], F32, kind="ExternalOutput")
    IDXC = P.dram("IDXC", [128, NT, 16], I32, kind="ExternalOutput")
    phase_inproj(P, C, 0, X, PROJ, ins["w_in"], MODV)
    phase_attn(P, C, 0, PROJ, MIXT, ins["attn_sink"], cst)
    phase_s5(P, C, 0, PROJ, MIXT, {n: ins["ssm_" + n] for n in S5N}, cst)
    phase_gla(P, C, 0, PROJ, MIXT, OF, {n: ins["gla_" + n] for n in GLN}, cst)
    AFF = P.sb([128, NT, 16], F32, "AFF")
    IDXS = P.sb([128, NT, 16], I32, "IDXS")
    phase_wout(P, C, 0, X, MIXT, X1, H2R, AFF, ins["w_out"], MODV, ins["ln1_g"], ins["ln1_b"], ins["router"])
    phase_route(P, C, AFF, IDXS, IDXC, cst)
    phase_scatter(P, C, H2R, XE, IDXS)
    P.finish()
    return P.emit()


def build_experts():
    P = Prog(); C = Ctx()
    cst = {"ident": P.dram("c_ident", [128, 128], F32, kind="ExternalInput")}
    XEc = P.dram("XEc", [4, 2, NSLOT, ROWW], F32, kind="ExternalInput")
    GATE = P.dram("GATE", [4, 2, NSLOT], F32, kind="ExternalInput")
    WG = P.dram("WG", [2, D, D], F32, kind="ExternalInput")
    WU = P.dram("WU", [2, D, D], F32, kind="ExternalInput")
    WD = P.dram("WD", [2, D, D], F32, kind="ExternalInput")
    YE = P.dram("YE", [4, 2, NSLOT, D], F32, kind="ExternalOutput")
    setup_consts(P, C, cst)
    wmap = {"gate": WG, "up": WU, "down": WD}
    phase_experts(P, C, 4, 2, lambda b, el, r0, r1: XEc[b, el, r0:r1, 0:D], lambda b, el, r0, r1: GATE[b, el, r0:r1],
                  lambda kind, el: wmap[kind][el], lambda b, el, r0, r1: YE[b, el, r0:r1, :])
    P.finish()
    return P.emit()


def build_final():
    P = Prog(); C = Ctx()
    cst = {"ident": P.dram("c_ident", [128, 128], F32, kind="ExternalInput")}
    X1p = P.dram("X1p", [NTOK, D], F32, kind="ExternalInput")
    YEp = P.dram("YEp", [16, 545, D], F32, kind="ExternalInput")
    IDXp = P.dram("IDXp", [128, NT, 16], I32, kind="ExternalInput")
    MODVp = P.dram("MODVp", [1, 2, 6 * D], F32, kind="ExternalInput")
    l2g = P.dram("ln2_g", [1, D], F32, kind="ExternalInput")
    l2b = P.dram("ln2_b", [1, D], F32, kind="ExternalInput")
    OUT = P.dram("OUT", [4096, D], F32, kind="ExternalOutput")
    setup_consts(P, C, cst)
    phase_combine(P, C, X1p, YEp, IDXp, OUT, MODVp, l2g, l2b, 2, 256)
    P.finish()
    return P.emit()


LAYER_KEYS = ["w_in", "attn_sink", "ssm_lam_re", "ssm_lam_im", "ssm_log_dt", "ssm_b_re", "ssm_b_im", "ssm_c_re", "ssm_c_im",
              "ssm_d", "ssm_w_glu", "ssm_b_glu", "gla_w_gate", "gla_b_gate", "gla_norm_g", "w_out", "ln1_g", "ln1_b", "router"]


def kernel_multi(**inp):
    inp = {k: np.ascontiguousarray(np.asarray(v)) for k, v in inp.items()}
    f32 = np.float32
    cmap = {"c_" + k: v for k, v in HOSTC.items()}
    ident = {"c_ident": HOSTC["ident"]}
    c_in = np.concatenate([inp["c"], inp["c_ctx"][None]], 0).astype(f32)
    maps = []
    for c in range(8):
        sl = slice(c * 1536, (c + 1) * 1536)
        maps.append(dict(ident, c_in=c_in, w_ada=np.ascontiguousarray(inp["w_ada"][:, :, sl]), b_ada=np.ascontiguousarray(inp["b_ada"][:, sl])))
    res = run_bass_kernel_spmd(build_mod(), maps, core_ids=list(range(8)))
    mods = np.concatenate([r["MODS"] for r in res.results], axis=2)
    modv = [[np.ascontiguousarray(np.stack([mods[l, b], mods[l, 4]], 0)[None]) for l in range(2)] for b in range(4)]
    shapes = {k: (1,) + inp[k].shape[1:] for k in LAYER_KEYS}
    prev = None
    nc_exp = None
    for l in range(2):
        lw = {k: np.ascontiguousarray(inp[k][l:l + 1]) for k in LAYER_KEYS}
        maps = []
        for b in range(4):
            m = dict(cmap); m.update(lw); m["MODV"] = modv[b][l]
            if l == 0:
                m["xin"] = np.concatenate([inp["ctx"][b], inp["x"][b]], 0)
            else:
                m.update(prev[b])
            maps.append(m)
        res = run_bass_kernel_spmd(build_layer(l > 0, shapes), maps, core_ids=list(range(4)))
        X1 = [res.results[b]["X1"] for b in range(4)]
        XE = [res.results[b]["XE"] for b in range(4)]
        IDX = [res.results[b]["IDXC"] for b in range(4)]
        maps = []
        for c in range(8):
            xec = np.stack([XE[b][2 * c:2 * c + 2] for b in range(4)], 0)
            gate = np.stack([np.stack([XE[b][2 * c + el, :, 2048 + 2 * c + el] for el in range(2)], 0) for b in range(4)], 0)
            maps.append(dict(ident, XEc=np.ascontiguousarray(xec), GATE=np.ascontiguousarray(gate),
                             WG=np.ascontiguousarray(inp["exp_w_gate"][l, 2 * c:2 * c + 2]),
                             WU=np.ascontiguousarray(inp["exp_w_up"][l, 2 * c:2 * c + 2]),
                             WD=np.ascontiguousarray(inp["exp_w_down"][l, 2 * c:2 * c + 2])))
        if nc_exp is None:
            nc_exp = build_experts()
        res = run_bass_kernel_spmd(nc_exp if l == 0 else build_experts(), maps, core_ids=list(range(8)))
        prev = []
        for b in range(4):
            yep = np.zeros((16, 545, D), f32)
            for c in range(8):
                yep[2 * c:2 * c + 2, :544] = res.results[c]["YE"][b]
            prev.append({"X1p": X1[b], "YEp": yep, "IDXp": IDX[b], "MODVp": modv[b][l],
                         "ln2_g": np.ascontiguousarray(inp["ln2_g"][l:l + 1]), "ln2_b": np.ascontiguousarray(inp["ln2_b"][l:l + 1])})
    maps = [dict(ident, **prev[b]) for b in range(4)]
    res = run_bass_kernel_spmd(build_final(), maps, core_ids=list(range(4)))
    return np.stack([res.results[b]["OUT"] for b in range(4)], 0).astype(f32)


FUSED_KEYS = LAYER_KEYS + ["ln2_g", "ln2_b", "w_ada", "b_ada", "exp_w_gate", "exp_w_up", "exp_w_down"]


def build_fused(shapes):
    P = Prog(); C = Ctx()
    cst = _decl_consts(P)
    ins = {k: P.dram(k, list(shapes[k]), F32, kind="ExternalInput") for k in FUSED_KEYS}
    xin = P.dram("xin", [NTOK, D], F32, kind="ExternalInput")
    c_in = P.dram("c_in", [2, D], F32, kind="ExternalInput")
    OUT = P.dram("OUT", [4096, D], F32, kind="ExternalOutput")
    MODV = P.dram("MODV", [2, 2, 6 * D], F32)
    PROJ = P.dram("PROJ", [NTOK, NIN], F32)
    MIXT = P.dram("MIXT", [16, 128, NTOK], BF16)
    OF = P.dram("OF", [NTOK, 512], F32)
    H2R = P.dram("H2R", [NTOK, ROWW], F32)
    X1 = P.dram("X1", [NTOK, D], F32)
    XN = P.dram("XN", [NTOK, D], F32)
    XE = P.dram("XE", [16, NSLOT, ROWW], F32)
    FFN = P.dram("FFN", [NTOK + NSLOT, D], F32)
    IDXC = P.dram("IDXC", [128, NT, 16], I32)
    setup_consts(P, C, cst)
    phase_mod(P, C, c_in, ins["w_ada"], ins["b_ada"], MODV, R=2, NBLK=24)
    X = xin
    for l in range(2):
        phase_inproj(P, C, l, X, PROJ, ins["w_in"], MODV)
        phase_attn(P, C, l, PROJ, MIXT, ins["attn_sink"], cst)
        phase_s5(P, C, l, PROJ, MIXT, {n: ins["ssm_" + n] for n in S5N}, cst)
        phase_gla(P, C, l, PROJ, MIXT, OF, {n: ins["gla_" + n] for n in GLN}, cst)
        m0 = P.sb_mark()
        AFF = P.sb([128, NT, 16], F32, "AFF")
        IDXS = P.sb([128, NT, 16], I32, "IDXS")
        phase_wout(P, C, l, X, MIXT, X1, H2R, AFF, ins["w_out"], MODV, ins["ln1_g"], ins["ln1_b"], ins["router"], cst)
        phase_route(P, C, AFF, IDXS, IDXC, cst)
        phase_scatter(P, C, H2R, XE, IDXS, cst)
        P.sb_reset(m0)
        phase_experts(P, C, 1, 16, lambda b, el, r0, r1: XE[el, r0:r1, 0:D], lambda b, el, r0, r1: XE[el, r0:r1, 2048 + el],
                      lambda kind, el, l=l: ins["exp_w_" + kind][l, el], None,
                      scat={"FFN": FFN, "tok_ap": lambda b, el, r0, r1: XE[el, r0:r1, 2064]})
        if l == 0:
            phase_ln2(P, C, X1, FFN, XN, MODV, ins["ln2_g"], ins["ln2_b"], 0, 0, l)
            X = XN
        else:
            phase_ln2(P, C, X1, FFN, OUT, MODV, ins["ln2_g"], ins["ln2_b"], 2, 256, l)
    P.finish()
    nc = P.emit()
    return nc


def fused_maps(inp, samples):
    cmap = {"c_" + k: v for k, v in HOSTC.items()}
    shared = {k: inp[k] for k in FUSED_KEYS}
    maps = []
    for b in samples:
        m = dict(cmap); m.update(shared)
        m["xin"] = np.concatenate([inp["ctx"][b], inp["x"][b]], 0)
        m["c_in"] = np.ascontiguousarray(np.stack([inp["c"][b], inp["c_ctx"]], 0))
        maps.append(m)
    return maps


def kernel(**inp):
    inp = {k: np.ascontiguousarray(np.asarray(v)) for k, v in inp.items()}
    shapes = {k: inp[k].shape for k in FUSED_KEYS}
    nc = build_fused(shapes)
    maps = fused_maps(inp, [c % 4 for c in range(8)])
    res = run_bass_kernel_spmd(nc, maps, core_ids=list(range(8)))
    return np.stack([res.results[b]["OUT"] for b in range(4)], 0).astype(np.float32)
```

```python
import numpy as np
import concourse.bass as bass
import concourse.mybir as mybir
from concourse.bass_utils import run_bass_kernel_spmd

F32 = mybir.dt.float32
BF16 = mybir.dt.bfloat16
I32 = mybir.dt.int32
ALU = mybir.AluOpType
AF = mybir.ActivationFunctionType
AX = mybir.AxisListType

PE, ACT, DVE, POOL, SP = "pe", "act", "dve", "pool", "sp"
ENGS = (PE, ACT, DVE, POOL, SP)
NDMASEM = 8


class Prog:
    def __init__(self):
        self.nc = bass.Bass("TRN2", target_bir_lowering=False)
        self.ops = {e: [] for e in ENGS}
        self.state = {}
        self.dmas = []
        self.ndma = {e: 0 for e in ENGS}
        self.sb_off = 20608
        self.sb_hi = 0
        self.nname = 0
        self.psum = []
        self.sb_cap = 229376

    def dram(self, name, shape, dtype, kind="Internal"):
        return self.nc.dram_tensor(name, list(shape), dtype, kind=kind)

    def sb(self, shape, dtype, name=None):
        size = int(np.prod(shape[1:])) * mybir.dt.size(dtype) if hasattr(mybir.dt, "size") else None
        if size is None:
            size = int(np.prod(shape[1:])) * {F32: 4, BF16: 2, I32: 4}[dtype]
        size = (size + 31) // 32 * 32
        self.nname += 1
        nm = (name or "t") + "_%d" % self.nname
        t = self.nc.alloc_sbuf_tensor_at(nm, list(shape), dtype, offset=self.sb_off)
        self.sb_off += size
        assert self.sb_off <= self.sb_cap, ("SBUF overflow", nm, self.sb_off)
        self.sb_hi = max(self.sb_hi, self.sb_off)
        return t

    def sb_mark(self):
        return self.sb_off

    def sb_reset(self, mark):
        self.sb_off = mark

    @staticmethod
    def _conf(a, b):
        n = min(len(a), len(b))
        return a[:n] == b[:n]

    def _deps(self, reads, writes):
        deps = set()
        for k in reads:
            root = self.state.setdefault(k[0], {})
            for k2, st in root.items():
                if self._conf(k, k2) and st[0] is not None:
                    deps.add(st[0])
        for k in writes:
            root = self.state.setdefault(k[0], {})
            for k2, st in root.items():
                if self._conf(k, k2):
                    if st[0] is not None:
                        deps.add(st[0])
                    deps.update(st[1])
        return deps

    def _commit(self, ev, reads, writes):
        for k in reads:
            root = self.state[k[0]]
            st = root.setdefault(k, [None, []])
            st[1].append(ev)
        for k in writes:
            root = self.state[k[0]]
            for k2 in [k2 for k2 in root if len(k2) > len(k) and self._conf(k, k2)]:
                del root[k2]
            root[k] = [ev, []]

    @staticmethod
    def _norm(keys):
        out = []
        for k in keys:
            if isinstance(k, str):
                k = (k,)
            assert isinstance(k[0], str), k
            out.append(tuple(k))
        return out

    def I(self, eng, name, reads, writes, *a, **kw):
        return self.op(eng, lambda e: getattr(e, name)(*a, **kw), reads, writes)

    def D(self, q, reads, writes, **kw):
        return self.dma(q, lambda e: e.dma_start(**kw), reads, writes)

    def op(self, eng, fn, reads=(), writes=()):
        reads = self._norm(reads)
        writes = self._norm(writes)
        deps = self._deps(reads, writes)
        idx = len(self.ops[eng])
        ev = ("c", eng, idx)
        self.ops[eng].append(dict(fn=fn, waits=deps, dma=None, signal=False))
        self._commit(ev, reads, writes)
        return ev

    def dma(self, q, fn, reads=(), writes=()):
        reads = self._norm(reads)
        writes = self._norm(writes)
        deps = self._deps(reads, writes)
        j = self.ndma[q]
        self.ndma[q] += 1
        si, val = j % NDMASEM, 16 * (j // NDMASEM + 1)
        did = len(self.dmas)
        self.dmas.append((q, si, val))
        if j >= NDMASEM:
            deps.add(("d", self._last_dma[(q, si)]))
        if not hasattr(self, "_last_dma"):
            self._last_dma = {}
        self._last_dma[(q, si)] = did
        ev = ("d", did)
        self.ops[q].append(dict(fn=fn, waits=deps, dma=did, signal=False))
        self._commit(ev, reads, writes)
        return ev

    def barrier(self):
        evs = set()
        for e in ENGS:
            for i in range(len(self.ops[e]) - 1, -1, -1):
                o = self.ops[e][i]
                if o["fn"] is not None and o["dma"] is None:
                    evs.add(("c", e, i))
                    break
        if hasattr(self, "_last_dma"):
            for did in self._last_dma.values():
                evs.add(("d", did))
        for e in ENGS:
            self.ops[e].append(dict(fn=None, waits=set(evs), dma=None, signal=False))
        self.state = {}

    def emit(self):
        nc = self.nc
        plan = {e: [] for e in ENGS}
        for e in ENGS:
            wc = {}
            wd = {}
            for i, o in enumerate(self.ops[e]):
                need_c, need_d = {}, {}
                for ev in o["waits"]:
                    if ev[0] == "c":
                        _, se, si_ = ev
                        if se == e and e == PE:
                            continue
                        if se == e and si_ >= i:
                            continue
                        if wc.get(se, -1) >= si_:
                            continue
                        need_c[se] = max(need_c.get(se, -1), si_)
                    else:
                        q, si_, val = self.dmas[ev[1]]
                        if wd.get((q, si_), 0) >= val:
                            continue
                        need_d[(q, si_)] = max(need_d.get((q, si_), 0), val)
                for se, si_ in need_c.items():
                    wc[se] = si_
                    self.ops[se][si_]["signal"] = True
                for k, v in need_d.items():
                    wd[k] = v
                plan[e].append((need_c, need_d))
        semval = {}
        for e in ENGS:
            c = 0
            for i, o in enumerate(self.ops[e]):
                if o["signal"]:
                    c += 1
                    semval[(e, i)] = c
            self.nsig = getattr(self, "nsig", {})
            self.nsig[e] = c
        from contextlib import ExitStack
        with ExitStack() as es:
            csem = {e: es.enter_context(nc.semaphore("c_" + e)) for e in ENGS}
            dsem = {(q, s): es.enter_context(nc.semaphore("d_%s_%d" % (q, s)))
                    for q in ENGS for s in range(NDMASEM) if self.ndma[q] > s}
            block = es.enter_context(nc.Block())

            def run(e, engobj):
                for i, o in enumerate(self.ops[e]):
                    need_c, need_d = plan[e][i]
                    for se, si_ in need_c.items():
                        engobj.wait_ge(csem[se], semval[(se, si_)])
                    for k, v in need_d.items():
                        engobj.wait_ge(dsem[k], v)
                    if o["fn"] is None:
                        continue
                    inst = o["fn"](engobj)
                    if o["dma"] is not None:
                        q, s, v = self.dmas[o["dma"]]
                        inst.then_inc(dsem[(q, s)], 16)
                    elif o["signal"]:
                        inst.then_inc(csem[e], 1)

            @block.tensor
            def _(eng):
                run(PE, eng)

            @block.scalar
            def _(eng):
                run(ACT, eng)

            @block.vector
            def _(eng):
                run(DVE, eng)

            @block.gpsimd
            def _(eng):
                run(POOL, eng)

            @block.sync
            def _(eng):
                run(SP, eng)
        return nc

    def finish(self):
        self.barrier()


NT = 34
NTOK = 4352
D = 2048
NIN = 3616


def alt(i):
    return ACT if i % 2 == 0 else DVE


def copy_op(P, eng, out, in_, reads, writes):
    if eng == ACT:
        P.op(ACT, lambda e: e.copy(out=out, in_=in_), reads, writes)
    else:
        P.op(eng, lambda e: e.tensor_copy(out=out, in_=in_), reads, writes)


class Ctx:
    pass


def setup_consts(P, C, cst):
    C.ident = P.sb([128, 128], F32, "ident")
    C.identb = P.sb([128, 128], BF16, "identb")
    P.dma(SP, lambda e: e.dma_start(out=C.ident[:], in_=cst["ident"][:, :]), writes=["ident"])
    P.op(DVE, lambda e: e.tensor_copy(out=C.identb[:], in_=C.ident[:]), reads=["ident"], writes=["identb"])
    C.ps = [P.nc.alloc_psum_tensor("psb%d" % i, [128, 512], F32) for i in range(8)]
    C.nbank = 0

    def bank():
        b = C.nbank % 8
        C.nbank += 1
        return b
    C.bank = bank


def phase_mod(P, C, c_in, w_ada, b_ada, MODS, R=5, NBLK=3):
    m0 = P.sb_mark()
    cT = P.sb([128, 16, R], F32, "cT")
    for r in range(R):
        P.D(SP, [], [("cT", r)], out=cT[:, :, r], in_=c_in[r, :].rearrange("(kt p) -> p kt", p=128), allow_slow_non_contiguous=True)
    P.I(ACT, "activation", ["cT"], ["cT"], out=cT[:], in_=cT[:], func=AF.Silu)
    wts = [P.sb([128, 16, 512], F32, "wada") for _ in range(2)]
    bts = [P.sb([R, 512], F32, "bada") for _ in range(2)]
    rts = [P.sb([R, 512], F32, "rada") for _ in range(2)]
    it = 0
    for l in range(2):
        for nb in range(NBLK):
            b = it % 2
            it += 1
            for kh in range(2):
                P.D(SP if kh == 0 else ACT, [], [("wada", b, kh)], out=wts[b][:, kh * 8:(kh + 1) * 8, :],
                    in_=w_ada[l, kh * 1024:(kh + 1) * 1024, nb * 512:(nb + 1) * 512].rearrange("(kt p) n -> p kt n", p=128))
            P.D(SP, [], [("bada", b)], out=bts[b][:], in_=b_ada[l, nb * 512:(nb + 1) * 512].partition_broadcast(R))
            bk = C.bank()
            for kt in range(16):
                P.I(PE, "matmul", ["cT", ("wada", b)], [("ps", bk)], C.ps[bk][0:R, :], lhsT=cT[:, kt, :], rhs=wts[b][:, kt, :], start=(kt == 0), stop=(kt == 15))
            P.I(DVE, "tensor_tensor", [("bada", b)], [("ps", bk), ("rada", b)], out=rts[b][:], in0=C.ps[bk][0:R, :], in1=bts[b][:], op=ALU.add)
            P.D(SP, [("rada", b)], [("MODS", l, nb)], out=MODS[l, :, nb * 512:(nb + 1) * 512], in_=rts[b][:])
    P.barrier()
    P.sb_reset(m0)


def ln_stats(P, C, xt, key, st, mv, rs, tag):
    for j in range(4):
        P.op(DVE, lambda e, j=j: e.bn_stats(out=st[:, j, :], in_=xt[:, j * 512:(j + 1) * 512]),
             reads=[key], writes=[(tag + "st", j)])
    P.op(DVE, lambda e: e.bn_aggr(out=mv[:], in_=st[:].rearrange("p a b -> p (a b)")), reads=[tag + "st"], writes=[tag + "mv"])
    P.op(DVE, lambda e: e.tensor_scalar_add(out=rs[:, 0:1], in0=mv[:, 1:2], scalar1=1e-6), reads=[tag + "mv"], writes=[(tag + "rs", 0)])
    P.op(ACT, lambda e: e.activation(out=rs[:, 0:1], in_=rs[:, 0:1], func=AF.Ln), reads=[(tag + "rs", 0)], writes=[(tag + "rs", 0)])
    P.op(ACT, lambda e: e.activation(out=rs[:, 0:1], in_=rs[:, 0:1], func=AF.Exp, scale=-0.5), reads=[(tag + "rs", 0)], writes=[(tag + "rs", 0)])
    P.op(DVE, lambda e: e.scalar_tensor_tensor(out=rs[:, 1:2], in0=mv[:, 0:1], scalar=-1.0, in1=rs[:, 0:1], op0=ALU.mult, op1=ALU.mult),
         reads=[tag + "mv", (tag + "rs", 0)], writes=[(tag + "rs", 1)])


def load_modT(P, MODV, l, chunk, dst, key, plus1):
    for r in range(2):
        P.dma(SP, lambda e, r=r: e.dma_start(out=dst[:, r, :], in_=MODV[l, r, chunk * 2048:(chunk + 1) * 2048].rearrange("(kt p) -> p kt", p=128),
                                            allow_slow_non_contiguous=True), reads=[("MODV", l)], writes=[(key, r)])
    if plus1:
        P.op(DVE, lambda e: e.tensor_scalar_add(out=dst[:], in0=dst[:], scalar1=1.0), reads=[key], writes=[key])


def phase_inproj(P, C, l, X, PROJ, w_in, MODV):
    m0 = P.sb_mark()
    wbf = P.sb([128, 16, NIN], BF16, "wbf")
    for kt in range(16):
        for h in range(2):
            P.dma(POOL, lambda e, kt=kt, h=h: e.dma_start(out=wbf[:, kt, h * 1808:(h + 1) * 1808],
                                                         in_=w_in[l, kt * 128:(kt + 1) * 128, h * 1808:(h + 1) * 1808]),
                  writes=[("wbf", kt, h)])
    scT = P.sb([128, 2, 16], F32, "scT")
    shT = P.sb([128, 2, 16], F32, "shT")
    load_modT(P, MODV, l, 1, scT, "scT", True)
    load_modT(P, MODV, l, 0, shT, "shT", False)
    xts = [P.sb([128, D], F32, "xt") for _ in range(2)]
    xns = [P.sb([128, D], BF16, "xn") for _ in range(2)]
    hTs = [P.sb([128, 16, 128], BF16, "hT") for _ in range(2)]
    ots = [P.sb([128, NIN], F32, "ot") for _ in range(2)]
    st = P.sb([128, 4, 6], F32, "st")
    mv = P.sb([128, 2], F32, "mv")
    rs = P.sb([128, 2], F32, "rs")
    for i in range(NT):
        b = i % 2
        r = 1 if i < 2 else 0
        xt, xn, hT, ot = xts[b], xns[b], hTs[b], ots[b]
        P.dma(SP, lambda e, i=i, xt=xt: e.dma_start(out=xt[:], in_=X[i * 128:(i + 1) * 128, :]), reads=[("X", i)], writes=[("xt", b)])
        ln_stats(P, C, xt, ("xt", b), st, mv, rs, "ip")
        P.op(ACT, lambda e, xt=xt, xn=xn: e.activation(out=xn[:], in_=xt[:], func=AF.Identity, bias=rs[:, 1:2], scale=rs[:, 0:1]),
             reads=[("xt", b), "iprs"], writes=[("xn", b)])
        for kg in range(4):
            bk = C.bank()
            psb = C.ps[bk][:].bitcast(BF16)
            for j in range(4):
                kt = kg * 4 + j
                P.op(PE, lambda e, j=j, kt=kt, psb=psb, xn=xn: e.transpose(out=psb[:, j * 128:(j + 1) * 128], in_=xn[:, kt * 128:(kt + 1) * 128], identity=C.identb[:]),
                     reads=[("xn", b), "identb"], writes=[("ps", bk)])
            for j in range(4):
                kt = kg * 4 + j
                if j % 2 == 0:
                    P.op(ACT, lambda e, j=j, kt=kt, psb=psb, hT=hT, r=r: e.activation(out=hT[:, kt, :], in_=psb[:, j * 128:(j + 1) * 128], func=AF.Identity,
                                                                              bias=shT[:, r, kt:kt + 1], scale=scT[:, r, kt:kt + 1]),
                         reads=["scT", "shT"], writes=[("ps", bk), ("hT", b, kt)])
                else:
                    P.op(DVE, lambda e, j=j, kt=kt, psb=psb, hT=hT, r=r: e.tensor_scalar(out=hT[:, kt, :], in0=psb[:, j * 128:(j + 1) * 128],
                                                                                 scalar1=scT[:, r, kt:kt + 1], scalar2=shT[:, r, kt:kt + 1], op0=ALU.mult, op1=ALU.add),
                         reads=["scT", "shT"], writes=[("ps", bk), ("hT", b, kt)])
        for nb in range(8):
            n0 = nb * 512
            w = min(512, NIN - n0)
            bk = C.bank()
            for kt in range(16):
                P.op(PE, lambda e, kt=kt, bk=bk, n0=n0, w=w, hT=hT: e.matmul(C.ps[bk][:, 0:w], lhsT=hT[:, kt, :], rhs=wbf[:, kt, n0:n0 + w],
                                                                          start=(kt == 0), stop=(kt == 15)),
                     reads=[("hT", b), ("wbf", kt)], writes=[("ps", bk)])
            copy_op(P, alt(nb), ot[:, n0:n0 + w], C.ps[bk][:, 0:w], reads=[], writes=[("ps", bk), ("ot", b, nb)])
        P.dma(SP, lambda e, i=i, ot=ot: e.dma_start(out=PROJ[i * 128:(i + 1) * 128, :], in_=ot[:]), reads=[("ot", b)], writes=[("PROJ", i)])
    P.barrier()
    P.sb_reset(m0)


def phase_attn(P, C, l, PROJ, MIXT, sink, cst):
    m0 = P.sb_mark()
    kT = P.sb([128, 2, NTOK], BF16, "kT")
    vbf = P.sb([128, NT, 256], BF16, "vbf")
    onesb = P.sb([128, 128], BF16, "onesb")
    P.op(POOL, lambda e: e.memset(onesb[:], 1.0), writes=["onesb"])
    mk = []
    for nm in ("mprev", "mnext"):
        tf = P.sb([128, 512], F32, nm + "f")
        tb = P.sb([128, 512], BF16, nm + "b")
        P.dma(SP, lambda e, tf=tf, nm=nm: e.dma_start(out=tf[:], in_=cst[nm][:, :]), writes=[nm + "f"])
        P.op(DVE, lambda e, tf=tf, tb=tb: e.tensor_copy(out=tb[:], in_=tf[:]), reads=[nm + "f"], writes=[nm + "b"])
        mk.append(tb)
    esb = P.sb([128, 8], F32, "esb")
    P.dma(SP, lambda e: e.dma_start(out=esb[:], in_=sink[l, :].partition_broadcast(128)), writes=["esb"])
    P.op(ACT, lambda e: e.activation(out=esb[:], in_=esb[:], func=AF.Exp), reads=["esb"], writes=["esb"])
    cos = [P.sb([128, 128], F32, "cos") for _ in range(2)]
    sin = [P.sb([128, 128], F32, "sin") for _ in range(2)]

    def load_rope(i, b):
        P.dma(SP, lambda e: e.dma_start(out=cos[b][:], in_=cst["cos"][i - 2, :, :]), writes=[("cos", b)])
        P.dma(SP, lambda e: e.dma_start(out=sin[b][:], in_=cst["sin"][i - 2, :, :]), writes=[("sin", b)])

    def rope(x, xo, t, H, b, kx, ko, kt_):
        xv = x[:, 0:H * 128].rearrange("p (h a c d) -> p h a c d", h=H, a=2, c=2)
        tv = t[:, 0:H * 128].rearrange("p (h a c d) -> p h a c d", h=H, a=2, c=2)
        sv = sin[b][:].rearrange("p (a c d) -> p a c d", a=2, c=2)
        x3 = x[:, 0:H * 128].rearrange("p (h e) -> p h e", h=H)
        o3 = xo[:, 0:H * 128].rearrange("p (h e) -> p h e", h=H)
        P.I(POOL, "tensor_tensor", [kx, ("cos", b)], [ko], out=o3, in0=x3, in1=cos[b][:].unsqueeze(1).to_broadcast([128, H, 128]), op=ALU.mult)
        for c in range(2):
            P.I(DVE, "tensor_tensor", [kx, ("sin", b)], [kt_ + (c,)], out=tv[:, :, :, c, :], in0=xv[:, :, :, 1 - c, :],
                in1=sv[:, :, c, :].unsqueeze(1).to_broadcast([128, H, 2, 32]), op=ALU.mult)
        P.I(DVE, "tensor_tensor", [kt_, ko], [ko], out=xo[:, 0:H * 128], in0=xo[:, 0:H * 128], in1=t[:, 0:H * 128], op=ALU.add)

    kin = [P.sb([128, 512], F32, "kin") for _ in range(2)]
    kro = [P.sb([128, 256], F32, "kro") for _ in range(2)]
    ktm = [P.sb([128, 256], F32, "ktm") for _ in range(2)]
    for i in range(NT):
        b = i % 2
        P.dma(SP, lambda e, i=i, b=b: e.dma_start(out=kin[b][:], in_=PROJ[i * 128:(i + 1) * 128, 1024:1536]),
              reads=[("PROJ", i)], writes=[("kin", b)])
        P.op(ACT, lambda e, i=i, b=b: e.copy(out=vbf[:, i, :], in_=kin[b][:, 256:512]), reads=[("kin", b)], writes=[("vbf", i)])
        if i >= 2:
            load_rope(i, b)
            rope(kin[b], kro[b], ktm[b], 2, b, ("kin", b), ("kro", b), ("ktm", b))
            src, skey = kro[b], ("kro", b)
        else:
            src, skey = kin[b], ("kin", b)
        bk = C.bank()
        for h in range(2):
            P.op(PE, lambda e, h=h, bk=bk, src=src: e.transpose(out=C.ps[bk][:, h * 128:(h + 1) * 128], in_=src[:, h * 128:(h + 1) * 128], identity=C.ident[:]),
                 reads=[skey, "ident"], writes=[("ps", bk)])
        P.op(DVE, lambda e, i=i, bk=bk: e.tensor_copy(out=kT[:, :, i * 128:(i + 1) * 128], in_=C.ps[bk][:, 0:256].rearrange("p (h t) -> p h t", h=2)),
             writes=[("ps", bk), ("kT", i)])
    qin = [P.sb([128, 1024], F32, "qin") for _ in range(2)]
    qro = [P.sb([128, 1024], F32, "qro") for _ in range(2)]
    qtm = [P.sb([128, 1024], F32, "qtm") for _ in range(2)]
    qT = [P.sb([128, 8, 128], BF16, "qT") for _ in range(2)]
    pT = [P.sb([128, 512], BF16, "pT") for _ in range(4)]
    den = [P.sb([128, 512], F32, "den") for _ in range(2)]
    oT = [P.sb([128, 4, 128], BF16, "oT") for _ in range(2)]
    npt = 0
    scale = 128.0 ** -0.5
    for iq in range(NT):
        b = iq % 2
        P.dma(SP, lambda e, iq=iq, b=b: e.dma_start(out=qin[b][:], in_=PROJ[iq * 128:(iq + 1) * 128, 0:1024]),
              reads=[("PROJ", iq)], writes=[("qin", b)])
        if iq >= 2:
            load_rope(iq, b)
            rope(qin[b], qro[b], qtm[b], 8, b, ("qin", b), ("qro", b), ("qtm", b))
            src, skey = qro[b], ("qro", b)
        else:
            src, skey = qin[b], ("qin", b)
        for g in range(2):
            bk = C.bank()
            for h in range(4):
                hh = g * 4 + h
                P.op(PE, lambda e, h=h, hh=hh, bk=bk, src=src: e.transpose(out=C.ps[bk][:, h * 128:(h + 1) * 128], in_=src[:, hh * 128:(hh + 1) * 128], identity=C.ident[:]),
                     reads=[skey, "ident"], writes=[("ps", bk)])
            copy_op(P, alt(g), qT[b][:, g * 4:(g + 1) * 4, :], C.ps[bk][:, :].rearrange("p (h t) -> p h t", h=4), reads=[], writes=[("ps", bk), ("qT", b, g)])
        if iq < 2:
            keys = [(0, None), (1, None)]
        else:
            keys = [(0, None), (1, None)]
            if iq - 1 >= 2:
                keys.append((iq - 1, 0))
            keys.append((iq, None))
            if iq + 1 < NT:
                keys.append((iq + 1, 1))
        for kvh in range(2):
            bo = C.bank()
            bd = C.bank()
            for n, (kt_, mi) in enumerate(keys):
                bs = C.bank()
                pb = npt % 4
                npt += 1
                P.op(PE, lambda e, kt_=kt_, bs=bs, kvh=kvh, b=b: e.matmul(C.ps[bs][:, :], lhsT=kT[:, kvh, kt_ * 128:(kt_ + 1) * 128],
                                                                        rhs=qT[b][:, kvh * 4:(kvh + 1) * 4, :].rearrange("p h t -> p (h t)"), start=True, stop=True),
                     reads=[("kT", kt_), ("qT", b, kvh)], writes=[("ps", bs)])
                P.op(ACT, lambda e, bs=bs, pb=pb: e.activation(out=pT[pb][:], in_=C.ps[bs][:, :], func=AF.Exp, scale=scale),
                     writes=[("ps", bs), ("pT", pb)])
                if mi is not None:
                    P.op(POOL, lambda e, pb=pb, mi=mi: e.tensor_tensor(out=pT[pb][:], in0=pT[pb][:], in1=mk[mi][:], op=ALU.mult),
                         reads=["mprevb", "mnextb"], writes=[("pT", pb)])
                st_, sp_ = (n == 0), (n == len(keys) - 1)
                P.op(PE, lambda e, kt_=kt_, bo=bo, kvh=kvh, pb=pb, st_=st_, sp_=sp_: e.matmul(C.ps[bo][:, :], lhsT=vbf[:, kt_, kvh * 128:(kvh + 1) * 128], rhs=pT[pb][:],
                                                                              start=st_, stop=sp_),
                     reads=[("vbf", kt_), ("pT", pb)], writes=[("ps", bo)])
                P.op(PE, lambda e, bd=bd, pb=pb, st_=st_, sp_=sp_: e.matmul(C.ps[bd][:, :], lhsT=onesb[:], rhs=pT[pb][:], start=st_, stop=sp_),
                     reads=["onesb", ("pT", pb)], writes=[("ps", bd)])
            db = kvh
            for h in range(4):
                hh = kvh * 4 + h
                P.op(DVE, lambda e, h=h, hh=hh, bd=bd, db=db: e.tensor_scalar_add(out=den[db][:, h * 128:(h + 1) * 128], in0=C.ps[bd][:, h * 128:(h + 1) * 128], scalar1=esb[:, hh:hh + 1]),
                     reads=["esb"], writes=[("ps", bd), ("den", db, h)])
            P.op(DVE, lambda e, db=db: e.reciprocal(out=den[db][:], in_=den[db][:]), reads=[("den", db)], writes=[("den", db)])
            P.op(DVE, lambda e, db=db, bo=bo: e.tensor_tensor(out=oT[db][:].rearrange("p h t -> p (h t)"), in0=C.ps[bo][:, :], in1=den[db][:], op=ALU.mult),
                 reads=[("den", db)], writes=[("ps", bo), ("oT", db)])
            P.dma(SP, lambda e, db=db, kvh=kvh, iq=iq: e.dma_start(out=MIXT[kvh * 4:(kvh + 1) * 4, :, iq * 128:(iq + 1) * 128].rearrange("c p t -> p c t"), in_=oT[db][:]),
                  reads=[("oT", db)], writes=[("MIXT", "att", iq, kvh)])
    P.barrier()
    P.sb_reset(m0)


import math
NCH = 544
CB = 272
TWO_PI = 2.0 * math.pi


def bc(ap, axis, shape):
    return ap.unsqueeze(axis).to_broadcast(shape)


def phase_s5(P, C, l, PROJ, MIXT, prm, cst):
    m0 = P.sb_mark()
    nsb = [0]

    def T(shape, dt=F32, nm="s5"):
        nsb[0] += 1
        return P.sb(shape, dt, nm), "%s%d" % (nm, nsb[0])

    SH = [128, 2, 16, 48]
    PWR, kPWR = T(SH); PWI, kPWI = T(SH)
    NK = 9
    AR, kAR = T([128, 2, 16, NK]); AI, kAI = T([128, 2, 16, NK]); NAI, kNAI = T([128, 2, 16, NK])
    FR, kFR = T([128, 2, 16]); FI, kFI = T([128, 2, 16])
    SB4 = [128, 2, 16, 16]
    BBR, kBBR = T(SB4); BBI, kBBI = T(SB4)
    CR, kCR = T(SB4); CI, kCI = T(SB4)
    mscr = P.sb_mark()
    LR, kLR = T([128, 2, 16]); LI, kLI = T([128, 2, 16]); DTt, kDT = T([128, 2, 16])
    BR, kBR = T([128, 2, 16, 16]); BI, kBI = T([128, 2, 16, 16])
    CRr, kCRr = T([16, 2, 16, 128]); CIr, kCIr = T([16, 2, 16, 128])
    erow, kerow = T([128, 48])
    P.D(SP, [], [kerow], out=erow[:], in_=cst["erow"][:, :])
    for d in range(2):
        for g2 in range(2):
            ps_ = slice(g2 * 64, (g2 + 1) * 64)
            P.D(SP, [], [(kLR, d, g2)], out=LR[ps_, d, :], in_=prm["lam_re"][l, d].rearrange("(gp g2) p -> g2 p gp", g2=2)[g2], allow_slow_non_contiguous=True)
            P.D(SP, [], [(kLI, d, g2)], out=LI[ps_, d, :], in_=prm["lam_im"][l, d].rearrange("(gp g2) p -> g2 p gp", g2=2)[g2], allow_slow_non_contiguous=True)
            P.D(SP, [], [(kDT, d, g2)], out=DTt[ps_, d, :], in_=prm["log_dt"][l, d].rearrange("(gp g2) -> g2 gp", g2=2)[g2].partition_broadcast(64), allow_slow_non_contiguous=True)
            P.D(SP, [], [(kBR, d, g2)], out=BR[ps_, d, :, :], in_=prm["b_re"][l, d].rearrange("(gp g2) p j -> g2 p gp j", g2=2)[g2])
            P.D(SP, [], [(kBI, d, g2)], out=BI[ps_, d, :, :], in_=prm["b_im"][l, d].rearrange("(gp g2) p j -> g2 p gp j", g2=2)[g2])
        for g2 in range(2):
            P.D(SP, [], [(kCRr, d, g2)], out=CRr[:, d, :, g2 * 64:(g2 + 1) * 64], in_=prm["c_re"][l, d].rearrange("(gp g2) i p -> g2 i gp p", g2=2)[g2])
            P.D(SP, [], [(kCIr, d, g2)], out=CIr[:, d, :, g2 * 64:(g2 + 1) * 64], in_=prm["c_im"][l, d].rearrange("(gp g2) i p -> g2 i gp p", g2=2)[g2])
    P.I(ACT, "activation", [kDT], [kDT], out=DTt[:], in_=DTt[:], func=AF.Exp)
    LRD, kLRD = T([128, 2, 16]); TH, kTH = T([128, 2, 16])
    P.I(DVE, "tensor_tensor", [kLR, kDT], [kLRD], out=LRD[:], in0=LR[:], in1=DTt[:], op=ALU.mult)
    P.I(DVE, "tensor_tensor", [kLI, kDT], [kTH], out=TH[:], in0=LI[:], in1=DTt[:], op=ALU.mult)
    SH = [128, 2, 16, 48]
    ANG, kANG = T(SH); MAG, kMAG = T(SH); TMP, kTMP = T(SH)
    eb_ = erow[:].unsqueeze(1).unsqueeze(1).to_broadcast(SH)
    P.I(DVE, "tensor_tensor", [kTH, kerow], [kANG], out=ANG[:], in0=bc(TH[:], 3, SH), in1=eb_, op=ALU.mult)
    P.I(DVE, "tensor_tensor", [kLRD, kerow], [kMAG], out=MAG[:], in0=bc(LRD[:], 3, SH), in1=eb_, op=ALU.mult)
    P.I(ACT, "activation", [kMAG], [kMAG], out=MAG[:], in_=MAG[:], func=AF.Exp)
    KI, kKI = T(SH, I32); KF, kKF = T(SH); MK, kMK = T(SH)

    def sin_rr(OUT, kOUT, off):
        P.I(DVE, "tensor_scalar_add", [kANG], [kTMP], out=TMP[:], in0=ANG[:], scalar1=off)
        P.I(DVE, "tensor_scalar_mul", [kTMP], [kKF], out=KF[:], in0=TMP[:], scalar1=1.0 / TWO_PI)
        P.I(DVE, "tensor_copy", [kKF], [kKI], out=KI[:], in_=KF[:])
        P.I(DVE, "tensor_copy", [kKI], [kKF], out=KF[:], in_=KI[:])
        P.I(DVE, "scalar_tensor_tensor", [kKF, kTMP], [kTMP], out=TMP[:], in0=KF[:], scalar=-TWO_PI, in1=TMP[:], op0=ALU.mult, op1=ALU.add)
        P.I(DVE, "tensor_single_scalar", [kTMP], [kMK], out=MK[:], in_=TMP[:], scalar=math.pi, op=ALU.is_gt)
        P.I(DVE, "scalar_tensor_tensor", [kMK, kTMP], [kTMP], out=TMP[:], in0=MK[:], scalar=-TWO_PI, in1=TMP[:], op0=ALU.mult, op1=ALU.add)
        P.I(ACT, "activation", [kTMP], [kOUT], out=OUT[:], in_=TMP[:], func=AF.Sin)
    sin_rr(PWI, kPWI, TWO_PI * 32)
    sin_rr(PWR, kPWR, TWO_PI * 32 + math.pi / 2)
    P.I(DVE, "tensor_tensor", [kPWR, kMAG], [kPWR], out=PWR[:], in0=PWR[:], in1=MAG[:], op=ALU.mult)
    P.I(DVE, "tensor_tensor", [kPWI, kMAG], [kPWI], out=PWI[:], in0=PWI[:], in1=MAG[:], op=ALU.mult)
    t1, kt1 = T([128, 2, 16]); t2, kt2 = T([128, 2, 16])
    P.I(DVE, "tensor_copy", [kPWR], [(kAR, 0)], out=AR[:, :, :, 0], in_=PWR[:, :, :, 23])
    P.I(DVE, "tensor_copy", [kPWI], [(kAI, 0)], out=AI[:, :, :, 0], in_=PWI[:, :, :, 23])
    for k in range(NK - 1):
        P.I(DVE, "tensor_tensor", [(kAR, k)], [kt1], out=t1[:], in0=AR[:, :, :, k], in1=AR[:, :, :, k], op=ALU.mult)
        P.I(DVE, "tensor_tensor", [(kAI, k)], [kt2], out=t2[:], in0=AI[:, :, :, k], in1=AI[:, :, :, k], op=ALU.mult)
        P.I(DVE, "tensor_tensor", [kt1, kt2], [(kAR, k + 1)], out=AR[:, :, :, k + 1], in0=t1[:], in1=t2[:], op=ALU.subtract)
        P.I(DVE, "scalar_tensor_tensor", [(kAR, k), (kAI, k)], [(kAI, k + 1)], out=AI[:, :, :, k + 1], in0=AR[:, :, :, k], scalar=2.0, in1=AI[:, :, :, k], op0=ALU.mult, op1=ALU.mult)
    P.I(DVE, "tensor_scalar_mul", [kAI], [kNAI], out=NAI[:], in0=AI[:], scalar1=-1.0)
    NR, kNR = T([128, 2, 16]); DEN, kDEN = T([128, 2, 16])
    P.I(DVE, "tensor_scalar_add", [kPWR], [kNR], out=NR[:], in0=PWR[:, :, :, 16], scalar1=-1.0)
    P.I(DVE, "tensor_tensor", [kLR], [kDEN], out=DEN[:], in0=LR[:], in1=LR[:], op=ALU.mult)
    P.I(DVE, "tensor_tensor", [kLI], [kt1], out=t1[:], in0=LI[:], in1=LI[:], op=ALU.mult)
    P.I(DVE, "tensor_tensor", [kDEN, kt1], [kDEN], out=DEN[:], in0=DEN[:], in1=t1[:], op=ALU.add)
    P.I(DVE, "reciprocal", [kDEN], [kDEN], out=DEN[:], in_=DEN[:])
    P.I(DVE, "tensor_tensor", [kNR, kLR], [kFR], out=FR[:], in0=NR[:], in1=LR[:], op=ALU.mult)
    P.I(DVE, "tensor_tensor", [kPWI, kLI], [kt1], out=t1[:], in0=PWI[:, :, :, 16], in1=LI[:], op=ALU.mult)
    P.I(DVE, "tensor_tensor", [kFR, kt1], [kFR], out=FR[:], in0=FR[:], in1=t1[:], op=ALU.add)
    P.I(DVE, "tensor_tensor", [kFR, kDEN], [kFR], out=FR[:], in0=FR[:], in1=DEN[:], op=ALU.mult)
    P.I(DVE, "tensor_tensor", [kPWI, kLR], [kFI], out=FI[:], in0=PWI[:, :, :, 16], in1=LR[:], op=ALU.mult)
    P.I(DVE, "tensor_tensor", [kNR, kLI], [kt1], out=t1[:], in0=NR[:], in1=LI[:], op=ALU.mult)
    P.I(DVE, "tensor_tensor", [kFI, kt1], [kFI], out=FI[:], in0=FI[:], in1=t1[:], op=ALU.subtract)
    P.I(DVE, "tensor_tensor", [kFI, kDEN], [kFI], out=FI[:], in0=FI[:], in1=DEN[:], op=ALU.mult)
    T4, kT4 = T(SB4)
    P.I(DVE, "tensor_tensor", [kFR, kBR], [kBBR], out=BBR[:], in0=bc(FR[:], 3, SB4), in1=BR[:], op=ALU.mult)
    P.I(DVE, "tensor_tensor", [kFI, kBI], [kT4], out=T4[:], in0=bc(FI[:], 3, SB4), in1=BI[:], op=ALU.mult)
    P.I(DVE, "tensor_tensor", [kBBR, kT4], [kBBR], out=BBR[:], in0=BBR[:], in1=T4[:], op=ALU.subtract)
    P.I(DVE, "tensor_tensor", [kFR, kBI], [kBBI], out=BBI[:], in0=bc(FR[:], 3, SB4), in1=BI[:], op=ALU.mult)
    P.I(DVE, "tensor_tensor", [kFI, kBR], [kT4], out=T4[:], in0=bc(FI[:], 3, SB4), in1=BR[:], op=ALU.mult)
    P.I(DVE, "tensor_tensor", [kBBI, kT4], [kBBI], out=BBI[:], in0=BBI[:], in1=T4[:], op=ALU.add)
    for (src, ksrc, dst, kdst) in ((CRr, kCRr, CR, kCR), (CIr, kCIr, CI, kCI)):
        for d in range(2):
            bk = C.bank()
            for gp in range(16):
                P.I(PE, "transpose", [ksrc, "ident"], [("ps", bk)], out=C.ps[bk][:, gp * 16:(gp + 1) * 16], in_=src[:, d, gp, :], identity=C.ident[0:16, 0:16])
            P.I(DVE, "tensor_copy", [], [("ps", bk), (kdst, d)], out=dst[:, d, :, :], in_=C.ps[bk][:, 0:256].rearrange("p (g i) -> p g i", g=16))
    P.barrier()
    P.sb_reset(mscr)
    E, kE = T([128, 8, 240]); Eb, kEb = T([128, 8, 240], BF16)
    P.D(SP, [], [kE], out=E[:], in_=cst["E"].rearrange("r p c -> p r c"))
    P.I(DVE, "tensor_copy", [kE], [kEb], out=Eb[:], in_=E[:])
    MF, kMF = T([128, 128]); MB, kMB = T([128, 128])
    P.D(SP, [], [kMF], out=MF[:], in_=cst["toemf"][:, :])
    P.D(SP, [], [kMB], out=MB[:], in_=cst["toemb"][:, :])
    Dall, kDall = T([128, 32])
    for t in range(8):
        P.D(SP, [], [(kDall, t)], out=Dall[t * 16:(t + 1) * 16, :], in_=prm["d"][l].rearrange("(g i) -> i g", i=16), allow_slow_non_contiguous=True)
    zT, kzT = T([128, 4, NTOK], BF16)
    zTv = zT[:].rearrange("p a (c s) -> p a c s", s=8)
    SG = [128, 4, 8, 16]
    tabs = {}
    for nm in ("WTR", "WTI", "XR", "XI", "VR", "VI"):
        for d in range(2):
            tabs[(nm, d)] = T(SG)
    TG, kTG = T(SG)
    suin, ksuin = T([128, NT, 128])
    suT, ksuT = T([128, NTOK])
    suTv = suT[:].rearrange("p (c s) -> p c s", s=8)
    U8, kU8 = T([128, 8, NCH])
    Z8, kZ8 = T([128, 8, NCH], BF16)
    Wt, kWt = T([128, 4, 128])
    Toe, kToe = T([128, 2, 128])
    H = {}
    for d in range(2):
        for pp in range(2):
            for c_ in range(2):
                H[(d, pp, c_)] = T([128, NCH])
    xg, kxg = T([128, CB]); ug, kug = T([128, CB]); sg, ksg = T([128, CB])
    for ct in range(4):
        gsl = slice(ct * 4, ct * 4 + 4)
        for d in range(2):
            ea, eb2, ec = (0, 8, 16) if d == 0 else (24, 32, 40)
            def pw(tile_, e0):
                return tile_[:, d, gsl, e0:e0 + 8].unsqueeze(3).to_broadcast(SG)
            def bb(tile_):
                return tile_[:, d, gsl, :].unsqueeze(2).to_broadcast(SG)
            (WTR, kWTR), (WTI, kWTI) = tabs[("WTR", d)], tabs[("WTI", d)]
            (XR, kXR), (XI, kXI) = tabs[("XR", d)], tabs[("XI", d)]
            (VR, kVR), (VI, kVI) = tabs[("VR", d)], tabs[("VI", d)]
            P.I(DVE, "tensor_tensor", [kPWR, kBBR], [kWTR], out=WTR[:], in0=pw(PWR, ea), in1=bb(BBR), op=ALU.mult)
            P.I(DVE, "tensor_tensor", [kPWI, kBBI], [kTG], out=TG[:], in0=pw(PWI, ea), in1=bb(BBI), op=ALU.mult)
            P.I(DVE, "tensor_tensor", [kWTR, kTG], [kWTR], out=WTR[:], in0=WTR[:], in1=TG[:], op=ALU.subtract)
            P.I(DVE, "tensor_tensor", [kPWR, kBBI], [kWTI], out=WTI[:], in0=pw(PWR, ea), in1=bb(BBI), op=ALU.mult)
            P.I(DVE, "tensor_tensor", [kPWI, kBBR], [kTG], out=TG[:], in0=pw(PWI, ea), in1=bb(BBR), op=ALU.mult)
            P.I(DVE, "tensor_tensor", [kWTI, kTG], [kWTI], out=WTI[:], in0=WTI[:], in1=TG[:], op=ALU.add)
            for (RR, kRR, II, kII, e0) in ((XR, kXR, XI, kXI, eb2), (VR, kVR, VI, kVI, ec)):
                P.I(DVE, "tensor_tensor", [kPWR, kCR], [kRR], out=RR[:], in0=pw(PWR, e0), in1=bb(CR), op=ALU.mult)
                P.I(DVE, "tensor_tensor", [kPWI, kCI], [kTG], out=TG[:], in0=pw(PWI, e0), in1=bb(CI), op=ALU.mult)
                P.I(DVE, "tensor_tensor", [kRR, kTG], [kRR], out=RR[:], in0=RR[:], in1=TG[:], op=ALU.subtract)
                P.I(DVE, "tensor_tensor", [kPWI, kCR], [kII], out=II[:], in0=pw(PWI, e0), in1=bb(CR), op=ALU.mult)
                P.I(DVE, "tensor_tensor", [kPWR, kCI], [kTG], out=TG[:], in0=pw(PWR, e0), in1=bb(CI), op=ALU.mult)
                P.I(DVE, "scalar_tensor_tensor", [kII, kTG], [kII], out=II[:], in0=II[:], scalar=-1.0, in1=TG[:], op0=ALU.mult, op1=ALU.subtract)
        P.D(SP, [("PROJ",)], [ksuin], out=suin[:], in_=PROJ[:, 1536 + ct * 128:1536 + (ct + 1) * 128].rearrange("(i p) c -> p i c", p=128))
        for i4 in range(0, NT, 4):
            n = min(4, NT - i4)
            bk = C.bank()
            for j in range(n):
                P.I(PE, "transpose", [ksuin, "ident"], [("ps", bk)], out=C.ps[bk][:, j * 128:(j + 1) * 128], in_=suin[:, i4 + j, :], identity=C.ident[:])
            copy_op(P, alt(i4 // 4), suT[:, i4 * 128:(i4 + n) * 128], C.ps[bk][:, 0:n * 128], [], [("ps", bk), (ksuT, i4)])
        for g8 in range(8):
            for cb in range(2):
                bk = C.bank()
                for s in range(8):
                    P.I(PE, "matmul", [kE, ksuT], [("ps", bk)], C.ps[bk][:, 0:CB], lhsT=E[:, g8, (7 - s) * 16:(7 - s) * 16 + 128],
                        rhs=suTv[:, cb * CB:(cb + 1) * CB, s], start=(s == 0), stop=(s == 7))
                copy_op(P, alt(cb), U8[:, g8, cb * CB:(cb + 1) * CB], C.ps[bk][:, 0:CB], [], [("ps", bk), (kU8, g8, cb)])
        for gpl in range(4):
            gp = ct * 4 + gpl
            bk = C.bank()
            for d in range(2):
                for c_, nm in enumerate(("WTR", "WTI")):
                    tt, ktt = tabs[(nm, d)]
                    j = d * 2 + c_
                    P.I(PE, "transpose", [ktt, "ident"], [("ps", bk)], out=C.ps[bk][:, j * 128:(j + 1) * 128], in_=tt[:, gpl, :, :].rearrange("p s j -> p (s j)"), identity=C.ident[:])
            P.I(DVE, "tensor_copy", [], [("ps", bk), kWt], out=Wt[:], in_=C.ps[bk][:, :].rearrange("p (a b) -> p a b", a=4))
            for g2 in range(2):
                rs_ = slice(g2 * 64, (g2 + 1) * 64)
                bks = []
                for d in range(2):
                    bk = C.bank()
                    bks.append(bk)
                    (WTR, kWTR), (WTI, kWTI) = tabs[("WTR", d)], tabs[("WTI", d)]
                    (XR, kXR), (XI, kXI) = tabs[("XR", d)], tabs[("XI", d)]
                    P.I(PE, "matmul", [kWTR, kXR], [("ps", bk)], C.ps[bk][:, 0:128], lhsT=WTR[rs_, gpl, :, :].rearrange("p s j -> p (s j)"),
                        rhs=XR[rs_, gpl, :, :].rearrange("p s j -> p (s j)"), start=True, stop=False)
                    P.I(PE, "matmul", [kWTI, kXI], [("ps", bk)], C.ps[bk][:, 0:128], lhsT=WTI[rs_, gpl, :, :].rearrange("p s j -> p (s j)"),
                        rhs=XI[rs_, gpl, :, :].rearrange("p s j -> p (s j)"), start=False, stop=True)
                P.I(DVE, "tensor_tensor", [kMF], [("ps", bks[0]), (kToe, g2)], out=Toe[:, g2, :], in0=C.ps[bks[0]][:, 0:128], in1=MF[:], op=ALU.mult)
                P.I(DVE, "tensor_tensor", [kMB], [("ps", bks[1]), kTG], out=TG[:, 0, :, :].rearrange("p s j -> p (s j)"), in0=C.ps[bks[1]][:, 0:128], in1=MB[:], op=ALU.mult)
                P.I(DVE, "tensor_tensor", [kTG], [(kToe, g2)], out=Toe[:, g2, :], in0=Toe[:, g2, :], in1=TG[:, 0, :, :].rearrange("p s j -> p (s j)"), op=ALU.add)
            for d in range(2):
                eng = DVE
                for c_ in range(2):
                    Ht, kHt = H[(d, 0, c_)]
                    for cb in range(2):
                        bk = C.bank()
                        for g2 in range(2):
                            rs_ = slice(g2 * 64, (g2 + 1) * 64)
                            P.I(PE, "matmul", [kWt, kU8], [("ps", bk)], C.ps[bk][rs_, 0:CB], lhsT=Wt[:, d * 2 + c_, rs_], rhs=U8[:, gpl * 2 + g2, cb * CB:(cb + 1) * CB], start=True, stop=True)
                        copy_op(P, ACT, Ht[:, cb * CB:(cb + 1) * CB], C.ps[bk][:, 0:CB], [], [("ps", bk), (kHt, cb)])
                cur = 0
                def arK(k):
                    return AR[:, d, gp, k:k + 1], AI[:, d, gp, k:k + 1], NAI[:, d, gp, k:k + 1]
                def scan(lo, hi, cur):
                    n = hi - lo
                    k = 0
                    sh = 1
                    while sh < n:
                        (A_, kA), (B_, kB) = H[(d, cur, 0)], H[(d, cur, 1)]
                        (An, kAn), (Bn, kBn) = H[(d, 1 - cur, 0)], H[(d, 1 - cur, 1)]
                        ar, ai, nai = arK(k)
                        if d == 0:
                            dst, src, keep = slice(lo + sh, hi), slice(lo, hi - sh), slice(lo, lo + sh)
                        else:
                            dst, src, keep = slice(lo, hi - sh), slice(lo + sh, hi), slice(hi - sh, hi)
                        P.I(eng, "scalar_tensor_tensor", [kA, kAR], [kAn], out=An[:, dst], in0=A_[:, src], scalar=ar, in1=A_[:, dst], op0=ALU.mult, op1=ALU.add)
                        P.I(eng, "scalar_tensor_tensor", [kB, kNAI, kAn], [kAn], out=An[:, dst], in0=B_[:, src], scalar=nai, in1=An[:, dst], op0=ALU.mult, op1=ALU.add)
                        P.I(eng, "scalar_tensor_tensor", [kB, kAR], [kBn], out=Bn[:, dst], in0=B_[:, src], scalar=ar, in1=B_[:, dst], op0=ALU.mult, op1=ALU.add)
                        P.I(eng, "scalar_tensor_tensor", [kA, kAI, kBn], [kBn], out=Bn[:, dst], in0=A_[:, src], scalar=ai, in1=Bn[:, dst], op0=ALU.mult, op1=ALU.add)
                        P.I(eng, "tensor_copy", [kA], [kAn], out=An[:, keep], in_=A_[:, keep])
                        P.I(eng, "tensor_copy", [kB], [kBn], out=Bn[:, keep], in_=B_[:, keep])
                        cur = 1 - cur
                        sh *= 2
                        k += 1
                    return cur
                cur = scan(0, 32, 0)
                (A_, kA), (B_, kB) = H[(d, cur, 0)], H[(d, cur, 1)]
                if cur != 0:
                    (A0, kA0), (B0, kB0) = H[(d, 0, 0)], H[(d, 0, 1)]
                    P.I(eng, "tensor_copy", [kA0], [kA], out=A_[:, 32:NCH], in_=A0[:, 32:NCH])
                    P.I(eng, "tensor_copy", [kB0], [kB], out=B_[:, 32:NCH], in_=B0[:, 32:NCH])
                ar, ai, nai = arK(0)
                if d == 0:
                    inj, frm = slice(32, 33), slice(31, 32)
                else:
                    inj, frm = slice(NCH - 1, NCH), slice(0, 1)
                P.I(eng, "scalar_tensor_tensor", [kA, kAR], [kA], out=A_[:, inj], in0=A_[:, frm], scalar=ar, in1=A_[:, inj], op0=ALU.mult, op1=ALU.add)
                P.I(eng, "scalar_tensor_tensor", [kB, kNAI, kA], [kA], out=A_[:, inj], in0=B_[:, frm], scalar=nai, in1=A_[:, inj], op0=ALU.mult, op1=ALU.add)
                P.I(eng, "scalar_tensor_tensor", [kB, kAR], [kB], out=B_[:, inj], in0=B_[:, frm], scalar=ar, in1=B_[:, inj], op0=ALU.mult, op1=ALU.add)
                P.I(eng, "scalar_tensor_tensor", [kA, kAI, kB], [kB], out=B_[:, inj], in0=A_[:, frm], scalar=ai, in1=B_[:, inj], op0=ALU.mult, op1=ALU.add)
                cur0 = cur
                cur = scan(32, NCH, cur)
                if cur != cur0:
                    (An, kAn), (Bn, kBn) = H[(d, cur, 0)], H[(d, cur, 1)]
                    P.I(eng, "tensor_copy", [kA], [kAn], out=An[:, 0:32], in_=A_[:, 0:32])
                    P.I(eng, "tensor_copy", [kB], [kBn], out=Bn[:, 0:32], in_=B_[:, 0:32])
                H[("fin", d)] = cur
            for g2 in range(2):
                g8 = gpl * 2 + g2
                g = gp * 2 + g2
                rs_ = slice(g2 * 64, (g2 + 1) * 64)
                for cb in range(2):
                    c0 = cb * CB
                    bk = C.bank()
                    mms = [(Toe[:, g2, :], U8[:, g8, c0:c0 + CB], 0, CB, [kToe, kU8])]
                    cf = H[("fin", 0)]
                    lo = max(c0, 1)
                    for c_, nm in enumerate(("VR", "VI")):
                        tt, ktt = tabs[(nm, 0)]
                        Hh, kHh = H[(0, cf, c_)]
                        mms.append((tt[rs_, gpl, :, :].rearrange("p s j -> p (s j)"), Hh[rs_, lo - 1:c0 + CB - 1], lo - c0, CB, [ktt, kHh]))
                    cbk = H[("fin", 1)]
                    if cb == 0:
                        segs = [(0, 31, 1), (32, CB, 33)]
                    else:
                        segs = [(CB, NCH - 1, CB + 1), (NCH - 1, NCH, 0)]
                    for c_, nm in enumerate(("VR", "VI")):
                        tt, ktt = tabs[(nm, 1)]
                        Hh, kHh = H[(1, cbk, c_)]
                        for (a0, a1, s0) in segs:
                            mms.append((tt[rs_, gpl, :, :].rearrange("p s j -> p (s j)"), Hh[rs_, s0:s0 + (a1 - a0)], a0 - c0, a1 - c0, [ktt, kHh]))
                    for n, (lt, rh, o0, o1, rd) in enumerate(mms):
                        P.I(PE, "matmul", rd, [("ps", bk)], C.ps[bk][:, o0:o1], lhsT=lt, rhs=rh, start=(n == 0), stop=(n == len(mms) - 1))
                    P.I(DVE, "scalar_tensor_tensor", [kU8, kDall], [("ps", bk), kxg], out=xg[:], in0=U8[:, g8, c0:c0 + CB], scalar=Dall[:, g:g + 1], in1=C.ps[bk][:, 0:CB], op0=ALU.mult, op1=ALU.add)
                    P.I(POOL, "tensor_tensor", [kxg], [kug], out=ug[:], in0=xg[:], in1=xg[:], op=ALU.mult)
                    P.I(POOL, "tensor_scalar", [kug], [kug], out=ug[:], in0=ug[:], scalar1=0.044715, scalar2=1.0, op0=ALU.mult, op1=ALU.add)
                    P.I(POOL, "tensor_tensor", [kug, kxg], [kug], out=ug[:], in0=ug[:], in1=xg[:], op=ALU.mult)
                    P.I(ACT, "activation", [kug], [ksg], out=sg[:], in_=ug[:], func=AF.Sigmoid, scale=2.0 * math.sqrt(2.0 / math.pi))
                    P.I(DVE, "tensor_tensor", [kxg, ksg], [(kZ8, g8, cb)], out=Z8[:, g8, c0:c0 + CB], in0=xg[:], in1=sg[:], op=ALU.mult)
        for s in range(8):
            for cb in range(2):
                bk = C.bank()
                for g8 in range(8):
                    P.I(PE, "matmul", [kEb, kZ8], [("ps", bk)], C.ps[bk][:, 0:CB], lhsT=Eb[:, s, (7 - g8) * 16:(7 - g8) * 16 + 128], rhs=Z8[:, g8, cb * CB:(cb + 1) * CB], start=(g8 == 0), stop=(g8 == 7))
                copy_op(P, alt(s), zTv[:, ct, cb * CB:(cb + 1) * CB, s], C.ps[bk][:, 0:CB], [], [("ps", bk), (kzT, ct, s, cb)])
    wg, kwg = T([128, 4, 512], BF16)
    for kt in range(4):
        P.D(POOL, [], [(kwg, kt)], out=wg[:, kt, :], in_=prm["w_glu"][l, kt * 128:(kt + 1) * 128, :])
    bg, kbg = T([128, 4])
    P.D(SP, [], [kbg], out=bg[:], in_=prm["b_glu"][l].rearrange("(m p) -> p m", p=128), allow_slow_non_contiguous=True)
    gts = [T([128, 512]) for _ in range(2)]
    ots = [T([128, 512], BF16) for _ in range(2)]
    it = 0
    for mt in range(4):
        for t0 in range(0, NTOK, 512):
            w = min(512, NTOK - t0)
            b = it % 2
            it += 1
            (gt_, kgt), (ot_, kot) = gts[b], ots[b]
            bk = C.bank()
            for kt in range(4):
                P.I(PE, "matmul", [kwg, kzT], [("ps", bk)], C.ps[bk][:, 0:w], lhsT=wg[:, kt, mt * 128:(mt + 1) * 128], rhs=zT[:, kt, t0:t0 + w], start=(kt == 0), stop=(kt == 3))
            P.I(ACT, "activation", [kbg], [("ps", bk), kgt], out=gt_[:, 0:w], in_=C.ps[bk][:, 0:w], func=AF.Sigmoid, bias=bg[:, mt:mt + 1], scale=1.0)
            P.I(DVE, "tensor_tensor", [kgt, kzT], [kot], out=ot_[:, 0:w], in0=gt_[:, 0:w], in1=zT[:, mt, t0:t0 + w], op=ALU.mult)
            P.D(SP, [kot], [("MIXT", "ssm", mt, t0)], out=MIXT[8 + mt, :, t0:t0 + w], in_=ot_[:, 0:w])
    P.barrier()
    P.sb_reset(m0)


def phase_gla(P, C, l, PROJ, MIXT, OF, prm, cst):
    m0 = P.sb_mark()
    n_ = [0]

    def T(shape, dt=F32, nm="gl"):
        n_[0] += 1
        return P.sb(shape, dt, nm), "%s%d" % (nm, n_[0])

    def ld(name, shape):
        t, k = T(shape)
        P.D(SP, [], [k], out=t[:], in_=cst[name][:, :])
        return t, k
    TRI = [ld("trif", [128, 128]), ld("trib", [128, 128])]
    BLK, kBLK = ld("blk", [128, 128])
    CIND, kCIND = ld("cind", [128, 2])
    MSK = [ld("gmaskf", [128, 512]), ld("gmaskb", [128, 512])]
    WG = []
    for d in range(2):
        t, k = T([17, 256])
        P.D(SP, [], [(k, 0)], out=t[0:16, :], in_=prm["w_gate"][l, d, :, :])
        P.D(SP, [], [(k, 1)], out=t[16:17, :], in_=prm["b_gate"][l, d:d + 1, :])
        WG.append((t, k))
    NG, kNG = T([128, 128])
    P.D(SP, [], [kNG], out=NG[:], in_=prm["norm_g"][l, :].partition_broadcast(128))
    S, kS = T([64, 4, 128])
    zaug = [T([17, 128]) for _ in range(2)]
    for (t, k) in zaug:
        P.I(POOL, "memset", [], [k], t[:], 1.0)
    NB = 2
    qk = [T([128, 512]) for _ in range(NB)]
    vv = [T([128, 512]) for _ in range(NB)]
    zz = [T([128, 16]) for _ in range(NB)]
    gp_ = [T([128, 256]) for _ in range(NB)]
    bS = [T([128, 256]) for _ in range(NB)]
    eb = [T([128, 256]) for _ in range(NB)]
    enb = [T([128, 256]) for _ in range(NB)]
    ebl = [T([128, 256]) for _ in range(NB)]
    qd = [T([128, 256]) for _ in range(NB)]
    kd = [T([128, 256]) for _ in range(NB)]
    kl = [T([128, 256]) for _ in range(NB)]
    qdT = [T([64, 4, 128]) for _ in range(NB)]
    kdT = [T([64, 4, 128]) for _ in range(NB)]
    ATm = [T([128, 4, 128]) for _ in range(NB)]
    edec = [T([64, 4, 2]) for _ in range(NB)]
    ot = [T([128, 512]) for _ in range(NB)]
    of_ = [T([128, 512]) for _ in range(NB)]
    rr = [T([128, 512]) for _ in range(NB)]
    sq = [T([128, 512]) for _ in range(NB)]
    ssq = [T([128, 4]) for _ in range(NB)]
    oTb = [T([128, 4, 128], BF16) for _ in range(NB)]
    it = 0
    for d in range(2):
        P.I(DVE, "memset", [], [kS], S[:], 0.0)
        order = list(range(NT)) if d == 0 else [1, 0] + list(range(NT - 1, 1, -1))
        corder = (0, 1) if d == 0 else (1, 0)
        (TRId, kTRI), (MK, kMK), (WGd, kWG) = TRI[d], MSK[d], WG[d]
        for i in order:
            b = it % NB
            it += 1
            r0 = i * 128
            (QK, kQK), (V, kV), (Z, kZ), (GP, kGP) = qk[b], vv[b], zz[b], gp_[b]
            (ZA, kZA) = zaug[b]
            P.D(SP, [("PROJ", i)], [kQK], out=QK[:], in_=PROJ[r0:r0 + 128, 2048:2560])
            P.D(SP, [("PROJ", i)], [kV], out=V[:], in_=PROJ[r0:r0 + 128, 2560:3072])
            P.D(SP, [("PROJ", i)], [kZ], out=Z[:], in_=PROJ[r0:r0 + 128, 3584 + 16 * d:3600 + 16 * d])
            bk = C.bank()
            P.I(PE, "transpose", [kZ, "ident"], [("ps", bk)], out=C.ps[bk][0:16, 0:128], in_=Z[:], identity=C.ident[:])
            P.I(ACT, "copy", [], [("ps", bk), kZA], out=ZA[0:16, :], in_=C.ps[bk][0:16, 0:128])
            bk = C.bank()
            P.I(PE, "matmul", [kZA, kWG], [("ps", bk)], C.ps[bk][:, 0:256], lhsT=ZA[:], rhs=WGd[:], start=True, stop=True)
            P.I(ACT, "activation", [], [("ps", bk), kGP], out=GP[:], in_=C.ps[bk][:, 0:256], func=AF.Exp, scale=-1.0)
            P.I(ACT, "activation", [kGP], [kGP], out=GP[:], in_=GP[:], func=AF.Ln, bias=1.0, scale=1.0)
            bkb = C.bank()
            P.I(PE, "matmul", [kTRI, kGP], [("ps", bkb)], C.ps[bkb][:, 0:256], lhsT=TRId[:], rhs=GP[:], start=True, stop=True)
            P.I(PE, "matmul", [kBLK, kGP], [("ps", bkb)], C.ps[bkb][:, 256:512], lhsT=BLK[:], rhs=GP[:], start=True, stop=True)
            (BS, kBS), (EB, kEB), (ENB, kENB), (EBL, kEBL) = bS[b], eb[b], enb[b], ebl[b]
            P.I(ACT, "copy", [], [("ps", bkb), kBS], out=BS[:], in_=C.ps[bkb][:, 0:256])
            P.I(DVE, "tensor_tensor", [kBS], [("ps", bkb), kEBL], out=EBL[:], in0=C.ps[bkb][:, 256:512], in1=BS[:], op=ALU.subtract)
            P.I(ACT, "activation", [kBS], [kEB], out=EB[:], in_=BS[:], func=AF.Exp)
            P.I(ACT, "activation", [kBS], [kENB], out=ENB[:], in_=BS[:], func=AF.Exp, scale=-1.0)
            P.I(ACT, "activation", [kEBL], [kEBL], out=EBL[:], in_=EBL[:], func=AF.Exp)
            (ED, kED) = edec[b]
            bk = C.bank()
            for h in range(4):
                P.I(PE, "matmul", [kGP, kCIND], [("ps", bk)], C.ps[bk][0:64, h * 2:h * 2 + 2], lhsT=GP[:, h * 64:(h + 1) * 64], rhs=CIND[:], start=True, stop=True)
            P.I(ACT, "activation", [], [("ps", bk), kED], out=ED[:].rearrange("p h c -> p (h c)"), in_=C.ps[bk][0:64, 0:8], func=AF.Exp)
            (QD, kQD), (KD, kKD), (KL, kKL) = qd[b], kd[b], kl[b]
            P.I(DVE, "scalar_tensor_tensor", [kQK, kEB], [kQD], out=QD[:], in0=QK[:, 0:256], scalar=0.125, in1=EB[:], op0=ALU.mult, op1=ALU.mult)
            P.I(POOL, "tensor_tensor", [kQK, kENB], [kKD], out=KD[:], in0=QK[:, 256:512], in1=ENB[:], op=ALU.mult)
            P.I(POOL, "tensor_tensor", [kQK, kEBL], [kKL], out=KL[:], in0=QK[:, 256:512], in1=EBL[:], op=ALU.mult)
            (QT, kQT), (KT, kKT) = qdT[b], kdT[b]
            for (src, ksrc, dst, kdst, eng) in ((QD, kQD, QT, kQT, ACT), (KD, kKD, KT, kKT, DVE)):
                bk = C.bank()
                for h in range(4):
                    P.I(PE, "transpose", [ksrc, "ident"], [("ps", bk)], out=C.ps[bk][0:64, h * 128:(h + 1) * 128], in_=src[:, h * 64:(h + 1) * 64], identity=C.ident[:])
                copy_op(P, eng, dst[:].rearrange("p h t -> p (h t)"), C.ps[bk][0:64, :], [], [("ps", bk), kdst])
            (AT, kAT) = ATm[b]
            bk = C.bank()
            for h in range(4):
                P.I(PE, "matmul", [kKT, kQT], [("ps", bk)], C.ps[bk][:, h * 128:(h + 1) * 128], lhsT=KT[:, h, :], rhs=QT[:, h, :], start=True, stop=True)
            P.I(DVE, "tensor_tensor", [kMK], [("ps", bk), kAT], out=AT[:].rearrange("p h t -> p (h t)"), in0=C.ps[bk][:, :], in1=MK[:], op=ALU.mult)
            bo = C.bank()
            for h in range(4):
                P.I(PE, "matmul", [kAT, kV], [("ps", bo)], C.ps[bo][:, h * 128:(h + 1) * 128], lhsT=AT[:, h, :], rhs=V[:, h * 128:(h + 1) * 128], start=(h == 0), stop=False)
            for ci, c in enumerate(corder):
                cs = slice(c * 64, (c + 1) * 64)
                for h in range(4):
                    P.I(PE, "matmul", [kQT, (kS, h)], [("ps", bo)], C.ps[bo][cs, h * 128:(h + 1) * 128], lhsT=QT[:, h, cs], rhs=S[:, h, :], start=False, stop=(ci == 1 and h == 3))
                bu = C.bank()
                for h in range(4):
                    P.I(PE, "matmul", [kKL, kV], [("ps", bu)], C.ps[bu][0:64, h * 128:(h + 1) * 128], lhsT=KL[cs, h * 64:(h + 1) * 64], rhs=V[cs, h * 128:(h + 1) * 128], start=True, stop=True)
                for h in range(4):
                    P.I(DVE, "scalar_tensor_tensor", [kED], [("ps", bu), (kS, h)], out=S[:, h, :], in0=S[:, h, :], scalar=ED[:, h, c:c + 1], in1=C.ps[bu][0:64, h * 128:(h + 1) * 128], op0=ALU.mult, op1=ALU.add)
            (OT, kOT) = ot[b]
            if d == 0:
                P.I(ACT, "copy", [], [("ps", bo), kOT], out=OT[:], in_=C.ps[bo][:, :])
                P.D(SP, [kOT], [("OF", i)], out=OF[r0:r0 + 128, :], in_=OT[:])
                continue
            (OFt, kOFt), (RR, kRR), (SQ, kSQ), (SS, kSS), (OB, kOB) = of_[b], rr[b], sq[b], ssq[b], oTb[b]
            P.D(SP, [("OF", i)], [kOFt], out=OFt[:], in_=OF[r0:r0 + 128, :])
            P.D(SP, [("PROJ", i)], [kRR], out=RR[:], in_=PROJ[r0:r0 + 128, 3072:3584])
            P.I(DVE, "tensor_tensor", [kOFt], [("ps", bo), kOT], out=OT[:], in0=C.ps[bo][:, :], in1=OFt[:], op=ALU.add)
            P.I(POOL, "tensor_tensor", [kOT], [kSQ], out=SQ[:], in0=OT[:], in1=OT[:], op=ALU.mult)
            P.I(DVE, "tensor_reduce", [kSQ], [kSS], out=SS[:], in_=SQ[:].rearrange("p (h v) -> p h v", h=4), axis=AX.X, op=ALU.add)
            P.I(DVE, "tensor_scalar", [kSS], [kSS], out=SS[:], in0=SS[:], scalar1=1.0 / 128.0, scalar2=1e-6, op0=ALU.mult, op1=ALU.add)
            P.I(ACT, "activation", [kSS], [kSS], out=SS[:], in_=SS[:], func=AF.Ln)
            P.I(ACT, "activation", [kSS], [kSS], out=SS[:], in_=SS[:], func=AF.Exp, scale=-0.5)
            P.I(ACT, "activation", [kRR], [kRR], out=RR[:], in_=RR[:], func=AF.Silu)
            o3 = OT[:].rearrange("p (h v) -> p h v", h=4)
            P.I(DVE, "tensor_tensor", [kOT, kSS], [kOT], out=o3, in0=o3, in1=SS[:].unsqueeze(2).to_broadcast([128, 4, 128]), op=ALU.mult)
            P.I(POOL, "tensor_tensor", [kOT, kNG], [kOT], out=o3, in0=o3, in1=NG[:].unsqueeze(1).to_broadcast([128, 4, 128]), op=ALU.mult)
            P.I(DVE, "tensor_tensor", [kOT, kRR], [kOT], out=OT[:], in0=OT[:], in1=RR[:], op=ALU.mult)
            bk = C.bank()
            for h in range(4):
                P.I(PE, "transpose", [kOT, "ident"], [("ps", bk)], out=C.ps[bk][:, h * 128:(h + 1) * 128], in_=OT[:, h * 128:(h + 1) * 128], identity=C.ident[:])
            P.I(ACT, "copy", [], [("ps", bk), kOB], out=OB[:].rearrange("p h t -> p (h t)"), in_=C.ps[bk][:, :])
            P.D(SP, [kOB], [("MIXT", "gla", i)], out=MIXT[12:16, :, r0:r0 + 128].rearrange("c p t -> p c t"), in_=OB[:])
    P.barrier()
    P.sb_reset(m0)


ALPHA_ = (2 * 2) ** 0.25
NSLOT = 544
ROWW = 2080


def bcast_load(P, q, dst, key, src_row):
    P.D(q, [], [key], out=dst[:], in_=src_row.partition_broadcast(128))


def ln_apply(P, C, src, ksrc, dst, kdst, st, mv, rs, tag, gmul, kg, badd, kb, eng2=POOL):
    ln_stats(P, C, src, ksrc, st, mv, rs, tag)
    P.I(ACT, "activation", [ksrc, tag + "rs"], [kdst], out=dst[:], in_=src[:], func=AF.Identity, bias=rs[:, 1:2], scale=rs[:, 0:1])
    P.I(DVE, "tensor_tensor", [kdst, kg], [kdst], out=dst[:], in0=dst[:], in1=gmul, op=ALU.mult)
    P.I(eng2, "tensor_tensor", [kdst, kb], [kdst], out=dst[:], in0=dst[:], in1=badd, op=ALU.add)


def phase_wout(P, C, l, X, MIXT, X1, H2R, AFF, w_out, MODV, ln_g, ln_b, router, cst=None):
    m0 = P.sb_mark()
    wo = P.sb([128, 16, D], BF16, "wo")
    for kt in range(16):
        P.D(POOL, [], [("wo", kt)], out=wo[:, kt, :], in_=w_out[l, kt * 128:(kt + 1) * 128, :])
    names = {}
    for nm, src in (("gt1", lambda r: MODV[l, r, 2 * D:3 * D]), ("sc2", lambda r: MODV[l, r, 4 * D:5 * D]), ("sh2", lambda r: MODV[l, r, 3 * D:4 * D])):
        for r in range(2):
            t = P.sb([128, D], F32, nm)
            bcast_load(P, SP, t, (nm, r), src(r))
            names[(nm, r)] = t
    for r in range(2):
        P.I(POOL, "tensor_scalar_add", [("sc2", r)], [("sc2", r)], out=names[("sc2", r)][:], in0=names[("sc2", r)][:], scalar1=1.0)
    g1 = P.sb([128, D], F32, "g1"); b1 = P.sb([128, D], F32, "b1")
    bcast_load(P, SP, g1, "g1", ln_g[l, :]); bcast_load(P, SP, b1, "b1", ln_b[l, :])
    rt = P.sb([128, 16, 16], F32, "rt")
    P.D(SP, [], ["rt"], out=rt[:], in_=router[l].rearrange("(kt p) e -> p kt e", p=128))
    mx = [P.sb([128, 16, 128], BF16, "mx") for _ in range(2)]
    xs = [P.sb([128, D], F32, "xs") for _ in range(2)]
    x1s = [P.sb([128, D], F32, "x1s") for _ in range(2)]
    rows = [P.sb([128, ROWW], F32, "row") for _ in range(2)]
    h2T = P.sb([128, 16, 128], F32, "h2T")
    st = P.sb([128, 4, 6], F32, "st"); mv = P.sb([128, 2], F32, "mv"); rs = P.sb([128, 2], F32, "rs")
    st2 = P.sb([128, 4, 6], F32, "st2"); mv2 = P.sb([128, 2], F32, "mv2"); rs2 = P.sb([128, 2], F32, "rs2")
    lmx = P.sb([128, 1], F32, "lmx"); lsum = P.sb([128, 1], F32, "lsum")
    for (t, k) in ((rows[0], ("row", 0)), (rows[1], ("row", 1))):
        P.I(POOL, "memset", [], [k], t[:, 2064:ROWW], 0.0)
    TOK = P.sb([128, NT], F32, "TOK")
    if cst is not None and "tokid" in cst:
        P.D(SP, [], ["TOK"], out=TOK[:], in_=cst["tokid"][:, :])
    else:
        P.I(POOL, "memset", [], ["TOK"], TOK[:], 0.0)
    for i in range(NT):
        b = i % 2
        r = 1 if i < 2 else 0
        r0 = i * 128
        MX, XS, X1S, ROW = mx[b], xs[b], x1s[b], rows[b]
        kMX, kXS, kX1, kROW = ("mx", b), ("xs", b), ("x1s", b), ("row", b)
        P.D(SP, [("MIXT",)], [kMX], out=MX[:], in_=MIXT[:, :, r0:r0 + 128].rearrange("c p t -> p c t"))
        P.D(SP, [("X", i)], [kXS], out=XS[:], in_=X[r0:r0 + 128, :])
        for nb in range(4):
            bk = C.bank()
            for kt in range(16):
                P.I(PE, "matmul", [kMX, ("wo", kt)], [("ps", bk)], C.ps[bk][:, :], lhsT=MX[:, kt, :], rhs=wo[:, kt, nb * 512:(nb + 1) * 512], start=(kt == 0), stop=(kt == 15))
            sl = slice(nb * 512, (nb + 1) * 512)
            P.I(DVE, "tensor_tensor", [("gt1", r)], [("ps", bk), kX1 + (nb,)], out=X1S[:, sl], in0=C.ps[bk][:, :], in1=names[("gt1", r)][:, sl], op=ALU.mult)
        P.I(DVE, "scalar_tensor_tensor", [kXS, kX1], [kX1], out=X1S[:], in0=XS[:], scalar=ALPHA_, in1=X1S[:], op0=ALU.mult, op1=ALU.add)
        ln_apply(P, C, X1S, kX1, X1S, kX1, st, mv, rs, "w1", g1[:], "g1", b1[:], "b1")
        P.D(SP, [kX1], [("X1", i)], out=X1[r0:r0 + 128, :], in_=X1S[:])
        ln_stats(P, C, X1S, kX1, st2, mv2, rs2, "w2")
        P.I(ACT, "activation", [kX1, "w2rs"], [kROW + (0,)], out=ROW[:, 0:D], in_=X1S[:], func=AF.Identity, bias=rs2[:, 1:2], scale=rs2[:, 0:1])
        P.I(DVE, "tensor_tensor", [kROW + (0,), ("sc2", r)], [kROW + (0,)], out=ROW[:, 0:D], in0=ROW[:, 0:D], in1=names[("sc2", r)][:], op=ALU.mult)
        P.I(POOL, "tensor_tensor", [kROW + (0,), ("sh2", r)], [kROW + (0,)], out=ROW[:, 0:D], in0=ROW[:, 0:D], in1=names[("sh2", r)][:], op=ALU.add)
        for kg in range(4):
            bk = C.bank()
            for j in range(4):
                kt = kg * 4 + j
                P.I(PE, "transpose", [kROW + (0,), "ident"], [("ps", bk)], out=C.ps[bk][:, j * 128:(j + 1) * 128], in_=ROW[:, kt * 128:(kt + 1) * 128], identity=C.ident[:])
            copy_op(P, alt(kg), h2T[:, kg * 4:(kg + 1) * 4, :].rearrange("p a t -> p (a t)"), C.ps[bk][:, :], [], [("ps", bk), ("h2T", kg)])
        bk = C.bank()
        for kt in range(16):
            P.I(PE, "matmul", [("h2T", kt // 4), "rt"], [("ps", bk)], C.ps[bk][:, 0:16], lhsT=h2T[:, kt, :], rhs=rt[:, kt, :], start=(kt == 0), stop=(kt == 15))
        P.I(DVE, "tensor_reduce", [], [("ps", bk), "lmx"], out=lmx[:], in_=C.ps[bk][:, 0:16], axis=AX.X, op=ALU.max)
        P.I(DVE, "tensor_scalar_mul", ["lmx"], ["lmx"], out=lmx[:], in0=lmx[:], scalar1=-1.0)
        P.I(DVE, "memset", [], ["lsum"], lsum[:], 0.0)
        P.I(ACT, "activation", ["lmx"], [("ps", bk), kROW + (1,), "lsum"], out=ROW[:, D:D + 16], in_=C.ps[bk][:, 0:16], func=AF.Exp, bias=lmx[:, 0:1], scale=1.0, accum_out=lsum[:])
        P.I(DVE, "reciprocal", ["lsum"], ["lsum"], out=lsum[:], in_=lsum[:])
        P.I(DVE, "tensor_scalar_mul", [kROW + (1,), "lsum"], [kROW + (1,)], out=ROW[:, D:D + 16], in0=ROW[:, D:D + 16], scalar1=lsum[:, 0:1])
        P.I(POOL, "tensor_copy", [kROW + (1,)], [("AFF", i)], out=AFF[:, i, :], in_=ROW[:, D:D + 16])
        P.I(POOL, "tensor_copy", ["TOK"], [kROW + (2,)], out=ROW[:, 2064:2065], in_=TOK[:, i:i + 1])
        P.D(SP, [kROW], [("H2R", i)], out=H2R[r0:r0 + 128, :], in_=ROW[:])
    P.barrier()
    P.sb_reset(m0)


def phase_route(P, C, AFF, IDXS, IDXC_dram, cst, niter=30):
    m0 = P.sb_mark()
    n_ = [0]

    def T(shape, dt=F32, nm="rt"):
        n_[0] += 1
        return P.sb(shape, dt, nm), "%s%d" % (nm, n_[0])
    ones, kones = T([128, 128]); strict, kstrict = T([128, 128]); TGT, kTGT = T([128, 2, 16])
    P.D(SP, [], [kones], out=ones[:], in_=cst["ones"][:, :])
    P.D(SP, [], [kstrict], out=strict[:], in_=cst["strict"][:, :])
    P.D(SP, [], [kTGT], out=TGT[:], in_=cst["tgt"].rearrange("p (s e) -> p s e", s=2))
    LO, kLO = T([128, 2, 16]); HI, kHI = T([128, 2, 16]); MID, kMID = T([128, 2, 16])
    CNT, kCNT = T([128, 2, 16]); GE, kGE = T([128, 2, 16]); D1, kD1 = T([128, 2, 16])
    CMP, kCMP = T([128, NT, 16])
    P.I(DVE, "memset", [], [kLO], LO[:], 0.0)
    P.I(DVE, "memset", [], [kHI], HI[:], 1.0001)
    segs = ((0, 0, 2), (1, 2, NT))

    def compare(TH, kTH):
        for (s, a, b_) in segs:
            P.I(DVE, "tensor_tensor", [("AFF",), kTH], [(kCMP, s)], out=CMP[:, a:b_, :], in0=AFF[:, a:b_, :],
                in1=TH[:, s, :].unsqueeze(1).to_broadcast([128, b_ - a, 16]), op=ALU.is_ge)
    for it in range(niter):
        P.I(DVE, "tensor_tensor", [kLO, kHI], [kMID], out=MID[:], in0=LO[:], in1=HI[:], op=ALU.add)
        P.I(DVE, "tensor_scalar_mul", [kMID], [kMID], out=MID[:], in0=MID[:], scalar1=0.5)
        compare(MID, kMID)
        for (s, a, b_) in segs:
            P.I(DVE, "tensor_reduce", [(kCMP, s)], [(kCNT, s)], out=CNT[:, s, :], in_=CMP[:, a:b_, :].rearrange("p t e -> p e t"), axis=AX.X, op=ALU.add)
        bk = C.bank()
        P.I(PE, "matmul", [kones, kCNT], [("ps", bk)], C.ps[bk][:, 0:32], lhsT=ones[:], rhs=CNT[:].rearrange("p s e -> p (s e)"), start=True, stop=True)
        P.I(DVE, "tensor_tensor", [kTGT], [("ps", bk), kGE], out=GE[:].rearrange("p s e -> p (s e)"), in0=C.ps[bk][:, 0:32], in1=TGT[:].rearrange("p s e -> p (s e)"), op=ALU.is_ge)
        P.I(DVE, "tensor_tensor", [kMID, kLO], [kD1], out=D1[:], in0=MID[:], in1=LO[:], op=ALU.subtract)
        P.I(DVE, "tensor_tensor", [kD1, kGE], [kD1], out=D1[:], in0=D1[:], in1=GE[:], op=ALU.mult)
        P.I(DVE, "tensor_tensor", [kLO, kD1], [kLO], out=LO[:], in0=LO[:], in1=D1[:], op=ALU.add)
        P.I(DVE, "tensor_tensor", [kHI, kMID], [kD1], out=D1[:], in0=HI[:], in1=MID[:], op=ALU.subtract)
        P.I(DVE, "tensor_tensor", [kD1, kGE], [kD1], out=D1[:], in0=D1[:], in1=GE[:], op=ALU.mult)
        P.I(DVE, "tensor_tensor", [kMID, kD1], [kHI], out=HI[:], in0=MID[:], in1=D1[:], op=ALU.add)
    compare(LO, kLO)
    PRE, kPRE = T([128, NT, 16]); TOT, kTOT = T([128, NT, 16]); OFFS, kOFFS = T([128, NT, 16])
    cm = CMP[:].rearrange("p t e -> p (t e)")
    for (dst, kdst, lt, klt) in ((PRE, kPRE, strict, kstrict), (TOT, kTOT, ones, kones)):
        for (c0, c1) in ((0, 512), (512, 544)):
            bk = C.bank()
            P.I(PE, "matmul", [klt, kCMP], [("ps", bk)], C.ps[bk][:, 0:c1 - c0], lhsT=lt[:], rhs=cm[:, c0:c1], start=True, stop=True)
            P.I(ACT, "copy", [], [("ps", bk), (kdst, c0)], out=dst[:].rearrange("p t e -> p (t e)")[:, c0:c1], in_=C.ps[bk][:, 0:c1 - c0])
    for (s, a, b_) in segs:
        P.I(DVE, "memset", [], [(kOFFS, a)], OFFS[:, a, :], 0.0)
        for i in range(a, b_ - 1):
            P.I(DVE, "tensor_tensor", [(kOFFS, i), kTOT], [(kOFFS, i + 1)], out=OFFS[:, i + 1, :], in0=OFFS[:, i, :], in1=TOT[:, i, :], op=ALU.add)
    P.I(DVE, "tensor_tensor", [kPRE, kOFFS], [kPRE], out=PRE[:], in0=PRE[:], in1=OFFS[:], op=ALU.add)
    for (s, a, b_) in segs:
        cap = 32.0 if s == 0 else 512.0
        P.I(DVE, "tensor_single_scalar", [kPRE], [(kTOT, s)], out=TOT[:, a:b_, :], in_=PRE[:, a:b_, :], scalar=cap, op=ALU.is_lt)
    P.I(DVE, "tensor_tensor", [kTOT, kCMP], [kCMP], out=CMP[:], in0=CMP[:], in1=TOT[:], op=ALU.mult)
    P.I(DVE, "tensor_scalar_add", [kPRE], [(kPRE, 0)], out=PRE[:, 0:2, :], in0=PRE[:, 0:2, :], scalar1=512.0)
    IDXC, kIDXC = T([128, NT, 16], I32)
    for (big, dst, kdst) in ((10000.0, IDXS, ("IDXS",)), (544.0, IDXC, kIDXC)):
        P.I(DVE, "tensor_scalar_add", [kPRE], [kOFFS], out=OFFS[:], in0=PRE[:], scalar1=-big)
        P.I(DVE, "tensor_tensor", [kOFFS, kCMP], [kOFFS], out=OFFS[:], in0=OFFS[:], in1=CMP[:], op=ALU.mult)
        P.I(DVE, "tensor_scalar_add", [kOFFS], [kOFFS], out=OFFS[:], in0=OFFS[:], scalar1=big)
        P.I(DVE, "tensor_copy", [kOFFS], [kdst], out=dst[:], in_=OFFS[:])
    P.D(SP, [kIDXC], [("IDXC",)], out=IDXC_dram[:, :, :], in_=IDXC[:])
    P.barrier()
    P.sb_reset(m0)


def phase_scatter(P, C, H2R, XE, IDXS, cst=None):
    m0 = P.sb_mark()
    rows = [P.sb([128, ROWW], F32, "srow") for _ in range(3)]
    holder = {}

    P.nname += 1
    rname = "bcreg%d" % P.nname

    def f0(eng):
        holder["reg"] = eng.alloc_register(rname)
        return eng.reg_mov(holder["reg"], NSLOT - 1)
    P.op(POOL, f0, [], [])
    pre = []
    if cst is not None and "trash" in cst:
        TR = P.sb([128, 5], F32, "TR")
        P.D(SP, [], ["TR"], out=TR[:], in_=cst["trash"][:, :])
        for e in range(16):
            P.D(SP, ["TR"], [("XEpre", e, 0)], out=XE[e, 0:512, 2064].rearrange("(j p) -> p j", p=128), in_=TR[:, 0:4], allow_slow_non_contiguous=True)
            P.D(SP, ["TR"], [("XEpre", e, 1)], out=XE[e, 512:544, 2064].rearrange("(p o) -> p o", o=1), in_=TR[0:32, 4:5], allow_slow_non_contiguous=True)
        pre = [("XEpre",)]
    for i in range(NT):
        b = i % 3
        P.D(SP, [("H2R", i)], [("srow", b)], out=rows[b][:], in_=H2R[i * 128:(i + 1) * 128, :])
        for e in range(16):
            P.dma(POOL, (lambda eng, b=b, i=i, e=e: eng.indirect_dma_start(
                out=XE.ap().rearrange("e s w -> (e s) w"), out_offset=bass.IndirectOffsetOnAxis(ap=IDXS[:, i, e:e + 1], axis=0),
                in_=rows[b][:], in_offset=None, element_offset=e * NSLOT * ROWW, bounds_check=holder["reg"], oob_is_err=False)),
                reads=[("srow", b), ("IDXS",)] + pre, writes=[("XE", i, e)])
    P.barrier()
    P.sb_reset(m0)


NSL = 2176


def phase_experts(P, C, nsamp, nexp, xe_ap, gate_ap, w_ap, ye_ap, scat=None):
    m0 = P.sb_mark()
    nsl = nsamp * 544
    cx0 = nsamp * 512
    hidT = P.sb([128, 16, nsl], BF16, "hidT")
    stg = [P.sb([128, D], F32, "stg") for _ in range(2)]
    stgb = [P.sb([128, D], BF16, "stgb") for _ in range(2)]
    nld = 0
    wgb = [P.sb([128, 16, 256], BF16, "wgb") for _ in range(2)]
    wub = [P.sb([128, 16, 256], BF16, "wub") for _ in range(2)]
    sil = [P.sb([128, 512], F32, "sil") for _ in range(2)]
    GT = P.sb([128, 2, 5, 4], F32, "GT")
    TKf = P.sb([128, 2, 5], F32, "TKf")
    TKi = P.sb([128, 2, 5], I32, "TKi")
    if scat is not None:
        FFN = scat["FFN"]
        zt = P.sb([128, D], F32, "zt")
        P.I(POOL, "memset", [], ["zt"], zt[:], 0.0)
        nrow = FFN.shape[0]
        for r0 in range(0, nrow, 128):
            n = min(128, nrow - r0)
            P.D(SP, ["zt"], [("FFN", r0)], out=FFN[r0:r0 + n, :], in_=zt[0:n, :])
    alias = nsamp > 1
    if not alias:
        xeT_s = P.sb([128, 16, nsl], BF16, "xeT")
        wd_s = P.sb([128, 16, D], BF16, "wd")
    mA = P.sb_mark()
    nst = 0
    nw = 0
    blocks = [(b * 512, 512) for b in range(nsamp)] + [(cx0, nsamp * 32)]
    for el in range(nexp):
        ep = el % 2
        if alias:
            P.sb_reset(mA)
            xeT = P.sb([128, 16, nsl], BF16, "xeT")
            kx = "xeT%d" % el
        else:
            xeT = xeT_s
            kx = "xeT"
        for b in range(nsamp):
            P.D(SP, [("XE",)], [("GT", ep, b)], out=GT[:, ep, 0:4, b], in_=gate_ap(b, el, 0, 512).rearrange("(j p) -> p j", p=128), allow_slow_non_contiguous=True)
            P.D(SP, [("XE",)], [("GTc", ep, b)], out=GT[b * 32:(b + 1) * 32, ep, 4, 0:1], in_=gate_ap(b, el, 512, 544).rearrange("(p o) -> p o", o=1), allow_slow_non_contiguous=True)
        if scat is not None:
            P.D(SP, [("XE",)], [("TKf", ep, 0)], out=TKf[:, ep, 0:4], in_=scat["tok_ap"](0, el, 0, 512).rearrange("(j p) -> p j", p=128), allow_slow_non_contiguous=True)
            P.D(SP, [("XE",)], [("TKf", ep, 1)], out=TKf[0:32, ep, 4:5], in_=scat["tok_ap"](0, el, 512, 544).rearrange("(p o) -> p o", o=1), allow_slow_non_contiguous=True)
            P.I(DVE, "tensor_copy", [("TKf", ep)], [("TKi", ep, 0)], out=TKi[:, ep, 0:4], in_=TKf[:, ep, 0:4])
            P.I(DVE, "tensor_copy", [("TKf", ep)], [("TKi", ep, 1)], out=TKi[0:32, ep, 4:5], in_=TKf[0:32, ep, 4:5])
        for b in range(nsamp):
            for j in range(5):
                np_ = 128 if j < 4 else 32
                col0 = b * 512 + j * 128 if j < 4 else cx0 + b * 32
                sb_ = nld % 2
                nld += 1
                STB = stgb[sb_]
                P.D(POOL, [("XE",)], [("stgb", sb_)], out=STB[0:np_, :], in_=xe_ap(b, el, j * 128, j * 128 + np_))
                for kg in range(4):
                    bk = C.bank()
                    psb = C.ps[bk][:].bitcast(BF16)
                    for q in range(4):
                        kt = kg * 4 + q
                        P.I(PE, "transpose", [("stgb", sb_), "identb"], [("ps", bk)], out=psb[:, q * 128:q * 128 + np_], in_=STB[0:np_, kt * 128:(kt + 1) * 128], identity=C.identb[0:np_, 0:np_])
                    copy_op(P, alt(kg), xeT[:, kg * 4:(kg + 1) * 4, col0:col0 + np_], psb[:, 0:512].rearrange("p (q t) -> p q t", q=4)[:, :, 0:np_], [], [("ps", bk), (kx, b, j, kg)])
        for fb in range(8):
            wb = nw % 2
            nw += 1
            P.D(POOL, [], [("wgb", wb)], out=wgb[wb][:], in_=w_ap("gate", el)[:, fb * 256:(fb + 1) * 256].rearrange("(kt p) n -> p kt n", p=128))
            P.D(POOL, [], [("wub", wb)], out=wub[wb][:], in_=w_ap("up", el)[:, fb * 256:(fb + 1) * 256].rearrange("(kt p) n -> p kt n", p=128))
            for fl in range(2):
                ft = fb * 2 + fl
                for nb, (c0, w) in enumerate(blocks):
                    bg = C.bank()
                    bu = C.bank()
                    for kt in range(16):
                        P.I(PE, "matmul", [("wgb", wb), (kx,)], [("ps", bg)], C.ps[bg][:, 0:w], lhsT=wgb[wb][:, kt, fl * 128:(fl + 1) * 128], rhs=xeT[:, kt, c0:c0 + w], start=(kt == 0), stop=(kt == 15))
                    for kt in range(16):
                        P.I(PE, "matmul", [("wub", wb), (kx,)], [("ps", bu)], C.ps[bu][:, 0:w], lhsT=wub[wb][:, kt, fl * 128:(fl + 1) * 128], rhs=xeT[:, kt, c0:c0 + w], start=(kt == 0), stop=(kt == 15))
                    sb_ = (ft * len(blocks) + nb) % 2
                    P.I(ACT, "activation", [], [("ps", bg), ("sil", sb_)], out=sil[sb_][:, 0:w], in_=C.ps[bg][:, 0:w], func=AF.Silu)
                    P.I(DVE, "tensor_tensor", [("sil", sb_)], [("ps", bu), ("hidT", ft, nb)], out=hidT[:, ft, c0:c0 + w], in0=C.ps[bu][:, 0:w], in1=sil[sb_][:, 0:w], op=ALU.mult)
        if alias:
            P.barrier()
            P.sb_reset(mA)
            wd = P.sb([128, 16, D], BF16, "wd")
            kw = "wd%d" % el
        else:
            wd = wd_s
            kw = "wd"
        for db in range(8):
            P.D(POOL, [], [(kw, db)], out=wd[:, :, db * 256:(db + 1) * 256], in_=w_ap("down", el)[:, db * 256:(db + 1) * 256].rearrange("(kt p) n -> p kt n", p=128))
        for b in range(nsamp):
            for j in range(5):
                if j == 4 and b > 0:
                    continue
                if j < 4:
                    col0, np_ = b * 512 + j * 128, 128
                    gcol = GT[:, ep, j, b:b + 1]
                else:
                    col0, np_ = cx0, nsamp * 32
                    gcol = GT[0:np_, ep, 4, 0:1]
                sb_ = nst % 2
                nst += 1
                ST = stg[sb_]
                for dk in range(4):
                    bk = C.bank()
                    for ft in range(16):
                        P.I(PE, "matmul", [("hidT", ft), (kw,)], [("ps", bk)], C.ps[bk][0:np_, :], lhsT=hidT[:, ft, col0:col0 + np_], rhs=wd[:, ft, dk * 512:(dk + 1) * 512], start=(ft == 0), stop=(ft == 15))
                    if dk % 2 == 0:
                        P.I(ACT, "activation", [("GT", ep), ("GTc", ep)], [("ps", bk), ("stg", sb_, dk)], out=ST[0:np_, dk * 512:(dk + 1) * 512], in_=C.ps[bk][0:np_, :], func=AF.Copy, scale=gcol)
                    else:
                        P.I(DVE, "tensor_scalar_mul", [("GT", ep), ("GTc", ep)], [("ps", bk), ("stg", sb_, dk)], out=ST[0:np_, dk * 512:(dk + 1) * 512], in0=C.ps[bk][0:np_, :], scalar1=gcol)
                if scat is not None:
                    P.dma(POOL, (lambda eng, ST=ST, np_=np_, ep=ep, j=j: eng.indirect_dma_start(
                        out=scat["FFN"].ap(), out_offset=bass.IndirectOffsetOnAxis(ap=TKi[0:np_, ep, j:j + 1], axis=0),
                        in_=ST[0:np_, :], in_offset=None, compute_op=ALU.add)), reads=[("stg", sb_), ("TKi", ep)], writes=[("FFN",)])
                elif j < 4:
                    P.D(SP, [("stg", sb_)], [("YE", el, b, j)], out=ye_ap(b, el, j * 128, (j + 1) * 128), in_=ST[:])
                else:
                    for b2 in range(nsamp):
                        P.D(SP, [("stg", sb_)], [("YE", el, b2, 4)], out=ye_ap(b2, el, 512, 544), in_=ST[b2 * 32:(b2 + 1) * 32, :])
        if alias:
            P.barrier()
    P.barrier()
    P.sb_reset(m0)


def phase_combine(P, C, X1, YEp, IDXC_in, OUT, MODV, ln_g, ln_b, t_lo, out_off, l=0):
    m0 = P.sb_mark()
    idx = P.sb([128, NT, 16], I32, "cidx")
    P.D(SP, [], ["cidx"], out=idx[:], in_=IDXC_in[:, :, :])
    gt2 = []
    for r in range(2):
        t = P.sb([128, D], F32, "gt2")
        bcast_load(P, SP, t, ("gt2", r), MODV[l, r, 5 * D:6 * D])
        gt2.append(t)
    g2 = P.sb([128, D], F32, "g2"); b2 = P.sb([128, D], F32, "b2")
    bcast_load(P, SP, g2, "g2", ln_g[l, :]); bcast_load(P, SP, b2, "b2", ln_b[l, :])
    NG = 4
    gb = [P.sb([128, D], F32, "gb") for _ in range(NG)]
    acc = [P.sb([128, D], F32, "acc") for _ in range(2)]
    xs = [P.sb([128, D], F32, "cxs") for _ in range(2)]
    st = P.sb([128, 4, 6], F32, "cst"); mv = P.sb([128, 2], F32, "cmv"); rs = P.sb([128, 2], F32, "crs")
    yv = YEp.ap().rearrange("e s w -> (e s) w")
    ng = 0
    for i in range(t_lo, NT):
        b = i % 2
        r = 1 if i < 2 else 0
        A, XS = acc[b], xs[b]
        kA, kXS = ("acc", b), ("cxs", b)
        P.D(SP, [("X1", i)], [kXS], out=XS[:], in_=X1[i * 128:(i + 1) * 128, :])
        for e in range(16):
            if e == 0:
                dst, kdst = A, kA
            else:
                gi = ng % NG
                ng += 1
                dst, kdst = gb[gi], ("gb", gi)
            P.dma(POOL, (lambda eng, dst=dst, i=i, e=e: eng.indirect_dma_start(
                out=dst[:], out_offset=None, in_=yv, in_offset=bass.IndirectOffsetOnAxis(ap=idx[:, i, e:e + 1], axis=0),
                element_offset=e * 545 * D)), reads=["cidx", ("YEp",)], writes=[kdst])
            if e > 0:
                P.I(DVE if e % 2 else POOL, "tensor_tensor", [kdst, kA], [kA], out=A[:], in0=A[:], in1=dst[:], op=ALU.add)
        P.I(DVE, "tensor_tensor", [kA, ("gt2", r)], [kA], out=A[:], in0=A[:], in1=gt2[r][:], op=ALU.mult)
        P.I(DVE, "scalar_tensor_tensor", [kXS, kA], [kA], out=A[:], in0=XS[:], scalar=ALPHA_, in1=A[:], op0=ALU.mult, op1=ALU.add)
        ln_apply(P, C, A, kA, A, kA, st, mv, rs, "c1", g2[:], "g2", b2[:], "b2")
        o0 = i * 128 - out_off
        P.D(SP, [kA], [("X", i)], out=OUT[o0:o0 + 128, :], in_=A[:])
    P.barrier()
    P.sb_reset(m0)


def phase_ln2(P, C, X1, FFN, OUT, MODV, ln_g, ln_b, t_lo, out_off, l):
    m0 = P.sb_mark()
    gt2 = []
    for r in range(2):
        t = P.sb([128, D], F32, "gt2")
        bcast_load(P, SP, t, ("gt2", r), MODV[l, r, 5 * D:6 * D])
        gt2.append(t)
    g2 = P.sb([128, D], F32, "g2"); b2 = P.sb([128, D], F32, "b2")
    bcast_load(P, SP, g2, "g2", ln_g[l, :]); bcast_load(P, SP, b2, "b2", ln_b[l, :])
    acc = [P.sb([128, D], F32, "acc") for _ in range(3)]
    xs = [P.sb([128, D], F32, "cxs") for _ in range(3)]
    st = P.sb([128, 4, 6], F32, "cst"); mv = P.sb([128, 2], F32, "cmv"); rs = P.sb([128, 2], F32, "crs")
    for i in range(t_lo, NT):
        b = i % 3
        r = 1 if i < 2 else 0
        A, XS = acc[b], xs[b]
        kA, kXS = ("acc", b), ("cxs", b)
        P.D(SP, [("X1", i)], [kXS], out=XS[:], in_=X1[i * 128:(i + 1) * 128, :])
        P.D(ACT, [("FFN",)], [kA], out=A[:], in_=FFN[i * 128:(i + 1) * 128, :])
        P.I(POOL, "tensor_tensor", [kA, ("gt2", r)], [kA], out=A[:], in0=A[:], in1=gt2[r][:], op=ALU.mult)
        P.I(DVE, "scalar_tensor_tensor", [kXS, kA], [kA], out=A[:], in0=XS[:], scalar=ALPHA_, in1=A[:], op0=ALU.mult, op1=ALU.add)
        ln_apply(P, C, A, kA, A, kA, st, mv, rs, "c1", g2[:], "g2", b2[:], "b2")
        o0 = i * 128 - out_off
        P.D(SP, [kA], [("X", i)], out=OUT[o0:o0 + 128, :], in_=A[:])
    P.barrier()
    P.sb_reset(m0)


def _host_consts():
    c = {}
    c["ident"] = np.eye(128, dtype=np.float32)
    kp = np.arange(128)[:, None]; qp = np.arange(128)[None, :]
    c["mprev"] = np.tile((kp >= qp).astype(np.float32), (1, 4))
    c["mnext"] = np.tile((kp <= qp).astype(np.float32), (1, 4))
    nf = 32
    inv = (10000.0 ** (-np.arange(nf, dtype=np.float32) / nf)).astype(np.float32)
    t = np.arange(4096)
    rows = (t // 64).astype(np.float32); cols = (t % 64).astype(np.float32)
    ang_r = rows[:, None] * inv[None, :]; ang_c = cols[:, None] * inv[None, :]
    cos = np.concatenate([np.cos(ang_r), np.cos(ang_r), np.cos(ang_c), np.cos(ang_c)], 1)
    sin = np.concatenate([-np.sin(ang_r), np.sin(ang_r), -np.sin(ang_c), np.sin(ang_c)], 1)
    c["cos"] = cos.reshape(32, 128, 128).astype(np.float32)
    c["sin"] = sin.reshape(32, 128, 128).astype(np.float32)
    s_ = np.arange(8, dtype=np.float32)
    er = np.concatenate([7 - s_, s_ - 7, s_ + 1, s_, -s_, 8 - s_]).astype(np.float32)
    c["erow"] = np.tile(er[None, :], (128, 1))
    E = np.zeros((8, 128, 240), np.float32)
    for r in range(8):
        for j in range(16):
            E[r, r * 16 + j, 7 * 16 + j] = 1.0
    c["E"] = E
    sb = np.arange(128)[:, None] // 16; tb = np.arange(128)[None, :] // 16
    c["toemf"] = (tb >= sb).astype(np.float32)
    c["toemb"] = (sb >= tb).astype(np.float32)
    a = np.arange(128)
    same = (a[:, None] // 64) == (a[None, :] // 64)
    le = a[:, None] <= a[None, :]
    ge = a[:, None] >= a[None, :]
    c["trif"] = (same & le).astype(np.float32) * (-1.0 / 16.0)
    c["trib"] = (same & ge).astype(np.float32) * (-1.0 / 16.0)
    c["blk"] = same.astype(np.float32) * (-1.0 / 16.0)
    c["cind"] = np.stack([(a < 64), (a >= 64)], 1).astype(np.float32) * (-1.0 / 16.0)
    c["gmaskf"] = np.tile((same & le).astype(np.float32), (1, 4))
    c["gmaskb"] = np.tile((same & ge).astype(np.float32), (1, 4))
    c["ones"] = np.ones((128, 128), np.float32)
    c["strict"] = (a[:, None] < a[None, :]).astype(np.float32)
    c["tgt"] = np.tile(np.concatenate([np.full(16, 32.0), np.full(16, 512.0)])[None, :], (128, 1)).astype(np.float32)
    c["tokid"] = (np.arange(34)[None, :] * 128 + np.arange(128)[:, None]).astype(np.float32)
    c["trash"] = (4352 + np.arange(5)[None, :] * 128 + np.arange(128)[:, None]).astype(np.float32)
    return c
HOSTC = _host_consts()


S5N = ["lam_re", "lam_im", "log_dt", "b_re", "b_im", "c_re", "c_im", "d", "w_glu", "b_glu"]
GLN = ["w_gate", "b_gate", "norm_g"]


def _decl_consts(P):
    return {k: P.dram("c_" + k, list(v.shape), F32, kind="ExternalInput") for k, v in HOSTC.items()}


def build_mod():
    P = Prog(); C = Ctx()
    cst = {"ident": P.dram("c_ident", [128, 128], F32, kind="ExternalInput")}
    c_in = P.dram("c_in", [5, D], F32, kind="ExternalInput")
    w_ada = P.dram("w_ada", [2, D, 1536], F32, kind="ExternalInput")
    b_ada = P.dram("b_ada", [2, 1536], F32, kind="ExternalInput")
    MODS = P.dram("MODS", [2, 5, 1536], F32, kind="ExternalOutput")
    setup_consts(P, C, cst)
    phase_mod(P, C, c_in, w_ada, b_ada, MODS, R=5, NBLK=3)
    P.finish()
    return P.emit()


def build_layer(with_combine, shapes):
    P = Prog(); C = Ctx()
    cst = _decl_consts(P)
    ins = {k: P.dram(k, list(s), F32, kind="ExternalInput") for k, s in shapes.items()}
    MODV = P.dram("MODV", [1, 2, 6 * D], F32, kind="ExternalInput")
    setup_consts(P, C, cst)
    if with_combine:
        X1p = P.dram("X1p", [NTOK, D], F32, kind="ExternalInput")
        YEp = P.dram("YEp", [16, 545, D], F32, kind="ExternalInput")
        IDXp = P.dram("IDXp", [128, NT, 16], I32, kind="ExternalInput")
        MODVp = P.dram("MODVp", [1, 2, 6 * D], F32, kind="ExternalInput")
        l2g = P.dram("ln2_g", [1, D], F32, kind="ExternalInput")
        l2b = P.dram("ln2_b", [1, D], F32, kind="ExternalInput")
        X = P.dram("X2s", [NTOK, D], F32)
        phase_combine(P, C, X1p, YEp, IDXp, X, MODVp, l2g, l2b, 0, 0)
    else:
        X = P.dram("xin", [NTOK, D], F32, kind="ExternalInput")
    PROJ = P.dram("PROJ", [NTOK, NIN], F32)
    MIXT = P.dram("MIXT", [16, 128, NTOK], BF16)
    OF = P.dram("OF", [NTOK, 512], F32)
    H2R = P.dram("H2R", [NTOK, ROWW], F32)
    X1 = P.dram("X1", [NTOK, D], F32, kind="ExternalOutput")
    XE = P.dram("XE", [16, NSLOT, ROWW], F32, kind="ExternalOutput")
    IDXC = P.dram("IDXC", [128, NT, 16], I32, kind="ExternalOutput")
    phase_inproj(P, C, 0, X, PROJ, ins["w_in"], MODV)
    phase_attn(P, C, 0, PROJ, MIXT, ins["attn_sink"], cst)
    phase_s5(P, C, 0, PROJ, MIXT, {n: ins["ssm_" + n] for n in S5N}, cst)
    phase_gla(P, C, 0, PROJ, MIXT, OF, {n: ins["gla_" + n] for n in GLN}, cst)
    AFF = P.sb([128, NT, 16], F32, "AFF")
    IDXS = P.sb([128, NT, 16], I32, "IDXS")
    phase_wout(P, C, 0, X, MIXT, X1, H2R, AFF, ins["w_out"], MODV, ins["ln1_g"], ins["ln1_b"], ins["router"])
    phase_route(P, C, AFF, IDXS, IDXC, cst)
    phase_scatter(P, C, H2R, XE, IDXS)
    P.finish()
    return P.emit()


def build_experts():
    P = Prog(); C = Ctx()
    cst = {"ident": P.dram("c_ident", [128, 128], F32, kind="ExternalInput")}
    XEc = P.dram("XEc", [4, 2, NSLOT, ROWW], F32, kind="ExternalInput")
    GATE = P.dram("GATE", [4, 2, NSLOT], F32, kind="ExternalInput")
    WG = P.dram("WG", [2, D, D], F32, kind="ExternalInput")
    WU = P.dram("WU", [2, D, D], F32, kind="ExternalInput")
    WD = P.dram("WD", [2, D, D], F32, kind="ExternalInput")
    YE = P.dram("YE", [4, 2, NSLOT, D], F32, kind="ExternalOutput")
    setup_consts(P, C, cst)
    wmap = {"gate": WG, "up": WU, "down": WD}
    phase_experts(P, C, 4, 2, lambda b, el, r0, r1: XEc[b, el, r0:r1, 0:D], lambda b, el, r0, r1: GATE[b, el, r0:r1],
                  lambda kind, el: wmap[kind][el], lambda b, el, r0, r1: YE[b, el, r0:r1, :])
    P.finish()
    return P.emit()


def build_final():
    P = Prog(); C = Ctx()
    cst = {"ident": P.dram("c_ident", [128, 128], F32, kind="ExternalInput")}
    X1p = P.dram("X1p", [NTOK, D], F32, kind="ExternalInput")
    YEp = P.dram("YEp", [16, 545, D], F32, kind="ExternalInput")
    IDXp = P.dram("IDXp", [128, NT, 16], I32, kind="ExternalInput")
    MODVp = P.dram("MODVp", [1, 2, 6 * D], F32, kind="ExternalInput")
    l2g = P.dram("ln2_g", [1, D], F32, kind="ExternalInput")
    l2b = P.dram("ln2_b", [1, D], F32, kind="ExternalInput")
    OUT = P.dram("OUT", [4096, D], F32, kind="ExternalOutput")
    setup_consts(P, C, cst)
    phase_combine(P, C, X1p, YEp, IDXp, OUT, MODVp, l2g, l2b, 2, 256)
    P.finish()
    return P.emit()


LAYER_KEYS = ["w_in", "attn_sink", "ssm_lam_re", "ssm_lam_im", "ssm_log_dt", "ssm_b_re", "ssm_b_im", "ssm_c_re", "ssm_c_im",
              "ssm_d", "ssm_w_glu", "ssm_b_glu", "gla_w_gate", "gla_b_gate", "gla_norm_g", "w_out", "ln1_g", "ln1_b", "router"]


def kernel_multi(**inp):
    inp = {k: np.ascontiguousarray(np.asarray(v)) for k, v in inp.items()}
    f32 = np.float32
    cmap = {"c_" + k: v for k, v in HOSTC.items()}
    ident = {"c_ident": HOSTC["ident"]}
    c_in = np.concatenate([inp["c"], inp["c_ctx"][None]], 0).astype(f32)
    maps = []
    for c in range(8):
        sl = slice(c * 1536, (c + 1) * 1536)
        maps.append(dict(ident, c_in=c_in, w_ada=np.ascontiguousarray(inp["w_ada"][:, :, sl]), b_ada=np.ascontiguousarray(inp["b_ada"][:, sl])))
    res = run_bass_kernel_spmd(build_mod(), maps, core_ids=list(range(8)))
    mods = np.concatenate([r["MODS"] for r in res.results], axis=2)
    modv = [[np.ascontiguousarray(np.stack([mods[l, b], mods[l, 4]], 0)[None]) for l in range(2)] for b in range(4)]
    shapes = {k: (1,) + inp[k].shape[1:] for k in LAYER_KEYS}
    prev = None
    nc_exp = None
    for l in range(2):
        lw = {k: np.ascontiguousarray(inp[k][l:l + 1]) for k in LAYER_KEYS}
        maps = []
        for b in range(4):
            m = dict(cmap); m.update(lw); m["MODV"] = modv[b][l]
            if l == 0:
                m["xin"] = np.concatenate([inp["ctx"][b], inp["x"][b]], 0)
            else:
                m.update(prev[b])
            maps.append(m)
        res = run_bass_kernel_spmd(build_layer(l > 0, shapes), maps, core_ids=list(range(4)))
        X1 = [res.results[b]["X1"] for b in range(4)]
        XE = [res.results[b]["XE"] for b in range(4)]
        IDX = [res.results[b]["IDXC"] for b in range(4)]
        maps = []
        for c in range(8):
            xec = np.stack([XE[b][2 * c:2 * c + 2] for b in range(4)], 0)
            gate = np.stack([np.stack([XE[b][2 * c + el, :, 2048 + 2 * c + el] for el in range(2)], 0) for b in range(4)], 0)
            maps.append(dict(ident, XEc=np.ascontiguousarray(xec), GATE=np.ascontiguousarray(gate),
                             WG=np.ascontiguousarray(inp["exp_w_gate"][l, 2 * c:2 * c + 2]),
                             WU=np.ascontiguousarray(inp["exp_w_up"][l, 2 * c:2 * c + 2]),
                             WD=np.ascontiguousarray(inp["exp_w_down"][l, 2 * c:2 * c + 2])))
        if nc_exp is None:
            nc_exp = build_experts()
        res = run_bass_kernel_spmd(nc_exp if l == 0 else build_experts(), maps, core_ids=list(range(8)))
        prev = []
        for b in range(4):
            yep = np.zeros((16, 545, D), f32)
            for c in range(8):
                yep[2 * c:2 * c + 2, :544] = res.results[c]["YE"][b]
            prev.append({"X1p": X1[b], "YEp": yep, "IDXp": IDX[b], "MODVp": modv[b][l],
                         "ln2_g": np.ascontiguousarray(inp["ln2_g"][l:l + 1]), "ln2_b": np.ascontiguousarray(inp["ln2_b"][l:l + 1])})
    maps = [dict(ident, **prev[b]) for b in range(4)]
    res = run_bass_kernel_spmd(build_final(), maps, core_ids=list(range(4)))
    return np.stack([res.results[b]["OUT"] for b in range(4)], 0).astype(f32)


FUSED_KEYS = LAYER_KEYS + ["ln2_g", "ln2_b", "w_ada", "b_ada", "exp_w_gate", "exp_w_up", "exp_w_down"]


def build_fused(shapes):
    P = Prog(); C = Ctx()
    cst = _decl_consts(P)
    ins = {k: P.dram(k, list(shapes[k]), F32, kind="ExternalInput") for k in FUSED_KEYS}
    xin = P.dram("xin", [NTOK, D], F32, kind="ExternalInput")
    c_in = P.dram("c_in", [2, D], F32, kind="ExternalInput")
    OUT = P.dram("OUT", [4096, D], F32, kind="ExternalOutput")
    MODV = P.dram("MODV", [2, 2, 6 * D], F32)
    PROJ = P.dram("PROJ", [NTOK, NIN], F32)
    MIXT = P.dram("MIXT", [16, 128, NTOK], BF16)
    OF = P.dram("OF", [NTOK, 512], F32)
    H2R = P.dram("H2R", [NTOK, ROWW], F32)
    X1 = P.dram("X1", [NTOK, D], F32)
    XN = P.dram("XN", [NTOK, D], F32)
    XE = P.dram("XE", [16, NSLOT, ROWW], F32)
    FFN = P.dram("FFN", [NTOK + NSLOT, D], F32)
    IDXC = P.dram("IDXC", [128, NT, 16], I32)
    setup_consts(P, C, cst)
    phase_mod(P, C, c_in, ins["w_ada"], ins["b_ada"], MODV, R=2, NBLK=24)
    X = xin
    for l in range(2):
        phase_inproj(P, C, l, X, PROJ, ins["w_in"], MODV)
        phase_attn(P, C, l, PROJ, MIXT, ins["attn_sink"], cst)
        phase_s5(P, C, l, PROJ, MIXT, {n: ins["ssm_" + n] for n in S5N}, cst)
        phase_gla(P, C, l, PROJ, MIXT, OF, {n: ins["gla_" + n] for n in GLN}, cst)
        m0 = P.sb_mark()
        AFF = P.sb([128, NT, 16], F32, "AFF")
        IDXS = P.sb([128, NT, 16], I32, "IDXS")
        phase_wout(P, C, l, X, MIXT, X1, H2R, AFF, ins["w_out"], MODV, ins["ln1_g"], ins["ln1_b"], ins["router"], cst)
        phase_route(P, C, AFF, IDXS, IDXC, cst)
        phase_scatter(P, C, H2R, XE, IDXS, cst)
        P.sb_reset(m0)
        phase_experts(P, C, 1, 16, lambda b, el, r0, r1: XE[el, r0:r1, 0:D], lambda b, el, r0, r1: XE[el, r0:r1, 2048 + el],
                      lambda kind, el, l=l: ins["exp_w_" + kind][l, el], None,
                      scat={"FFN": FFN, "tok_ap": lambda b, el, r0, r1: XE[el, r0:r1, 2064]})
        if l == 0:
            phase_ln2(P, C, X1, FFN, XN, MODV, ins["ln2_g"], ins["ln2_b"], 0, 0, l)
            X = XN
        else:
            phase_ln2(P, C, X1, FFN, OUT, MODV, ins["ln2_g"], ins["ln2_b"], 2, 256, l)
    P.finish()
    nc = P.emit()
    return nc


def fused_maps(inp, samples):
    cmap = {"c_" + k: v for k, v in HOSTC.items()}
    shared = {k: inp[k] for k in FUSED_KEYS}
    maps = []
    for b in samples:
        m = dict(cmap); m.update(shared)
        m["xin"] = np.concatenate([inp["ctx"][b], inp["x"][b]], 0)
        m["c_in"] = np.ascontiguousarray(np.stack([inp["c"][b], inp["c_ctx"]], 0))
        maps.append(m)
    return maps


def kernel(**inp):
    inp = {k: np.ascontiguousarray(np.asarray(v)) for k, v in inp.items()}
    shapes = {k: inp[k].shape for k in FUSED_KEYS}
    nc = build_fused(shapes)
    maps = fused_maps(inp, [c % 4 for c in range(8)])
    res = run_bass_kernel_spmd(nc, maps, core_ids=list(range(8)))
    return np.stack([res.results[b]["OUT"] for b in range(4)], 0).astype(np.float32)
```
